# Optimizing a Trainium2 kernel written in Bass

```python
import math
import jax, jax.numpy as jnp
from jax import lax
import numpy as np

D_MODEL = 2048
BATCH = 4
SEQ = 4096
DEPTH = 2

N_EVEN = (DEPTH + 1) // 2
N_ODD = DEPTH // 2
EPS = 1e-6
D_FF = 5632

A_WIDTH = D_MODEL // 2
A_CHUNK = 128
A_GROUPS = 8
A_GROUP_DIM = A_WIDTH // A_GROUPS

B_HEADS = 4
B_WIDTH = D_MODEL // 2
B_HEAD_DIM = B_WIDTH // B_HEADS
B_CHUNK = 128
B_CONV = 4

P_AB = 2 * A_WIDTH + 4 * B_WIDTH + 2 * B_HEADS

C_HEADS = 16
C_KV = 4
C_REP = C_HEADS // C_KV
C_HEAD_DIM = D_MODEL // C_HEADS
CMP_LEN = 32
CMP_STRIDE = 16
SEL_LEN = 64
SEL_TOPK = 16
WINDOW = 512
C_QBLOCK = 32
P_C = C_HEADS * C_HEAD_DIM + 3 * 2 * C_KV * C_HEAD_DIM + 3 * C_HEADS

ROPE_THETA = 500000.0
ROPE_DIMS = C_HEAD_DIM // 4

NEG_INF = -1e30
BIG = 1e9

kernel_name = "hybrid_gmlp_mlstm_nsa_macaron"


def rmsnorm(x, g):
    xf = x.astype(jnp.float32)
    y = xf * lax.rsqrt(jnp.mean(xf * xf, axis=-1, keepdims=True) + EPS)
    return (y * g.astype(jnp.float32)).astype(x.dtype)


def swiglu(x, w_gate, w_up, w_down):
    return (jax.nn.silu(x @ w_gate) * (x @ w_up)) @ w_down


def partial_rope(x):
    S = x.shape[1]
    half = ROPE_DIMS // 2
    inv = 1.0 / (ROPE_THETA ** (jnp.arange(half, dtype=jnp.float32) / half))
    ang = jnp.arange(S, dtype=jnp.float32)[:, None] * inv[None, :]
    cos = jnp.cos(ang)[None, :, None, :]
    sin = jnp.sin(ang)[None, :, None, :]
    xr = x[..., :ROPE_DIMS].astype(jnp.float32)
    x1, x2 = xr[..., :half], xr[..., half:]
    rot = jnp.concatenate([x1 * cos - x2 * sin, x1 * sin + x2 * cos], axis=-1).astype(x.dtype)
    return jnp.concatenate([rot, x[..., ROPE_DIMS:]], axis=-1)


def causal_depthwise_conv(x, w):
    K, C = w.shape
    return lax.conv_general_dilated(x, w[:, None, :].astype(x.dtype), window_strides=(1,),
                                    padding=[(K - 1, 0)], dimension_numbers=('NWC', 'WIO', 'NWC'),
                                    feature_group_count=C)


def masked_softmax(s, valid):
    p = jax.nn.softmax(jnp.where(valid, s, NEG_INF), axis=-1)
    return jnp.where(valid, p, 0.0)


def gmlp_chunk_gating(z, norm_g, w_s, b_s):
    B_, S, _ = z.shape
    z = jax.nn.gelu(z)
    u, v = jnp.split(z, 2, axis=-1)
    v = rmsnorm(v, norm_g)
    nc = S // A_CHUNK
    v = v.reshape(B_, nc, A_CHUNK, A_GROUPS, A_GROUP_DIM)
    mask = jnp.tril(jnp.ones((A_CHUNK, A_CHUNK), dtype=bool))
    w = jnp.where(mask, w_s, 0.0).astype(v.dtype)
    sv = jnp.einsum('gts,bcsge->bctge', w, v) + b_s.T.astype(v.dtype)[None, None, :, :, None]
    return u * sv.reshape(B_, S, A_WIDTH)


def mlstm_chunkwise(q, k, v, i_pre, f_pre):
    B_, S, H, d = q.shape
    L = B_CHUNK
    nc = S // L
    f32 = jnp.float32
    def chunks(a):
        return a.astype(f32).reshape(B_, nc, L, H, d).transpose(1, 0, 3, 2, 4)
    def gchunks(a):
        return a.astype(f32).reshape(B_, nc, L, H).transpose(1, 0, 3, 2)
    qc, kc, vc = chunks(q), chunks(k) * (d ** -0.5), chunks(v)
    ic, lfc = gchunks(i_pre), gchunks(jax.nn.log_sigmoid(f_pre.astype(f32)))
    causal = jnp.tril(jnp.ones((L, L), dtype=bool))

    def step(carry, inp):
        C, n, m = carry
        qb, kb, vb, ib, lfb = inp
        b = jnp.cumsum(lfb, axis=-1)
        logD = jnp.where(causal, b[..., :, None] - b[..., None, :] + ib[..., None, :], -jnp.inf)
        inter = b + m[..., None]
        m_t = jnp.maximum(jnp.max(logD, axis=-1), inter)
        Dm = jnp.exp(logD - m_t[..., None])
        w_inter = jnp.exp(inter - m_t)
        s = jnp.einsum('bhtd,bhsd->bhts', qb, kb) * Dm
        num = jnp.einsum('bhts,bhse->bhte', s, vb) + w_inter[..., None] * jnp.einsum('bhed,bhtd->bhte', C, qb)
        den = jnp.sum(s, axis=-1) + w_inter * jnp.einsum('bhd,bhtd->bht', n, qb)
        h = num / jnp.maximum(jnp.abs(den), jnp.exp(-m_t))[..., None]
        m_new = m_t[..., -1]
        w_s = jnp.exp(b[..., -1:] - b + ib - m_new[..., None])
        w_prev = jnp.exp(b[..., -1] + m - m_new)
        C_new = w_prev[..., None, None] * C + jnp.einsum('bhs,bhse,bhsd->bhed', w_s, vb, kb)
        n_new = w_prev[..., None] * n + jnp.einsum('bhs,bhsd->bhd', w_s, kb)
        return (C_new, n_new, m_new), h

    init = (jnp.zeros((B_, H, d, d), f32), jnp.zeros((B_, H, d), f32), jnp.zeros((B_, H), f32))
    _, hs = lax.scan(step, init, (qc, kc, vc, ic, lfc))
    return hs.transpose(1, 0, 3, 2, 4).reshape(B_, S, H, d)


def gmlp_mlstm_mixer(h, w_in, gate_b, g_norm, g_ws, g_bs, conv_w, m_norm, w_out):
    B_, S, _ = h.shape
    p = h @ w_in
    o1 = 2 * A_WIDTH
    o2 = o1 + 2 * B_WIDTH
    o3 = o2 + B_WIDTH
    o4 = o3 + B_WIDTH
    z_a, qk, v, o, gates = jnp.split(p, [o1, o2, o3, o4], axis=-1)
    y_a = gmlp_chunk_gating(z_a, g_norm, g_ws, g_bs)
    qk = jax.nn.silu(causal_depthwise_conv(qk, conv_w))
    q, k = jnp.split(qk, 2, axis=-1)
    gates = gates.reshape(B_, S, 2, B_HEADS) + gate_b.astype(gates.dtype)
    shp = (B_, S, B_HEADS, B_HEAD_DIM)
    ht = mlstm_chunkwise(q.reshape(shp), k.reshape(shp), v.reshape(shp), gates[:, :, 0], gates[:, :, 1])
    ht = rmsnorm(ht, m_norm.reshape(B_HEADS, B_HEAD_DIM)).reshape(B_, S, B_WIDTH)
    y_b = (jax.nn.sigmoid(o.astype(jnp.float32)) * ht).astype(h.dtype)
    return jnp.concatenate([y_a, y_b], axis=-1) @ w_out


def compress_blocks(x, pos_emb, w1, w2):
    B_, S, G, dh = x.shape
    n_cmp = (S - CMP_LEN) // CMP_STRIDE + 1
    idx = jnp.arange(n_cmp)[:, None] * CMP_STRIDE + jnp.arange(CMP_LEN)[None, :]
    blocks = x[:, idx] + pos_emb.astype(x.dtype)[None, None, :, None, :]
    flat = blocks.transpose(0, 1, 3, 2, 4).reshape(B_, n_cmp, G, CMP_LEN * dh)
    return jax.nn.gelu(flat @ w1) @ w2


def gather_blocks(blocks, idx):
    return jax.vmap(jax.vmap(lambda bl, ix: bl[ix]))(blocks, idx)


def nsa_mixer(h, w_in, cmp_pos, cmp_w1, cmp_w2, w_out):
    B_, S, _ = h.shape
    G, R, dh = C_KV, C_REP, C_HEAD_DIM
    f32 = jnp.float32
    p = h @ w_in
    q, kvs, g = jnp.split(p, [C_HEADS * dh, C_HEADS * dh + 6 * G * dh], axis=-1)
    q = partial_rope(q.reshape(B_, S, C_HEADS, dh)) * (dh ** -0.5)
    kvs = kvs.reshape(B_, S, 3, 2, G, dh)
    gates = jax.nn.sigmoid(g.astype(f32)).reshape(B_, S, 3, G, R).astype(h.dtype)

    k_cmp = compress_blocks(partial_rope(kvs[:, :, 0, 0]), cmp_pos[0], cmp_w1[0], cmp_w2[0])
    v_cmp = compress_blocks(kvs[:, :, 0, 1], cmp_pos[1], cmp_w1[1], cmp_w2[1])
    n_cmp = k_cmp.shape[1]
    cmp_end = jnp.arange(n_cmp) * CMP_STRIDE + CMP_LEN - 1

    n_slc = S // SEL_LEN
    n_top = min(SEL_TOPK, n_slc)
    k_sel = partial_rope(kvs[:, :, 1, 0]).reshape(B_, n_slc, SEL_LEN, G, dh).transpose(0, 3, 1, 2, 4)
    v_sel = kvs[:, :, 1, 1].reshape(B_, n_slc, SEL_LEN, G, dh).transpose(0, 3, 1, 2, 4)
    ci = jnp.arange(n_cmp)[:, None] * CMP_STRIDE
    sj = jnp.arange(n_slc)[None, :] * SEL_LEN
    overlap = jnp.clip(jnp.minimum(ci + CMP_LEN, sj + SEL_LEN) - jnp.maximum(ci, sj), 0).astype(f32) / CMP_LEN
    blk = jnp.arange(n_slc)

    pad = ((0, 0), (WINDOW, 0), (0, 0), (0, 0))
    k_win = jnp.pad(partial_rope(kvs[:, :, 2, 0]), pad)
    v_win = jnp.pad(kvs[:, :, 2, 1], pad)

    def q_block(i):
        t0 = i * C_QBLOCK
        pos = t0 + jnp.arange(C_QBLOCK)
        qb = lax.dynamic_slice_in_dim(q, t0, C_QBLOCK, axis=1).reshape(B_, C_QBLOCK, G, R, dh)
        gb = lax.dynamic_slice_in_dim(gates, t0, C_QBLOCK, axis=1)
        s_c = jnp.einsum('bqgrd,bcgd->bgrqc', qb, k_cmp).astype(f32)
        p_c = masked_softmax(s_c, cmp_end[None, :] <= pos[:, None])
        o_c = jnp.einsum('bgrqc,bcgd->bqgrd', p_c.astype(v_cmp.dtype), v_cmp)
        imp = jnp.einsum('bgrqc,cj->bgqj', p_c, overlap)
        cur = pos // SEL_LEN
        forced = (blk[None, :] == 0) | (blk[None, :] == cur[:, None]) | (blk[None, :] == cur[:, None] - 1)
        future = blk[None, :] > cur[:, None]
        imp = jnp.where(forced, BIG, jnp.where(future, -BIG, imp))
        _, top = lax.top_k(imp, n_top)
        ks = gather_blocks(k_sel, top).reshape(B_, G, C_QBLOCK, n_top * SEL_LEN, dh)
        vs = gather_blocks(v_sel, top).reshape(B_, G, C_QBLOCK, n_top * SEL_LEN, dh)
        tok = (top[..., None] * SEL_LEN + jnp.arange(SEL_LEN)).reshape(B_, G, C_QBLOCK, n_top * SEL_LEN)
        valid_s = (tok <= pos[None, None, :, None])[:, :, None]
        s_s = jnp.einsum('bqgrd,bgqnd->bgrqn', qb, ks).astype(f32)
        p_s = masked_softmax(s_s, valid_s)
        o_s = jnp.einsum('bgrqn,bgqnd->bqgrd', p_s.astype(vs.dtype), vs)
        kw = lax.dynamic_slice_in_dim(k_win, t0, WINDOW + C_QBLOCK, axis=1)
        vw = lax.dynamic_slice_in_dim(v_win, t0, WINDOW + C_QBLOCK, axis=1)
        kpos = t0 - WINDOW + jnp.arange(WINDOW + C_QBLOCK)
        valid_w = (kpos[None, :] <= pos[:, None]) & (pos[:, None] - kpos[None, :] < WINDOW) & (kpos[None, :] >= 0)
        s_w = jnp.einsum('bqgrd,bkgd->bgrqk', qb, kw).astype(f32)
        p_w = masked_softmax(s_w, valid_w)
        o_w = jnp.einsum('bgrqk,bkgd->bqgrd', p_w.astype(vw.dtype), vw)
        out = gb[:, :, 0, :, :, None] * o_c + gb[:, :, 1, :, :, None] * o_s + gb[:, :, 2, :, :, None] * o_w
        return out.reshape(B_, C_QBLOCK, C_HEADS * dh)

    outs = lax.map(q_block, jnp.arange(S // C_QBLOCK))
    y = outs.transpose(1, 0, 2, 3).reshape(B_, S, C_HEADS * dh)
    return y @ w_out


def setup_inputs(seed: int = 0) -> dict:
    key = jax.random.key(seed)
    ks = jax.random.split(key, 24)
    nrm = lambda k, shape, s: jax.random.normal(k, shape, jnp.float32) * s
    f_bias = jnp.broadcast_to(jnp.linspace(3.0, 6.0, B_HEADS, dtype=jnp.float32), (N_EVEN, B_HEADS))
    return {
        "x": nrm(ks[0], (BATCH, SEQ, D_MODEL), 1.0),
        "ffn_norm": 1.0 + nrm(ks[1], (DEPTH, 2, D_MODEL), 0.01),
        "ffn_w_gate": nrm(ks[2], (DEPTH, 2, D_MODEL, D_FF), D_MODEL ** -0.5),
        "ffn_w_up": nrm(ks[3], (DEPTH, 2, D_MODEL, D_FF), D_MODEL ** -0.5),
        "ffn_w_down": nrm(ks[4], (DEPTH, 2, D_FF, D_MODEL), D_FF ** -0.5),
        "mix_norm": 1.0 + nrm(ks[5], (DEPTH, D_MODEL), 0.01),
        "ab_w_in": nrm(ks[6], (N_EVEN, D_MODEL, P_AB), D_MODEL ** -0.5),
        "mlstm_gate_bias": jnp.stack([nrm(ks[7], (N_EVEN, B_HEADS), 0.1),
                                      f_bias + nrm(ks[8], (N_EVEN, B_HEADS), 0.1)], axis=1),
        "gmlp_norm": 1.0 + nrm(ks[9], (N_EVEN, A_WIDTH), 0.01),
        "gmlp_w_s": nrm(ks[10], (N_EVEN, A_GROUPS, A_CHUNK, A_CHUNK), A_CHUNK ** -0.5),
        "gmlp_b_s": 1.0 + nrm(ks[11], (N_EVEN, A_GROUPS, A_CHUNK), 0.02),
        "mlstm_conv": nrm(ks[12], (N_EVEN, B_CONV, 2 * B_WIDTH), B_CONV ** -0.5),
        "mlstm_norm": 1.0 + nrm(ks[13], (N_EVEN, B_WIDTH), 0.01),
        "ab_w_out": nrm(ks[14], (N_EVEN, A_WIDTH + B_WIDTH, D_MODEL), (A_WIDTH + B_WIDTH) ** -0.5),
        "nsa_w_in": nrm(ks[15], (N_ODD, D_MODEL, P_C), D_MODEL ** -0.5),
        "nsa_cmp_pos": nrm(ks[16], (N_ODD, 2, CMP_LEN, C_HEAD_DIM), 0.02),
        "nsa_cmp_w1": nrm(ks[17], (N_ODD, 2, CMP_LEN * C_HEAD_DIM, C_HEAD_DIM), (CMP_LEN * C_HEAD_DIM) ** -0.5),
        "nsa_cmp_w2": nrm(ks[18], (N_ODD, 2, C_HEAD_DIM, C_HEAD_DIM), C_HEAD_DIM ** -0.5),
        "nsa_w_out": nrm(ks[19], (N_ODD, C_HEADS * C_HEAD_DIM, D_MODEL), (C_HEADS * C_HEAD_DIM) ** -0.5),
        "final_norm": 1.0 + nrm(ks[20], (D_MODEL,), 0.01),
    }


def reference(x, ffn_norm, ffn_w_gate, ffn_w_up, ffn_w_down, mix_norm, ab_w_in, mlstm_gate_bias,
              gmlp_norm, gmlp_w_s, gmlp_b_s, mlstm_conv, mlstm_norm, ab_w_out, nsa_w_in, nsa_cmp_pos,
              nsa_cmp_w1, nsa_cmp_w2, nsa_w_out, final_norm):
    h = x
    for layer in range(DEPTH):
        j = layer // 2
        h = h + 0.5 * swiglu(rmsnorm(h, ffn_norm[layer, 0]), ffn_w_gate[layer, 0], ffn_w_up[layer, 0], ffn_w_down[layer, 0])
        hn = rmsnorm(h, mix_norm[layer])
        if layer % 2 == 0:
            h = h + gmlp_mlstm_mixer(hn, ab_w_in[j], mlstm_gate_bias[j], gmlp_norm[j], gmlp_w_s[j], gmlp_b_s[j],
                                     mlstm_conv[j], mlstm_norm[j], ab_w_out[j])
        else:
            h = h + nsa_mixer(hn, nsa_w_in[j], nsa_cmp_pos[j], nsa_cmp_w1[j], nsa_cmp_w2[j], nsa_w_out[j])
        h = h + 0.5 * swiglu(rmsnorm(h, ffn_norm[layer, 1]), ffn_w_gate[layer, 1], ffn_w_up[layer, 1], ffn_w_down[layer, 1])
    return rmsnorm(h, final_norm)
```

```python
import bisect
import contextlib
import numpy as np
import concourse.bass as bass
import concourse.mybir as mybir
from concourse.bass_utils import run_bass_kernel_spmd

F32 = mybir.dt.float32
F32R = mybir.dt.float32r
BF16 = mybir.dt.bfloat16
AF = mybir.ActivationFunctionType
ALU = mybir.AluOpType
AX = mybir.AxisListType

ENGS = ("pe", "act", "dve", "pool", "sp")


class Tile:
    __slots__ = ("name", "w", "r", "psum")

    def __init__(self, name="", psum=False):
        self.name = name
        self.w = None
        self.r = []
        self.psum = psum


class Chan:
    __slots__ = ("sem", "cnt", "ops")

    def __init__(self):
        self.sem = None
        self.cnt = 0
        self.ops = []


class Sched:
    def __init__(self, nc):
        self.nc = nc
        self.ops = []
        self.chans = []

    def chan(self):
        c = Chan()
        self.chans.append(c)
        return c

    def op(self, eng, fn, reads=(), writes=(), chan=None, nosync_same=False):
        idx = len(self.ops)
        deps = set()
        for t in reads:
            if t.w is not None:
                deps.add(t.w)
            if t.psum:
                for r in t.r:
                    if self.ops[r]["eng"] != eng:
                        deps.add(r)
        for t in writes:
            if t.w is not None:
                deps.add(t.w)
            for r in t.r:
                deps.add(r)
        deps.discard(idx)
        rec = dict(eng=eng, fn=fn, deps=deps, chan=chan, idx=idx, nosync_same=nosync_same,
                   sig=None)
        if chan is not None:
            chan.cnt += 16
            rec["sig"] = (chan, chan.cnt)
            chan.ops.append(idx)
        self.ops.append(rec)
        for t in reads:
            t.r.append(idx)
        for t in writes:
            t.w = idx
            t.r = []
        return idx

    def emit(self, final_waits=()):
        nc = self.nc
        ops = self.ops
        needed = set()
        for o in ops:
            for d in o["deps"]:
                od = ops[d]
                if od["chan"] is None:
                    if od["eng"] == o["eng"] and (o["nosync_same"] or od["eng"] == "pe" or od["eng"] == "sp"):
                        continue
                    needed.add(d)
        cnt = {e: 0 for e in ENGS}
        for o in ops:
            if o["chan"] is None and o["idx"] in needed:
                cnt[o["eng"]] += 1
                o["sig"] = (o["eng"], cnt[o["eng"]])
        with contextlib.ExitStack() as st:
            esem = {e: st.enter_context(nc.semaphore("s_" + e)) for e in ENGS}
            for i, c in enumerate(self.chans):
                c.sem = st.enter_context(nc.semaphore("c%d" % i))
            block = st.enter_context(nc.Block())

            def run_engine(ename, eobj):
                known = {}
                for o in ops:
                    if o["eng"] != ename:
                        continue
                    want = {}
                    for d in o["deps"]:
                        od = ops[d]
                        sig = od["sig"]
                        if sig is None:
                            continue
                        key, val = sig
                        if od["chan"] is None and od["eng"] == ename and (
                                o["nosync_same"] or ename in ("pe", "sp")):
                            continue
                        if isinstance(key, Chan):
                            val = 16 * bisect.bisect_left(key.ops, o["idx"])
                        if want.get(key, 0) < val:
                            want[key] = val
                    for key, val in want.items():
                        if known.get(key, 0) >= val:
                            continue
                        known[key] = val
                        sem = key.sem if isinstance(key, Chan) else esem[key]
                        eobj.wait_ge(sem, val)
                    ins = o["fn"](eobj)
                    if o["sig"] is not None:
                        key, val = o["sig"]
                        if isinstance(key, Chan):
                            ins.then_inc(key.sem, 16)
                        else:
                            ins.then_inc(esem[key], 1)
                if ename == "sp":
                    for c in final_waits:
                        eobj.wait_ge(c.sem, c.cnt)

            @block.tensor
            def _(e):
                run_engine("pe", e)

            @block.scalar
            def _(e):
                run_engine("act", e)

            @block.vector
            def _(e):
                run_engine("dve", e)

            @block.gpsimd
            def _(e):
                run_engine("pool", e)

            @block.sync
            def _(e):
                run_engine("sp", e)


class Ctx:
    def __init__(self):
        self.nc = bass.Bass("TRN2", target_bir_lowering=False)
        self.st = contextlib.ExitStack()
        self.S = Sched(self.nc)
        self.n = 0

    def sb(self, shape, dt, name=None):
        self.n += 1
        return self.st.enter_context(self.nc.sbuf_tensor("%s_%d" % (name or "sb", self.n), list(shape), dt))

    def ps(self, shape=(128, 512), dt=F32, name=None):
        self.n += 1
        return self.st.enter_context(self.nc.psum_tensor("%s_%d" % (name or "ps", self.n), list(shape), dt))

    def din(self, name, shape, dt=F32):
        return self.nc.dram_tensor(name, list(shape), dt, kind="ExternalInput").ap()

    def dout(self, name, shape, dt=F32):
        return self.nc.dram_tensor(name, list(shape), dt, kind="ExternalOutput").ap()


EPS = 1e-6


def emit_rstd(S, rstd, trstd, ps_ssq, tssq, Dn, SUB=512):
    S.op("act", lambda e: e.activation(out=rstd[:, 0:SUB], in_=ps_ssq[:, 0:SUB], func=AF.Sqrt, bias=EPSB[0][:, 0:1], scale=1.0 / Dn),
         reads=[tssq, EPSB[1]], writes=[trstd])
    S.op("dve", lambda e: e.reciprocal(out=rstd[:, 0:SUB], in_=rstd[:, 0:SUB]), reads=[trstd], writes=[trstd])


EPSB = [None, None]


def emit_consts(C):
    S = C.S
    eps = C.sb([128, 1], F32, "eps")
    teps = Tile()
    S.op("dve", lambda e: e.memset(eps[:], EPS), writes=[teps])
    EPSB[0], EPSB[1] = eps, teps
    ones32 = C.sb([128, 128], F32, "ones32")
    ones_r = C.sb([128, 128], F32R, "ones")
    t32, tones = Tile(), Tile()
    S.op("dve", lambda e: e.memset(ones32[:], 1.0), writes=[t32])
    S.op("dve", lambda e: e.tensor_copy(out=ones_r[:], in_=ones32[:]), reads=[t32], writes=[tones])
    return ones_r, tones


def emit_rmsnorm_T(C, h_sb, th, aT, taT, g_sb, tg, ones_r, tones, DC, TB, sq, tsq, ps_ssq, tssq, rstd, trstd,
                   Dn, out_dt_cast=None):
    S = C.S
    SUB = min(512, TB)
    NS = TB // SUB
    for s in range(NS):
        sl = slice(s * SUB, (s + 1) * SUB)
        for c in range(DC):
            k = (s * DC + c) % 2
            S.op("act", lambda e, c=c, k=k, sl=sl: e.activation(out=sq[k][:, 0:SUB], in_=h_sb[:, c, sl], func=AF.Square),
                 reads=[th[c][s]], writes=[tsq[k]])
            S.op("pe", lambda e, c=c, k=k: e.matmul(ps_ssq[:, 0:SUB], ones_r[:], sq[k][:, 0:SUB], start=(c == 0), stop=(c == DC - 1)),
                 reads=[tsq[k], tones], writes=[tssq])
        emit_rstd(S, rstd, trstd, ps_ssq, tssq, Dn, SUB)
        for c in range(DC):
            S.op("dve", lambda e, c=c, sl=sl: e.scalar_tensor_tensor(out=aT[:, c, sl], in0=h_sb[:, c, sl],
                                                                      scalar=g_sb[:, c:c + 1], in1=rstd[:, 0:SUB],
                                                                      op0=ALU.mult, op1=ALU.mult),
                 reads=[th[c][s], tg, trstd], writes=[taT[c][s]])


def build_ffn(D, FF, NT, TB, NH, pre=False, final=False):
    C = Ctx()
    nc, S = C.nc, C.S
    DC, FC = D // 128, FF // 128
    FH = FC // NH
    NS = TB // 512
    NB = NT // TB
    hT = C.din("hT", [D, NT])
    g_d = C.din("g", [128, DC])
    wg_d = C.din("wg", [FC, 128, DC * 128])
    wu_d = C.din("wu", [FC, 128, DC * 128])
    wd_d = C.din("wd", [DC, 128, FC * 128])
    if pre:
        yT = C.din("yT", [D, NT])
        wo_d = C.din("wo", [DC, 128, DC * 128])
    if final:
        gf_d = C.din("gf", [128, DC])
    oT = C.dout("oT", [D, NT])
    hTv = hT.rearrange("(c p) t -> p c t", p=128)
    oTv = oT.rearrange("(c p) t -> p c t", p=128)

    h_sb = C.sb([128, DC, TB], F32, "h")
    aT = C.sb([128, DC, TB], BF16, "aT")
    HT = C.sb([128, FH, TB], BF16, "HT")
    wg = [C.sb([128, DC * 128], BF16, "wg") for _ in range(2)]
    wu = [C.sb([128, DC * 128], BF16, "wu") for _ in range(2)]
    wd = [C.sb([128, FH * 128], BF16, "wd") for _ in range(2)]
    sq = [C.sb([128, 512], F32R, "sq") for _ in range(2)]
    sg = [C.sb([128, 512], F32, "sg") for _ in range(2)]
    rstd = C.sb([128, 512], F32, "rstd")
    g_sb = C.sb([128, DC], F32, "g")
    ps_ssq = C.ps(name="ssq")
    psG = [C.ps(name="G") for _ in range(2)]
    psU = [C.ps(name="U") for _ in range(2)]
    psY = [C.ps(name="Y") for _ in range(2)]
    if pre:
        wo = [C.sb([128, DC * 128], BF16, "wo") for _ in range(2)]
        two = [Tile() for _ in range(2)]
        cwo = [S.chan() for _ in range(2)]
    if final:
        gf_sb = C.sb([128, DC], F32, "gf")
        tgf = Tile()
        fo = C.sb([128, DC, TB], F32, "fo") if False else None

    th = [[Tile() for _ in range(NS)] for _ in range(DC)]
    taT = [[Tile() for _ in range(NS)] for _ in range(DC)]
    tHT = [[Tile() for _ in range(NS)] for _ in range(FH)]
    twg = [Tile() for _ in range(2)]
    twu = [Tile() for _ in range(2)]
    twd = [Tile() for _ in range(2)]
    tsq = [Tile() for _ in range(2)]
    tsg = [Tile() for _ in range(2)]
    trstd, tg, tssq = Tile(), Tile(), Tile()
    tG = [Tile(psum=True) for _ in range(2)]
    tU = [Tile(psum=True) for _ in range(2)]
    tY = [Tile(psum=True) for _ in range(2)]
    cwg = [S.chan() for _ in range(2)]
    cwu = [S.chan() for _ in range(2)]
    cwd = [S.chan() for _ in range(2)]
    ch_in = S.chan()
    ch_out = S.chan()
    cg = S.chan()

    S.op("sp", lambda e: e.dma_start(out=g_sb[:], in_=g_d), writes=[tg], chan=cg)
    if final:
        cgf = S.chan()
        S.op("sp", lambda e: e.dma_start(out=gf_sb[:], in_=gf_d), writes=[tgf], chan=cgf)
    ones_r, tones = emit_consts(C)

    all_h = [th[c][s] for c in range(DC) for s in range(NS)]
    all_aT = [taT[c][s] for c in range(DC) for s in range(NS)]
    CG = min(4, DC)
    nwd = 0
    nwgu = 0
    for tb in range(NB):
        tsl = slice(tb * TB, (tb + 1) * TB)
        for c0 in range(0, DC, CG):
            S.op("sp", lambda e, c0=c0, tsl=tsl: e.dma_start(out=h_sb[:, c0:c0 + CG, :], in_=hTv[:, c0:c0 + CG, tsl]),
                 writes=[th[c][s] for c in range(c0, c0 + CG) for s in range(NS)], chan=ch_in)
        if pre:
            for c0 in range(0, DC, CG):
                yv = yT.rearrange("(c p) t -> p c t", p=128)
                S.op("pool", lambda e, c0=c0, tsl=tsl, yv=yv: e.dma_start(out=aT[:, c0:c0 + CG, :], in_=yv[:, c0:c0 + CG, tsl]),
                     writes=[taT[c][s] for c in range(c0, c0 + CG) for s in range(NS)], chan=ch_in)
            for dc in range(DC):
                k = dc % 2
                S.op("pool", lambda e, dc=dc, k=k: e.dma_start(out=wo[k][:], in_=wo_d[dc]), writes=[two[k]], chan=cwo[k])
                for s in range(NS):
                    sl = slice(s * 512, (s + 1) * 512)
                    b = (dc * NS + s) % 2
                    for c in range(DC):
                        S.op("pe", lambda e, c=c, k=k, b=b, sl=sl: e.matmul(psY[b][:], wo[k][:, c * 128:(c + 1) * 128], aT[:, c, sl],
                                                                          start=(c == 0), stop=(c == DC - 1)),
                             reads=[two[k], taT[c][s]], writes=[tY[b]])
                    S.op("dve", lambda e, dc=dc, b=b, sl=sl: e.tensor_tensor(out=h_sb[:, dc, sl], in0=psY[b][:], in1=h_sb[:, dc, sl], op=ALU.add),
                         reads=[tY[b], th[dc][s]], writes=[th[dc][s]])
        emit_rmsnorm_T(C, h_sb, th, aT, taT, g_sb, tg, ones_r, tones, DC, TB, sq, tsq, ps_ssq, tssq, rstd, trstd, D)
        for hf in range(NH):
            for fi in range(FH):
                f = hf * FH + fi
                k = nwgu % 2
                nwgu += 1
                S.op("pool", lambda e, f=f, k=k: e.dma_start(out=wg[k][:], in_=wg_d[f]), writes=[twg[k]], chan=cwg[k])
                S.op("pool", lambda e, f=f, k=k: e.dma_start(out=wu[k][:], in_=wu_d[f]), writes=[twu[k]], chan=cwu[k])
                for s in range(NS):
                    sl = slice(s * 512, (s + 1) * 512)
                    b = (fi * NS + s) % 2
                    for c in range(DC):
                        S.op("pe", lambda e, c=c, k=k, b=b, sl=sl: e.matmul(psG[b][:], wg[k][:, c * 128:(c + 1) * 128], aT[:, c, sl],
                                                                          start=(c == 0), stop=(c == DC - 1)),
                             reads=[twg[k], taT[c][s]], writes=[tG[b]])
                    for c in range(DC):
                        S.op("pe", lambda e, c=c, k=k, b=b, sl=sl: e.matmul(psU[b][:], wu[k][:, c * 128:(c + 1) * 128], aT[:, c, sl],
                                                                          start=(c == 0), stop=(c == DC - 1)),
                             reads=[twu[k], taT[c][s]], writes=[tU[b]])
                    S.op("act", lambda e, b=b: e.activation(out=sg[b][:], in_=psG[b][:], func=AF.Silu),
                         reads=[tG[b]], writes=[tsg[b]])
                    S.op("dve", lambda e, b=b, fi=fi, sl=sl: e.tensor_tensor(out=HT[:, fi, sl], in0=psU[b][:], in1=sg[b][:], op=ALU.mult),
                         reads=[tU[b], tsg[b]], writes=[tHT[fi][s]])
            for dc in range(DC):
                k = nwd % 2
                nwd += 1
                S.op("pool", lambda e, dc=dc, hf=hf, k=k: e.dma_start(out=wd[k][:], in_=wd_d[dc, :, hf * FH * 128:(hf + 1) * FH * 128]),
                     writes=[twd[k]], chan=cwd[k])
                for s in range(NS):
                    sl = slice(s * 512, (s + 1) * 512)
                    b = (dc * NS + s) % 2
                    for fi in range(FH):
                        S.op("pe", lambda e, fi=fi, k=k, b=b, sl=sl: e.matmul(psY[b][:], wd[k][:, fi * 128:(fi + 1) * 128], HT[:, fi, sl],
                                                                            start=(fi == 0), stop=(fi == FH - 1)),
                             reads=[twd[k], tHT[fi][s]], writes=[tY[b]])
                    S.op("dve", lambda e, dc=dc, b=b, sl=sl: e.scalar_tensor_tensor(out=h_sb[:, dc, sl], in0=psY[b][:], scalar=0.5,
                                                                                 in1=h_sb[:, dc, sl], op0=ALU.mult, op1=ALU.add),
                         reads=[tY[b], th[dc][s]], writes=[th[dc][s]])
        if final:
            NSx = NS
            for s in range(NSx):
                sl = slice(s * 512, (s + 1) * 512)
                for c in range(DC):
                    k = (s * DC + c) % 2
                    S.op("act", lambda e, c=c, k=k, sl=sl: e.activation(out=sq[k][:], in_=h_sb[:, c, sl], func=AF.Square),
                         reads=[th[c][s]], writes=[tsq[k]])
                    S.op("pe", lambda e, c=c, k=k: e.matmul(ps_ssq[:], ones_r[:], sq[k][:], start=(c == 0), stop=(c == DC - 1)),
                         reads=[tsq[k], tones], writes=[tssq])
                emit_rstd(S, rstd, trstd, ps_ssq, tssq, D)
                for c in range(DC):
                    S.op("dve", lambda e, c=c, sl=sl: e.scalar_tensor_tensor(out=h_sb[:, c, sl], in0=h_sb[:, c, sl],
                                                                              scalar=gf_sb[:, c:c + 1], in1=rstd[:],
                                                                              op0=ALU.mult, op1=ALU.mult),
                         reads=[th[c][s], tgf, trstd], writes=[th[c][s]])
        for c0 in range(0, DC, CG):
            S.op("sp", lambda e, c0=c0, tsl=tsl: e.dma_start(out=oTv[:, c0:c0 + CG, tsl], in_=h_sb[:, c0:c0 + CG, :]),
                 reads=[th[c][s] for c in range(c0, c0 + CG) for s in range(NS)], chan=ch_out)
    S.emit(final_waits=[ch_out])
    C.st.close()
    return nc


GELU_C = 1.5957691216057308


def emit_gelu(S, out_ap, x_ap, t1_ap, t2_ap, reads, writes, tt1, tt2, eng2="dve"):
    S.op("act", lambda e: e.activation(out=t1_ap, in_=x_ap, func=AF.Square), reads=reads, writes=[tt1])
    S.op("dve", lambda e: e.tensor_scalar(out=t1_ap, in0=t1_ap, scalar1=0.044715, scalar2=1.0, op0=ALU.mult, op1=ALU.add),
         reads=[tt1], writes=[tt1])
    S.op("dve", lambda e: e.tensor_tensor(out=t1_ap, in0=x_ap, in1=t1_ap, op=ALU.mult), reads=reads + [tt1], writes=[tt1])
    S.op("act", lambda e: e.activation(out=t2_ap, in_=t1_ap, func=AF.Sigmoid, scale=GELU_C), reads=[tt1], writes=[tt2])
    S.op("dve", lambda e: e.tensor_tensor(out=out_ap, in0=x_ap, in1=t2_ap, op=ALU.mult), reads=reads + [tt2], writes=writes)


def build_ab(Sq, D=2048, debug=False):
    C = Ctx()
    nc, S = C.nc, C.S
    DC = D // 128
    TB = 512
    NBLK = Sq // TB
    SH = Sq // 2
    DH = 256
    hT = C.din("hT", [D, Sq])
    hTh = C.din("hTh", [D, SH])
    g_d = C.din("g", [128, DC])
    wz_d = C.din("wz", [128, DC, 2048])
    wqk_d = C.din("wqk", [128, DC, 1024])
    wvog_d = C.din("wvog", [128, DC, 1028])
    gb_d = C.din("gb", [128, 4])
    conv_d = C.din("conv", [128, 8, 4])
    mn_d = C.din("mn", [128, 512])
    gn_d = C.din("gn", [128, 1024])
    wsT_d = C.din("wsT", [128, 8, 128])
    bs_d = C.din("bs", [128, 8])
    ident_d = C.din("ident", [128, 128])
    mask_d = C.din("mask", [128, 128])
    ya = C.dout("ya", [SH, 1024])
    yb = C.dout("yb", [Sq, 512])
    hTv = hT.rearrange("(c p) t -> p c t", p=128)
    hThv = hTh.rearrange("(c p) t -> p c t", p=128)
    if debug:
        dbg = C.dout("dbg", [Sq // 128, 128, 16])
        dbg_sb = C.sb([128, 16], F32, "dbg")
        tdbg = Tile()
        cdbg = S.chan()

    W = C.sb([128, DC, 2052], BF16, "W")
    h_sb = C.sb([128, DC, TB], F32, "h")
    hn = C.sb([128, DC, TB], BF16, "hn")
    sq = [C.sb([128, 512], F32R, "sq") for _ in range(2)]
    rstd = C.sb([128, 512], F32, "rstd")
    g_sb = C.sb([128, DC], F32, "g")
    gb_sb = C.sb([128, 4], F32, "gb")
    conv_sb = C.sb([128, 8, 4], F32, "conv")
    mn_sb = C.sb([128, 512], F32, "mn")
    gn_sb = C.sb([128, 1024], F32, "gn")
    wsT32 = C.sb([128, 8, 128], F32, "wsT32")
    wsT = C.sb([128, 8, 128], BF16, "wsT")
    bs_sb = C.sb([128, 8], F32, "bs")
    ident = C.sb([128, 128], F32, "ident")
    identb = C.sb([128, 128], BF16, "identb")
    mask = C.sb([128, 128], F32, "mask")
    ones32 = C.sb([128, 128], F32, "ones32")
    qkpre = C.sb([128, 8, 3 + TB], F32, "qkpre")
    acc = C.sb([128, TB], F32, "acc")
    qkT = C.sb([128, 8, TB], BF16, "qkT")
    vaug = [C.sb([128, 2, 257], F32, "vaug") for _ in range(2)]
    so = [C.sb([128, 512], F32, "so") for _ in range(2)]
    gts = [C.sb([128, 4], F32, "gts") for _ in range(2)]
    lf = C.sb([128, 2], F32, "lf")
    iv = C.sb([128, 2], F32, "iv")
    bcol = C.sb([128, 2], F32, "bcol")
    acol = C.sb([128, 2], F32, "acol")
    a_bc = C.sb([128, 2, 128], F32, "a_bc")
    amax = C.sb([128, 2], F32, "amax")
    Mx = C.sb([128, 2], F32, "Mx")
    mprev = C.sb([128, 2], F32, "mprev")
    wprev = C.sb([128, 2], F32, "wprev")
    ws = C.sb([128, 2], F32, "ws")
    thr = C.sb([128, 2], F32, "thr")
    tmp2 = C.sb([128, 2], F32, "tmp2")
    sTm = C.sb([128, 128], BF16, "sTm")
    vw = C.sb([128, 257], BF16, "vw")
    CT = C.sb([128, 2, 2, 257], F32, "CT")
    CTb = C.sb([128, 2, 257], BF16, "CTb")
    ktok = C.sb([128, 256], BF16, "ktok")
    den = C.sb([128, 1], F32, "den")
    hh = C.sb([128, 256], F32, "hh")
    junk = C.sb([128, 256], F32, "junk")
    ssq1 = C.sb([128, 2], F32, "ssq1")
    ybt = [C.sb([128, 512], F32, "ybt") for _ in range(2)]
    u_sb = C.sb([128, 1024], F32, "u")
    v_sb = C.sb([128, 1024], F32, "v")
    vn = C.sb([128, 1024], BF16, "vn")
    t1 = C.sb([128, 512], F32, "t1")
    t2 = C.sb([128, 512], F32, "t2")
    yat = [C.sb([128, 1024], F32, "yat") for _ in range(2)]

    ps_ssq = C.ps(name="ssq")
    psA = [C.ps(name="A") for _ in range(2)]
    psB = [C.ps(name="B") for _ in range(2)]
    psS = C.ps(name="S")
    psN = C.ps(name="N")
    psT = C.ps([128, 256], BF16, name="T")

    T = Tile
    tW, tg, tgb, tconv, tmn, tgn, tws32, tws, tbs, tid, tidb, tmask, tones32 = [T() for _ in range(13)]
    th = [[T()] for _ in range(DC)]
    thn = [[T()] for _ in range(DC)]
    tsq = [T(), T()]
    trstd, tssq = T(), T()
    tqkpre = [T() for _ in range(8)]
    tacc = T()
    tqkT = [T() for _ in range(8)]
    tvaug = [T(), T()]
    tso = [T(), T()]
    tgts = [T(), T()]
    tlf, tiv, tbcol, tacol, tabc, tamax, tMx, tmprev, twprev, tws_, tthr, ttmp2 = [T() for _ in range(12)]
    tsTm, tvw, tCT, tCTb, tktok, tden, thh, tjunk, tssq1 = [T() for _ in range(9)]
    tybt = [T(), T()]
    tu, tv, tvn, tt1, tt2 = [T() for _ in range(5)]
    tyat = [T(), T()]
    tpsA = [T(psum=True), T(psum=True)]
    tpsB = [T(psum=True), T(psum=True)]
    tpsS, tpsN, tpsT = T(psum=True), T(psum=True), T(psum=True)

    cmisc = S.chan()
    cW = S.chan()
    ch_in = S.chan()
    cyb = [S.chan(), S.chan()]
    cya = [S.chan(), S.chan()]

    def ld(dst, src, tile, eng="sp", chan=None):
        S.op(eng, lambda e: e.dma_start(out=dst, in_=src), writes=[tile], chan=chan or cmisc)

    ld(g_sb[:], g_d, tg)
    ld(gb_sb[:], gb_d, tgb)
    ld(conv_sb[:], conv_d, tconv)
    ld(mn_sb[:], mn_d, tmn)
    ld(gn_sb[:], gn_d, tgn)
    ld(wsT32[:], wsT_d, tws32)
    ld(bs_sb[:], bs_d, tbs)
    ld(ident[:], ident_d, tid)
    ld(mask[:], mask_d, tmask)
    ones_r, tones = emit_consts(C)
    S.op("dve", lambda e: e.memset(ones32[:], 1.0), writes=[tones32])
    S.op("dve", lambda e: e.tensor_copy(out=identb[:], in_=ident[:]), reads=[tid], writes=[tidb])
    for g in range(8):
        S.op("dve", lambda e, g=g: e.tensor_tensor(out=wsT[:, g, :], in0=wsT32[:, g, :], in1=mask[:], op=ALU.mult),
             reads=[tws32, tmask], writes=[tws])
    S.op("pool", lambda e: e.dma_start(out=W[:, :, 0:1024], in_=wqk_d), writes=[tW], chan=cW)
    S.op("pool", lambda e: e.dma_start(out=W[:, :, 1024:2052], in_=wvog_d), writes=[tW], chan=cW)
    S.op("dve", lambda e: e.memset(CT[:], 0.0), writes=[tCT])
    S.op("dve", lambda e: e.memset(mprev[:], 0.0), writes=[tmprev])
    for k in range(2):
        S.op("dve", lambda e, k=k: e.memset(vaug[k][:], 1.0), writes=[tvaug[k]])
    for m in range(8):
        S.op("dve", lambda e, m=m: e.memset(qkpre[:, m, 0:3], 0.0), writes=[tqkpre[m]])

    def norm_block(src_v, tsl):
        CG = 4
        for c0 in range(0, DC, CG):
            S.op("sp", lambda e, c0=c0: e.dma_start(out=h_sb[:, c0:c0 + CG, :], in_=src_v[:, c0:c0 + CG, tsl]),
                 writes=[th[c][0] for c in range(c0, c0 + CG)], chan=ch_in)
        emit_rmsnorm_T(C, h_sb, th, hn, thn, g_sb, tg, ones_r, tones, DC, TB, sq, tsq, ps_ssq, tssq, rstd, trstd, D)

    all_hn = [thn[c][0] for c in range(DC)]
    nA = 0
    nB = 0
    for tb in range(NBLK):
        norm_block(hTv, slice(tb * TB, (tb + 1) * TB))
        for m in range(8):
            b = nA % 2
            nA += 1
            for c in range(DC):
                S.op("pe", lambda e, c=c, m=m, b=b: e.matmul(psA[b][:], W[:, c, m * 128:(m + 1) * 128], hn[:, c, :],
                                                          start=(c == 0), stop=(c == DC - 1)),
                     reads=[tW, thn[c][0]], writes=[tpsA[b]])
            S.op("act", lambda e, m=m, b=b: e.copy(out=qkpre[:, m, 3:3 + TB], in_=psA[b][:]), reads=[tpsA[b]], writes=[tqkpre[m]])
            S.op("dve", lambda e, m=m: e.tensor_scalar(out=acc[:], in0=qkpre[:, m, 0:TB], scalar1=conv_sb[:, m, 0:1], scalar2=None,
                                                       op0=ALU.mult), reads=[tqkpre[m], tconv], writes=[tacc])
            for k in range(1, 4):
                S.op("dve", lambda e, m=m, k=k: e.scalar_tensor_tensor(out=acc[:], in0=qkpre[:, m, k:k + TB], scalar=conv_sb[:, m, k:k + 1],
                                                                        in1=acc[:], op0=ALU.mult, op1=ALU.add),
                     reads=[tqkpre[m], tconv, tacc], writes=[tacc])
            S.op("act", lambda e, m=m: e.activation(out=qkT[:, m, :], in_=acc[:], func=AF.Silu), reads=[tacc], writes=[tqkT[m]])
            S.op("dve", lambda e, m=m: e.tensor_copy(out=qkpre[:, m, 0:3], in_=qkpre[:, m, TB:TB + 3]), reads=[tqkpre[m]], writes=[tqkpre[m]])
        for j in range(TB // 128):
            jsl = slice(j * 128, (j + 1) * 128)
            kk = j % 2
            b = nB % 2
            nB += 1
            for c in range(DC):
                S.op("pe", lambda e, c=c, b=b, jsl=jsl: e.matmul(psB[b][:], hn[:, c, jsl], W[:, c, 1024:1536], start=(c == 0), stop=(c == DC - 1)),
                     reads=[tW, thn[c][0]], writes=[tpsB[b]])
            for h in range(2):
                S.op("act", lambda e, h=h, b=b, kk=kk: e.copy(out=vaug[kk][:, h, 0:256], in_=psB[b][:, h * 256:(h + 1) * 256]),
                     reads=[tpsB[b]], writes=[tvaug[kk]])
            b = nB % 2
            nB += 1
            for c in range(DC):
                S.op("pe", lambda e, c=c, b=b, jsl=jsl: e.matmul(psB[b][:], hn[:, c, jsl], W[:, c, 1536:2048], start=(c == 0), stop=(c == DC - 1)),
                     reads=[tW, thn[c][0]], writes=[tpsB[b]])
            S.op("act", lambda e, b=b, kk=kk: e.activation(out=so[kk][:], in_=psB[b][:], func=AF.Sigmoid), reads=[tpsB[b]], writes=[tso[kk]])
            for c in range(DC):
                S.op("pe", lambda e, c=c, jsl=jsl: e.matmul(psS[:, 0:4], hn[:, c, jsl], W[:, c, 2048:2052], start=(c == 0), stop=(c == DC - 1)),
                     reads=[tW, thn[c][0]], writes=[tpsS])
            S.op("dve", lambda e, kk=kk: e.tensor_tensor(out=gts[kk][:], in0=psS[:, 0:4], in1=gb_sb[:], op=ALU.add),
                 reads=[tpsS, tgb], writes=[tgts[kk]])
            S.op("act", lambda e, kk=kk: e.activation(out=lf[:], in_=gts[kk][:, 2:4], func=AF.Exp, scale=-1.0), reads=[tgts[kk]], writes=[tlf])
            S.op("act", lambda e: e.activation(out=lf[:], in_=lf[:], func=AF.Ln, bias=ones32[:, 0:1], scale=1.0), reads=[tlf, tones32], writes=[tlf])
            S.op("dve", lambda e: e.tensor_scalar(out=lf[:], in0=lf[:], scalar1=-1.0, scalar2=None, op0=ALU.mult), reads=[tlf], writes=[tlf])
            S.op("pe", lambda e: e.matmul(psS[:, 8:10], mask[:], lf[:], start=True, stop=True), reads=[tmask, tlf], writes=[tpsS])
            S.op("pe", lambda e: e.matmul(psS[:, 16:18], ones32[:], lf[:], start=True, stop=True), reads=[tones32, tlf], writes=[tpsS])
            S.op("dve", lambda e: e.tensor_copy(out=bcol[:], in_=psS[:, 8:10]), reads=[tpsS], writes=[tbcol])
            S.op("dve", lambda e, kk=kk: e.tensor_tensor(out=acol[:], in0=gts[kk][:, 0:2], in1=bcol[:], op=ALU.subtract),
                 reads=[tgts[kk], tbcol], writes=[tacol])
            for h in range(2):
                S.op("dve", lambda e, h=h: e.tensor_copy(out=a_bc[:, h, :], in_=acol[:, h:h + 1].to_broadcast([128, 128])),
                     reads=[tacol], writes=[tabc])
            for h in range(2):
                S.op("pe", lambda e, h=h: e.matmul(psS[:, 128 + h * 128:256 + h * 128], a_bc[:, h, :], ident[:], start=True, stop=True),
                     reads=[tabc, tid], writes=[tpsS])
            S.op("dve", lambda e: e.tensor_reduce(out=amax[:], in_=psS[:, 128:384].rearrange("p (h s) -> p h s", h=2), axis=AX.X, op=ALU.max),
                 reads=[tpsS], writes=[tamax])
            S.op("dve", lambda e: e.tensor_tensor(out=Mx[:], in0=amax[:], in1=mprev[:], op=ALU.max), reads=[tamax, tmprev], writes=[tMx])
            S.op("dve", lambda e: e.tensor_tensor(out=tmp2[:], in0=mprev[:], in1=Mx[:], op=ALU.subtract), reads=[tmprev, tMx], writes=[ttmp2])
            S.op("act", lambda e: e.activation(out=wprev[:], in_=tmp2[:], func=AF.Exp), reads=[ttmp2], writes=[twprev])
            S.op("dve", lambda e: e.tensor_tensor(out=tmp2[:], in0=acol[:], in1=Mx[:], op=ALU.subtract), reads=[tacol, tMx], writes=[ttmp2])
            S.op("act", lambda e: e.activation(out=ws[:], in_=tmp2[:], func=AF.Exp), reads=[ttmp2], writes=[tws_])
            S.op("dve", lambda e: e.tensor_tensor(out=tmp2[:], in0=bcol[:], in1=Mx[:], op=ALU.add), reads=[tbcol, tMx], writes=[ttmp2])
            S.op("act", lambda e: e.activation(out=thr[:], in_=tmp2[:], func=AF.Exp, scale=-1.0), reads=[ttmp2], writes=[tthr])
            S.op("dve", lambda e: e.tensor_tensor(out=mprev[:], in0=psS[:, 16:18], in1=Mx[:], op=ALU.add), reads=[tpsS, tMx], writes=[tmprev])
            if debug:
                for i_, (src_, tl_) in enumerate(((lf, tlf), (acol, tacol), (bcol, tbcol), (amax, tamax), (Mx, tMx), (wprev, twprev),
                                                  (ws, tws_), (thr, tthr))):
                    S.op("dve", lambda e, i_=i_, src_=src_: e.tensor_copy(out=dbg_sb[:, 2 * i_:2 * i_ + 2], in_=src_[:]),
                         reads=[tl_], writes=[tdbg])
                cidx = tb * (TB // 128) + j
                S.op("sp", lambda e, cidx=cidx: e.dma_start(out=dbg[cidx], in_=dbg_sb[:]), reads=[tdbg], chan=cdbg)
            for h in range(2):
                qi = [h * 2, h * 2 + 1]
                ki = [4 + h * 2, 4 + h * 2 + 1]
                for dc in range(2):
                    S.op("pe", lambda e, dc=dc, ki=ki, qi=qi, jsl=jsl: e.matmul(psA[0][:, 0:128], qkT[:, ki[dc], jsl], qkT[:, qi[dc], jsl],
                                                                           start=(dc == 0), stop=(dc == 1)),
                         reads=[tqkT[ki[dc]], tqkT[qi[dc]]], writes=[tpsA[0]])
                S.op("dve", lambda e: e.scalar_tensor_tensor(out=sTm[:], in0=psA[0][:, 0:128], scalar=DH ** -0.5, in1=mask[:],
                                                             op0=ALU.mult, op1=ALU.mult), reads=[tpsA[0], tmask], writes=[tsTm])
                S.op("dve", lambda e, h=h, kk=kk: e.tensor_scalar(out=vw[:], in0=vaug[kk][:, h, :], scalar1=ws[:, h:h + 1], scalar2=None, op0=ALU.mult),
                     reads=[tvaug[kk], tws_], writes=[tvw])
                S.op("dve", lambda e, h=h: e.tensor_scalar(out=CT[:, h, :, :], in0=CT[:, h, :, :], scalar1=wprev[:, h:h + 1], scalar2=None, op0=ALU.mult),
                     reads=[tCT, twprev], writes=[tCT])
                S.op("act", lambda e, h=h: e.copy(out=CTb[:], in_=CT[:, h, :, :]), reads=[tCT], writes=[tCTb])
                S.op("pe", lambda e: e.matmul(psN[:, 0:257], sTm[:], vw[:], start=True, stop=False), reads=[tsTm, tvw], writes=[tpsN])
                for dc in range(2):
                    S.op("pe", lambda e, dc=dc, qi=qi, jsl=jsl: e.matmul(psN[:, 0:257], qkT[:, qi[dc], jsl], CTb[:, dc, :], start=False, stop=(dc == 1)),
                         reads=[tqkT[qi[dc]], tCTb], writes=[tpsN])
                S.op("act", lambda e: e.activation(out=den[:], in_=psN[:, 256:257], func=AF.Abs), reads=[tpsN], writes=[tden])
                S.op("dve", lambda e, h=h: e.tensor_tensor(out=den[:], in0=den[:], in1=thr[:, h:h + 1], op=ALU.max),
                     reads=[tden, tthr], writes=[tden])
                S.op("dve", lambda e: e.reciprocal(out=den[:], in_=den[:]), reads=[tden], writes=[tden])
                S.op("dve", lambda e: e.tensor_scalar(out=hh[:], in0=psN[:, 0:256], scalar1=den[:, 0:1], scalar2=None, op0=ALU.mult),
                     reads=[tpsN, tden], writes=[thh])
                S.op("dve", lambda e, h=h: e.memset(ssq1[:, h:h + 1], 0.0), writes=[tssq1])
                S.op("act", lambda e, h=h: e.activation(out=junk[:], in_=hh[:], func=AF.Square, accum_out=ssq1[:, h:h + 1]),
                     reads=[thh], writes=[tjunk, tssq1])
                S.op("act", lambda e, h=h: e.activation(out=ssq1[:, h:h + 1], in_=ssq1[:, h:h + 1], func=AF.Sqrt, bias=EPSB[0][:, 0:1], scale=1.0 / DH),
                     reads=[tssq1, EPSB[1]], writes=[tssq1])
                S.op("dve", lambda e, h=h: e.reciprocal(out=ssq1[:, h:h + 1], in_=ssq1[:, h:h + 1]), reads=[tssq1], writes=[tssq1])
                S.op("dve", lambda e, h=h, kk=kk: e.scalar_tensor_tensor(out=ybt[kk][:, h * 256:(h + 1) * 256], in0=hh[:], scalar=ssq1[:, h:h + 1],
                                                                          in1=mn_sb[:, h * 256:(h + 1) * 256], op0=ALU.mult, op1=ALU.mult),
                     reads=[thh, tssq1, tmn], writes=[tybt[kk]])
                S.op("dve", lambda e, h=h, kk=kk: e.tensor_tensor(out=ybt[kk][:, h * 256:(h + 1) * 256], in0=ybt[kk][:, h * 256:(h + 1) * 256],
                                                                  in1=so[kk][:, h * 256:(h + 1) * 256], op=ALU.mult),
                     reads=[tybt[kk], tso[kk]], writes=[tybt[kk]])
                for dc in range(2):
                    S.op("pe", lambda e, dc=dc, ki=ki, jsl=jsl: e.transpose(psT[:, dc * 128:(dc + 1) * 128], qkT[:, ki[dc], jsl], identb[:]),
                         reads=[tqkT[ki[dc]], tidb], writes=[tpsT])
                S.op("act", lambda e: e.copy(out=ktok[:], in_=psT[:]), reads=[tpsT], writes=[tktok])
                for dc in range(2):
                    S.op("pe", lambda e, dc=dc: e.matmul(psA[1][:, 0:257], ktok[:, dc * 128:(dc + 1) * 128], vw[:], start=True, stop=True),
                         reads=[tktok, tvw], writes=[tpsA[1]])
                    S.op("dve", lambda e, dc=dc, h=h: e.scalar_tensor_tensor(out=CT[:, h, dc, :], in0=psA[1][:, 0:257], scalar=DH ** -0.5,
                                                                              in1=CT[:, h, dc, :], op0=ALU.mult, op1=ALU.add),
                         reads=[tCT, tpsA[1]], writes=[tCT])
            tok0 = tb * TB + j * 128
            S.op("sp", lambda e, kk=kk, tok0=tok0: e.dma_start(out=yb[tok0:tok0 + 128, :], in_=ybt[kk][:]), reads=[tybt[kk]], chan=cyb[kk])

    S.op("pool", lambda e: e.dma_start(out=W[:, :, 0:2048], in_=wz_d), writes=[tW], chan=cW)
    nj = 0
    for tb in range(SH // TB):
        norm_block(hThv, slice(tb * TB, (tb + 1) * TB))
        for j in range(TB // 128):
            jsl = slice(j * 128, (j + 1) * 128)
            kk = nj % 2
            nj += 1
            for cb in range(4):
                b = cb % 2
                for c in range(DC):
                    S.op("pe", lambda e, c=c, b=b, cb=cb, jsl=jsl: e.matmul(psA[b][:], hn[:, c, jsl], W[:, c, cb * 512:(cb + 1) * 512],
                                                                        start=(c == 0), stop=(c == DC - 1)),
                         reads=[tW, thn[c][0]], writes=[tpsA[b]])
                dst = u_sb if cb < 2 else v_sb
                tdst = tu if cb < 2 else tv
                csl = slice((cb % 2) * 512, (cb % 2 + 1) * 512)
                emit_gelu(S, dst[:, csl], psA[b][:], t1[:], t2[:], [tpsA[b]], [tdst], tt1, tt2)
            S.op("dve", lambda e: e.memset(ssq1[:], 0.0), writes=[tssq1])
            for q in range(2):
                S.op("act", lambda e, q=q: e.activation(out=t1[:], in_=v_sb[:, q * 512:(q + 1) * 512], func=AF.Square, accum_out=ssq1[:, q:q + 1]),
                     reads=[tv], writes=[tt1, tssq1])
            S.op("dve", lambda e: e.tensor_tensor(out=ssq1[:, 0:1], in0=ssq1[:, 0:1], in1=ssq1[:, 1:2], op=ALU.add), reads=[tssq1], writes=[tssq1])
            S.op("act", lambda e: e.activation(out=ssq1[:, 0:1], in_=ssq1[:, 0:1], func=AF.Sqrt, bias=EPSB[0][:, 0:1], scale=1.0 / 1024),
                 reads=[tssq1, EPSB[1]], writes=[tssq1])
            S.op("dve", lambda e: e.reciprocal(out=ssq1[:, 0:1], in_=ssq1[:, 0:1]), reads=[tssq1], writes=[tssq1])
            S.op("dve", lambda e: e.scalar_tensor_tensor(out=vn[:], in0=v_sb[:], scalar=ssq1[:, 0:1], in1=gn_sb[:], op0=ALU.mult, op1=ALU.mult),
                 reads=[tv, tssq1, tgn], writes=[tvn])
            for g in range(8):
                b = g // 4
                S.op("pe", lambda e, g=g, b=b: e.matmul(psB[b][:, (g % 4) * 128:(g % 4 + 1) * 128], wsT[:, g, :], vn[:, g * 128:(g + 1) * 128],
                                                    start=True, stop=True), reads=[tws, tvn], writes=[tpsB[b]])
            for g in range(8):
                b = g // 4
                S.op("dve", lambda e, g=g, b=b, kk=kk: e.scalar_tensor_tensor(out=yat[kk][:, g * 128:(g + 1) * 128],
                                                                               in0=psB[b][:, (g % 4) * 128:(g % 4 + 1) * 128],
                                                                               scalar=bs_sb[:, g:g + 1], in1=u_sb[:, g * 128:(g + 1) * 128],
                                                                               op0=ALU.add, op1=ALU.mult),
                     reads=[tpsB[b], tbs, tu], writes=[tyat[kk]])
            tok0 = tb * TB + j * 128
            S.op("sp", lambda e, kk=kk, tok0=tok0: e.dma_start(out=ya[tok0:tok0 + 128, :], in_=yat[kk][:]), reads=[tyat[kk]], chan=cya[kk])
    S.emit(final_waits=cyb + cya)
    C.st.close()
    return nc


NEG = -1.0e30


def nsa_tables(Sq):
    NT = Sq // 128
    NCMP = (Sq - 32) // 16 + 1
    CH = (NCMP + 127) // 128
    NSL = Sq // 64
    half = 16
    inv = 1.0 / (500000.0 ** (np.arange(half, dtype=np.float32) / half))
    ang = np.arange(Sq, dtype=np.float32)[None, :] * inv[:, None]
    cos = np.ones((128, Sq), np.float32)
    sin = np.zeros((128, Sq), np.float32)
    cos[0:16] = np.cos(ang); cos[16:32] = np.cos(ang)
    sin[0:16] = np.sin(ang); sin[16:32] = np.sin(ang)
    sc = np.float32(128 ** -0.5)
    psw = np.zeros((128, 128), np.float32)
    for d in range(16):
        psw[d + 16, d] = -1.0
        psw[d, d + 16] = 1.0
    k = np.arange(128)[:, None]
    q = np.arange(128)[None, :]
    causal = np.where(k > q, NEG, 0.0).astype(np.float32)
    winneg = np.where(k <= q, NEG, 0.0).astype(np.float32)
    cm = np.zeros((NT, 128, CH, 128), np.float32)
    for T in range(NT):
        for ch in range(CH):
            c = ch * 128 + np.arange(128)[:, None]
            t = T * 128 + np.arange(128)[None, :]
            cm[T, :, ch, :] = np.where((16 * c + 31 > t) | (c >= NCMP), NEG, 0.0)
    E = np.zeros((64, NT, 128), np.float32)
    for kc in range(NT):
        for kk in range(128):
            E[(kc * 128 + kk) // 64, kc, kk] = 1.0
    fpos = np.zeros((NT, 128, 64), np.float32)
    fneg = np.full((NT, 128, 64), 1.0e30, np.float32)
    j = np.arange(64)[None, :]
    for T in range(NT):
        cur = ((T * 128 + np.arange(128)) // 64)[:, None]
        fp = np.zeros((128, 64), np.float32)
        fp = np.where(j == cur - 1, 1.0e9, fp)
        fp = np.where(j == cur, 2.0e9, fp)
        fp = np.where(j == 0, 3.0e9, fp)
        fpos[T] = fp
        fneg[T] = np.where((j > cur) | (j >= NSL), -1.0e9, 1.0e30)
    ci = np.arange(CH * 128)[:, None] * 16
    sj = np.arange(64)[None, :] * 64
    ov = np.clip(np.minimum(ci + 32, sj + 64) - np.maximum(ci, sj), 0, None).astype(np.float32) / 32.0
    ov[NCMP:] = 0.0
    ov = np.ascontiguousarray(ov.reshape(CH, 128, 64).transpose(1, 0, 2))
    return dict(cosq=cos * sc, sinq=sin * sc, cosk=cos, sink=sin, psw=psw, ident=np.eye(128, dtype=np.float32),
                causal=causal, winneg=winneg, cmpmask=cm, E=E, fpos=fpos, fneg=fneg, ov=ov)


def build_nsa(Sq, D=2048, stop=0, branches=(0, 1, 2)):
    C = Ctx()
    nc, S = C.nc, C.S
    DC = D // 128
    TB = 256
    TPB = TB // 128
    NBLK = Sq // TB
    NT = Sq // 128
    NCMP = (Sq - 32) // 16 + 1
    CH = (NCMP + 127) // 128
    T_ = Tile

    hT = C.din("hT", [D, Sq])
    g_d = C.din("g", [128, DC])
    wkv_d = C.din("wkv", [128, DC, 1536])
    wq_d = C.din("wq", [128, DC, 1024])
    wgt_d = C.din("wgt", [128, DC, 24])
    w1_d = C.din("w1", [128, 2, 32, 128])
    w2_d = C.din("w2", [128, 2, 128])
    pos_d = C.din("posT", [128, 2, 32])
    tabs = {}
    for nm, shp in (("cosq", [128, Sq]), ("sinq", [128, Sq]), ("cosk", [128, Sq]), ("sink", [128, Sq]), ("psw", [128, 128]),
                    ("ident", [128, 128]), ("causal", [128, 128]), ("winneg", [128, 128]), ("cmpmask", [NT, 128, CH, 128]),
                    ("E", [64, NT, 128]), ("fpos", [NT, 128, 64]), ("fneg", [NT, 128, 64]), ("ov", [128, CH, 64])):
        tabs[nm] = C.din(nm, shp)
    y_d = C.dout("y", [Sq, 1024])
    hTv = hT.rearrange("(c p) t -> p c t", p=128)

    W = C.sb([128, DC, 1024], BF16, "W")
    Wgt = C.sb([128, DC, 24], BF16, "Wgt")
    tWgt = Tile()
    h_sb = C.sb([128, DC, TB], BF16, "h")
    hn = C.sb([128, DC, TB], BF16, "hn")
    sq = [C.sb([128, 256], F32R, "sq") for _ in range(2)]
    rstd = C.sb([128, 256], F32, "rstd")
    g_sb = C.sb([128, DC], F32, "g")
    kTs = C.sb([128, 2, 2, Sq], BF16, "kTs")
    cmpin = C.sb([128, 2, 2, Sq], BF16, "cmpin")
    vsel = C.sb([128, NT, 2, 129], BF16, "vsel")
    vwin = C.sb([128, NT, 2, 129], BF16, "vwin")
    kcmpT = C.sb([128, 2, CH * 128], BF16, "kcmpT")
    vcmp = C.sb([128, 2, CH, 193], BF16, "vcmp")
    rope_c = C.sb([128, TB], F32, "ropec")
    rope_s = C.sb([128, TB], F32, "ropes")
    psw = C.sb([128, 128], BF16, "psw")
    identb = C.sb([128, 128], BF16, "identb")
    ident = C.sb([128, 128], F32, "ident")
    causal = C.sb([128, 128], BF16, "causal")
    winneg = C.sb([128, 128], BF16, "winneg")
    Eb = C.sb([64, NT, 128], BF16, "E")
    xb = C.sb([128, TB], BF16, "xb")
    xc = C.sb([128, TB], F32, "xc")
    xs = C.sb([128, TB], F32, "xs")
    kmax2 = C.sb([128, 6], F32, "kmax2")
    kred = C.sb([128, 1], F32, "kred")
    ones32 = C.sb([128, 128], F32, "ones32")
    sqf = C.sb([128, 256], F32, "sqf")
    tsqf = Tile()

    psB = [C.ps(name="B") for _ in range(2)]
    psA = psB
    psO = [C.ps(name="O") for _ in range(4)]
    psS = C.ps(name="S")
    ps_ssq = C.ps(name="ssq")

    tW, tg = T_(), T_()
    th = [[T_()] for _ in range(DC)]
    thn = [[T_()] for _ in range(DC)]
    tsq = [T_(), T_()]
    trstd, tssq = T_(), T_()
    tkTs = [[[T_() for _ in range(NBLK)] for _ in range(2)] for _ in range(2)]
    tcmpin = [[T_() for _ in range(2)] for _ in range(2)]
    tvsel = [T_() for _ in range(NT)]
    tvwin = [T_() for _ in range(NT)]
    tkcmpT, tvcmp = T_(), T_()
    trope, tpsw, tidb, tid, tcausal, twinneg, tE = [T_() for _ in range(7)]
    txb, txc, txs, tkmax2, tkred, tones32 = [T_() for _ in range(6)]
    tpsB = [T_(psum=True), T_(psum=True)]
    tpsA = tpsB
    tpsO = [T_(psum=True) for _ in range(4)]
    tpsS = T_(psum=True)

    cmisc = S.chan()
    cW = S.chan()
    ch_in = S.chan()
    crope = S.chan()

    S.op("sp", lambda e: e.dma_start(out=g_sb[:], in_=g_d), writes=[tg], chan=cmisc)
    S.op("sp", lambda e: e.dma_start(out=ident[:], in_=tabs["ident"]), writes=[tid], chan=cmisc)
    for dst, nm, tl in ((psw, "psw", tpsw), (identb, "ident", tidb), (causal, "causal", tcausal), (winneg, "winneg", twinneg), (Eb, "E", tE)):
        S.op("pool", lambda e, dst=dst, nm=nm: e.dma_start(out=dst[:], in_=tabs[nm]), writes=[tl], chan=cmisc)
    ones_r, tones = emit_consts(C)
    S.op("dve", lambda e: e.memset(ones32[:], 1.0), writes=[tones32])
    S.op("dve", lambda e: e.memset(kmax2[:], 0.0), writes=[tkmax2])
    S.op("dve", lambda e: e.memset(vsel[:], 1.0), writes=tvsel)
    S.op("dve", lambda e: e.memset(vwin[:], 1.0), writes=tvwin)
    S.op("dve", lambda e: e.memset(vcmp[:], 1.0), writes=[tvcmp])
    S.op("dve", lambda e: e.memset(kcmpT[:], 0.0), writes=[tkcmpT])
    S.op("pool", lambda e: e.dma_start(out=Wgt[:], in_=wgt_d), writes=[tWgt], chan=cmisc)

    class _Stop(Exception):
        pass

    def ckpt(v):
        if stop == v:
            S.op("sp", lambda e: e.dma_start(out=y_d[0:128, 0:128], in_=ident[:]), reads=[tid], chan=cmisc)
            S.emit(final_waits=[cmisc])
            S.op = lambda *a, **k: None
            S.emit = lambda *a, **k: None

    def norm_block(tsl):
        CG = 4
        for c0 in range(0, DC, CG):
            S.op("pool", lambda e, c0=c0: e.dma_start(out=h_sb[:, c0:c0 + CG, :], in_=hTv[:, c0:c0 + CG, tsl]),
                 writes=[th[c][0] for c in range(c0, c0 + CG)], chan=ch_in)
        emit_rmsnorm_T(C, h_sb, th, hn, thn, g_sb, tg, ones_r, tones, DC, TB, sq, tsq, ps_ssq, tssq, rstd, trstd, D)

    def load_rope(cn, sn, tsl):
        S.op("sp", lambda e: e.dma_start(out=rope_c[:], in_=tabs[cn][:, tsl]), writes=[trope], chan=crope)
        S.op("sp", lambda e: e.dma_start(out=rope_s[:], in_=tabs[sn][:, tsl]), writes=[trope], chan=crope)

    nA = [0]

    def proj_fm(col0):
        b = nA[0] % 2
        nA[0] += 1
        for c in range(DC):
            S.op("pe", lambda e, c=c, b=b: e.matmul(psA[b][:, 0:TB], W[:, c, col0:col0 + 128], hn[:, c, :], start=(c == 0), stop=(c == DC - 1)),
                 reads=[tW, thn[c][0]], writes=[tpsA[b]])
        return b

    def rope_to(b, dst_ap, dst_tiles):
        S.op("act", lambda e: e.copy(out=xb[:], in_=psA[b][:, 0:TB]), reads=[tpsA[b]], writes=[txb])
        ckpt(151)
        S.op("dve", lambda e: e.tensor_tensor(out=xc[:], in0=psA[b][:, 0:TB], in1=rope_c[:], op=ALU.mult), reads=[tpsA[b], trope, txb], writes=[txc])
        ckpt(152)
        bb = nA[0] % 2
        nA[0] += 1
        S.op("pe", lambda e: e.matmul(psA[bb][:, 0:TB], psw[:], xb[:], start=True, stop=True), reads=[tpsw, txb], writes=[tpsA[bb]])
        ckpt(153)
        S.op("dve", lambda e: e.tensor_tensor(out=xs[:], in0=psA[bb][:, 0:TB], in1=rope_s[:], op=ALU.mult), reads=[tpsA[bb], trope], writes=[txs])
        ckpt(154)
        S.op("dve", lambda e: e.tensor_tensor(out=dst_ap, in0=xs[:], in1=xc[:], op=ALU.add), reads=[txs, txc], writes=dst_tiles)

    def colnorm_max(src_ap, src_tiles, kcol, ncols=TB):
        S.op("act", lambda e: e.activation(out=sqf[:, 0:ncols], in_=src_ap, func=AF.Square), reads=src_tiles, writes=[tsqf])
        S.op("pe", lambda e: e.matmul(psS[:, 0:ncols], ones32[:], sqf[:, 0:ncols], start=True, stop=True), reads=[tsqf, tones32], writes=[tpsS])
        S.op("dve", lambda e: e.tensor_reduce(out=kred[:], in_=psS[:, 0:ncols], axis=AX.X, op=ALU.max), reads=[tpsS], writes=[tkred])
        S.op("dve", lambda e: e.tensor_tensor(out=kmax2[:, kcol:kcol + 1], in0=kmax2[:, kcol:kcol + 1], in1=kred[:], op=ALU.max),
             reads=[tkmax2, tkred], writes=[tkmax2])

    nB = 0
    try:
        ckpt(11)
    except _Stop:
        return nc
    for tb in range(NBLK):
        tsl = slice(tb * TB, (tb + 1) * TB)
        try:
            norm_block(tsl)
            ckpt(12)
            load_rope("cosk", "sink", tsl)
            S.op("pool", lambda e: e.dma_start(out=W[:], in_=wkv_d[:, :, 0:1024]), writes=[tW], chan=cW)
            ckpt(13)
        except _Stop:
            return nc
        for slot in range(6):
            br, gl = slot // 2, slot % 2
            b = proj_fm(slot * 128)
            try:
                ckpt(14)
            except _Stop:
                return nc
            if br == 0:
                rope_to(b, cmpin[:, 0, gl, tsl], [tcmpin[0][gl]])
                try:
                    ckpt(15)
                except _Stop:
                    return nc
            else:
                rope_to(b, kTs[:, br - 1, gl, tsl], [tkTs[br - 1][gl][tb]])
                colnorm_max(kTs[:, br - 1, gl, tsl], [tkTs[br - 1][gl][tb]], 2 + (br - 1) * 2 + gl)
                try:
                    ckpt(16)
                except _Stop:
                    return nc
        for gl in range(2):
            b = proj_fm((6 + gl) * 128)
            S.op("act", lambda e, b=b, gl=gl, tsl=tsl: e.copy(out=cmpin[:, 1, gl, tsl], in_=psA[b][:, 0:TB]), reads=[tpsA[b]], writes=[tcmpin[1][gl]])
        S.op("pool", lambda e: e.dma_start(out=W[:, :, 0:512], in_=wkv_d[:, :, 1024:1536]), writes=[tW], chan=cW)
        for j in range(TPB):
            Tq = tb * TPB + j
            jsl = slice(j * 128, (j + 1) * 128)
            b = nB % 2
            nB += 1
            for c in range(DC):
                S.op("pe", lambda e, c=c, b=b, jsl=jsl: e.matmul(psB[b][:], hn[:, c, jsl], W[:, c, 0:512], start=(c == 0), stop=(c == DC - 1)),
                     reads=[tW, thn[c][0]], writes=[tpsB[b]])
            S.op("act", lambda e, b=b, Tq=Tq: e.copy(out=vsel[:, Tq, :, 0:128], in_=psB[b][:, 0:256].rearrange("p (g d) -> p g d", g=2)),
                 reads=[tpsB[b]], writes=[tvsel[Tq]])
            S.op("dve", lambda e, b=b, Tq=Tq: e.tensor_copy(out=vwin[:, Tq, :, 0:128], in_=psB[b][:, 256:512].rearrange("p (g d) -> p g d", g=2)),
                 reads=[tpsB[b]], writes=[tvwin[Tq]])

    if stop == 1:
        S.op("pool", lambda e: e.dma_start(out=y_d[0:128, 0:256], in_=h_sb[:, 0, :]), reads=[th[0][0]], chan=cmisc)
        S.emit(final_waits=[cmisc])
        C.st.close()
        return nc
    Wflat = W[:].rearrange("p a b -> p (a b)")
    w1 = Wflat[:, 0:8192].rearrange("p (k l o) -> p k l o", k=2, l=32)
    w2 = Wflat[:, 8192:8448].rearrange("p (k o) -> p k o", k=2)
    posb = Wflat[:, 8448:8512].rearrange("p (k l) -> p k l", k=2)
    S.op("pool", lambda e: e.dma_start(out=w1, in_=w1_d), writes=[tW], chan=cW)
    S.op("pool", lambda e: e.dma_start(out=w2, in_=w2_d), writes=[tW], chan=cW)
    S.op("pool", lambda e: e.dma_start(out=posb, in_=pos_d), writes=[tW], chan=cW)
    S.op("pool", lambda e: e.dma_start(out=vcmp[:, 0, :, 129:193], in_=tabs["ov"]), writes=[tvcmp], chan=cmisc)
    S.op("pool", lambda e: e.dma_start(out=vcmp[:, 1, :, 129:193], in_=tabs["ov"]), writes=[tvcmp], chan=cmisc)
    bias1 = C.sb([128, 2], F32, "bias1")
    xg = C.sb([128, 256], F32, "xg")
    t1 = C.sb([128, 256], F32, "t1")
    t2 = C.sb([128, 256], F32, "t2")
    H1g = C.sb([128, CH * 128], BF16, "H1g")
    tbias1, txg, tt1, tt2, tH1g = [T_() for _ in range(5)]
    NCP = NCMP
    for kv in range(2):
        for l in range(32):
            S.op("pe", lambda e, kv=kv, l=l: e.matmul(psS[:, kv:kv + 1], w1[:, kv, l, :], posb[:, kv, l:l + 1], start=(l == 0), stop=(l == 31)),
                 reads=[tW], writes=[tpsS])
        S.op("dve", lambda e, kv=kv: e.tensor_copy(out=bias1[:, kv:kv + 1], in_=psS[:, kv:kv + 1]), reads=[tpsS], writes=[tbias1])
    S.op("dve", lambda e: e.memset(H1g[:], 0.0), writes=[tH1g])
    for kv in range(2):
        for gl in range(2):
            b = nA[0] % 2
            nA[0] += 1
            for l in range(32):
                S.op("pe", lambda e, kv=kv, gl=gl, l=l, b=b: e.matmul(psA[b][:, 0:NCP], w1[:, kv, l, :],
                                                                     cmpin[:, kv, gl, l:l + 16 * (NCP - 1) + 1:16],
                                                                     start=(l == 0), stop=(l == 31)),
                     reads=[tW, tcmpin[kv][gl]], writes=[tpsA[b]])
            for c0 in range(0, NCP, 256):
                n = min(256, NCP - c0)
                S.op("dve", lambda e, b=b, kv=kv, c0=c0, n=n: e.tensor_scalar(out=xg[:, 0:n], in0=psA[b][:, c0:c0 + n], scalar1=bias1[:, kv:kv + 1],
                                                                            scalar2=None, op0=ALU.add), reads=[tpsA[b], tbias1], writes=[txg])
                emit_gelu(S, H1g[:, c0:c0 + n], xg[:, 0:n], t1[:, 0:n], t2[:, 0:n], [txg], [tH1g], tt1, tt2)
            if kv == 0:
                bb = nA[0] % 2
                nA[0] += 1
                S.op("pe", lambda e, bb=bb: e.matmul(psA[bb][:, 0:NCP], w2[:, 0, :], H1g[:, 0:NCP], start=True, stop=True),
                     reads=[tW, tH1g], writes=[tpsA[bb]])
                S.op("act", lambda e, bb=bb, gl=gl: e.copy(out=kcmpT[:, gl, 0:NCP], in_=psA[bb][:, 0:NCP]), reads=[tpsA[bb]], writes=[tkcmpT])
                colnorm_max(kcmpT[:, gl, 0:NCP], [tkcmpT], gl, ncols=NCP)
            else:
                for ch in range(CH):
                    bb = nA[0] % 2
                    nA[0] += 1
                    S.op("pe", lambda e, bb=bb, ch=ch: e.matmul(psA[bb][:, 0:128], H1g[:, ch * 128:(ch + 1) * 128], w2[:, 1, :], start=True, stop=True),
                         reads=[tW, tH1g], writes=[tpsA[bb]])
                    S.op("act", lambda e, bb=bb, gl=gl, ch=ch: e.copy(out=vcmp[:, gl, ch, 0:128], in_=psA[bb][:, 0:128]),
                         reads=[tpsA[bb]], writes=[tvcmp])

    if stop == 2:
        S.op("pool", lambda e: e.dma_start(out=y_d[0:128, 0:256], in_=h_sb[:, 0, :]), reads=[th[0][0]], chan=cmisc)
        S.emit(final_waits=[cmisc])
        C.st.close()
        return nc
    S.op("pool", lambda e: e.dma_start(out=W[:], in_=wq_d), writes=[tW], chan=cW)
    qT = C.sb([128, 8, TB], BF16, "qT")
    tqT = [T_() for _ in range(8)]
    gsb = [C.sb([128, 24], F32, "gsb") for _ in range(2)]
    tgsb = [T_(), T_()]
    cmk = [C.sb([128, CH, 128], BF16, "cmk") for _ in range(2)]
    tcmk = [T_(), T_()]
    ccmk = [S.chan(), S.chan()]
    fpt = [C.sb([128, 64], F32, "fpos") for _ in range(2)]
    fnt = [C.sb([128, 64], F32, "fneg") for _ in range(2)]
    tfp = [T_(), T_()]
    cfp = [S.chan(), S.chan()]
    P = [C.sb([128, 512], BF16, "P") for _ in range(3)]
    tP = [T_() for _ in range(3)]
    qsq = C.sb([128, 512], F32R, "qsq")
    tqsq = T_()
    mq2 = C.sb([128, 1], F32, "mq2")
    nbias = C.sb([128, 3], F32, "nbias")
    tmq2, tnbias = T_(), T_()
    rz = C.sb([128, 4], F32, "rz")
    coef = C.sb([128, 4], F32, "coef")
    trz, tcoef = T_(), T_()
    imp = C.sb([128, 64], F32, "imp")
    imp2 = C.sb([128, 64], F32, "imp2")
    mx8 = C.sb([128, 8], F32, "mx8")
    mx8b = C.sb([128, 8], F32, "mx8b")
    negblk = C.sb([128, 64], F32, "negblk")
    negblkT = C.sb([64, 128], BF16, "negblkT")
    timp, timp2, tmx8, tmx8b, tnegblk, tnegblkT = [T_() for _ in range(6)]
    yt = [C.sb([128, 1024], F32, "yt") for _ in range(2)]
    tyt = [T_(), T_()]
    cy = [S.chan(), S.chan()]
    nP = [0]
    nSc = [0]

    def softmax_chunk(mm_list, br, rhs_v, first, last, width):
        b = nSc[0] % 2
        nSc[0] += 1
        n = len(mm_list)
        for i, (lhsT, rhs, rd) in enumerate(mm_list):
            S.op("pe", lambda e, lhsT=lhsT, rhs=rhs, i=i, b=b: e.matmul(psB[b][:], lhsT, rhs, start=(i == 0), stop=(i == n - 1)),
                 reads=rd, writes=[tpsB[b]])
        p = nP[0] % 3
        nP[0] += 1
        S.op("act", lambda e, b=b, p=p, br=br: e.activation(out=P[p][:], in_=psB[b][:], func=AF.Exp, bias=nbias[:, br:br + 1], scale=1.0),
             reads=[tpsB[b], tnbias], writes=[tP[p]])
        for r in range(4):
            S.op("pe", lambda e, r=r, p=p: e.matmul(psO[r][:, 0:width], P[p][:, r * 128:(r + 1) * 128], rhs_v[0],
                                                  start=first, stop=last),
                 reads=[tP[p]] + rhs_v[1], writes=[tpsO[r]])

    def evac(br, gl, kk, width, first_branch, with_imp=False):
        first_branch = (br == branches[0])
        for r in range(4):
            o_ap = psO[r][:, 0:128]
            z_ap = psO[r][:, 128:129]
            S.op("dve", lambda e, r=r, z_ap=z_ap: e.tensor_scalar(out=rz[:, r:r + 1], in0=z_ap, scalar1=1.0e-30, scalar2=None, op0=ALU.add),
                 reads=[tpsO[r]], writes=[trz])
            S.op("dve", lambda e, r=r: e.reciprocal(out=rz[:, r:r + 1], in_=rz[:, r:r + 1]), reads=[trz], writes=[trz])
            gc = br * 8 + gl * 4 + r
            S.op("dve", lambda e, r=r, gc=gc, kk=kk: e.tensor_tensor(out=coef[:, r:r + 1], in0=rz[:, r:r + 1], in1=gsb[kk][:, gc:gc + 1], op=ALU.mult),
                 reads=[trz, tgsb[kk]], writes=[tcoef])
            ysl = slice((gl * 4 + r) * 128, (gl * 4 + r + 1) * 128)
            if br not in branches:
                pass
            elif first_branch:
                S.op("dve", lambda e, r=r, kk=kk, ysl=ysl, o_ap=o_ap: e.tensor_scalar(out=yt[kk][:, ysl], in0=o_ap, scalar1=coef[:, r:r + 1], scalar2=None, op0=ALU.mult),
                     reads=[tpsO[r], tcoef], writes=[tyt[kk]])
            else:
                S.op("dve", lambda e, r=r, kk=kk, ysl=ysl, o_ap=o_ap: e.scalar_tensor_tensor(out=yt[kk][:, ysl], in0=o_ap, scalar=coef[:, r:r + 1], in1=yt[kk][:, ysl],
                                                                                         op0=ALU.mult, op1=ALU.add),
                     reads=[tpsO[r], tcoef, tyt[kk]], writes=[tyt[kk]])
            if with_imp:
                i_ap = psO[r][:, 129:193]
                if r == 0:
                    S.op("dve", lambda e, r=r, i_ap=i_ap: e.tensor_scalar(out=imp[:], in0=i_ap, scalar1=rz[:, r:r + 1], scalar2=None, op0=ALU.mult),
                         reads=[tpsO[r], trz], writes=[timp])
                else:
                    S.op("dve", lambda e, r=r, i_ap=i_ap: e.scalar_tensor_tensor(out=imp[:], in0=i_ap, scalar=rz[:, r:r + 1], in1=imp[:], op0=ALU.mult, op1=ALU.add),
                         reads=[tpsO[r], trz, timp], writes=[timp])

    ntile = 0
    for tb in range(NBLK):
        tsl = slice(tb * TB, (tb + 1) * TB)
        norm_block(tsl)
        load_rope("cosq", "sinq", tsl)
        for hd in range(8):
            b = proj_fm(hd * 128)
            rope_to(b, qT[:, hd, :], [tqT[hd]])
        for j in range(TPB):
            Tq = tb * TPB + j
            jsl = slice(j * 128, (j + 1) * 128)
            kk = ntile % 2
            ntile += 1
            for c in range(DC):
                S.op("pe", lambda e, c=c, jsl=jsl: e.matmul(psS[:, 0:24], hn[:, c, jsl], Wgt[:, c, :], start=(c == 0), stop=(c == DC - 1)),
                     reads=[tWgt, thn[c][0]], writes=[tpsS])
            S.op("act", lambda e, kk=kk: e.activation(out=gsb[kk][:], in_=psS[:, 0:24], func=AF.Sigmoid), reads=[tpsS], writes=[tgsb[kk]])
            S.op("pool", lambda e, kk=kk, Tq=Tq: e.dma_start(out=cmk[kk][:], in_=tabs["cmpmask"][Tq]), writes=[tcmk[kk]], chan=ccmk[kk])
            S.op("sp", lambda e, kk=kk, Tq=Tq: e.dma_start(out=fpt[kk][:], in_=tabs["fpos"][Tq]), writes=[tfp[kk]], chan=cfp[kk])
            S.op("sp", lambda e, kk=kk, Tq=Tq: e.dma_start(out=fnt[kk][:], in_=tabs["fneg"][Tq]), writes=[tfp[kk]], chan=cfp[kk])
            for gl in range(2):
                qv = qT[:, gl * 4:(gl + 1) * 4, jsl]
                qrd = [tqT[gl * 4 + r] for r in range(4)]
                S.op("act", lambda e, qv=qv: e.activation(out=qsq[:].rearrange("p (r q) -> p r q", r=4), in_=qv, func=AF.Square), reads=qrd, writes=[tqsq])
                S.op("pe", lambda e: e.matmul(psS[:, 0:512], ones_r[:], qsq[:], start=True, stop=True), reads=[tqsq, tones], writes=[tpsS])
                S.op("dve", lambda e: e.tensor_reduce(out=mq2[:], in_=psS[:, 0:512], axis=AX.X, op=ALU.max), reads=[tpsS], writes=[tmq2])
                S.op("dve", lambda e, gl=gl: e.tensor_scalar(out=nbias[:], in0=kmax2[:, gl:gl + 5:2], scalar1=mq2[:, 0:1], scalar2=None, op0=ALU.mult),
                     reads=[tkmax2, tmq2], writes=[tnbias])
                S.op("act", lambda e: e.activation(out=nbias[:], in_=nbias[:], func=AF.Sqrt), reads=[tnbias], writes=[tnbias])
                S.op("dve", lambda e: e.tensor_scalar(out=nbias[:], in0=nbias[:], scalar1=-1.0, scalar2=None, op0=ALU.mult), reads=[tnbias], writes=[tnbias])
                chs = [ch for ch in range(CH) if 16 * (ch * 128) + 31 <= Tq * 128 + 127]
                for ci, ch in enumerate(chs):
                    mm = [(kcmpT[:, gl, ch * 128:(ch + 1) * 128], qv, [tkcmpT] + qrd),
                          (identb[:], cmk[kk][:, ch, None, :].to_broadcast([128, 4, 128]), [tidb, tcmk[kk]])]
                    softmax_chunk(mm, 0, (vcmp[:, gl, ch, :], [tvcmp]), ci == 0, ci == len(chs) - 1, 193)
                evac(0, gl, kk, 193, True, with_imp=True)
                S.op("dve", lambda e, kk=kk: e.tensor_tensor(out=imp[:], in0=imp[:], in1=fpt[kk][:], op=ALU.max), reads=[timp, tfp[kk]], writes=[timp])
                S.op("dve", lambda e, kk=kk: e.tensor_tensor(out=imp[:], in0=imp[:], in1=fnt[kk][:], op=ALU.min), reads=[timp, tfp[kk]], writes=[timp])
                S.op("dve", lambda e: e.max(out=mx8[:], in_=imp[:]), reads=[timp], writes=[tmx8])
                S.op("dve", lambda e: e.match_replace(out=imp2[:], in_to_replace=mx8[:], in_values=imp[:], imm_value=-3.0e38),
                     reads=[timp, tmx8], writes=[timp2])
                S.op("dve", lambda e: e.max(out=mx8b[:], in_=imp2[:]), reads=[timp2], writes=[tmx8b])
                S.op("dve", lambda e: e.tensor_reduce(out=kred[:], in_=mx8b[:], axis=AX.X, op=ALU.min), reads=[tmx8b], writes=[tkred])
                S.op("dve", lambda e: e.tensor_scalar(out=negblk[:], in0=imp[:], scalar1=kred[:, 0:1], scalar2=None, op0=ALU.is_ge),
                     reads=[timp, tkred], writes=[tnegblk])
                S.op("dve", lambda e: e.tensor_scalar(out=negblk[:], in0=negblk[:], scalar1=1.0e30, scalar2=-1.0e30, op0=ALU.mult, op1=ALU.add),
                     reads=[tnegblk], writes=[tnegblk])
                S.op("pe", lambda e: e.transpose(psS[0:64, 0:128], negblk[:], ident[:]), reads=[tnegblk, tid], writes=[tpsS])
                S.op("act", lambda e: e.copy(out=negblkT[:], in_=psS[0:64, 0:128]), reads=[tpsS], writes=[tnegblkT])
                for kc in range(Tq + 1):
                    mm = [(kTs[:, 0, gl, kc * 128:(kc + 1) * 128], qv, [tkTs[0][gl][kc // TPB]] + qrd),
                          (Eb[:, kc, :], negblkT[:, None, :].to_broadcast([64, 4, 128]), [tE, tnegblkT])]
                    if kc == Tq:
                        mm.append((identb[:], causal[:, None, :].to_broadcast([128, 4, 128]), [tidb, tcausal]))
                    softmax_chunk(mm, 1, (vsel[:, kc, gl, :], [tvsel[kc]]), kc == 0, kc == Tq, 129)
                evac(1, gl, kk, 129, False)
                k0 = max(0, Tq - 4)
                for kc in range(k0, Tq + 1):
                    mm = [(kTs[:, 1, gl, kc * 128:(kc + 1) * 128], qv, [tkTs[1][gl][kc // TPB]] + qrd)]
                    if kc == Tq:
                        mm.append((identb[:], causal[:, None, :].to_broadcast([128, 4, 128]), [tidb, tcausal]))
                    if kc == Tq - 4:
                        mm.append((identb[:], winneg[:, None, :].to_broadcast([128, 4, 128]), [tidb, twinneg]))
                    softmax_chunk(mm, 2, (vwin[:, kc, gl, :], [tvwin[kc]]), kc == k0, kc == Tq, 129)
                evac(2, gl, kk, 129, False)
            S.op("sp", lambda e, kk=kk, Tq=Tq: e.dma_start(out=y_d[Tq * 128:(Tq + 1) * 128, :], in_=yt[kk][:]), reads=[tyt[kk]], chan=cy[kk])
    S.emit(final_waits=cy)
    C.st.close()
    return nc


def _lay(w):
    dc = w.shape[0] // 128
    return np.ascontiguousarray(w.reshape(dc, 128, -1).transpose(1, 0, 2))


def _gT(g):
    return np.ascontiguousarray(np.asarray(g, np.float32).reshape(-1, 128).T)


def _prep_ffn_w(Wg, Wu, Wd):
    D, FF = Wg.shape
    DC, FC = D // 128, FF // 128
    t = lambda W: np.ascontiguousarray(W.reshape(DC, 128, FC, 128).transpose(2, 1, 0, 3).reshape(FC, 128, DC * 128))
    wd = np.ascontiguousarray(Wd.reshape(FC, 128, DC, 128).transpose(2, 1, 0, 3).reshape(DC, 128, FC * 128))
    return t(Wg), t(Wu), wd


def _prep_wo(Wo):
    DC = Wo.shape[0] // 128
    DO = Wo.shape[1] // 128
    return np.ascontiguousarray(Wo.reshape(DC, 128, DO, 128).transpose(2, 1, 0, 3).reshape(DO, 128, DC * 128))


def _prep_ab(w_in, gate_b, g_norm, g_ws, g_bs, conv_w, m_norm, hp):
    o1, o2, o3, o4 = 2048, 4096, 5120, 6144
    wz = _lay(w_in[:, :o1])
    hs = slice(hp * 512, (hp + 1) * 512)
    wqk = _lay(np.concatenate([w_in[:, o1:o1 + 1024][:, hs], w_in[:, o1 + 1024:o2][:, hs]], 1))
    gi = w_in[:, o4:o4 + 4][:, hp * 2:hp * 2 + 2]
    gf = w_in[:, o4 + 4:o4 + 8][:, hp * 2:hp * 2 + 2]
    wvog = _lay(np.concatenate([w_in[:, o2:o3][:, hs], w_in[:, o3:o4][:, hs], gi, gf], 1))
    gb = np.concatenate([gate_b[0, hp * 2:hp * 2 + 2], gate_b[1, hp * 2:hp * 2 + 2]])
    gb = np.ascontiguousarray(np.broadcast_to(gb[None, :], (128, 4))).astype(np.float32)
    cc = np.concatenate([conv_w[:, :1024][:, hs], conv_w[:, 1024:][:, hs]], 1)
    conv = np.ascontiguousarray(cc.T.reshape(8, 128, 4).transpose(1, 0, 2))
    mn = np.ascontiguousarray(np.broadcast_to(m_norm[hs][None, :], (128, 512)))
    gn = np.ascontiguousarray(np.broadcast_to(g_norm[None, :], (128, 1024)))
    wsT = np.ascontiguousarray(g_ws.transpose(2, 0, 1))
    bs = np.ascontiguousarray(g_bs.T)
    ident = np.eye(128, dtype=np.float32)
    mask = np.triu(np.ones((128, 128), np.float32))
    return dict(wz=wz, wqk=wqk, wvog=wvog, gb=gb, conv=conv, mn=mn, gn=gn, wsT=wsT, bs=bs, ident=ident, mask=mask)


def _prep_nsa(w_in, cmp_pos, cmp_w1, cmp_w2, hp):
    wq = _lay(w_in[:, hp * 1024:(hp + 1) * 1024])

    def kvcol(br, kv, g):
        o = 2048 + ((br * 2 + kv) * 4 + g) * 128
        return w_in[:, o:o + 128]
    cols = []
    for br in range(3):
        for gl in range(2):
            cols.append(kvcol(br, 0, 2 * hp + gl))
    for gl in range(2):
        cols.append(kvcol(0, 1, 2 * hp + gl))
    for br in (1, 2):
        for gl in range(2):
            cols.append(kvcol(br, 1, 2 * hp + gl))
    wkv = _lay(np.concatenate(cols, 1))
    og = 2048 + 3072
    gc = []
    for br in range(3):
        for gl in range(2):
            g = 2 * hp + gl
            gc.append(w_in[:, og + br * 16 + g * 4: og + br * 16 + g * 4 + 4])
    wgt = _lay(np.concatenate(gc, 1))
    w1 = np.ascontiguousarray(cmp_w1.reshape(2, 32, 128, 128).transpose(2, 0, 1, 3))
    w2 = np.ascontiguousarray(cmp_w2.transpose(1, 0, 2))
    posT = np.ascontiguousarray(cmp_pos.transpose(2, 0, 1))
    return dict(wq=wq, wkv=wkv, wgt=wgt, w1=w1, w2=w2, posT=posT)


_PROGS = {}


def _prog(key, fn):
    if key not in _PROGS:
        _PROGS[key] = fn()
    return _PROGS[key]


def _run(nc, maps):
    res = run_bass_kernel_spmd(nc, maps, core_ids=list(range(8)))
    return res.results


def kernel(x, ffn_norm, ffn_w_gate, ffn_w_up, ffn_w_down, mix_norm, ab_w_in, mlstm_gate_bias,
           gmlp_norm, gmlp_w_s, gmlp_b_s, mlstm_conv, mlstm_norm, ab_w_out, nsa_w_in, nsa_cmp_pos,
           nsa_cmp_w1, nsa_cmp_w2, nsa_w_out, final_norm):
    f = lambda a: np.asarray(a, dtype=np.float32)
    x = f(x)
    B, Sq, D = x.shape
    FF = ffn_w_gate.shape[-1]
    NTC = B * Sq // 8
    hT = [np.ascontiguousarray(x[c // 2, (c % 2) * NTC:(c % 2 + 1) * NTC, :].T) for c in range(8)]

    def ffn(hT, layer, which, yT=None, wo=None, final=False):
        wg, wu, wd = _prep_ffn_w(f(ffn_w_gate[layer, which]), f(ffn_w_up[layer, which]), f(ffn_w_down[layer, which]))
        g = _gT(ffn_norm[layer, which])
        pre = yT is not None
        nc = _prog(("ffn", pre, final), lambda: build_ffn(D, FF, NTC, 1024, 2, pre=pre, final=final))
        maps = []
        for c in range(8):
            m = {"hT": hT[c], "g": g, "wg": wg, "wu": wu, "wd": wd}
            if pre:
                m["yT"] = yT[c]
                m["wo"] = wo
            if final:
                m["gf"] = _gT(final_norm)
            maps.append(m)
        r = _run(nc, maps)
        return [r[c]["oT"] for c in range(8)]

    def full_seq(hT, b):
        return np.ascontiguousarray(np.concatenate([hT[2 * b], hT[2 * b + 1]], axis=1))

    hT = ffn(hT, 0, 0)
    nc = _prog(("ab",), lambda: build_ab(Sq, D))
    maps = []
    for c in range(8):
        b, hp = c // 2, c % 2
        m = _prep_ab(f(ab_w_in[0]), f(mlstm_gate_bias[0]), f(gmlp_norm[0]), f(gmlp_w_s[0]), f(gmlp_b_s[0]), f(mlstm_conv[0]),
                     f(mlstm_norm[0]), hp)
        m["hT"] = full_seq(hT, b)
        m["hTh"] = hT[c]
        m["g"] = _gT(mix_norm[0])
        maps.append(m)
    r = _run(nc, maps)
    yT = []
    for c in range(8):
        b, hp = c // 2, c % 2
        sl = slice(hp * NTC, (hp + 1) * NTC)
        ya = r[c]["ya"]
        yb = np.concatenate([r[2 * b]["yb"][sl], r[2 * b + 1]["yb"][sl]], axis=1)
        yT.append(np.ascontiguousarray(np.concatenate([ya, yb], axis=1).T))
    hT = ffn(hT, 0, 1, yT=yT, wo=_prep_wo(f(ab_w_out[0])))
    hT = ffn(hT, 1, 0)
    nc = _prog(("nsa",), lambda: build_nsa(Sq, D))
    tabs = nsa_tables(Sq)
    maps = []
    for c in range(8):
        b, hp = c // 2, c % 2
        m = _prep_nsa(f(nsa_w_in[0]), f(nsa_cmp_pos[0]), f(nsa_cmp_w1[0]), f(nsa_cmp_w2[0]), hp)
        m.update(tabs)
        m["hT"] = full_seq(hT, b)
        m["g"] = _gT(mix_norm[1])
        maps.append(m)
    r = _run(nc, maps)
    yT = []
    for c in range(8):
        b, hp = c // 2, c % 2
        sl = slice(hp * NTC, (hp + 1) * NTC)
        y = np.concatenate([r[2 * b]["y"][sl], r[2 * b + 1]["y"][sl]], axis=1)
        yT.append(np.ascontiguousarray(y.T))
    hT = ffn(hT, 1, 1, yT=yT, wo=_prep_wo(f(nsa_w_out[0])), final=True)
    out = np.empty((B, Sq, D), np.float32)
    for c in range(8):
        out[c // 2, (c % 2) * NTC:(c % 2 + 1) * NTC, :] = hT[c].T
    return out
```

```python
import bisect
import contextlib
import numpy as np
import concourse.bass as bass
import concourse.mybir as mybir
from concourse.bass_utils import run_bass_kernel_spmd

F32 = mybir.dt.float32
F32R = mybir.dt.float32r
BF16 = mybir.dt.bfloat16
AF = mybir.ActivationFunctionType
ALU = mybir.AluOpType
AX = mybir.AxisListType

ENGS = ("pe", "act", "dve", "pool", "sp")


class Tile:
    __slots__ = ("name", "w", "r", "psum")

    def __init__(self, name="", psum=False):
        self.name = name
        self.w = None
        self.r = []
        self.psum = psum


class Chan:
    __slots__ = ("sem", "cnt", "ops")

    def __init__(self):
        self.sem = None
        self.cnt = 0
        self.ops = []


class Sched:
    def __init__(self, nc):
        self.nc = nc
        self.ops = []
        self.chans = []

    def chan(self):
        c = Chan()
        self.chans.append(c)
        return c

    def op(self, eng, fn, reads=(), writes=(), chan=None, nosync_same=False):
        idx = len(self.ops)
        deps = set()
        for t in reads:
            if t.w is not None:
                deps.add(t.w)
            if t.psum:
                for r in t.r:
                    if self.ops[r]["eng"] != eng:
                        deps.add(r)
        for t in writes:
            if t.w is not None:
                deps.add(t.w)
            for r in t.r:
                deps.add(r)
        deps.discard(idx)
        rec = dict(eng=eng, fn=fn, deps=deps, chan=chan, idx=idx, nosync_same=nosync_same,
                   sig=None)
        if chan is not None:
            chan.cnt += 16
            rec["sig"] = (chan, chan.cnt)
            chan.ops.append(idx)
        self.ops.append(rec)
        for t in reads:
            t.r.append(idx)
        for t in writes:
            t.w = idx
            t.r = []
        return idx

    def emit(self, final_waits=()):
        nc = self.nc
        ops = self.ops
        needed = set()
        for o in ops:
            for d in o["deps"]:
                od = ops[d]
                if od["chan"] is None:
                    if od["eng"] == o["eng"] and (o["nosync_same"] or od["eng"] == "pe" or od["eng"] == "sp"):
                        continue
                    needed.add(d)
        cnt = {e: 0 for e in ENGS}
        for o in ops:
            if o["chan"] is None and o["idx"] in needed:
                cnt[o["eng"]] += 1
                o["sig"] = (o["eng"], cnt[o["eng"]])
        with contextlib.ExitStack() as st:
            esem = {e: st.enter_context(nc.semaphore("s_" + e)) for e in ENGS}
            for i, c in enumerate(self.chans):
                c.sem = st.enter_context(nc.semaphore("c%d" % i))
            block = st.enter_context(nc.Block())

            def run_engine(ename, eobj):
                known = {}
                for o in ops:
                    if o["eng"] != ename:
                        continue
                    want = {}
                    for d in o["deps"]:
                        od = ops[d]
                        sig = od["sig"]
                        if sig is None:
                            continue
                        key, val = sig
                        if od["chan"] is None and od["eng"] == ename and (
                                o["nosync_same"] or ename in ("pe", "sp")):
                            continue
                        if isinstance(key, Chan):
                            val = 16 * bisect.bisect_left(key.ops, o["idx"])
                        if want.get(key, 0) < val:
                            want[key] = val
                    for key, val in want.items():
                        if known.get(key, 0) >= val:
                            continue
                        known[key] = val
                        sem = key.sem if isinstance(key, Chan) else esem[key]
                        eobj.wait_ge(sem, val)
                    ins = o["fn"](eobj)
                    if o["sig"] is not None:
                        key, val = o["sig"]
                        if isinstance(key, Chan):
                            ins.then_inc(key.sem, 16)
                        else:
                            ins.then_inc(esem[key], 1)
                if ename == "sp":
                    for c in final_waits:
                        eobj.wait_ge(c.sem, c.cnt)

            @block.tensor
            def _(e):
                run_engine("pe", e)

            @block.scalar
            def _(e):
                run_engine("act", e)

            @block.vector
            def _(e):
                run_engine("dve", e)

            @block.gpsimd
            def _(e):
                run_engine("pool", e)

            @block.sync
            def _(e):
                run_engine("sp", e)


class Ctx:
    def __init__(self):
        self.nc = bass.Bass("TRN2", target_bir_lowering=False)
        self.st = contextlib.ExitStack()
        self.S = Sched(self.nc)
        self.n = 0

    def sb(self, shape, dt, name=None):
        self.n += 1
        return self.st.enter_context(self.nc.sbuf_tensor("%s_%d" % (name or "sb", self.n), list(shape), dt))

    def ps(self, shape=(128, 512), dt=F32, name=None):
        self.n += 1
        return self.st.enter_context(self.nc.psum_tensor("%s_%d" % (name or "ps", self.n), list(shape), dt))

    def din(self, name, shape, dt=F32):
        return self.nc.dram_tensor(name, list(shape), dt, kind="ExternalInput").ap()

    def dout(self, name, shape, dt=F32):
        return self.nc.dram_tensor(name, list(shape), dt, kind="ExternalOutput").ap()


EPS = 1e-6


def emit_rstd(S, rstd, trstd, ps_ssq, tssq, Dn, SUB=512):
    S.op("act", lambda e: e.activation(out=rstd[:, 0:SUB], in_=ps_ssq[:, 0:SUB], func=AF.Sqrt, bias=EPSB[0][:, 0:1], scale=1.0 / Dn),
         reads=[tssq, EPSB[1]], writes=[trstd])
    S.op("dve", lambda e: e.reciprocal(out=rstd[:, 0:SUB], in_=rstd[:, 0:SUB]), reads=[trstd], writes=[trstd])


EPSB = [None, None]


def emit_consts(C):
    S = C.S
    eps = C.sb([128, 1], F32, "eps")
    teps = Tile()
    S.op("dve", lambda e: e.memset(eps[:], EPS), writes=[teps])
    EPSB[0], EPSB[1] = eps, teps
    ones32 = C.sb([128, 128], F32, "ones32")
    ones_r = C.sb([128, 128], F32R, "ones")
    t32, tones = Tile(), Tile()
    S.op("dve", lambda e: e.memset(ones32[:], 1.0), writes=[t32])
    S.op("dve", lambda e: e.tensor_copy(out=ones_r[:], in_=ones32[:]), reads=[t32], writes=[tones])
    return ones_r, tones


def emit_rmsnorm_T(C, h_sb, th, aT, taT, g_sb, tg, ones_r, tones, DC, TB, sq, tsq, ps_ssq, tssq, rstd, trstd,
                   Dn, out_dt_cast=None):
    S = C.S
    SUB = min(512, TB)
    NS = TB // SUB
    for s in range(NS):
        sl = slice(s * SUB, (s + 1) * SUB)
        for c in range(DC):
            k = (s * DC + c) % 2
            S.op("act", lambda e, c=c, k=k, sl=sl: e.activation(out=sq[k][:, 0:SUB], in_=h_sb[:, c, sl], func=AF.Square),
                 reads=[th[c][s]], writes=[tsq[k]])
            S.op("pe", lambda e, c=c, k=k: e.matmul(ps_ssq[:, 0:SUB], ones_r[:], sq[k][:, 0:SUB], start=(c == 0), stop=(c == DC - 1)),
                 reads=[tsq[k], tones], writes=[tssq])
        emit_rstd(S, rstd, trstd, ps_ssq, tssq, Dn, SUB)
        for c in range(DC):
            S.op("dve", lambda e, c=c, sl=sl: e.scalar_tensor_tensor(out=aT[:, c, sl], in0=h_sb[:, c, sl],
                                                                      scalar=g_sb[:, c:c + 1], in1=rstd[:, 0:SUB],
                                                                      op0=ALU.mult, op1=ALU.mult),
                 reads=[th[c][s], tg, trstd], writes=[taT[c][s]])


def build_ffn(D, FF, NT, TB, NH, pre=False, final=False):
    C = Ctx()
    nc, S = C.nc, C.S
    DC, FC = D // 128, FF // 128
    FH = FC // NH
    NS = TB // 512
    NB = NT // TB
    hT = C.din("hT", [D, NT])
    g_d = C.din("g", [128, DC])
    wg_d = C.din("wg", [FC, 128, DC * 128])
    wu_d = C.din("wu", [FC, 128, DC * 128])
    wd_d = C.din("wd", [DC, 128, FC * 128])
    if pre:
        yT = C.din("yT", [D, NT])
        wo_d = C.din("wo", [DC, 128, DC * 128])
    if final:
        gf_d = C.din("gf", [128, DC])
    oT = C.dout("oT", [D, NT])
    hTv = hT.rearrange("(c p) t -> p c t", p=128)
    oTv = oT.rearrange("(c p) t -> p c t", p=128)

    h_sb = C.sb([128, DC, TB], F32, "h")
    aT = C.sb([128, DC, TB], BF16, "aT")
    HT = C.sb([128, FH, TB], BF16, "HT")
    wg = [C.sb([128, DC * 128], BF16, "wg") for _ in range(2)]
    wu = [C.sb([128, DC * 128], BF16, "wu") for _ in range(2)]
    wd = [C.sb([128, FH * 128], BF16, "wd") for _ in range(2)]
    sq = [C.sb([128, 512], F32R, "sq") for _ in range(2)]
    sg = [C.sb([128, 512], F32, "sg") for _ in range(2)]
    rstd = C.sb([128, 512], F32, "rstd")
    g_sb = C.sb([128, DC], F32, "g")
    ps_ssq = C.ps(name="ssq")
    psG = [C.ps(name="G") for _ in range(2)]
    psU = [C.ps(name="U") for _ in range(2)]
    psY = [C.ps(name="Y") for _ in range(2)]
    if pre:
        wo = [C.sb([128, DC * 128], BF16, "wo") for _ in range(2)]
        two = [Tile() for _ in range(2)]
        cwo = [S.chan() for _ in range(2)]
    if final:
        gf_sb = C.sb([128, DC], F32, "gf")
        tgf = Tile()
        fo = C.sb([128, DC, TB], F32, "fo") if False else None

    th = [[Tile() for _ in range(NS)] for _ in range(DC)]
    taT = [[Tile() for _ in range(NS)] for _ in range(DC)]
    tHT = [[Tile() for _ in range(NS)] for _ in range(FH)]
    twg = [Tile() for _ in range(2)]
    twu = [Tile() for _ in range(2)]
    twd = [Tile() for _ in range(2)]
    tsq = [Tile() for _ in range(2)]
    tsg = [Tile() for _ in range(2)]
    trstd, tg, tssq = Tile(), Tile(), Tile()
    tG = [Tile(psum=True) for _ in range(2)]
    tU = [Tile(psum=True) for _ in range(2)]
    tY = [Tile(psum=True) for _ in range(2)]
    cwg = [S.chan() for _ in range(2)]
    cwu = [S.chan() for _ in range(2)]
    cwd = [S.chan() for _ in range(2)]
    ch_in = S.chan()
    ch_out = S.chan()
    cg = S.chan()

    S.op("sp", lambda e: e.dma_start(out=g_sb[:], in_=g_d), writes=[tg], chan=cg)
    if final:
        cgf = S.chan()
        S.op("sp", lambda e: e.dma_start(out=gf_sb[:], in_=gf_d), writes=[tgf], chan=cgf)
    ones_r, tones = emit_consts(C)

    all_h = [th[c][s] for c in range(DC) for s in range(NS)]
    all_aT = [taT[c][s] for c in range(DC) for s in range(NS)]
    CG = min(4, DC)
    nwd = 0
    nwgu = 0
    for tb in range(NB):
        tsl = slice(tb * TB, (tb + 1) * TB)
        for c0 in range(0, DC, CG):
            S.op("sp", lambda e, c0=c0, tsl=tsl: e.dma_start(out=h_sb[:, c0:c0 + CG, :], in_=hTv[:, c0:c0 + CG, tsl]),
                 writes=[th[c][s] for c in range(c0, c0 + CG) for s in range(NS)], chan=ch_in)
        if pre:
            for c0 in range(0, DC, CG):
                yv = yT.rearrange("(c p) t -> p c t", p=128)
                S.op("pool", lambda e, c0=c0, tsl=tsl, yv=yv: e.dma_start(out=aT[:, c0:c0 + CG, :], in_=yv[:, c0:c0 + CG, tsl]),
                     writes=[taT[c][s] for c in range(c0, c0 + CG) for s in range(NS)], chan=ch_in)
            for dc in range(DC):
                k = dc % 2
                S.op("pool", lambda e, dc=dc, k=k: e.dma_start(out=wo[k][:], in_=wo_d[dc]), writes=[two[k]], chan=cwo[k])
                for s in range(NS):
                    sl = slice(s * 512, (s + 1) * 512)
                    b = (dc * NS + s) % 2
                    for c in range(DC):
                        S.op("pe", lambda e, c=c, k=k, b=b, sl=sl: e.matmul(psY[b][:], wo[k][:, c * 128:(c + 1) * 128], aT[:, c, sl],
                                                                          start=(c == 0), stop=(c == DC - 1)),
                             reads=[two[k], taT[c][s]], writes=[tY[b]])
                    S.op("dve", lambda e, dc=dc, b=b, sl=sl: e.tensor_tensor(out=h_sb[:, dc, sl], in0=psY[b][:], in1=h_sb[:, dc, sl], op=ALU.add),
                         reads=[tY[b], th[dc][s]], writes=[th[dc][s]])
        emit_rmsnorm_T(C, h_sb, th, aT, taT, g_sb, tg, ones_r, tones, DC, TB, sq, tsq, ps_ssq, tssq, rstd, trstd, D)
        for hf in range(NH):
            for fi in range(FH):
                f = hf * FH + fi
                k = nwgu % 2
                nwgu += 1
                S.op("pool", lambda e, f=f, k=k: e.dma_start(out=wg[k][:], in_=wg_d[f]), writes=[twg[k]], chan=cwg[k])
                S.op("pool", lambda e, f=f, k=k: e.dma_start(out=wu[k][:], in_=wu_d[f]), writes=[twu[k]], chan=cwu[k])
                for s in range(NS):
                    sl = slice(s * 512, (s + 1) * 512)
                    b = (fi * NS + s) % 2
                    for c in range(DC):
                        S.op("pe", lambda e, c=c, k=k, b=b, sl=sl: e.matmul(psG[b][:], wg[k][:, c * 128:(c + 1) * 128], aT[:, c, sl],
                                                                          start=(c == 0), stop=(c == DC - 1)),
                             reads=[twg[k], taT[c][s]], writes=[tG[b]])
                    for c in range(DC):
                        S.op("pe", lambda e, c=c, k=k, b=b, sl=sl: e.matmul(psU[b][:], wu[k][:, c * 128:(c + 1) * 128], aT[:, c, sl],
                                                                          start=(c == 0), stop=(c == DC - 1)),
                             reads=[twu[k], taT[c][s]], writes=[tU[b]])
                    S.op("act", lambda e, b=b: e.activation(out=sg[b][:], in_=psG[b][:], func=AF.Silu),
                         reads=[tG[b]], writes=[tsg[b]])
                    S.op("dve", lambda e, b=b, fi=fi, sl=sl: e.tensor_tensor(out=HT[:, fi, sl], in0=psU[b][:], in1=sg[b][:], op=ALU.mult),
                         reads=[tU[b], tsg[b]], writes=[tHT[fi][s]])
            for dc in range(DC):
                k = nwd % 2
                nwd += 1
                S.op("pool", lambda e, dc=dc, hf=hf, k=k: e.dma_start(out=wd[k][:], in_=wd_d[dc, :, hf * FH * 128:(hf + 1) * FH * 128]),
                     writes=[twd[k]], chan=cwd[k])
                for s in range(NS):
                    sl = slice(s * 512, (s + 1) * 512)
                    b = (dc * NS + s) % 2
                    for fi in range(FH):
                        S.op("pe", lambda e, fi=fi, k=k, b=b, sl=sl: e.matmul(psY[b][:], wd[k][:, fi * 128:(fi + 1) * 128], HT[:, fi, sl],
                                                                            start=(fi == 0), stop=(fi == FH - 1)),
                             reads=[twd[k], tHT[fi][s]], writes=[tY[b]])
                    S.op("dve", lambda e, dc=dc, b=b, sl=sl: e.scalar_tensor_tensor(out=h_sb[:, dc, sl], in0=psY[b][:], scalar=0.5,
                                                                                 in1=h_sb[:, dc, sl], op0=ALU.mult, op1=ALU.add),
                         reads=[tY[b], th[dc][s]], writes=[th[dc][s]])
        if final:
            NSx = NS
            for s in range(NSx):
                sl = slice(s * 512, (s + 1) * 512)
                for c in range(DC):
                    k = (s * DC + c) % 2
                    S.op("act", lambda e, c=c, k=k, sl=sl: e.activation(out=sq[k][:], in_=h_sb[:, c, sl], func=AF.Square),
                         reads=[th[c][s]], writes=[tsq[k]])
                    S.op("pe", lambda e, c=c, k=k: e.matmul(ps_ssq[:], ones_r[:], sq[k][:], start=(c == 0), stop=(c == DC - 1)),
                         reads=[tsq[k], tones], writes=[tssq])
                emit_rstd(S, rstd, trstd, ps_ssq, tssq, D)
                for c in range(DC):
                    S.op("dve", lambda e, c=c, sl=sl: e.scalar_tensor_tensor(out=h_sb[:, c, sl], in0=h_sb[:, c, sl],
                                                                              scalar=gf_sb[:, c:c + 1], in1=rstd[:],
                                                                              op0=ALU.mult, op1=ALU.mult),
                         reads=[th[c][s], tgf, trstd], writes=[th[c][s]])
        for c0 in range(0, DC, CG):
            S.op("sp", lambda e, c0=c0, tsl=tsl: e.dma_start(out=oTv[:, c0:c0 + CG, tsl], in_=h_sb[:, c0:c0 + CG, :]),
                 reads=[th[c][s] for c in range(c0, c0 + CG) for s in range(NS)], chan=ch_out)
    S.emit(final_waits=[ch_out])
    C.st.close()
    return nc


GELU_C = 1.5957691216057308


def emit_gelu(S, out_ap, x_ap, t1_ap, t2_ap, reads, writes, tt1, tt2, eng2="dve"):
    S.op("act", lambda e: e.activation(out=t1_ap, in_=x_ap, func=AF.Square), reads=reads, writes=[tt1])
    S.op("dve", lambda e: e.tensor_scalar(out=t1_ap, in0=t1_ap, scalar1=0.044715, scalar2=1.0, op0=ALU.mult, op1=ALU.add),
         reads=[tt1], writes=[tt1])
    S.op("dve", lambda e: e.tensor_tensor(out=t1_ap, in0=x_ap, in1=t1_ap, op=ALU.mult), reads=reads + [tt1], writes=[tt1])
    S.op("act", lambda e: e.activation(out=t2_ap, in_=t1_ap, func=AF.Sigmoid, scale=GELU_C), reads=[tt1], writes=[tt2])
    S.op("dve", lambda e: e.tensor_tensor(out=out_ap, in0=x_ap, in1=t2_ap, op=ALU.mult), reads=reads + [tt2], writes=writes)


def build_ab(Sq, D=2048, debug=False):
    C = Ctx()
    nc, S = C.nc, C.S
    DC = D // 128
    TB = 512
    NBLK = Sq // TB
    SH = Sq // 2
    DH = 256
    hT = C.din("hT", [D, Sq])
    hTh = C.din("hTh", [D, SH])
    g_d = C.din("g", [128, DC])
    wz_d = C.din("wz", [128, DC, 2048])
    wqk_d = C.din("wqk", [128, DC, 1024])
    wvog_d = C.din("wvog", [128, DC, 1028])
    gb_d = C.din("gb", [128, 4])
    conv_d = C.din("conv", [128, 8, 4])
    mn_d = C.din("mn", [128, 512])
    gn_d = C.din("gn", [128, 1024])
    wsT_d = C.din("wsT", [128, 8, 128])
    bs_d = C.din("bs", [128, 8])
    ident_d = C.din("ident", [128, 128])
    mask_d = C.din("mask", [128, 128])
    ya = C.dout("ya", [SH, 1024])
    yb = C.dout("yb", [Sq, 512])
    hTv = hT.rearrange("(c p) t -> p c t", p=128)
    hThv = hTh.rearrange("(c p) t -> p c t", p=128)
    if debug:
        dbg = C.dout("dbg", [Sq // 128, 128, 16])
        dbg_sb = C.sb([128, 16], F32, "dbg")
        tdbg = Tile()
        cdbg = S.chan()

    W = C.sb([128, DC, 2052], BF16, "W")
    h_sb = C.sb([128, DC, TB], F32, "h")
    hn = C.sb([128, DC, TB], BF16, "hn")
    sq = [C.sb([128, 512], F32R, "sq") for _ in range(2)]
    rstd = C.sb([128, 512], F32, "rstd")
    g_sb = C.sb([128, DC], F32, "g")
    gb_sb = C.sb([128, 4], F32, "gb")
    conv_sb = C.sb([128, 8, 4], F32, "conv")
    mn_sb = C.sb([128, 512], F32, "mn")
    gn_sb = C.sb([128, 1024], F32, "gn")
    wsT32 = C.sb([128, 8, 128], F32, "wsT32")
    wsT = C.sb([128, 8, 128], BF16, "wsT")
    bs_sb = C.sb([128, 8], F32, "bs")
    ident = C.sb([128, 128], F32, "ident")
    identb = C.sb([128, 128], BF16, "identb")
    mask = C.sb([128, 128], F32, "mask")
    ones32 = C.sb([128, 128], F32, "ones32")
    qkpre = C.sb([128, 8, 3 + TB], F32, "qkpre")
    acc = C.sb([128, TB], F32, "acc")
    qkT = C.sb([128, 8, TB], BF16, "qkT")
    vaug = [C.sb([128, 2, 257], F32, "vaug") for _ in range(2)]
    so = [C.sb([128, 512], F32, "so") for _ in range(2)]
    gts = [C.sb([128, 4], F32, "gts") for _ in range(2)]
    lf = C.sb([128, 2], F32, "lf")
    iv = C.sb([128, 2], F32, "iv")
    bcol = C.sb([128, 2], F32, "bcol")
    acol = C.sb([128, 2], F32, "acol")
    a_bc = C.sb([128, 2, 128], F32, "a_bc")
    amax = C.sb([128, 2], F32, "amax")
    Mx = C.sb([128, 2], F32, "Mx")
    mprev = C.sb([128, 2], F32, "mprev")
    wprev = C.sb([128, 2], F32, "wprev")
    ws = C.sb([128, 2], F32, "ws")
    thr = C.sb([128, 2], F32, "thr")
    tmp2 = C.sb([128, 2], F32, "tmp2")
    sTm = C.sb([128, 128], BF16, "sTm")
    vw = C.sb([128, 257], BF16, "vw")
    CT = C.sb([128, 2, 2, 257], F32, "CT")
    CTb = C.sb([128, 2, 257], BF16, "CTb")
    ktok = C.sb([128, 256], BF16, "ktok")
    den = C.sb([128, 1], F32, "den")
    hh = C.sb([128, 256], F32, "hh")
    junk = C.sb([128, 256], F32, "junk")
    ssq1 = C.sb([128, 2], F32, "ssq1")
    ybt = [C.sb([128, 512], F32, "ybt") for _ in range(2)]
    u_sb = C.sb([128, 1024], F32, "u")
    v_sb = C.sb([128, 1024], F32, "v")
    vn = C.sb([128, 1024], BF16, "vn")
    t1 = C.sb([128, 512], F32, "t1")
    t2 = C.sb([128, 512], F32, "t2")
    yat = [C.sb([128, 1024], F32, "yat") for _ in range(2)]

    ps_ssq = C.ps(name="ssq")
    psA = [C.ps(name="A") for _ in range(2)]
    psB = [C.ps(name="B") for _ in range(2)]
    psS = C.ps(name="S")
    psN = C.ps(name="N")
    psT = C.ps([128, 256], BF16, name="T")

    T = Tile
    tW, tg, tgb, tconv, tmn, tgn, tws32, tws, tbs, tid, tidb, tmask, tones32 = [T() for _ in range(13)]
    th = [[T()] for _ in range(DC)]
    thn = [[T()] for _ in range(DC)]
    tsq = [T(), T()]
    trstd, tssq = T(), T()
    tqkpre = [T() for _ in range(8)]
    tacc = T()
    tqkT = [T() for _ in range(8)]
    tvaug = [T(), T()]
    tso = [T(), T()]
    tgts = [T(), T()]
    tlf, tiv, tbcol, tacol, tabc, tamax, tMx, tmprev, twprev, tws_, tthr, ttmp2 = [T() for _ in range(12)]
    tsTm, tvw, tCT, tCTb, tktok, tden, thh, tjunk, tssq1 = [T() for _ in range(9)]
    tybt = [T(), T()]
    tu, tv, tvn, tt1, tt2 = [T() for _ in range(5)]
    tyat = [T(), T()]
    tpsA = [T(psum=True), T(psum=True)]
    tpsB = [T(psum=True), T(psum=True)]
    tpsS, tpsN, tpsT = T(psum=True), T(psum=True), T(psum=True)

    cmisc = S.chan()
    cW = S.chan()
    ch_in = S.chan()
    cyb = [S.chan(), S.chan()]
    cya = [S.chan(), S.chan()]

    def ld(dst, src, tile, eng="sp", chan=None):
        S.op(eng, lambda e: e.dma_start(out=dst, in_=src), writes=[tile], chan=chan or cmisc)

    ld(g_sb[:], g_d, tg)
    ld(gb_sb[:], gb_d, tgb)
    ld(conv_sb[:], conv_d, tconv)
    ld(mn_sb[:], mn_d, tmn)
    ld(gn_sb[:], gn_d, tgn)
    ld(wsT32[:], wsT_d, tws32)
    ld(bs_sb[:], bs_d, tbs)
    ld(ident[:], ident_d, tid)
    ld(mask[:], mask_d, tmask)
    ones_r, tones = emit_consts(C)
    S.op("dve", lambda e: e.memset(ones32[:], 1.0), writes=[tones32])
    S.op("dve", lambda e: e.tensor_copy(out=identb[:], in_=ident[:]), reads=[tid], writes=[tidb])
    for g in range(8):
        S.op("dve", lambda e, g=g: e.tensor_tensor(out=wsT[:, g, :], in0=wsT32[:, g, :], in1=mask[:], op=ALU.mult),
             reads=[tws32, tmask], writes=[tws])
    S.op("pool", lambda e: e.dma_start(out=W[:, :, 0:1024], in_=wqk_d), writes=[tW], chan=cW)
    S.op("pool", lambda e: e.dma_start(out=W[:, :, 1024:2052], in_=wvog_d), writes=[tW], chan=cW)
    S.op("dve", lambda e: e.memset(CT[:], 0.0), writes=[tCT])
    S.op("dve", lambda e: e.memset(mprev[:], 0.0), writes=[tmprev])
    for k in range(2):
        S.op("dve", lambda e, k=k: e.memset(vaug[k][:], 1.0), writes=[tvaug[k]])
    for m in range(8):
        S.op("dve", lambda e, m=m: e.memset(qkpre[:, m, 0:3], 0.0), writes=[tqkpre[m]])

    def norm_block(src_v, tsl):
        CG = 4
        for c0 in range(0, DC, CG):
            S.op("sp", lambda e, c0=c0: e.dma_start(out=h_sb[:, c0:c0 + CG, :], in_=src_v[:, c0:c0 + CG, tsl]),
                 writes=[th[c][0] for c in range(c0, c0 + CG)], chan=ch_in)
        emit_rmsnorm_T(C, h_sb, th, hn, thn, g_sb, tg, ones_r, tones, DC, TB, sq, tsq, ps_ssq, tssq, rstd, trstd, D)

    all_hn = [thn[c][0] for c in range(DC)]
    nA = 0
    nB = 0
    for tb in range(NBLK):
        norm_block(hTv, slice(tb * TB, (tb + 1) * TB))
        for m in range(8):
            b = nA % 2
            nA += 1
            for c in range(DC):
                S.op("pe", lambda e, c=c, m=m, b=b: e.matmul(psA[b][:], W[:, c, m * 128:(m + 1) * 128], hn[:, c, :],
                                                          start=(c == 0), stop=(c == DC - 1)),
                     reads=[tW, thn[c][0]], writes=[tpsA[b]])
            S.op("act", lambda e, m=m, b=b: e.copy(out=qkpre[:, m, 3:3 + TB], in_=psA[b][:]), reads=[tpsA[b]], writes=[tqkpre[m]])
            S.op("dve", lambda e, m=m: e.tensor_scalar(out=acc[:], in0=qkpre[:, m, 0:TB], scalar1=conv_sb[:, m, 0:1], scalar2=None,
                                                       op0=ALU.mult), reads=[tqkpre[m], tconv], writes=[tacc])
            for k in range(1, 4):
                S.op("dve", lambda e, m=m, k=k: e.scalar_tensor_tensor(out=acc[:], in0=qkpre[:, m, k:k + TB], scalar=conv_sb[:, m, k:k + 1],
                                                                        in1=acc[:], op0=ALU.mult, op1=ALU.add),
                     reads=[tqkpre[m], tconv, tacc], writes=[tacc])
            S.op("act", lambda e, m=m: e.activation(out=qkT[:, m, :], in_=acc[:], func=AF.Silu), reads=[tacc], writes=[tqkT[m]])
            S.op("dve", lambda e, m=m: e.tensor_copy(out=qkpre[:, m, 0:3], in_=qkpre[:, m, TB:TB + 3]), reads=[tqkpre[m]], writes=[tqkpre[m]])
        for j in range(TB // 128):
            jsl = slice(j * 128, (j + 1) * 128)
            kk = j % 2
            b = nB % 2
            nB += 1
            for c in range(DC):
                S.op("pe", lambda e, c=c, b=b, jsl=jsl: e.matmul(psB[b][:], hn[:, c, jsl], W[:, c, 1024:1536], start=(c == 0), stop=(c == DC - 1)),
                     reads=[tW, thn[c][0]], writes=[tpsB[b]])
            for h in range(2):
                S.op("act", lambda e, h=h, b=b, kk=kk: e.copy(out=vaug[kk][:, h, 0:256], in_=psB[b][:, h * 256:(h + 1) * 256]),
                     reads=[tpsB[b]], writes=[tvaug[kk]])
            b = nB % 2
            nB += 1
            for c in range(DC):
                S.op("pe", lambda e, c=c, b=b, jsl=jsl: e.matmul(psB[b][:], hn[:, c, jsl], W[:, c, 1536:2048], start=(c == 0), stop=(c == DC - 1)),
                     reads=[tW, thn[c][0]], writes=[tpsB[b]])
            S.op("act", lambda e, b=b, kk=kk: e.activation(out=so[kk][:], in_=psB[b][:], func=AF.Sigmoid), reads=[tpsB[b]], writes=[tso[kk]])
            for c in range(DC):
                S.op("pe", lambda e, c=c, jsl=jsl: e.matmul(psS[:, 0:4], hn[:, c, jsl], W[:, c, 2048:2052], start=(c == 0), stop=(c == DC - 1)),
                     reads=[tW, thn[c][0]], writes=[tpsS])
            S.op("dve", lambda e, kk=kk: e.tensor_tensor(out=gts[kk][:], in0=psS[:, 0:4], in1=gb_sb[:], op=ALU.add),
                 reads=[tpsS, tgb], writes=[tgts[kk]])
            S.op("act", lambda e, kk=kk: e.activation(out=lf[:], in_=gts[kk][:, 2:4], func=AF.Exp, scale=-1.0), reads=[tgts[kk]], writes=[tlf])
            S.op("act", lambda e: e.activation(out=lf[:], in_=lf[:], func=AF.Ln, bias=ones32[:, 0:1], scale=1.0), reads=[tlf, tones32], writes=[tlf])
            S.op("dve", lambda e: e.tensor_scalar(out=lf[:], in0=lf[:], scalar1=-1.0, scalar2=None, op0=ALU.mult), reads=[tlf], writes=[tlf])
            S.op("pe", lambda e: e.matmul(psS[:, 8:10], mask[:], lf[:], start=True, stop=True), reads=[tmask, tlf], writes=[tpsS])
            S.op("pe", lambda e: e.matmul(psS[:, 16:18], ones32[:], lf[:], start=True, stop=True), reads=[tones32, tlf], writes=[tpsS])
            S.op("dve", lambda e: e.tensor_copy(out=bcol[:], in_=psS[:, 8:10]), reads=[tpsS], writes=[tbcol])
            S.op("dve", lambda e, kk=kk: e.tensor_tensor(out=acol[:], in0=gts[kk][:, 0:2], in1=bcol[:], op=ALU.subtract),
                 reads=[tgts[kk], tbcol], writes=[tacol])
            for h in range(2):
                S.op("dve", lambda e, h=h: e.tensor_copy(out=a_bc[:, h, :], in_=acol[:, h:h + 1].to_broadcast([128, 128])),
                     reads=[tacol], writes=[tabc])
            for h in range(2):
                S.op("pe", lambda e, h=h: e.matmul(psS[:, 128 + h * 128:256 + h * 128], a_bc[:, h, :], ident[:], start=True, stop=True),
                     reads=[tabc, tid], writes=[tpsS])
            S.op("dve", lambda e: e.tensor_reduce(out=amax[:], in_=psS[:, 128:384].rearrange("p (h s) -> p h s", h=2), axis=AX.X, op=ALU.max),
                 reads=[tpsS], writes=[tamax])
            S.op("dve", lambda e: e.tensor_tensor(out=Mx[:], in0=amax[:], in1=mprev[:], op=ALU.max), reads=[tamax, tmprev], writes=[tMx])
            S.op("dve", lambda e: e.tensor_tensor(out=tmp2[:], in0=mprev[:], in1=Mx[:], op=ALU.subtract), reads=[tmprev, tMx], writes=[ttmp2])
            S.op("act", lambda e: e.activation(out=wprev[:], in_=tmp2[:], func=AF.Exp), reads=[ttmp2], writes=[twprev])
            S.op("dve", lambda e: e.tensor_tensor(out=tmp2[:], in0=acol[:], in1=Mx[:], op=ALU.subtract), reads=[tacol, tMx], writes=[ttmp2])
            S.op("act", lambda e: e.activation(out=ws[:], in_=tmp2[:], func=AF.Exp), reads=[ttmp2], writes=[tws_])
            S.op("dve", lambda e: e.tensor_tensor(out=tmp2[:], in0=bcol[:], in1=Mx[:], op=ALU.add), reads=[tbcol, tMx], writes=[ttmp2])
            S.op("act", lambda e: e.activation(out=thr[:], in_=tmp2[:], func=AF.Exp, scale=-1.0), reads=[ttmp2], writes=[tthr])
            S.op("dve", lambda e: e.tensor_tensor(out=mprev[:], in0=psS[:, 16:18], in1=Mx[:], op=ALU.add), reads=[tpsS, tMx], writes=[tmprev])
            if debug:
                for i_, (src_, tl_) in enumerate(((lf, tlf), (acol, tacol), (bcol, tbcol), (amax, tamax), (Mx, tMx), (wprev, twprev),
                                                  (ws, tws_), (thr, tthr))):
                    S.op("dve", lambda e, i_=i_, src_=src_: e.tensor_copy(out=dbg_sb[:, 2 * i_:2 * i_ + 2], in_=src_[:]),
                         reads=[tl_], writes=[tdbg])
                cidx = tb * (TB // 128) + j
                S.op("sp", lambda e, cidx=cidx: e.dma_start(out=dbg[cidx], in_=dbg_sb[:]), reads=[tdbg], chan=cdbg)
            for h in range(2):
                qi = [h * 2, h * 2 + 1]
                ki = [4 + h * 2, 4 + h * 2 + 1]
                for dc in range(2):
                    S.op("pe", lambda e, dc=dc, ki=ki, qi=qi, jsl=jsl: e.matmul(psA[0][:, 0:128], qkT[:, ki[dc], jsl], qkT[:, qi[dc], jsl],
                                                                           start=(dc == 0), stop=(dc == 1)),
                         reads=[tqkT[ki[dc]], tqkT[qi[dc]]], writes=[tpsA[0]])
                S.op("dve", lambda e: e.scalar_tensor_tensor(out=sTm[:], in0=psA[0][:, 0:128], scalar=DH ** -0.5, in1=mask[:],
                                                             op0=ALU.mult, op1=ALU.mult), reads=[tpsA[0], tmask], writes=[tsTm])
                S.op("dve", lambda e, h=h, kk=kk: e.tensor_scalar(out=vw[:], in0=vaug[kk][:, h, :], scalar1=ws[:, h:h + 1], scalar2=None, op0=ALU.mult),
                     reads=[tvaug[kk], tws_], writes=[tvw])
                S.op("dve", lambda e, h=h: e.tensor_scalar(out=CT[:, h, :, :], in0=CT[:, h, :, :], scalar1=wprev[:, h:h + 1], scalar2=None, op0=ALU.mult),
                     reads=[tCT, twprev], writes=[tCT])
                S.op("act", lambda e, h=h: e.copy(out=CTb[:], in_=CT[:, h, :, :]), reads=[tCT], writes=[tCTb])
                S.op("pe", lambda e: e.matmul(psN[:, 0:257], sTm[:], vw[:], start=True, stop=False), reads=[tsTm, tvw], writes=[tpsN])
                for dc in range(2):
                    S.op("pe", lambda e, dc=dc, qi=qi, jsl=jsl: e.matmul(psN[:, 0:257], qkT[:, qi[dc], jsl], CTb[:, dc, :], start=False, stop=(dc == 1)),
                         reads=[tqkT[qi[dc]], tCTb], writes=[tpsN])
                S.op("act", lambda e: e.activation(out=den[:], in_=psN[:, 256:257], func=AF.Abs), reads=[tpsN], writes=[tden])
                S.op("dve", lambda e, h=h: e.tensor_tensor(out=den[:], in0=den[:], in1=thr[:, h:h + 1], op=ALU.max),
                     reads=[tden, tthr], writes=[tden])
                S.op("dve", lambda e: e.reciprocal(out=den[:], in_=den[:]), reads=[tden], writes=[tden])
                S.op("dve", lambda e: e.tensor_scalar(out=hh[:], in0=psN[:, 0:256], scalar1=den[:, 0:1], scalar2=None, op0=ALU.mult),
                     reads=[tpsN, tden], writes=[thh])
                S.op("dve", lambda e, h=h: e.memset(ssq1[:, h:h + 1], 0.0), writes=[tssq1])
                S.op("act", lambda e, h=h: e.activation(out=junk[:], in_=hh[:], func=AF.Square, accum_out=ssq1[:, h:h + 1]),
                     reads=[thh], writes=[tjunk, tssq1])
                S.op("act", lambda e, h=h: e.activation(out=ssq1[:, h:h + 1], in_=ssq1[:, h:h + 1], func=AF.Sqrt, bias=EPSB[0][:, 0:1], scale=1.0 / DH),
                     reads=[tssq1, EPSB[1]], writes=[tssq1])
                S.op("dve", lambda e, h=h: e.reciprocal(out=ssq1[:, h:h + 1], in_=ssq1[:, h:h + 1]), reads=[tssq1], writes=[tssq1])
                S.op("dve", lambda e, h=h, kk=kk: e.scalar_tensor_tensor(out=ybt[kk][:, h * 256:(h + 1) * 256], in0=hh[:], scalar=ssq1[:, h:h + 1],
                                                                          in1=mn_sb[:, h * 256:(h + 1) * 256], op0=ALU.mult, op1=ALU.mult),
                     reads=[thh, tssq1, tmn], writes=[tybt[kk]])
                S.op("dve", lambda e, h=h, kk=kk: e.tensor_tensor(out=ybt[kk][:, h * 256:(h + 1) * 256], in0=ybt[kk][:, h * 256:(h + 1) * 256],
                                                                  in1=so[kk][:, h * 256:(h + 1) * 256], op=ALU.mult),
                     reads=[tybt[kk], tso[kk]], writes=[tybt[kk]])
                for dc in range(2):
                    S.op("pe", lambda e, dc=dc, ki=ki, jsl=jsl: e.transpose(psT[:, dc * 128:(dc + 1) * 128], qkT[:, ki[dc], jsl], identb[:]),
                         reads=[tqkT[ki[dc]], tidb], writes=[tpsT])
                S.op("act", lambda e: e.copy(out=ktok[:], in_=psT[:]), reads=[tpsT], writes=[tktok])
                for dc in range(2):
                    S.op("pe", lambda e, dc=dc: e.matmul(psA[1][:, 0:257], ktok[:, dc * 128:(dc + 1) * 128], vw[:], start=True, stop=True),
                         reads=[tktok, tvw], writes=[tpsA[1]])
                    S.op("dve", lambda e, dc=dc, h=h: e.scalar_tensor_tensor(out=CT[:, h, dc, :], in0=psA[1][:, 0:257], scalar=DH ** -0.5,
                                                                              in1=CT[:, h, dc, :], op0=ALU.mult, op1=ALU.add),
                         reads=[tCT, tpsA[1]], writes=[tCT])
            tok0 = tb * TB + j * 128
            S.op("sp", lambda e, kk=kk, tok0=tok0: e.dma_start(out=yb[tok0:tok0 + 128, :], in_=ybt[kk][:]), reads=[tybt[kk]], chan=cyb[kk])

    S.op("pool", lambda e: e.dma_start(out=W[:, :, 0:2048], in_=wz_d), writes=[tW], chan=cW)
    nj = 0
    for tb in range(SH // TB):
        norm_block(hThv, slice(tb * TB, (tb + 1) * TB))
        for j in range(TB // 128):
            jsl = slice(j * 128, (j + 1) * 128)
            kk = nj % 2
            nj += 1
            for cb in range(4):
                b = cb % 2
                for c in range(DC):
                    S.op("pe", lambda e, c=c, b=b, cb=cb, jsl=jsl: e.matmul(psA[b][:], hn[:, c, jsl], W[:, c, cb * 512:(cb + 1) * 512],
                                                                        start=(c == 0), stop=(c == DC - 1)),
                         reads=[tW, thn[c][0]], writes=[tpsA[b]])
                dst = u_sb if cb < 2 else v_sb
                tdst = tu if cb < 2 else tv
                csl = slice((cb % 2) * 512, (cb % 2 + 1) * 512)
                emit_gelu(S, dst[:, csl], psA[b][:], t1[:], t2[:], [tpsA[b]], [tdst], tt1, tt2)
            S.op("dve", lambda e: e.memset(ssq1[:], 0.0), writes=[tssq1])
            for q in range(2):
                S.op("act", lambda e, q=q: e.activation(out=t1[:], in_=v_sb[:, q * 512:(q + 1) * 512], func=AF.Square, accum_out=ssq1[:, q:q + 1]),
                     reads=[tv], writes=[tt1, tssq1])
            S.op("dve", lambda e: e.tensor_tensor(out=ssq1[:, 0:1], in0=ssq1[:, 0:1], in1=ssq1[:, 1:2], op=ALU.add), reads=[tssq1], writes=[tssq1])
            S.op("act", lambda e: e.activation(out=ssq1[:, 0:1], in_=ssq1[:, 0:1], func=AF.Sqrt, bias=EPSB[0][:, 0:1], scale=1.0 / 1024),
                 reads=[tssq1, EPSB[1]], writes=[tssq1])
            S.op("dve", lambda e: e.reciprocal(out=ssq1[:, 0:1], in_=ssq1[:, 0:1]), reads=[tssq1], writes=[tssq1])
            S.op("dve", lambda e: e.scalar_tensor_tensor(out=vn[:], in0=v_sb[:], scalar=ssq1[:, 0:1], in1=gn_sb[:], op0=ALU.mult, op1=ALU.mult),
                 reads=[tv, tssq1, tgn], writes=[tvn])
            for g in range(8):
                b = g // 4
                S.op("pe", lambda e, g=g, b=b: e.matmul(psB[b][:, (g % 4) * 128:(g % 4 + 1) * 128], wsT[:, g, :], vn[:, g * 128:(g + 1) * 128],
                                                    start=True, stop=True), reads=[tws, tvn], writes=[tpsB[b]])
            for g in range(8):
                b = g // 4
                S.op("dve", lambda e, g=g, b=b, kk=kk: e.scalar_tensor_tensor(out=yat[kk][:, g * 128:(g + 1) * 128],
                                                                               in0=psB[b][:, (g % 4) * 128:(g % 4 + 1) * 128],
                                                                               scalar=bs_sb[:, g:g + 1], in1=u_sb[:, g * 128:(g + 1) * 128],
                                                                               op0=ALU.add, op1=ALU.mult),
                     reads=[tpsB[b], tbs, tu], writes=[tyat[kk]])
            tok0 = tb * TB + j * 128
            S.op("sp", lambda e, kk=kk, tok0=tok0: e.dma_start(out=ya[tok0:tok0 + 128, :], in_=yat[kk][:]), reads=[tyat[kk]], chan=cya[kk])
    S.emit(final_waits=cyb + cya)
    C.st.close()
    return nc


NEG = -1.0e30


def nsa_tables(Sq):
    NT = Sq // 128
    NCMP = (Sq - 32) // 16 + 1
    CH = (NCMP + 127) // 128
    NSL = Sq // 64
    half = 16
    inv = 1.0 / (500000.0 ** (np.arange(half, dtype=np.float32) / half))
    ang = np.arange(Sq, dtype=np.float32)[None, :] * inv[:, None]
    cos = np.ones((128, Sq), np.float32)
    sin = np.zeros((128, Sq), np.float32)
    cos[0:16] = np.cos(ang); cos[16:32] = np.cos(ang)
    sin[0:16] = np.sin(ang); sin[16:32] = np.sin(ang)
    sc = np.float32(128 ** -0.5)
    psw = np.zeros((128, 128), np.float32)
    for d in range(16):
        psw[d + 16, d] = -1.0
        psw[d, d + 16] = 1.0
    k = np.arange(128)[:, None]
    q = np.arange(128)[None, :]
    causal = np.where(k > q, NEG, 0.0).astype(np.float32)
    winneg = np.where(k <= q, NEG, 0.0).astype(np.float32)
    cm = np.zeros((NT, 128, CH, 128), np.float32)
    for T in range(NT):
        for ch in range(CH):
            c = ch * 128 + np.arange(128)[:, None]
            t = T * 128 + np.arange(128)[None, :]
            cm[T, :, ch, :] = np.where((16 * c + 31 > t) | (c >= NCMP), NEG, 0.0)
    E = np.zeros((64, NT, 128), np.float32)
    for kc in range(NT):
        for kk in range(128):
            E[(kc * 128 + kk) // 64, kc, kk] = 1.0
    fpos = np.zeros((NT, 128, 64), np.float32)
    fneg = np.full((NT, 128, 64), 1.0e30, np.float32)
    j = np.arange(64)[None, :]
    for T in range(NT):
        cur = ((T * 128 + np.arange(128)) // 64)[:, None]
        fp = np.zeros((128, 64), np.float32)
        fp = np.where(j == cur - 1, 1.0e9, fp)
        fp = np.where(j == cur, 2.0e9, fp)
        fp = np.where(j == 0, 3.0e9, fp)
        fpos[T] = fp
        fneg[T] = np.where((j > cur) | (j >= NSL), -1.0e9, 1.0e30)
    ci = np.arange(CH * 128)[:, None] * 16
    sj = np.arange(64)[None, :] * 64
    ov = np.clip(np.minimum(ci + 32, sj + 64) - np.maximum(ci, sj), 0, None).astype(np.float32) / 32.0
    ov[NCMP:] = 0.0
    ov = np.ascontiguousarray(ov.reshape(CH, 128, 64).transpose(1, 0, 2))
    return dict(cosq=cos * sc, sinq=sin * sc, cosk=cos, sink=sin, psw=psw, ident=np.eye(128, dtype=np.float32),
                causal=causal, winneg=winneg, cmpmask=cm, E=E, fpos=fpos, fneg=fneg, ov=ov)


def build_nsa(Sq, D=2048, stop=0, branches=(0, 1, 2)):
    C = Ctx()
    nc, S = C.nc, C.S
    DC = D // 128
    TB = 256
    TPB = TB // 128
    NBLK = Sq // TB
    NT = Sq // 128
    NCMP = (Sq - 32) // 16 + 1
    CH = (NCMP + 127) // 128
    T_ = Tile

    hT = C.din("hT", [D, Sq])
    g_d = C.din("g", [128, DC])
    wkv_d = C.din("wkv", [128, DC, 1536])
    wq_d = C.din("wq", [128, DC, 1024])
    wgt_d = C.din("wgt", [128, DC, 24])
    w1_d = C.din("w1", [128, 2, 32, 128])
    w2_d = C.din("w2", [128, 2, 128])
    pos_d = C.din("posT", [128, 2, 32])
    tabs = {}
    for nm, shp in (("cosq", [128, Sq]), ("sinq", [128, Sq]), ("cosk", [128, Sq]), ("sink", [128, Sq]), ("psw", [128, 128]),
                    ("ident", [128, 128]), ("causal", [128, 128]), ("winneg", [128, 128]), ("cmpmask", [NT, 128, CH, 128]),
                    ("E", [64, NT, 128]), ("fpos", [NT, 128, 64]), ("fneg", [NT, 128, 64]), ("ov", [128, CH, 64])):
        tabs[nm] = C.din(nm, shp)
    y_d = C.dout("y", [Sq, 1024])
    hTv = hT.rearrange("(c p) t -> p c t", p=128)

    W = C.sb([128, DC, 1024], BF16, "W")
    Wgt = C.sb([128, DC, 24], BF16, "Wgt")
    WT = C.sb([128, DC, 512], BF16, "WT")
    tWT = Tile()
    tWgt = Tile()
    h_sb = C.sb([128, DC, TB], BF16, "h")
    hn = C.sb([128, DC, TB], BF16, "hn")
    sq = [C.sb([128, 256], F32R, "sq") for _ in range(2)]
    rstd = C.sb([128, 256], F32, "rstd")
    g_sb = C.sb([128, DC], F32, "g")
    kTs = C.sb([128, 2, 2, Sq], BF16, "kTs")
    cmpin = C.sb([128, 2, 2, Sq], BF16, "cmpin")
    vsel = C.sb([128, NT, 2, 129], BF16, "vsel")
    vwin = C.sb([128, NT, 2, 129], BF16, "vwin")
    kcmpT = C.sb([128, 2, CH * 128], BF16, "kcmpT")
    vcmp = C.sb([128, 2, CH, 193], BF16, "vcmp")
    rope_c = C.sb([128, TB], F32, "ropec")
    rope_s = C.sb([128, TB], F32, "ropes")
    psw = C.sb([128, 128], BF16, "psw")
    identb = C.sb([128, 128], BF16, "identb")
    ident = C.sb([128, 128], F32, "ident")
    causal = C.sb([128, 4, 128], BF16, "causal")
    winneg = C.sb([128, 4, 128], BF16, "winneg")
    Eb = C.sb([64, NT, 128], BF16, "E")
    xb = C.sb([128, TB], BF16, "xb")
    xc = C.sb([128, TB], F32, "xc")
    xs = C.sb([128, TB], F32, "xs")
    kmax2 = C.sb([128, 6], F32, "kmax2")
    kred = C.sb([128, 1], F32, "kred")
    ones32 = C.sb([128, 128], F32, "ones32")
    sqf = C.sb([128, 256], F32, "sqf")
    tsqf = Tile()

    psB = [C.ps(name="B") for _ in range(2)]
    psA = psB
    psO = [C.ps(name="O") for _ in range(4)]
    psS = C.ps(name="S")
    ps_ssq = C.ps(name="ssq")

    tW, tg = T_(), T_()
    th = [[T_()] for _ in range(DC)]
    thn = [[T_()] for _ in range(DC)]
    tsq = [T_(), T_()]
    trstd, tssq = T_(), T_()
    tkTs = [[[T_() for _ in range(NBLK)] for _ in range(2)] for _ in range(2)]
    tcmpin = [[T_() for _ in range(2)] for _ in range(2)]
    tvsel = [T_() for _ in range(NT)]
    tvwin = [T_() for _ in range(NT)]
    tkcmpT, tvcmp = T_(), T_()
    trope, tpsw, tidb, tid, tcausal, twinneg, tE = [T_() for _ in range(7)]
    txb, txc, txs, tkmax2, tkred, tones32 = [T_() for _ in range(6)]
    tpsB = [T_(psum=True), T_(psum=True)]
    tpsA = tpsB
    tpsO = [T_(psum=True) for _ in range(4)]
    tpsS = T_(psum=True)

    cmisc = S.chan()
    cW = S.chan()
    ch_in = S.chan()
    crope = S.chan()

    S.op("sp", lambda e: e.dma_start(out=g_sb[:], in_=g_d), writes=[tg], chan=cmisc)
    S.op("sp", lambda e: e.dma_start(out=ident[:], in_=tabs["ident"]), writes=[tid], chan=cmisc)
    for dst, nm, tl in ((psw, "psw", tpsw), (identb, "ident", tidb), (Eb, "E", tE)):
        S.op("pool", lambda e, dst=dst, nm=nm: e.dma_start(out=dst[:], in_=tabs[nm]), writes=[tl], chan=cmisc)
    for dst, nm, tl in ((causal, "causal", tcausal), (winneg, "winneg", twinneg)):
        for r4 in range(4):
            S.op("pool", lambda e, dst=dst, nm=nm, r4=r4: e.dma_start(out=dst[:, r4, :], in_=tabs[nm]), writes=[tl], chan=cmisc)
    ones_r, tones = emit_consts(C)
    S.op("dve", lambda e: e.memset(ones32[:], 1.0), writes=[tones32])
    S.op("dve", lambda e: e.memset(kmax2[:], 0.0), writes=[tkmax2])
    S.op("dve", lambda e: e.memset(vsel[:], 1.0), writes=tvsel)
    S.op("dve", lambda e: e.memset(vwin[:], 1.0), writes=tvwin)
    S.op("dve", lambda e: e.memset(vcmp[:], 1.0), writes=[tvcmp])
    S.op("dve", lambda e: e.memset(kcmpT[:], 0.0), writes=[tkcmpT])
    S.op("pool", lambda e: e.dma_start(out=Wgt[:], in_=wgt_d), writes=[tWgt], chan=cmisc)

    class _Stop(Exception):
        pass

    def ckpt(v):
        if stop == v:
            S.op("sp", lambda e: e.dma_start(out=y_d[0:128, 0:128], in_=ident[:]), reads=[tid], chan=cmisc)
            S.emit(final_waits=[cmisc])
            S.op = lambda *a, **k: None
            S.emit = lambda *a, **k: None

    def norm_block(tsl):
        CG = 4
        for c0 in range(0, DC, CG):
            S.op("pool", lambda e, c0=c0: e.dma_start(out=h_sb[:, c0:c0 + CG, :], in_=hTv[:, c0:c0 + CG, tsl]),
                 writes=[th[c][0] for c in range(c0, c0 + CG)], chan=ch_in)
        emit_rmsnorm_T(C, h_sb, th, hn, thn, g_sb, tg, ones_r, tones, DC, TB, sq, tsq, ps_ssq, tssq, rstd, trstd, D)

    def load_rope(cn, sn, tsl):
        S.op("sp", lambda e: e.dma_start(out=rope_c[:], in_=tabs[cn][:, tsl]), writes=[trope], chan=crope)
        S.op("sp", lambda e: e.dma_start(out=rope_s[:], in_=tabs[sn][:, tsl]), writes=[trope], chan=crope)

    nA = [0]

    def proj_fm(col0):
        b = nA[0] % 2
        nA[0] += 1
        for c in range(DC):
            S.op("pe", lambda e, c=c, b=b: e.matmul(psA[b][:, 0:TB], W[:, c, col0:col0 + 128], hn[:, c, :], start=(c == 0), stop=(c == DC - 1)),
                 reads=[tW, thn[c][0]], writes=[tpsA[b]])
        return b

    def rope_to(b, dst_ap, dst_tiles, split=False):
        S.op("act", lambda e: e.copy(out=xb[:], in_=psA[b][:, 0:TB]), reads=[tpsA[b]], writes=[txb])
        ckpt(151)
        S.op("dve", lambda e: e.tensor_tensor(out=xc[:], in0=psA[b][:, 0:TB], in1=rope_c[:], op=ALU.mult), reads=[tpsA[b], trope, txb], writes=[txc])
        ckpt(152)
        bb = nA[0] % 2
        nA[0] += 1
        S.op("pe", lambda e: e.matmul(psA[bb][:, 0:TB], psw[:], xb[:], start=True, stop=True), reads=[tpsw, txb], writes=[tpsA[bb]])
        ckpt(153)
        S.op("dve", lambda e: e.tensor_tensor(out=xs[:], in0=psA[bb][:, 0:TB], in1=rope_s[:], op=ALU.mult), reads=[tpsA[bb], trope], writes=[txs])
        ckpt(154)
        if split:
            S.op("dve", lambda e: e.tensor_tensor(out=dst_ap, in0=xs[:].rearrange("p (j q) -> p j q", q=128),
                                                  in1=xc[:].rearrange("p (j q) -> p j q", q=128), op=ALU.add), reads=[txs, txc], writes=dst_tiles)
        else:
            S.op("dve", lambda e: e.tensor_tensor(out=dst_ap, in0=xs[:], in1=xc[:], op=ALU.add), reads=[txs, txc], writes=dst_tiles)

    def colnorm_max(src_ap, src_tiles, kcol, ncols=TB):
        S.op("act", lambda e: e.activation(out=sqf[:, 0:ncols], in_=src_ap, func=AF.Square), reads=src_tiles, writes=[tsqf])
        S.op("pe", lambda e: e.matmul(psS[:, 0:ncols], ones32[:], sqf[:, 0:ncols], start=True, stop=True), reads=[tsqf, tones32], writes=[tpsS])
        S.op("dve", lambda e: e.tensor_reduce(out=kred[:], in_=psS[:, 0:ncols], axis=AX.X, op=ALU.max), reads=[tpsS], writes=[tkred])
        S.op("dve", lambda e: e.tensor_tensor(out=kmax2[:, kcol:kcol + 1], in0=kmax2[:, kcol:kcol + 1], in1=kred[:], op=ALU.max),
             reads=[tkmax2, tkred], writes=[tkmax2])

    nB = 0
    try:
        ckpt(11)
    except _Stop:
        return nc
    for tb in range(NBLK):
        tsl = slice(tb * TB, (tb + 1) * TB)
        try:
            norm_block(tsl)
            ckpt(12)
            load_rope("cosk", "sink", tsl)
            if tb == 0:
                S.op("pool", lambda e: e.dma_start(out=W[:], in_=wkv_d[:, :, 0:1024]), writes=[tW], chan=cW)
                S.op("pool", lambda e: e.dma_start(out=WT[:], in_=wkv_d[:, :, 1024:1536]), writes=[tWT], chan=cW)
            ckpt(13)
        except _Stop:
            return nc
        for slot in range(6):
            br, gl = slot // 2, slot % 2
            b = proj_fm(slot * 128)
            try:
                ckpt(14)
            except _Stop:
                return nc
            if br == 0:
                rope_to(b, cmpin[:, 0, gl, tsl], [tcmpin[0][gl]])
                try:
                    ckpt(15)
                except _Stop:
                    return nc
            else:
                rope_to(b, kTs[:, br - 1, gl, tsl], [tkTs[br - 1][gl][tb]])
                colnorm_max(kTs[:, br - 1, gl, tsl], [tkTs[br - 1][gl][tb]], 2 + (br - 1) * 2 + gl)
                try:
                    ckpt(16)
                except _Stop:
                    return nc
        for gl in range(2):
            b = proj_fm((6 + gl) * 128)
            S.op("act", lambda e, b=b, gl=gl, tsl=tsl: e.copy(out=cmpin[:, 1, gl, tsl], in_=psA[b][:, 0:TB]), reads=[tpsA[b]], writes=[tcmpin[1][gl]])
        for j in range(TPB):
            Tq = tb * TPB + j
            jsl = slice(j * 128, (j + 1) * 128)
            b = nB % 2
            nB += 1
            for c in range(DC):
                S.op("pe", lambda e, c=c, b=b, jsl=jsl: e.matmul(psB[b][:], hn[:, c, jsl], WT[:, c, :], start=(c == 0), stop=(c == DC - 1)),
                     reads=[tWT, thn[c][0]], writes=[tpsB[b]])
            S.op("act", lambda e, b=b, Tq=Tq: e.copy(out=vsel[:, Tq, :, 0:128], in_=psB[b][:, 0:256].rearrange("p (g d) -> p g d", g=2)),
                 reads=[tpsB[b]], writes=[tvsel[Tq]])
            S.op("dve", lambda e, b=b, Tq=Tq: e.tensor_copy(out=vwin[:, Tq, :, 0:128], in_=psB[b][:, 256:512].rearrange("p (g d) -> p g d", g=2)),
                 reads=[tpsB[b]], writes=[tvwin[Tq]])

    if stop == 1:
        S.op("pool", lambda e: e.dma_start(out=y_d[0:128, 0:256], in_=h_sb[:, 0, :]), reads=[th[0][0]], chan=cmisc)
        S.emit(final_waits=[cmisc])
        C.st.close()
        return nc
    Wflat = W[:].rearrange("p a b -> p (a b)")
    w1 = Wflat[:, 0:8192].rearrange("p (k l o) -> p k l o", k=2, l=32)
    w2 = Wflat[:, 8192:8448].rearrange("p (k o) -> p k o", k=2)
    posb = Wflat[:, 8448:8512].rearrange("p (k l) -> p k l", k=2)
    S.op("pool", lambda e: e.dma_start(out=w1, in_=w1_d), writes=[tW], chan=cW)
    S.op("pool", lambda e: e.dma_start(out=w2, in_=w2_d), writes=[tW], chan=cW)
    S.op("pool", lambda e: e.dma_start(out=posb, in_=pos_d), writes=[tW], chan=cW)
    S.op("pool", lambda e: e.dma_start(out=vcmp[:, 0, :, 129:193], in_=tabs["ov"]), writes=[tvcmp], chan=cmisc)
    S.op("pool", lambda e: e.dma_start(out=vcmp[:, 1, :, 129:193], in_=tabs["ov"]), writes=[tvcmp], chan=cmisc)
    bias1 = C.sb([128, 2], F32, "bias1")
    xg, t1, t2 = rstd, xc, xs
    H1g = C.sb([128, CH * 128], BF16, "H1g")
    tbias1, tH1g = T_(), T_()
    txg, tt1, tt2 = trstd, txc, txs
    NCP = NCMP
    for kv in range(2):
        for l in range(32):
            S.op("pe", lambda e, kv=kv, l=l: e.matmul(psS[:, kv:kv + 1], w1[:, kv, l, :], posb[:, kv, l:l + 1], start=(l == 0), stop=(l == 31)),
                 reads=[tW], writes=[tpsS])
        S.op("dve", lambda e, kv=kv: e.tensor_copy(out=bias1[:, kv:kv + 1], in_=psS[:, kv:kv + 1]), reads=[tpsS], writes=[tbias1])
    S.op("dve", lambda e: e.memset(H1g[:], 0.0), writes=[tH1g])
    for kv in range(2):
        for gl in range(2):
            b = nA[0] % 2
            nA[0] += 1
            for l in range(32):
                S.op("pe", lambda e, kv=kv, gl=gl, l=l, b=b: e.matmul(psA[b][:, 0:NCP], w1[:, kv, l, :],
                                                                     cmpin[:, kv, gl, l:l + 16 * (NCP - 1) + 1:16],
                                                                     start=(l == 0), stop=(l == 31)),
                     reads=[tW, tcmpin[kv][gl]], writes=[tpsA[b]])
            for c0 in range(0, NCP, 256):
                n = min(256, NCP - c0)
                S.op("dve", lambda e, b=b, kv=kv, c0=c0, n=n: e.tensor_scalar(out=xg[:, 0:n], in0=psA[b][:, c0:c0 + n], scalar1=bias1[:, kv:kv + 1],
                                                                            scalar2=None, op0=ALU.add), reads=[tpsA[b], tbias1], writes=[txg])
                emit_gelu(S, H1g[:, c0:c0 + n], xg[:, 0:n], t1[:, 0:n], t2[:, 0:n], [txg], [tH1g], tt1, tt2)
            if kv == 0:
                bb = nA[0] % 2
                nA[0] += 1
                S.op("pe", lambda e, bb=bb: e.matmul(psA[bb][:, 0:NCP], w2[:, 0, :], H1g[:, 0:NCP], start=True, stop=True),
                     reads=[tW, tH1g], writes=[tpsA[bb]])
                S.op("act", lambda e, bb=bb, gl=gl: e.copy(out=kcmpT[:, gl, 0:NCP], in_=psA[bb][:, 0:NCP]), reads=[tpsA[bb]], writes=[tkcmpT])
                colnorm_max(kcmpT[:, gl, 0:NCP], [tkcmpT], gl, ncols=NCP)
            else:
                for ch in range(CH):
                    bb = nA[0] % 2
                    nA[0] += 1
                    S.op("pe", lambda e, bb=bb, ch=ch: e.matmul(psA[bb][:, 0:128], H1g[:, ch * 128:(ch + 1) * 128], w2[:, 1, :], start=True, stop=True),
                         reads=[tW, tH1g], writes=[tpsA[bb]])
                    S.op("act", lambda e, bb=bb, gl=gl, ch=ch: e.copy(out=vcmp[:, gl, ch, 0:128], in_=psA[bb][:, 0:128]),
                         reads=[tpsA[bb]], writes=[tvcmp])

    if stop == 2:
        S.op("pool", lambda e: e.dma_start(out=y_d[0:128, 0:256], in_=h_sb[:, 0, :]), reads=[th[0][0]], chan=cmisc)
        S.emit(final_waits=[cmisc])
        C.st.close()
        return nc
    S.op("pool", lambda e: e.dma_start(out=W[:], in_=wq_d), writes=[tW], chan=cW)
    qT = C.sb([128, TPB, 8, 128], BF16, "qT")
    tqT = [T_() for _ in range(8)]
    gsb = [C.sb([128, 24], F32, "gsb") for _ in range(2)]
    tgsb = [T_(), T_()]
    cmk = [C.sb([128, CH, 4, 128], BF16, "cmk") for _ in range(2)]
    tcmk = [T_(), T_()]
    ccmk = [S.chan(), S.chan()]
    fpt = [C.sb([128, 64], F32, "fpos") for _ in range(2)]
    fnt = [C.sb([128, 64], F32, "fneg") for _ in range(2)]
    tfp = [T_(), T_()]
    cfp = [S.chan(), S.chan()]
    P = [C.sb([128, 512], BF16, "P") for _ in range(3)]
    tP = [T_() for _ in range(3)]
    qsq = C.sb([128, 512], F32R, "qsq")
    tqsq = T_()
    mq2 = C.sb([128, 1], F32, "mq2")
    nbias = C.sb([128, 3], F32, "nbias")
    tmq2, tnbias = T_(), T_()
    rz = C.sb([128, 4], F32, "rz")
    coef = C.sb([128, 4], F32, "coef")
    trz, tcoef = T_(), T_()
    imp = C.sb([128, 64], F32, "imp")
    imp2 = C.sb([128, 64], F32, "imp2")
    mx8 = C.sb([128, 8], F32, "mx8")
    mx8b = C.sb([128, 8], F32, "mx8b")
    negblk = C.sb([128, 64], F32, "negblk")
    negblkT = C.sb([64, 4, 128], BF16, "negblkT")
    timp, timp2, tmx8, tmx8b, tnegblk, tnegblkT = [T_() for _ in range(6)]
    yt0 = C.sb([128, 1024], F32, "yt")
    yt = [yt0, yt0]
    tyt0 = T_()
    tyt = [tyt0, tyt0]
    cy = [S.chan(), S.chan()]
    nP = [0]
    nSc = [0]

    def score(mm_list, br):
        b = nSc[0] % 2
        nSc[0] += 1
        n = len(mm_list)
        for i, (lhsT, rhs, rd) in enumerate(mm_list):
            S.op("pe", lambda e, lhsT=lhsT, rhs=rhs, i=i, b=b: e.matmul(psB[b][:], lhsT, rhs, start=(i == 0), stop=(i == n - 1)),
                 reads=rd, writes=[tpsB[b]])
        p = nP[0] % 3
        nP[0] += 1
        S.op("act", lambda e, b=b, p=p, br=br: e.activation(out=P[p][:], in_=psB[b][:], func=AF.Exp, bias=nbias[:, br:br + 1], scale=1.0),
             reads=[tpsB[b], tnbias], writes=[tP[p]])
        return p

    def pv(p, rhs_v, first, last, width):
        for r in range(4):
            S.op("pe", lambda e, r=r, p=p: e.matmul(psO[r][:, 0:width], P[p][:, r * 128:(r + 1) * 128], rhs_v[0],
                                                  start=first, stop=last),
                 reads=[tP[p]] + rhs_v[1], writes=[tpsO[r]])

    def evac(br, gl, kk, width, first_branch, with_imp=False):
        first_branch = (br == branches[0])
        for r in range(4):
            o_ap = psO[r][:, 0:128]
            z_ap = psO[r][:, 128:129]
            S.op("dve", lambda e, r=r, z_ap=z_ap: e.tensor_scalar(out=rz[:, r:r + 1], in0=z_ap, scalar1=1.0e-30, scalar2=None, op0=ALU.add),
                 reads=[tpsO[r]], writes=[trz])
            S.op("dve", lambda e, r=r: e.reciprocal(out=rz[:, r:r + 1], in_=rz[:, r:r + 1]), reads=[trz], writes=[trz])
            gc = br * 8 + gl * 4 + r
            S.op("dve", lambda e, r=r, gc=gc, kk=kk: e.tensor_tensor(out=coef[:, r:r + 1], in0=rz[:, r:r + 1], in1=gsb[kk][:, gc:gc + 1], op=ALU.mult),
                 reads=[trz, tgsb[kk]], writes=[tcoef])
            ysl = slice((gl * 4 + r) * 128, (gl * 4 + r + 1) * 128)
            if br not in branches:
                pass
            elif first_branch:
                S.op("dve", lambda e, r=r, kk=kk, ysl=ysl, o_ap=o_ap: e.tensor_scalar(out=yt[kk][:, ysl], in0=o_ap, scalar1=coef[:, r:r + 1], scalar2=None, op0=ALU.mult),
                     reads=[tpsO[r], tcoef], writes=[tyt[kk]])
            else:
                S.op("dve", lambda e, r=r, kk=kk, ysl=ysl, o_ap=o_ap: e.scalar_tensor_tensor(out=yt[kk][:, ysl], in0=o_ap, scalar=coef[:, r:r + 1], in1=yt[kk][:, ysl],
                                                                                         op0=ALU.mult, op1=ALU.add),
                     reads=[tpsO[r], tcoef, tyt[kk]], writes=[tyt[kk]])
            if with_imp:
                i_ap = psO[r][:, 129:193]
                if r == 0:
                    S.op("dve", lambda e, r=r, i_ap=i_ap: e.tensor_scalar(out=imp[:], in0=i_ap, scalar1=rz[:, r:r + 1], scalar2=None, op0=ALU.mult),
                         reads=[tpsO[r], trz], writes=[timp])
                else:
                    S.op("dve", lambda e, r=r, i_ap=i_ap: e.scalar_tensor_tensor(out=imp[:], in0=i_ap, scalar=rz[:, r:r + 1], in1=imp[:], op0=ALU.mult, op1=ALU.add),
                         reads=[tpsO[r], trz, timp], writes=[timp])

    ntile = 0
    for tb in range(NBLK):
        tsl = slice(tb * TB, (tb + 1) * TB)
        norm_block(tsl)
        load_rope("cosq", "sinq", tsl)
        for hd in range(8):
            b = proj_fm(hd * 128)
            rope_to(b, qT[:, :, hd, :], [tqT[hd]], split=True)
        for j in range(TPB):
            Tq = tb * TPB + j
            jsl = slice(j * 128, (j + 1) * 128)
            kk = ntile % 2
            ntile += 1
            for c in range(DC):
                S.op("pe", lambda e, c=c, jsl=jsl: e.matmul(psS[:, 0:24], hn[:, c, jsl], Wgt[:, c, :], start=(c == 0), stop=(c == DC - 1)),
                     reads=[tWgt, thn[c][0]], writes=[tpsS])
            S.op("act", lambda e, kk=kk: e.activation(out=gsb[kk][:], in_=psS[:, 0:24], func=AF.Sigmoid), reads=[tpsS], writes=[tgsb[kk]])
            for r4 in range(4):
                S.op("pool", lambda e, kk=kk, Tq=Tq, r4=r4: e.dma_start(out=cmk[kk][:, :, r4, :], in_=tabs["cmpmask"][Tq]), writes=[tcmk[kk]], chan=ccmk[kk])
            S.op("sp", lambda e, kk=kk, Tq=Tq: e.dma_start(out=fpt[kk][:], in_=tabs["fpos"][Tq]), writes=[tfp[kk]], chan=cfp[kk])
            S.op("sp", lambda e, kk=kk, Tq=Tq: e.dma_start(out=fnt[kk][:], in_=tabs["fneg"][Tq]), writes=[tfp[kk]], chan=cfp[kk])
            for gl in range(2):
                qv = qT[:, j, gl * 4:(gl + 1) * 4, :]
                qrd = [tqT[gl * 4 + r] for r in range(4)]
                S.op("act", lambda e, qv=qv: e.activation(out=qsq[:].rearrange("p (r q) -> p r q", r=4), in_=qv, func=AF.Square), reads=qrd, writes=[tqsq])
                S.op("pe", lambda e: e.matmul(psS[:, 0:512], ones_r[:], qsq[:], start=True, stop=True), reads=[tqsq, tones], writes=[tpsS])
                S.op("dve", lambda e: e.tensor_reduce(out=mq2[:], in_=psS[:, 0:512], axis=AX.X, op=ALU.max), reads=[tpsS], writes=[tmq2])
                S.op("dve", lambda e, gl=gl: e.tensor_scalar(out=nbias[:], in0=kmax2[:, gl:gl + 5:2], scalar1=mq2[:, 0:1], scalar2=None, op0=ALU.mult),
                     reads=[tkmax2, tmq2], writes=[tnbias])
                S.op("act", lambda e: e.activation(out=nbias[:], in_=nbias[:], func=AF.Sqrt), reads=[tnbias], writes=[tnbias])
                S.op("dve", lambda e: e.tensor_scalar(out=nbias[:], in0=nbias[:], scalar1=-1.0, scalar2=None, op0=ALU.mult), reads=[tnbias], writes=[tnbias])
                chunks = []
                chs = [ch for ch in range(CH) if 16 * (ch * 128) + 31 <= Tq * 128 + 127]
                for ci, ch in enumerate(chs):
                    mm = [(kcmpT[:, gl, ch * 128:(ch + 1) * 128], qv, [tkcmpT] + qrd),
                          (identb[:], cmk[kk][:, ch, :, :], [tidb, tcmk[kk]])]
                    chunks.append((0, mm, (vcmp[:, gl, ch, :], [tvcmp]), ci == 0, ci == len(chs) - 1, 193))
                k0 = max(0, Tq - 4)
                for kc in range(k0, Tq + 1):
                    mm = [(kTs[:, 1, gl, kc * 128:(kc + 1) * 128], qv, [tkTs[1][gl][kc // TPB]] + qrd)]
                    if kc == Tq:
                        mm.append((identb[:], causal[:], [tidb, tcausal]))
                    if kc == Tq - 4:
                        mm.append((identb[:], winneg[:], [tidb, twinneg]))
                    chunks.append((2, mm, (vwin[:, kc, gl, :], [tvwin[kc]]), kc == k0, kc == Tq, 129))
                for kc in range(Tq + 1):
                    mm = [(kTs[:, 0, gl, kc * 128:(kc + 1) * 128], qv, [tkTs[0][gl][kc // TPB]] + qrd),
                          (Eb[:, kc, :], negblkT[:], [tE, tnegblkT])]
                    if kc == Tq:
                        mm.append((identb[:], causal[:], [tidb, tcausal]))
                    chunks.append((1, mm, (vsel[:, kc, gl, :], [tvsel[kc]]), kc == 0, kc == Tq, 129))

                def topk_dve(kk=kk):
                    S.op("dve", lambda e: e.tensor_tensor(out=imp[:], in0=imp[:], in1=fpt[kk][:], op=ALU.max), reads=[timp, tfp[kk]], writes=[timp])
                    S.op("dve", lambda e: e.tensor_tensor(out=imp[:], in0=imp[:], in1=fnt[kk][:], op=ALU.min), reads=[timp, tfp[kk]], writes=[timp])
                    S.op("dve", lambda e: e.max(out=mx8[:], in_=imp[:]), reads=[timp], writes=[tmx8])
                    S.op("dve", lambda e: e.match_replace(out=imp2[:], in_to_replace=mx8[:], in_values=imp[:], imm_value=-3.0e38),
                         reads=[timp, tmx8], writes=[timp2])
                    S.op("dve", lambda e: e.max(out=mx8b[:], in_=imp2[:]), reads=[timp2], writes=[tmx8b])
                    S.op("dve", lambda e: e.tensor_reduce(out=kred[:], in_=mx8b[:], axis=AX.X, op=ALU.min), reads=[tmx8b], writes=[tkred])
                    S.op("dve", lambda e: e.tensor_scalar(out=negblk[:], in0=imp[:], scalar1=kred[:, 0:1], scalar2=None, op0=ALU.is_ge),
                         reads=[timp, tkred], writes=[tnegblk])
                    S.op("dve", lambda e: e.tensor_scalar(out=negblk[:], in0=negblk[:], scalar1=1.0e30, scalar2=-1.0e30, op0=ALU.mult, op1=ALU.add),
                         reads=[tnegblk], writes=[tnegblk])

                def topk_T():
                    S.op("pe", lambda e: e.transpose(psS[0:64, 0:128], negblk[:], ident[:]), reads=[tnegblk, tid], writes=[tpsS])
                    S.op("act", lambda e: e.copy(out=negblkT[:], in_=psS[0:64, 0:128][:, None, :].to_broadcast([64, 4, 128])), reads=[tpsS], writes=[tnegblkT])

                nch = len(chunks)
                first_sel = next(i for i, c in enumerate(chunks) if c[0] == 1)

                def do_score(i):
                    if i == first_sel:
                        topk_T()
                    return score(chunks[i][1], chunks[i][0])

                pend = do_score(0)
                for i in range(nch):
                    br_i, _, rhs_v, first, last, width = chunks[i]
                    nxt = do_score(i + 1) if i + 1 < nch else None
                    pv(pend, rhs_v, first, last, width)
                    pend = nxt
                    if last:
                        evac(br_i, gl, kk, width, br_i == 0, with_imp=(br_i == 0))
                        if br_i == 0:
                            topk_dve()
            S.op("sp", lambda e, kk=kk, Tq=Tq: e.dma_start(out=y_d[Tq * 128:(Tq + 1) * 128, :], in_=yt[kk][:]), reads=[tyt[kk]], chan=cy[kk])
    S.emit(final_waits=cy)
    C.st.close()
    return nc


def _lay(w):
    dc = w.shape[0] // 128
    return np.ascontiguousarray(w.reshape(dc, 128, -1).transpose(1, 0, 2))


def _gT(g):
    return np.ascontiguousarray(np.asarray(g, np.float32).reshape(-1, 128).T)


def _prep_ffn_w(Wg, Wu, Wd):
    D, FF = Wg.shape
    DC, FC = D // 128, FF // 128
    t = lambda W: np.ascontiguousarray(W.reshape(DC, 128, FC, 128).transpose(2, 1, 0, 3).reshape(FC, 128, DC * 128))
    wd = np.ascontiguousarray(Wd.reshape(FC, 128, DC, 128).transpose(2, 1, 0, 3).reshape(DC, 128, FC * 128))
    return t(Wg), t(Wu), wd


def _prep_wo(Wo):
    DC = Wo.shape[0] // 128
    DO = Wo.shape[1] // 128
    return np.ascontiguousarray(Wo.reshape(DC, 128, DO, 128).transpose(2, 1, 0, 3).reshape(DO, 128, DC * 128))


def _prep_ab(w_in, gate_b, g_norm, g_ws, g_bs, conv_w, m_norm, hp):
    o1, o2, o3, o4 = 2048, 4096, 5120, 6144
    wz = _lay(w_in[:, :o1])
    hs = slice(hp * 512, (hp + 1) * 512)
    wqk = _lay(np.concatenate([w_in[:, o1:o1 + 1024][:, hs], w_in[:, o1 + 1024:o2][:, hs]], 1))
    gi = w_in[:, o4:o4 + 4][:, hp * 2:hp * 2 + 2]
    gf = w_in[:, o4 + 4:o4 + 8][:, hp * 2:hp * 2 + 2]
    wvog = _lay(np.concatenate([w_in[:, o2:o3][:, hs], w_in[:, o3:o4][:, hs], gi, gf], 1))
    gb = np.concatenate([gate_b[0, hp * 2:hp * 2 + 2], gate_b[1, hp * 2:hp * 2 + 2]])
    gb = np.ascontiguousarray(np.broadcast_to(gb[None, :], (128, 4))).astype(np.float32)
    cc = np.concatenate([conv_w[:, :1024][:, hs], conv_w[:, 1024:][:, hs]], 1)
    conv = np.ascontiguousarray(cc.T.reshape(8, 128, 4).transpose(1, 0, 2))
    mn = np.ascontiguousarray(np.broadcast_to(m_norm[hs][None, :], (128, 512)))
    gn = np.ascontiguousarray(np.broadcast_to(g_norm[None, :], (128, 1024)))
    wsT = np.ascontiguousarray(g_ws.transpose(2, 0, 1))
    bs = np.ascontiguousarray(g_bs.T)
    ident = np.eye(128, dtype=np.float32)
    mask = np.triu(np.ones((128, 128), np.float32))
    return dict(wz=wz, wqk=wqk, wvog=wvog, gb=gb, conv=conv, mn=mn, gn=gn, wsT=wsT, bs=bs, ident=ident, mask=mask)


def _prep_nsa(w_in, cmp_pos, cmp_w1, cmp_w2, hp):
    wq = _lay(w_in[:, hp * 1024:(hp + 1) * 1024])

    def kvcol(br, kv, g):
        o = 2048 + ((br * 2 + kv) * 4 + g) * 128
        return w_in[:, o:o + 128]
    cols = []
    for br in range(3):
        for gl in range(2):
            cols.append(kvcol(br, 0, 2 * hp + gl))
    for gl in range(2):
        cols.append(kvcol(0, 1, 2 * hp + gl))
    for br in (1, 2):
        for gl in range(2):
            cols.append(kvcol(br, 1, 2 * hp + gl))
    wkv = _lay(np.concatenate(cols, 1))
    og = 2048 + 3072
    gc = []
    for br in range(3):
        for gl in range(2):
            g = 2 * hp + gl
            gc.append(w_in[:, og + br * 16 + g * 4: og + br * 16 + g * 4 + 4])
    wgt = _lay(np.concatenate(gc, 1))
    w1 = np.ascontiguousarray(cmp_w1.reshape(2, 32, 128, 128).transpose(2, 0, 1, 3))
    w2 = np.ascontiguousarray(cmp_w2.transpose(1, 0, 2))
    posT = np.ascontiguousarray(cmp_pos.transpose(2, 0, 1))
    return dict(wq=wq, wkv=wkv, wgt=wgt, w1=w1, w2=w2, posT=posT)


_PROGS = {}


def _prog(key, fn):
    if key not in _PROGS:
        _PROGS[key] = fn()
    return _PROGS[key]


def _run(nc, maps):
    res = run_bass_kernel_spmd(nc, maps, core_ids=list(range(8)))
    return res.results


def kernel(x, ffn_norm, ffn_w_gate, ffn_w_up, ffn_w_down, mix_norm, ab_w_in, mlstm_gate_bias,
           gmlp_norm, gmlp_w_s, gmlp_b_s, mlstm_conv, mlstm_norm, ab_w_out, nsa_w_in, nsa_cmp_pos,
           nsa_cmp_w1, nsa_cmp_w2, nsa_w_out, final_norm):
    f = lambda a: np.asarray(a, dtype=np.float32)
    x = f(x)
    B, Sq, D = x.shape
    FF = ffn_w_gate.shape[-1]
    NTC = B * Sq // 8
    hT = [np.ascontiguousarray(x[c // 2, (c % 2) * NTC:(c % 2 + 1) * NTC, :].T) for c in range(8)]

    def ffn(hT, layer, which, yT=None, wo=None, final=False):
        wg, wu, wd = _prep_ffn_w(f(ffn_w_gate[layer, which]), f(ffn_w_up[layer, which]), f(ffn_w_down[layer, which]))
        g = _gT(ffn_norm[layer, which])
        pre = yT is not None
        nc = _prog(("ffn", pre, final), lambda: build_ffn(D, FF, NTC, 1024, 2, pre=pre, final=final))
        maps = []
        for c in range(8):
            m = {"hT": hT[c], "g": g, "wg": wg, "wu": wu, "wd": wd}
            if pre:
                m["yT"] = yT[c]
                m["wo"] = wo
            if final:
                m["gf"] = _gT(final_norm)
            maps.append(m)
        r = _run(nc, maps)
        return [r[c]["oT"] for c in range(8)]

    def full_seq(hT, b):
        return np.ascontiguousarray(np.concatenate([hT[2 * b], hT[2 * b + 1]], axis=1))

    hT = ffn(hT, 0, 0)
    nc = _prog(("ab",), lambda: build_ab(Sq, D))
    maps = []
    for c in range(8):
        b, hp = c // 2, c % 2
        m = _prep_ab(f(ab_w_in[0]), f(mlstm_gate_bias[0]), f(gmlp_norm[0]), f(gmlp_w_s[0]), f(gmlp_b_s[0]), f(mlstm_conv[0]),
                     f(mlstm_norm[0]), hp)
        m["hT"] = full_seq(hT, b)
        m["hTh"] = hT[c]
        m["g"] = _gT(mix_norm[0])
        maps.append(m)
    r = _run(nc, maps)
    yT = []
    for c in range(8):
        b, hp = c // 2, c % 2
        sl = slice(hp * NTC, (hp + 1) * NTC)
        ya = r[c]["ya"]
        yb = np.concatenate([r[2 * b]["yb"][sl], r[2 * b + 1]["yb"][sl]], axis=1)
        yT.append(np.ascontiguousarray(np.concatenate([ya, yb], axis=1).T))
    hT = ffn(hT, 0, 1, yT=yT, wo=_prep_wo(f(ab_w_out[0])))
    hT = ffn(hT, 1, 0)
    nc = _prog(("nsa",), lambda: build_nsa(Sq, D))
    tabs = nsa_tables(Sq)
    maps = []
    for c in range(8):
        b, hp = c // 2, c % 2
        m = _prep_nsa(f(nsa_w_in[0]), f(nsa_cmp_pos[0]), f(nsa_cmp_w1[0]), f(nsa_cmp_w2[0]), hp)
        m.update(tabs)
        m["hT"] = full_seq(hT, b)
        m["g"] = _gT(mix_norm[1])
        maps.append(m)
    r = _run(nc, maps)
    yT = []
    for c in range(8):
        b, hp = c // 2, c % 2
        sl = slice(hp * NTC, (hp + 1) * NTC)
        y = np.concatenate([r[2 * b]["y"][sl], r[2 * b + 1]["y"][sl]], axis=1)
        yT.append(np.ascontiguousarray(y.T))
    hT = ffn(hT, 1, 1, yT=yT, wo=_prep_wo(f(nsa_w_out[0])), final=True)
    out = np.empty((B, Sq, D), np.float32)
    for c in range(8):
        out[c // 2, (c % 2) * NTC:(c % 2 + 1) * NTC, :] = hT[c].T
    return out
```

```python
import bisect
import contextlib
import numpy as np
import concourse.bass as bass
import concourse.mybir as mybir
from concourse.bass_utils import run_bass_kernel_spmd

F32 = mybir.dt.float32
F32R = mybir.dt.float32r
BF16 = mybir.dt.bfloat16
AF = mybir.ActivationFunctionType
ALU = mybir.AluOpType
AX = mybir.AxisListType

ENGS = ("pe", "act", "dve", "pool", "sp")


class Tile:
    __slots__ = ("name", "w", "r", "psum")

    def __init__(self, name="", psum=False):
        self.name = name
        self.w = None
        self.r = []
        self.psum = psum


class Chan:
    __slots__ = ("sem", "cnt", "ops")

    def __init__(self):
        self.sem = None
        self.cnt = 0
        self.ops = []


class Sched:
    def __init__(self, nc):
        self.nc = nc
        self.ops = []
        self.chans = []

    def chan(self):
        c = Chan()
        self.chans.append(c)
        return c

    def op(self, eng, fn, reads=(), writes=(), chan=None, nosync_same=False):
        idx = len(self.ops)
        deps = set()
        for t in reads:
            if t.w is not None:
                deps.add(t.w)
            if t.psum:
                for r in t.r:
                    if self.ops[r]["eng"] != eng:
                        deps.add(r)
        for t in writes:
            if t.w is not None:
                deps.add(t.w)
            for r in t.r:
                deps.add(r)
        deps.discard(idx)
        rec = dict(eng=eng, fn=fn, deps=deps, chan=chan, idx=idx, nosync_same=nosync_same,
                   sig=None)
        if chan is not None:
            chan.cnt += 16
            rec["sig"] = (chan, chan.cnt)
            chan.ops.append(idx)
        self.ops.append(rec)
        for t in reads:
            t.r.append(idx)
        for t in writes:
            t.w = idx
            t.r = []
        return idx

    def emit(self, final_waits=()):
        nc = self.nc
        ops = self.ops
        needed = set()
        for o in ops:
            for d in o["deps"]:
                od = ops[d]
                if od["chan"] is None:
                    if od["eng"] == o["eng"] and (o["nosync_same"] or od["eng"] == "pe" or od["eng"] == "sp"):
                        continue
                    needed.add(d)
        cnt = {e: 0 for e in ENGS}
        for o in ops:
            if o["chan"] is None and o["idx"] in needed:
                cnt[o["eng"]] += 1
                o["sig"] = (o["eng"], cnt[o["eng"]])
        with contextlib.ExitStack() as st:
            esem = {e: st.enter_context(nc.semaphore("s_" + e)) for e in ENGS}
            for i, c in enumerate(self.chans):
                c.sem = st.enter_context(nc.semaphore("c%d" % i))
            block = st.enter_context(nc.Block())

            def run_engine(ename, eobj):
                known = {}
                for o in ops:
                    if o["eng"] != ename:
                        continue
                    want = {}
                    for d in o["deps"]:
                        od = ops[d]
                        sig = od["sig"]
                        if sig is None:
                            continue
                        key, val = sig
                        if od["chan"] is None and od["eng"] == ename and (
                                o["nosync_same"] or ename in ("pe", "sp")):
                            continue
                        if isinstance(key, Chan):
                            val = 16 * bisect.bisect_left(key.ops, o["idx"])
                        if want.get(key, 0) < val:
                            want[key] = val
                    for key, val in want.items():
                        if known.get(key, 0) >= val:
                            continue
                        known[key] = val
                        sem = key.sem if isinstance(key, Chan) else esem[key]
                        eobj.wait_ge(sem, val)
                    ins = o["fn"](eobj)
                    if o["sig"] is not None:
                        key, val = o["sig"]
                        if isinstance(key, Chan):
                            ins.then_inc(key.sem, 16)
                        else:
                            ins.then_inc(esem[key], 1)
                if ename == "sp":
                    for c in final_waits:
                        eobj.wait_ge(c.sem, c.cnt)

            @block.tensor
            def _(e):
                run_engine("pe", e)

            @block.scalar
            def _(e):
                run_engine("act", e)

            @block.vector
            def _(e):
                run_engine("dve", e)

            @block.gpsimd
            def _(e):
                run_engine("pool", e)

            @block.sync
            def _(e):
                run_engine("sp", e)


class Ctx:
    def __init__(self):
        self.nc = bass.Bass("TRN2", target_bir_lowering=False)
        self.st = contextlib.ExitStack()
        self.S = Sched(self.nc)
        self.n = 0

    def sb(self, shape, dt, name=None):
        self.n += 1
        return self.st.enter_context(self.nc.sbuf_tensor("%s_%d" % (name or "sb", self.n), list(shape), dt))

    def ps(self, shape=(128, 512), dt=F32, name=None):
        self.n += 1
        return self.st.enter_context(self.nc.psum_tensor("%s_%d" % (name or "ps", self.n), list(shape), dt))

    def din(self, name, shape, dt=F32):
        return self.nc.dram_tensor(name, list(shape), dt, kind="ExternalInput").ap()

    def dout(self, name, shape, dt=F32):
        return self.nc.dram_tensor(name, list(shape), dt, kind="ExternalOutput").ap()


EPS = 1e-6


def emit_rstd(S, rstd, trstd, ps_ssq, tssq, Dn, SUB=512):
    S.op("act", lambda e: e.activation(out=rstd[:, 0:SUB], in_=ps_ssq[:, 0:SUB], func=AF.Sqrt, bias=EPSB[0][:, 0:1], scale=1.0 / Dn),
         reads=[tssq, EPSB[1]], writes=[trstd])
    S.op("dve", lambda e: e.reciprocal(out=rstd[:, 0:SUB], in_=rstd[:, 0:SUB]), reads=[trstd], writes=[trstd])


EPSB = [None, None]


def emit_consts(C):
    S = C.S
    eps = C.sb([128, 1], F32, "eps")
    teps = Tile()
    S.op("dve", lambda e: e.memset(eps[:], EPS), writes=[teps])
    EPSB[0], EPSB[1] = eps, teps
    ones32 = C.sb([128, 128], F32, "ones32")
    ones_r = C.sb([128, 128], F32R, "ones")
    t32, tones = Tile(), Tile()
    S.op("dve", lambda e: e.memset(ones32[:], 1.0), writes=[t32])
    S.op("dve", lambda e: e.tensor_copy(out=ones_r[:], in_=ones32[:]), reads=[t32], writes=[tones])
    return ones_r, tones


def emit_rmsnorm_T(C, h_sb, th, aT, taT, g_sb, tg, ones_r, tones, DC, TB, sq, tsq, ps_ssq, tssq, rstd, trstd,
                   Dn, out_dt_cast=None):
    S = C.S
    SUB = min(512, TB)
    NS = TB // SUB
    for s in range(NS):
        sl = slice(s * SUB, (s + 1) * SUB)
        for c in range(DC):
            k = (s * DC + c) % 2
            S.op("act", lambda e, c=c, k=k, sl=sl: e.activation(out=sq[k][:, 0:SUB], in_=h_sb[:, c, sl], func=AF.Square),
                 reads=[th[c][s]], writes=[tsq[k]])
            S.op("pe", lambda e, c=c, k=k: e.matmul(ps_ssq[:, 0:SUB], ones_r[:], sq[k][:, 0:SUB], start=(c == 0), stop=(c == DC - 1)),
                 reads=[tsq[k], tones], writes=[tssq])
        emit_rstd(S, rstd, trstd, ps_ssq, tssq, Dn, SUB)
        for c in range(DC):
            S.op("dve", lambda e, c=c, sl=sl: e.scalar_tensor_tensor(out=aT[:, c, sl], in0=h_sb[:, c, sl],
                                                                      scalar=g_sb[:, c:c + 1], in1=rstd[:, 0:SUB],
                                                                      op0=ALU.mult, op1=ALU.mult),
                 reads=[th[c][s], tg, trstd], writes=[taT[c][s]])


def build_ffn(D, FF, NT, TB, NH, pre=False, final=False):
    C = Ctx()
    nc, S = C.nc, C.S
    DC, FC = D // 128, FF // 128
    FH = FC // NH
    NS = TB // 512
    NB = NT // TB
    hT = C.din("hT", [D, NT])
    g_d = C.din("g", [128, DC])
    wg_d = C.din("wg", [FC, 128, DC * 128])
    wu_d = C.din("wu", [FC, 128, DC * 128])
    wd_d = C.din("wd", [DC, 128, FC * 128])
    if pre:
        yT = C.din("yT", [D, NT])
        wo_d = C.din("wo", [DC, 128, DC * 128])
    if final:
        gf_d = C.din("gf", [128, DC])
    oT = C.dout("oT", [D, NT])
    hTv = hT.rearrange("(c p) t -> p c t", p=128)
    oTv = oT.rearrange("(c p) t -> p c t", p=128)

    h_sb = C.sb([128, DC, TB], F32, "h")
    aT = C.sb([128, DC, TB], BF16, "aT")
    HT = C.sb([128, FH, TB], BF16, "HT")
    wg = [C.sb([128, DC * 128], BF16, "wg") for _ in range(2)]
    wu = [C.sb([128, DC * 128], BF16, "wu") for _ in range(2)]
    wd = [C.sb([128, FH * 128], BF16, "wd") for _ in range(2)]
    sq = [C.sb([128, 512], F32R, "sq") for _ in range(2)]
    sg = [C.sb([128, 512], F32, "sg") for _ in range(2)]
    rstd = C.sb([128, 512], F32, "rstd")
    g_sb = C.sb([128, DC], F32, "g")
    ps_ssq = C.ps(name="ssq")
    psG = [C.ps(name="G") for _ in range(2)]
    psU = [C.ps(name="U") for _ in range(2)]
    psY = [C.ps(name="Y") for _ in range(2)]
    if pre:
        wo = [C.sb([128, DC * 128], BF16, "wo") for _ in range(2)]
        two = [Tile() for _ in range(2)]
        cwo = [S.chan() for _ in range(2)]
    if final:
        gf_sb = C.sb([128, DC], F32, "gf")
        tgf = Tile()
        fo = C.sb([128, DC, TB], F32, "fo") if False else None

    th = [[Tile() for _ in range(NS)] for _ in range(DC)]
    taT = [[Tile() for _ in range(NS)] for _ in range(DC)]
    tHT = [[Tile() for _ in range(NS)] for _ in range(FH)]
    twg = [Tile() for _ in range(2)]
    twu = [Tile() for _ in range(2)]
    twd = [Tile() for _ in range(2)]
    tsq = [Tile() for _ in range(2)]
    tsg = [Tile() for _ in range(2)]
    trstd, tg, tssq = Tile(), Tile(), Tile()
    tG = [Tile(psum=True) for _ in range(2)]
    tU = [Tile(psum=True) for _ in range(2)]
    tY = [Tile(psum=True) for _ in range(2)]
    cwg = [S.chan() for _ in range(2)]
    cwu = [S.chan() for _ in range(2)]
    cwd = [S.chan() for _ in range(2)]
    ch_in = S.chan()
    ch_out = S.chan()
    cg = S.chan()

    S.op("sp", lambda e: e.dma_start(out=g_sb[:], in_=g_d), writes=[tg], chan=cg)
    if final:
        cgf = S.chan()
        S.op("sp", lambda e: e.dma_start(out=gf_sb[:], in_=gf_d), writes=[tgf], chan=cgf)
    ones_r, tones = emit_consts(C)

    all_h = [th[c][s] for c in range(DC) for s in range(NS)]
    all_aT = [taT[c][s] for c in range(DC) for s in range(NS)]
    CG = min(4, DC)
    nwd = 0
    nwgu = 0
    for tb in range(NB):
        tsl = slice(tb * TB, (tb + 1) * TB)
        for c0 in range(0, DC, CG):
            S.op("sp", lambda e, c0=c0, tsl=tsl: e.dma_start(out=h_sb[:, c0:c0 + CG, :], in_=hTv[:, c0:c0 + CG, tsl]),
                 writes=[th[c][s] for c in range(c0, c0 + CG) for s in range(NS)], chan=ch_in)
        if pre:
            for c0 in range(0, DC, CG):
                yv = yT.rearrange("(c p) t -> p c t", p=128)
                S.op("pool", lambda e, c0=c0, tsl=tsl, yv=yv: e.dma_start(out=aT[:, c0:c0 + CG, :], in_=yv[:, c0:c0 + CG, tsl]),
                     writes=[taT[c][s] for c in range(c0, c0 + CG) for s in range(NS)], chan=ch_in)
            for dc in range(DC):
                k = dc % 2
                S.op("pool", lambda e, dc=dc, k=k: e.dma_start(out=wo[k][:], in_=wo_d[dc]), writes=[two[k]], chan=cwo[k])
                for s in range(NS):
                    sl = slice(s * 512, (s + 1) * 512)
                    b = (dc * NS + s) % 2
                    for c in range(DC):
                        S.op("pe", lambda e, c=c, k=k, b=b, sl=sl: e.matmul(psY[b][:], wo[k][:, c * 128:(c + 1) * 128], aT[:, c, sl],
                                                                          start=(c == 0), stop=(c == DC - 1)),
                             reads=[two[k], taT[c][s]], writes=[tY[b]])
                    S.op("dve", lambda e, dc=dc, b=b, sl=sl: e.tensor_tensor(out=h_sb[:, dc, sl], in0=psY[b][:], in1=h_sb[:, dc, sl], op=ALU.add),
                         reads=[tY[b], th[dc][s]], writes=[th[dc][s]])
        emit_rmsnorm_T(C, h_sb, th, aT, taT, g_sb, tg, ones_r, tones, DC, TB, sq, tsq, ps_ssq, tssq, rstd, trstd, D)
        for hf in range(NH):
            for fi in range(FH):
                f = hf * FH + fi
                k = nwgu % 2
                nwgu += 1
                S.op("pool", lambda e, f=f, k=k: e.dma_start(out=wg[k][:], in_=wg_d[f]), writes=[twg[k]], chan=cwg[k])
                S.op("pool", lambda e, f=f, k=k: e.dma_start(out=wu[k][:], in_=wu_d[f]), writes=[twu[k]], chan=cwu[k])
                for s in range(NS):
                    sl = slice(s * 512, (s + 1) * 512)
                    b = (fi * NS + s) % 2
                    for c in range(DC):
                        S.op("pe", lambda e, c=c, k=k, b=b, sl=sl: e.matmul(psG[b][:], wg[k][:, c * 128:(c + 1) * 128], aT[:, c, sl],
                                                                          start=(c == 0), stop=(c == DC - 1)),
                             reads=[twg[k], taT[c][s]], writes=[tG[b]])
                    for c in range(DC):
                        S.op("pe", lambda e, c=c, k=k, b=b, sl=sl: e.matmul(psU[b][:], wu[k][:, c * 128:(c + 1) * 128], aT[:, c, sl],
                                                                          start=(c == 0), stop=(c == DC - 1)),
                             reads=[twu[k], taT[c][s]], writes=[tU[b]])
                    S.op("act", lambda e, b=b: e.activation(out=sg[b][:], in_=psG[b][:], func=AF.Silu),
                         reads=[tG[b]], writes=[tsg[b]])
                    S.op("dve", lambda e, b=b, fi=fi, sl=sl: e.tensor_tensor(out=HT[:, fi, sl], in0=psU[b][:], in1=sg[b][:], op=ALU.mult),
                         reads=[tU[b], tsg[b]], writes=[tHT[fi][s]])
            for dc in range(DC):
                k = nwd % 2
                nwd += 1
                S.op("pool", lambda e, dc=dc, hf=hf, k=k: e.dma_start(out=wd[k][:], in_=wd_d[dc, :, hf * FH * 128:(hf + 1) * FH * 128]),
                     writes=[twd[k]], chan=cwd[k])
                for s in range(NS):
                    sl = slice(s * 512, (s + 1) * 512)
                    b = (dc * NS + s) % 2
                    for fi in range(FH):
                        S.op("pe", lambda e, fi=fi, k=k, b=b, sl=sl: e.matmul(psY[b][:], wd[k][:, fi * 128:(fi + 1) * 128], HT[:, fi, sl],
                                                                            start=(fi == 0), stop=(fi == FH - 1)),
                             reads=[twd[k], tHT[fi][s]], writes=[tY[b]])
                    S.op("dve", lambda e, dc=dc, b=b, sl=sl: e.scalar_tensor_tensor(out=h_sb[:, dc, sl], in0=psY[b][:], scalar=0.5,
                                                                                 in1=h_sb[:, dc, sl], op0=ALU.mult, op1=ALU.add),
                         reads=[tY[b], th[dc][s]], writes=[th[dc][s]])
        if final:
            NSx = NS
            for s in range(NSx):
                sl = slice(s * 512, (s + 1) * 512)
                for c in range(DC):
                    k = (s * DC + c) % 2
                    S.op("act", lambda e, c=c, k=k, sl=sl: e.activation(out=sq[k][:], in_=h_sb[:, c, sl], func=AF.Square),
                         reads=[th[c][s]], writes=[tsq[k]])
                    S.op("pe", lambda e, c=c, k=k: e.matmul(ps_ssq[:], ones_r[:], sq[k][:], start=(c == 0), stop=(c == DC - 1)),
                         reads=[tsq[k], tones], writes=[tssq])
                emit_rstd(S, rstd, trstd, ps_ssq, tssq, D)
                for c in range(DC):
                    S.op("dve", lambda e, c=c, sl=sl: e.scalar_tensor_tensor(out=h_sb[:, c, sl], in0=h_sb[:, c, sl],
                                                                              scalar=gf_sb[:, c:c + 1], in1=rstd[:],
                                                                              op0=ALU.mult, op1=ALU.mult),
                         reads=[th[c][s], tgf, trstd], writes=[th[c][s]])
        for c0 in range(0, DC, CG):
            S.op("sp", lambda e, c0=c0, tsl=tsl: e.dma_start(out=oTv[:, c0:c0 + CG, tsl], in_=h_sb[:, c0:c0 + CG, :]),
                 reads=[th[c][s] for c in range(c0, c0 + CG) for s in range(NS)], chan=ch_out)
    S.emit(final_waits=[ch_out])
    C.st.close()
    return nc


GELU_C = 1.5957691216057308


def emit_gelu(S, out_ap, x_ap, t1_ap, t2_ap, reads, writes, tt1, tt2, eng2="dve"):
    S.op("act", lambda e: e.activation(out=t1_ap, in_=x_ap, func=AF.Square), reads=reads, writes=[tt1])
    S.op("dve", lambda e: e.tensor_scalar(out=t1_ap, in0=t1_ap, scalar1=0.044715, scalar2=1.0, op0=ALU.mult, op1=ALU.add),
         reads=[tt1], writes=[tt1])
    S.op("dve", lambda e: e.tensor_tensor(out=t1_ap, in0=x_ap, in1=t1_ap, op=ALU.mult), reads=reads + [tt1], writes=[tt1])
    S.op("act", lambda e: e.activation(out=t2_ap, in_=t1_ap, func=AF.Sigmoid, scale=GELU_C), reads=[tt1], writes=[tt2])
    S.op("dve", lambda e: e.tensor_tensor(out=out_ap, in0=x_ap, in1=t2_ap, op=ALU.mult), reads=reads + [tt2], writes=writes)


def build_ab(Sq, D=2048, debug=False):
    C = Ctx()
    nc, S = C.nc, C.S
    DC = D // 128
    TB = 512
    NBLK = Sq // TB
    SH = Sq // 2
    DH = 256
    hT = C.din("hT", [D, Sq])
    hTh = C.din("hTh", [D, SH])
    g_d = C.din("g", [128, DC])
    wz_d = C.din("wz", [128, DC, 2048])
    wqk_d = C.din("wqk", [128, DC, 1024])
    wvog_d = C.din("wvog", [128, DC, 1028])
    gb_d = C.din("gb", [128, 4])
    conv_d = C.din("conv", [128, 8, 4])
    mn_d = C.din("mn", [128, 512])
    gn_d = C.din("gn", [128, 1024])
    wsT_d = C.din("wsT", [128, 8, 128])
    bs_d = C.din("bs", [128, 8])
    ident_d = C.din("ident", [128, 128])
    mask_d = C.din("mask", [128, 128])
    ya = C.dout("ya", [SH, 1024])
    yb = C.dout("yb", [Sq, 512])
    hTv = hT.rearrange("(c p) t -> p c t", p=128)
    hThv = hTh.rearrange("(c p) t -> p c t", p=128)
    if debug:
        dbg = C.dout("dbg", [Sq // 128, 128, 16])
        dbg_sb = C.sb([128, 16], F32, "dbg")
        tdbg = Tile()
        cdbg = S.chan()

    W = C.sb([128, DC, 2052], BF16, "W")
    h_sb = C.sb([128, DC, TB], F32, "h")
    hn = C.sb([128, DC, TB], BF16, "hn")
    sq = [C.sb([128, 512], F32R, "sq") for _ in range(2)]
    rstd = C.sb([128, 512], F32, "rstd")
    g_sb = C.sb([128, DC], F32, "g")
    gb_sb = C.sb([128, 4], F32, "gb")
    conv_sb = C.sb([128, 8, 4], F32, "conv")
    mn_sb = C.sb([128, 512], F32, "mn")
    gn_sb = C.sb([128, 1024], F32, "gn")
    wsT32 = C.sb([128, 8, 128], F32, "wsT32")
    wsT = C.sb([128, 8, 128], BF16, "wsT")
    bs_sb = C.sb([128, 8], F32, "bs")
    ident = C.sb([128, 128], F32, "ident")
    identb = C.sb([128, 128], BF16, "identb")
    mask = C.sb([128, 128], F32, "mask")
    ones32 = C.sb([128, 128], F32, "ones32")
    qkpre = C.sb([128, 8, 3 + TB], F32, "qkpre")
    acc = C.sb([128, TB], F32, "acc")
    qkT = C.sb([128, 8, TB], BF16, "qkT")
    vaug = [C.sb([128, 2, 257], F32, "vaug") for _ in range(2)]
    so = [C.sb([128, 512], F32, "so") for _ in range(2)]
    gts = [C.sb([128, 4], F32, "gts") for _ in range(2)]
    lf = C.sb([128, 2], F32, "lf")
    iv = C.sb([128, 2], F32, "iv")
    bcol = C.sb([128, 2], F32, "bcol")
    acol = C.sb([128, 2], F32, "acol")
    a_bc = C.sb([128, 2, 128], F32, "a_bc")
    amax = C.sb([128, 2], F32, "amax")
    Mx = C.sb([128, 2], F32, "Mx")
    mprev = C.sb([128, 2], F32, "mprev")
    wprev = C.sb([128, 2], F32, "wprev")
    ws = C.sb([128, 2], F32, "ws")
    thr = C.sb([128, 2], F32, "thr")
    tmp2 = C.sb([128, 2], F32, "tmp2")
    sTm = C.sb([128, 128], BF16, "sTm")
    vw = C.sb([128, 257], BF16, "vw")
    CT = C.sb([128, 2, 2, 257], F32, "CT")
    CTb = C.sb([128, 2, 257], BF16, "CTb")
    ktok = C.sb([128, 256], BF16, "ktok")
    den = C.sb([128, 1], F32, "den")
    hh = C.sb([128, 256], F32, "hh")
    junk = C.sb([128, 256], F32, "junk")
    ssq1 = C.sb([128, 2], F32, "ssq1")
    ybt = [C.sb([128, 512], F32, "ybt") for _ in range(2)]
    u_sb = C.sb([128, 1024], F32, "u")
    v_sb = C.sb([128, 1024], F32, "v")
    vn = C.sb([128, 1024], BF16, "vn")
    t1 = C.sb([128, 512], F32, "t1")
    t2 = C.sb([128, 512], F32, "t2")
    yat = [C.sb([128, 1024], F32, "yat") for _ in range(2)]

    ps_ssq = C.ps(name="ssq")
    psA = [C.ps(name="A") for _ in range(2)]
    psB = [C.ps(name="B") for _ in range(2)]
    psS = C.ps(name="S")
    psN = C.ps(name="N")
    psT = C.ps([128, 256], BF16, name="T")

    T = Tile
    tW, tg, tgb, tconv, tmn, tgn, tws32, tws, tbs, tid, tidb, tmask, tones32 = [T() for _ in range(13)]
    th = [[T()] for _ in range(DC)]
    thn = [[T()] for _ in range(DC)]
    tsq = [T(), T()]
    trstd, tssq = T(), T()
    tqkpre = [T() for _ in range(8)]
    tacc = T()
    tqkT = [T() for _ in range(8)]
    tvaug = [T(), T()]
    tso = [T(), T()]
    tgts = [T(), T()]
    tlf, tiv, tbcol, tacol, tabc, tamax, tMx, tmprev, twprev, tws_, tthr, ttmp2 = [T() for _ in range(12)]
    tsTm, tvw, tCT, tCTb, tktok, tden, thh, tjunk, tssq1 = [T() for _ in range(9)]
    tybt = [T(), T()]
    tu, tv, tvn, tt1, tt2 = [T() for _ in range(5)]
    tyat = [T(), T()]
    tpsA = [T(psum=True), T(psum=True)]
    tpsB = [T(psum=True), T(psum=True)]
    tpsS, tpsN, tpsT = T(psum=True), T(psum=True), T(psum=True)

    cmisc = S.chan()
    cW = S.chan()
    ch_in = S.chan()
    cyb = [S.chan(), S.chan()]
    cya = [S.chan(), S.chan()]

    def ld(dst, src, tile, eng="sp", chan=None):
        S.op(eng, lambda e: e.dma_start(out=dst, in_=src), writes=[tile], chan=chan or cmisc)

    ld(g_sb[:], g_d, tg)
    ld(gb_sb[:], gb_d, tgb)
    ld(conv_sb[:], conv_d, tconv)
    ld(mn_sb[:], mn_d, tmn)
    ld(gn_sb[:], gn_d, tgn)
    ld(wsT32[:], wsT_d, tws32)
    ld(bs_sb[:], bs_d, tbs)
    ld(ident[:], ident_d, tid)
    ld(mask[:], mask_d, tmask)
    ones_r, tones = emit_consts(C)
    S.op("dve", lambda e: e.memset(ones32[:], 1.0), writes=[tones32])
    S.op("dve", lambda e: e.tensor_copy(out=identb[:], in_=ident[:]), reads=[tid], writes=[tidb])
    for g in range(8):
        S.op("dve", lambda e, g=g: e.tensor_tensor(out=wsT[:, g, :], in0=wsT32[:, g, :], in1=mask[:], op=ALU.mult),
             reads=[tws32, tmask], writes=[tws])
    S.op("pool", lambda e: e.dma_start(out=W[:, :, 0:1024], in_=wqk_d), writes=[tW], chan=cW)
    S.op("pool", lambda e: e.dma_start(out=W[:, :, 1024:2052], in_=wvog_d), writes=[tW], chan=cW)
    S.op("dve", lambda e: e.memset(CT[:], 0.0), writes=[tCT])
    S.op("dve", lambda e: e.memset(mprev[:], 0.0), writes=[tmprev])
    for k in range(2):
        S.op("dve", lambda e, k=k: e.memset(vaug[k][:], 1.0), writes=[tvaug[k]])
    for m in range(8):
        S.op("dve", lambda e, m=m: e.memset(qkpre[:, m, 0:3], 0.0), writes=[tqkpre[m]])

    def norm_block(src_v, tsl):
        CG = 4
        for c0 in range(0, DC, CG):
            S.op("sp", lambda e, c0=c0: e.dma_start(out=h_sb[:, c0:c0 + CG, :], in_=src_v[:, c0:c0 + CG, tsl]),
                 writes=[th[c][0] for c in range(c0, c0 + CG)], chan=ch_in)
        emit_rmsnorm_T(C, h_sb, th, hn, thn, g_sb, tg, ones_r, tones, DC, TB, sq, tsq, ps_ssq, tssq, rstd, trstd, D)

    all_hn = [thn[c][0] for c in range(DC)]
    nA = 0
    nB = 0
    for tb in range(NBLK):
        norm_block(hTv, slice(tb * TB, (tb + 1) * TB))
        for m in range(8):
            b = nA % 2
            nA += 1
            for c in range(DC):
                S.op("pe", lambda e, c=c, m=m, b=b: e.matmul(psA[b][:], W[:, c, m * 128:(m + 1) * 128], hn[:, c, :],
                                                          start=(c == 0), stop=(c == DC - 1)),
                     reads=[tW, thn[c][0]], writes=[tpsA[b]])
            S.op("act", lambda e, m=m, b=b: e.copy(out=qkpre[:, m, 3:3 + TB], in_=psA[b][:]), reads=[tpsA[b]], writes=[tqkpre[m]])
            S.op("dve", lambda e, m=m: e.tensor_scalar(out=acc[:], in0=qkpre[:, m, 0:TB], scalar1=conv_sb[:, m, 0:1], scalar2=None,
                                                       op0=ALU.mult), reads=[tqkpre[m], tconv], writes=[tacc])
            for k in range(1, 4):
                S.op("dve", lambda e, m=m, k=k: e.scalar_tensor_tensor(out=acc[:], in0=qkpre[:, m, k:k + TB], scalar=conv_sb[:, m, k:k + 1],
                                                                        in1=acc[:], op0=ALU.mult, op1=ALU.add),
                     reads=[tqkpre[m], tconv, tacc], writes=[tacc])
            S.op("act", lambda e, m=m: e.activation(out=qkT[:, m, :], in_=acc[:], func=AF.Silu), reads=[tacc], writes=[tqkT[m]])
            S.op("dve", lambda e, m=m: e.tensor_copy(out=qkpre[:, m, 0:3], in_=qkpre[:, m, TB:TB + 3]), reads=[tqkpre[m]], writes=[tqkpre[m]])
        for j in range(TB // 128):
            jsl = slice(j * 128, (j + 1) * 128)
            kk = j % 2
            b = nB % 2
            nB += 1
            for c in range(DC):
                S.op("pe", lambda e, c=c, b=b, jsl=jsl: e.matmul(psB[b][:], hn[:, c, jsl], W[:, c, 1024:1536], start=(c == 0), stop=(c == DC - 1)),
                     reads=[tW, thn[c][0]], writes=[tpsB[b]])
            for h in range(2):
                S.op("act", lambda e, h=h, b=b, kk=kk: e.copy(out=vaug[kk][:, h, 0:256], in_=psB[b][:, h * 256:(h + 1) * 256]),
                     reads=[tpsB[b]], writes=[tvaug[kk]])
            b = nB % 2
            nB += 1
            for c in range(DC):
                S.op("pe", lambda e, c=c, b=b, jsl=jsl: e.matmul(psB[b][:], hn[:, c, jsl], W[:, c, 1536:2048], start=(c == 0), stop=(c == DC - 1)),
                     reads=[tW, thn[c][0]], writes=[tpsB[b]])
            S.op("act", lambda e, b=b, kk=kk: e.activation(out=so[kk][:], in_=psB[b][:], func=AF.Sigmoid), reads=[tpsB[b]], writes=[tso[kk]])
            for c in range(DC):
                S.op("pe", lambda e, c=c, jsl=jsl: e.matmul(psS[:, 0:4], hn[:, c, jsl], W[:, c, 2048:2052], start=(c == 0), stop=(c == DC - 1)),
                     reads=[tW, thn[c][0]], writes=[tpsS])
            S.op("dve", lambda e, kk=kk: e.tensor_tensor(out=gts[kk][:], in0=psS[:, 0:4], in1=gb_sb[:], op=ALU.add),
                 reads=[tpsS, tgb], writes=[tgts[kk]])
            S.op("act", lambda e, kk=kk: e.activation(out=lf[:], in_=gts[kk][:, 2:4], func=AF.Exp, scale=-1.0), reads=[tgts[kk]], writes=[tlf])
            S.op("act", lambda e: e.activation(out=lf[:], in_=lf[:], func=AF.Ln, bias=ones32[:, 0:1], scale=1.0), reads=[tlf, tones32], writes=[tlf])
            S.op("dve", lambda e: e.tensor_scalar(out=lf[:], in0=lf[:], scalar1=-1.0, scalar2=None, op0=ALU.mult), reads=[tlf], writes=[tlf])
            S.op("pe", lambda e: e.matmul(psS[:, 8:10], mask[:], lf[:], start=True, stop=True), reads=[tmask, tlf], writes=[tpsS])
            S.op("pe", lambda e: e.matmul(psS[:, 16:18], ones32[:], lf[:], start=True, stop=True), reads=[tones32, tlf], writes=[tpsS])
            S.op("dve", lambda e: e.tensor_copy(out=bcol[:], in_=psS[:, 8:10]), reads=[tpsS], writes=[tbcol])
            S.op("dve", lambda e, kk=kk: e.tensor_tensor(out=acol[:], in0=gts[kk][:, 0:2], in1=bcol[:], op=ALU.subtract),
                 reads=[tgts[kk], tbcol], writes=[tacol])
            for h in range(2):
                S.op("dve", lambda e, h=h: e.tensor_copy(out=a_bc[:, h, :], in_=acol[:, h:h + 1].to_broadcast([128, 128])),
                     reads=[tacol], writes=[tabc])
            for h in range(2):
                S.op("pe", lambda e, h=h: e.matmul(psS[:, 128 + h * 128:256 + h * 128], a_bc[:, h, :], ident[:], start=True, stop=True),
                     reads=[tabc, tid], writes=[tpsS])
            S.op("dve", lambda e: e.tensor_reduce(out=amax[:], in_=psS[:, 128:384].rearrange("p (h s) -> p h s", h=2), axis=AX.X, op=ALU.max),
                 reads=[tpsS], writes=[tamax])
            S.op("dve", lambda e: e.tensor_tensor(out=Mx[:], in0=amax[:], in1=mprev[:], op=ALU.max), reads=[tamax, tmprev], writes=[tMx])
            S.op("dve", lambda e: e.tensor_tensor(out=tmp2[:], in0=mprev[:], in1=Mx[:], op=ALU.subtract), reads=[tmprev, tMx], writes=[ttmp2])
            S.op("act", lambda e: e.activation(out=wprev[:], in_=tmp2[:], func=AF.Exp), reads=[ttmp2], writes=[twprev])
            S.op("dve", lambda e: e.tensor_tensor(out=tmp2[:], in0=acol[:], in1=Mx[:], op=ALU.subtract), reads=[tacol, tMx], writes=[ttmp2])
            S.op("act", lambda e: e.activation(out=ws[:], in_=tmp2[:], func=AF.Exp), reads=[ttmp2], writes=[tws_])
            S.op("dve", lambda e: e.tensor_tensor(out=tmp2[:], in0=bcol[:], in1=Mx[:], op=ALU.add), reads=[tbcol, tMx], writes=[ttmp2])
            S.op("act", lambda e: e.activation(out=thr[:], in_=tmp2[:], func=AF.Exp, scale=-1.0), reads=[ttmp2], writes=[tthr])
            S.op("dve", lambda e: e.tensor_tensor(out=mprev[:], in0=psS[:, 16:18], in1=Mx[:], op=ALU.add), reads=[tpsS, tMx], writes=[tmprev])
            if debug:
                for i_, (src_, tl_) in enumerate(((lf, tlf), (acol, tacol), (bcol, tbcol), (amax, tamax), (Mx, tMx), (wprev, twprev),
                                                  (ws, tws_), (thr, tthr))):
                    S.op("dve", lambda e, i_=i_, src_=src_: e.tensor_copy(out=dbg_sb[:, 2 * i_:2 * i_ + 2], in_=src_[:]),
                         reads=[tl_], writes=[tdbg])
                cidx = tb * (TB // 128) + j
                S.op("sp", lambda e, cidx=cidx: e.dma_start(out=dbg[cidx], in_=dbg_sb[:]), reads=[tdbg], chan=cdbg)
            for h in range(2):
                qi = [h * 2, h * 2 + 1]
                ki = [4 + h * 2, 4 + h * 2 + 1]
                for dc in range(2):
                    S.op("pe", lambda e, dc=dc, ki=ki, qi=qi, jsl=jsl: e.matmul(psA[0][:, 0:128], qkT[:, ki[dc], jsl], qkT[:, qi[dc], jsl],
                                                                           start=(dc == 0), stop=(dc == 1)),
                         reads=[tqkT[ki[dc]], tqkT[qi[dc]]], writes=[tpsA[0]])
                S.op("dve", lambda e: e.scalar_tensor_tensor(out=sTm[:], in0=psA[0][:, 0:128], scalar=DH ** -0.5, in1=mask[:],
                                                             op0=ALU.mult, op1=ALU.mult), reads=[tpsA[0], tmask], writes=[tsTm])
                S.op("dve", lambda e, h=h, kk=kk: e.tensor_scalar(out=vw[:], in0=vaug[kk][:, h, :], scalar1=ws[:, h:h + 1], scalar2=None, op0=ALU.mult),
                     reads=[tvaug[kk], tws_], writes=[tvw])
                S.op("dve", lambda e, h=h: e.tensor_scalar(out=CT[:, h, :, :], in0=CT[:, h, :, :], scalar1=wprev[:, h:h + 1], scalar2=None, op0=ALU.mult),
                     reads=[tCT, twprev], writes=[tCT])
                S.op("act", lambda e, h=h: e.copy(out=CTb[:], in_=CT[:, h, :, :]), reads=[tCT], writes=[tCTb])
                S.op("pe", lambda e: e.matmul(psN[:, 0:257], sTm[:], vw[:], start=True, stop=False), reads=[tsTm, tvw], writes=[tpsN])
                for dc in range(2):
                    S.op("pe", lambda e, dc=dc, qi=qi, jsl=jsl: e.matmul(psN[:, 0:257], qkT[:, qi[dc], jsl], CTb[:, dc, :], start=False, stop=(dc == 1)),
                         reads=[tqkT[qi[dc]], tCTb], writes=[tpsN])
                S.op("act", lambda e: e.activation(out=den[:], in_=psN[:, 256:257], func=AF.Abs), reads=[tpsN], writes=[tden])
                S.op("dve", lambda e, h=h: e.tensor_tensor(out=den[:], in0=den[:], in1=thr[:, h:h + 1], op=ALU.max),
                     reads=[tden, tthr], writes=[tden])
                S.op("dve", lambda e: e.reciprocal(out=den[:], in_=den[:]), reads=[tden], writes=[tden])
                S.op("dve", lambda e: e.tensor_scalar(out=hh[:], in0=psN[:, 0:256], scalar1=den[:, 0:1], scalar2=None, op0=ALU.mult),
                     reads=[tpsN, tden], writes=[thh])
                S.op("dve", lambda e, h=h: e.memset(ssq1[:, h:h + 1], 0.0), writes=[tssq1])
                S.op("act", lambda e, h=h: e.activation(out=junk[:], in_=hh[:], func=AF.Square, accum_out=ssq1[:, h:h + 1]),
                     reads=[thh], writes=[tjunk, tssq1])
                S.op("act", lambda e, h=h: e.activation(out=ssq1[:, h:h + 1], in_=ssq1[:, h:h + 1], func=AF.Sqrt, bias=EPSB[0][:, 0:1], scale=1.0 / DH),
                     reads=[tssq1, EPSB[1]], writes=[tssq1])
                S.op("dve", lambda e, h=h: e.reciprocal(out=ssq1[:, h:h + 1], in_=ssq1[:, h:h + 1]), reads=[tssq1], writes=[tssq1])
                S.op("dve", lambda e, h=h, kk=kk: e.scalar_tensor_tensor(out=ybt[kk][:, h * 256:(h + 1) * 256], in0=hh[:], scalar=ssq1[:, h:h + 1],
                                                                          in1=mn_sb[:, h * 256:(h + 1) * 256], op0=ALU.mult, op1=ALU.mult),
                     reads=[thh, tssq1, tmn], writes=[tybt[kk]])
                S.op("dve", lambda e, h=h, kk=kk: e.tensor_tensor(out=ybt[kk][:, h * 256:(h + 1) * 256], in0=ybt[kk][:, h * 256:(h + 1) * 256],
                                                                  in1=so[kk][:, h * 256:(h + 1) * 256], op=ALU.mult),
                     reads=[tybt[kk], tso[kk]], writes=[tybt[kk]])
                for dc in range(2):
                    S.op("pe", lambda e, dc=dc, ki=ki, jsl=jsl: e.transpose(psT[:, dc * 128:(dc + 1) * 128], qkT[:, ki[dc], jsl], identb[:]),
                         reads=[tqkT[ki[dc]], tidb], writes=[tpsT])
                S.op("act", lambda e: e.copy(out=ktok[:], in_=psT[:]), reads=[tpsT], writes=[tktok])
                for dc in range(2):
                    S.op("pe", lambda e, dc=dc: e.matmul(psA[1][:, 0:257], ktok[:, dc * 128:(dc + 1) * 128], vw[:], start=True, stop=True),
                         reads=[tktok, tvw], writes=[tpsA[1]])
                    S.op("dve", lambda e, dc=dc, h=h: e.scalar_tensor_tensor(out=CT[:, h, dc, :], in0=psA[1][:, 0:257], scalar=DH ** -0.5,
                                                                              in1=CT[:, h, dc, :], op0=ALU.mult, op1=ALU.add),
                         reads=[tCT, tpsA[1]], writes=[tCT])
            tok0 = tb * TB + j * 128
            S.op("sp", lambda e, kk=kk, tok0=tok0: e.dma_start(out=yb[tok0:tok0 + 128, :], in_=ybt[kk][:]), reads=[tybt[kk]], chan=cyb[kk])

    S.op("pool", lambda e: e.dma_start(out=W[:, :, 0:2048], in_=wz_d), writes=[tW], chan=cW)
    nj = 0
    for tb in range(SH // TB):
        norm_block(hThv, slice(tb * TB, (tb + 1) * TB))
        for j in range(TB // 128):
            jsl = slice(j * 128, (j + 1) * 128)
            kk = nj % 2
            nj += 1
            for cb in range(4):
                b = cb % 2
                for c in range(DC):
                    S.op("pe", lambda e, c=c, b=b, cb=cb, jsl=jsl: e.matmul(psA[b][:], hn[:, c, jsl], W[:, c, cb * 512:(cb + 1) * 512],
                                                                        start=(c == 0), stop=(c == DC - 1)),
                         reads=[tW, thn[c][0]], writes=[tpsA[b]])
                dst = u_sb if cb < 2 else v_sb
                tdst = tu if cb < 2 else tv
                csl = slice((cb % 2) * 512, (cb % 2 + 1) * 512)
                emit_gelu(S, dst[:, csl], psA[b][:], t1[:], t2[:], [tpsA[b]], [tdst], tt1, tt2)
            S.op("dve", lambda e: e.memset(ssq1[:], 0.0), writes=[tssq1])
            for q in range(2):
                S.op("act", lambda e, q=q: e.activation(out=t1[:], in_=v_sb[:, q * 512:(q + 1) * 512], func=AF.Square, accum_out=ssq1[:, q:q + 1]),
                     reads=[tv], writes=[tt1, tssq1])
            S.op("dve", lambda e: e.tensor_tensor(out=ssq1[:, 0:1], in0=ssq1[:, 0:1], in1=ssq1[:, 1:2], op=ALU.add), reads=[tssq1], writes=[tssq1])
            S.op("act", lambda e: e.activation(out=ssq1[:, 0:1], in_=ssq1[:, 0:1], func=AF.Sqrt, bias=EPSB[0][:, 0:1], scale=1.0 / 1024),
                 reads=[tssq1, EPSB[1]], writes=[tssq1])
            S.op("dve", lambda e: e.reciprocal(out=ssq1[:, 0:1], in_=ssq1[:, 0:1]), reads=[tssq1], writes=[tssq1])
            S.op("dve", lambda e: e.scalar_tensor_tensor(out=vn[:], in0=v_sb[:], scalar=ssq1[:, 0:1], in1=gn_sb[:], op0=ALU.mult, op1=ALU.mult),
                 reads=[tv, tssq1, tgn], writes=[tvn])
            for g in range(8):
                b = g // 4
                S.op("pe", lambda e, g=g, b=b: e.matmul(psB[b][:, (g % 4) * 128:(g % 4 + 1) * 128], wsT[:, g, :], vn[:, g * 128:(g + 1) * 128],
                                                    start=True, stop=True), reads=[tws, tvn], writes=[tpsB[b]])
            for g in range(8):
                b = g // 4
                S.op("dve", lambda e, g=g, b=b, kk=kk: e.scalar_tensor_tensor(out=yat[kk][:, g * 128:(g + 1) * 128],
                                                                               in0=psB[b][:, (g % 4) * 128:(g % 4 + 1) * 128],
                                                                               scalar=bs_sb[:, g:g + 1], in1=u_sb[:, g * 128:(g + 1) * 128],
                                                                               op0=ALU.add, op1=ALU.mult),
                     reads=[tpsB[b], tbs, tu], writes=[tyat[kk]])
            tok0 = tb * TB + j * 128
            S.op("sp", lambda e, kk=kk, tok0=tok0: e.dma_start(out=ya[tok0:tok0 + 128, :], in_=yat[kk][:]), reads=[tyat[kk]], chan=cya[kk])
    S.emit(final_waits=cyb + cya)
    C.st.close()
    return nc


NEG = -1.0e30


def nsa_tables(Sq):
    NT = Sq // 128
    NCMP = (Sq - 32) // 16 + 1
    CH = (NCMP + 127) // 128
    NSL = Sq // 64
    half = 16
    inv = 1.0 / (500000.0 ** (np.arange(half, dtype=np.float32) / half))
    ang = np.arange(Sq, dtype=np.float32)[None, :] * inv[:, None]
    cos = np.ones((128, Sq), np.float32)
    sin = np.zeros((128, Sq), np.float32)
    cos[0:16] = np.cos(ang); cos[16:32] = np.cos(ang)
    sin[0:16] = np.sin(ang); sin[16:32] = np.sin(ang)
    sc = np.float32(128 ** -0.5)
    psw = np.zeros((128, 128), np.float32)
    for d in range(16):
        psw[d + 16, d] = -1.0
        psw[d, d + 16] = 1.0
    k = np.arange(128)[:, None]
    q = np.arange(128)[None, :]
    causal = np.where(k > q, NEG, 0.0).astype(np.float32)
    winneg = np.where(k <= q, NEG, 0.0).astype(np.float32)
    cm = np.zeros((NT, 128, CH, 128), np.float32)
    for T in range(NT):
        for ch in range(CH):
            c = ch * 128 + np.arange(128)[:, None]
            t = T * 128 + np.arange(128)[None, :]
            cm[T, :, ch, :] = np.where((16 * c + 31 > t) | (c >= NCMP), NEG, 0.0)
    E = np.zeros((64, NT, 128), np.float32)
    for kc in range(NT):
        for kk in range(128):
            E[(kc * 128 + kk) // 64, kc, kk] = 1.0
    fpos = np.zeros((NT, 128, 64), np.float32)
    fneg = np.full((NT, 128, 64), 1.0e30, np.float32)
    j = np.arange(64)[None, :]
    for T in range(NT):
        cur = ((T * 128 + np.arange(128)) // 64)[:, None]
        fp = np.zeros((128, 64), np.float32)
        fp = np.where(j == cur - 1, 1.0e9, fp)
        fp = np.where(j == cur, 2.0e9, fp)
        fp = np.where(j == 0, 3.0e9, fp)
        fpos[T] = fp
        fneg[T] = np.where((j > cur) | (j >= NSL), -1.0e9, 1.0e30)
    ci = np.arange(CH * 128)[:, None] * 16
    sj = np.arange(64)[None, :] * 64
    ov = np.clip(np.minimum(ci + 32, sj + 64) - np.maximum(ci, sj), 0, None).astype(np.float32) / 32.0
    ov[NCMP:] = 0.0
    ov = np.ascontiguousarray(ov.reshape(CH, 128, 64).transpose(1, 0, 2))
    return dict(cosq=cos * sc, sinq=sin * sc, cosk=cos, sink=sin, psw=psw, ident=np.eye(128, dtype=np.float32),
                causal=causal, winneg=winneg, cmpmask=cm, E=E, fpos=fpos, fneg=fneg, ov=ov)


def build_nsa(Sq, D=2048, stop=0, branches=(0, 1, 2)):
    C = Ctx()
    nc, S = C.nc, C.S
    DC = D // 128
    TB = 256
    TPB = TB // 128
    NBLK = Sq // TB
    NT = Sq // 128
    NCMP = (Sq - 32) // 16 + 1
    CH = (NCMP + 127) // 128
    T_ = Tile

    hT = C.din("hT", [D, Sq])
    g_d = C.din("g", [128, DC])
    wkv_d = C.din("wkv", [128, DC, 1536])
    wq_d = C.din("wq", [128, DC, 1024])
    wgt_d = C.din("wgt", [128, DC, 24])
    w1_d = C.din("w1", [128, 2, 32, 128])
    w2_d = C.din("w2", [128, 2, 128])
    pos_d = C.din("posT", [128, 2, 32])
    tabs = {}
    for nm, shp in (("cosq", [128, Sq]), ("sinq", [128, Sq]), ("cosk", [128, Sq]), ("sink", [128, Sq]), ("psw", [128, 128]),
                    ("ident", [128, 128]), ("causal", [128, 128]), ("winneg", [128, 128]), ("cmpmask", [NT, 128, CH, 128]),
                    ("E", [64, NT, 128]), ("fpos", [NT, 128, 64]), ("fneg", [NT, 128, 64]), ("ov", [128, CH, 64])):
        tabs[nm] = C.din(nm, shp)
    y_d = C.dout("y", [Sq, 1024])
    hTv = hT.rearrange("(c p) t -> p c t", p=128)

    W = C.sb([128, DC, 1024], BF16, "W")
    Wgt = C.sb([128, DC, 24], BF16, "Wgt")
    WT = C.sb([128, DC, 512], BF16, "WT")
    tWT = Tile()
    tWgt = Tile()
    h_sb = C.sb([128, DC, TB], BF16, "h")
    hn = C.sb([128, DC, TB], BF16, "hn")
    sq = [C.sb([128, 256], F32R, "sq") for _ in range(2)]
    rstd = C.sb([128, 256], F32, "rstd")
    g_sb = C.sb([128, DC], F32, "g")
    kTs = C.sb([128, 2, 2, Sq], BF16, "kTs")
    cmpin = C.sb([128, 2, 2, Sq], BF16, "cmpin")
    vsel = C.sb([128, NT, 2, 129], BF16, "vsel")
    vwin = C.sb([128, NT, 2, 129], BF16, "vwin")
    kcmpT = C.sb([128, 2, CH * 128], BF16, "kcmpT")
    vcmp = C.sb([128, 2, CH, 193], BF16, "vcmp")
    rope_c = C.sb([128, TB], F32, "ropec")
    rope_s = C.sb([128, TB], F32, "ropes")
    psw = C.sb([128, 128], BF16, "psw")
    identb = C.sb([128, 128], BF16, "identb")
    ident = C.sb([128, 128], F32, "ident")
    causal = C.sb([128, 4, 128], BF16, "causal")
    winneg = C.sb([128, 4, 128], BF16, "winneg")
    Eb = C.sb([64, NT, 128], BF16, "E")
    xb = C.sb([128, TB], BF16, "xb")
    xc = C.sb([128, TB], F32, "xc")
    xs = C.sb([128, TB], F32, "xs")
    kmax2 = C.sb([128, 6], F32, "kmax2")
    kred = C.sb([128, 1], F32, "kred")
    ones32 = C.sb([128, 128], F32, "ones32")
    sqf = C.sb([128, 256], F32, "sqf")
    tsqf = Tile()

    psB = [C.ps(name="B") for _ in range(3)]
    psA = psB
    psO = [C.ps(name="O") for _ in range(4)]
    psS = C.ps(name="S")
    ps_ssq = psS

    tW, tg = T_(), T_()
    th = [[T_()] for _ in range(DC)]
    thn = [[T_()] for _ in range(DC)]
    tsq = [T_(), T_()]
    trstd = T_()
    tkTs = [[[T_() for _ in range(NBLK)] for _ in range(2)] for _ in range(2)]
    tcmpin = [[T_() for _ in range(2)] for _ in range(2)]
    tvsel = [T_() for _ in range(NT)]
    tvwin = [T_() for _ in range(NT)]
    tkcmpT, tvcmp = T_(), T_()
    trope, tpsw, tidb, tid, tcausal, twinneg, tE = [T_() for _ in range(7)]
    txb, txc, txs, tkmax2, tkred, tones32 = [T_() for _ in range(6)]
    tpsB = [T_(psum=True), T_(psum=True), T_(psum=True)]
    tpsA = tpsB
    tpsO = [T_(psum=True) for _ in range(4)]
    tpsS = T_(psum=True)
    tssq = tpsS

    cmisc = S.chan()
    cW = S.chan()
    ch_in = S.chan()
    crope = S.chan()

    S.op("sp", lambda e: e.dma_start(out=g_sb[:], in_=g_d), writes=[tg], chan=cmisc)
    S.op("sp", lambda e: e.dma_start(out=ident[:], in_=tabs["ident"]), writes=[tid], chan=cmisc)
    for dst, nm, tl in ((psw, "psw", tpsw), (identb, "ident", tidb), (Eb, "E", tE)):
        S.op("pool", lambda e, dst=dst, nm=nm: e.dma_start(out=dst[:], in_=tabs[nm]), writes=[tl], chan=cmisc)
    for dst, nm, tl in ((causal, "causal", tcausal), (winneg, "winneg", twinneg)):
        for r4 in range(4):
            S.op("pool", lambda e, dst=dst, nm=nm, r4=r4: e.dma_start(out=dst[:, r4, :], in_=tabs[nm]), writes=[tl], chan=cmisc)
    ones_r, tones = emit_consts(C)
    S.op("dve", lambda e: e.memset(ones32[:], 1.0), writes=[tones32])
    S.op("dve", lambda e: e.memset(kmax2[:], 0.0), writes=[tkmax2])
    S.op("dve", lambda e: e.memset(vsel[:], 1.0), writes=tvsel)
    S.op("dve", lambda e: e.memset(vwin[:], 1.0), writes=tvwin)
    S.op("dve", lambda e: e.memset(vcmp[:], 1.0), writes=[tvcmp])
    S.op("dve", lambda e: e.memset(kcmpT[:], 0.0), writes=[tkcmpT])
    S.op("pool", lambda e: e.dma_start(out=Wgt[:], in_=wgt_d), writes=[tWgt], chan=cmisc)

    class _Stop(Exception):
        pass

    def ckpt(v):
        if stop == v:
            S.op("sp", lambda e: e.dma_start(out=y_d[0:128, 0:128], in_=ident[:]), reads=[tid], chan=cmisc)
            S.emit(final_waits=[cmisc])
            S.op = lambda *a, **k: None
            S.emit = lambda *a, **k: None

    def norm_block(tsl):
        CG = 4
        for c0 in range(0, DC, CG):
            S.op("pool", lambda e, c0=c0: e.dma_start(out=h_sb[:, c0:c0 + CG, :], in_=hTv[:, c0:c0 + CG, tsl]),
                 writes=[th[c][0] for c in range(c0, c0 + CG)], chan=ch_in)
        emit_rmsnorm_T(C, h_sb, th, hn, thn, g_sb, tg, ones_r, tones, DC, TB, sq, tsq, ps_ssq, tssq, rstd, trstd, D)

    def load_rope(cn, sn, tsl):
        S.op("sp", lambda e: e.dma_start(out=rope_c[:], in_=tabs[cn][:, tsl]), writes=[trope], chan=crope)
        S.op("sp", lambda e: e.dma_start(out=rope_s[:], in_=tabs[sn][:, tsl]), writes=[trope], chan=crope)

    nA = [0]

    def proj_fm(col0):
        b = nA[0] % 2
        nA[0] += 1
        for c in range(DC):
            S.op("pe", lambda e, c=c, b=b: e.matmul(psA[b][:, 0:TB], W[:, c, col0:col0 + 128], hn[:, c, :], start=(c == 0), stop=(c == DC - 1)),
                 reads=[tW, thn[c][0]], writes=[tpsA[b]])
        return b

    def rope_to(b, dst_ap, dst_tiles, split=False):
        S.op("act", lambda e: e.copy(out=xb[:], in_=psA[b][:, 0:TB]), reads=[tpsA[b]], writes=[txb])
        ckpt(151)
        S.op("dve", lambda e: e.tensor_tensor(out=xc[:], in0=psA[b][:, 0:TB], in1=rope_c[:], op=ALU.mult), reads=[tpsA[b], trope, txb], writes=[txc])
        ckpt(152)
        bb = nA[0] % 2
        nA[0] += 1
        S.op("pe", lambda e: e.matmul(psA[bb][:, 0:TB], psw[:], xb[:], start=True, stop=True), reads=[tpsw, txb], writes=[tpsA[bb]])
        ckpt(153)
        S.op("dve", lambda e: e.tensor_tensor(out=xs[:], in0=psA[bb][:, 0:TB], in1=rope_s[:], op=ALU.mult), reads=[tpsA[bb], trope], writes=[txs])
        ckpt(154)
        if split:
            S.op("dve", lambda e: e.tensor_tensor(out=dst_ap, in0=xs[:].rearrange("p (j q) -> p j q", q=128),
                                                  in1=xc[:].rearrange("p (j q) -> p j q", q=128), op=ALU.add), reads=[txs, txc], writes=dst_tiles)
        else:
            S.op("dve", lambda e: e.tensor_tensor(out=dst_ap, in0=xs[:], in1=xc[:], op=ALU.add), reads=[txs, txc], writes=dst_tiles)

    def colnorm_max(src_ap, src_tiles, kcol, ncols=TB):
        S.op("act", lambda e: e.activation(out=sqf[:, 0:ncols], in_=src_ap, func=AF.Square), reads=src_tiles, writes=[tsqf])
        S.op("pe", lambda e: e.matmul(psS[:, 0:ncols], ones32[:], sqf[:, 0:ncols], start=True, stop=True), reads=[tsqf, tones32], writes=[tpsS])
        S.op("dve", lambda e: e.tensor_reduce(out=kred[:], in_=psS[:, 0:ncols], axis=AX.X, op=ALU.max), reads=[tpsS], writes=[tkred])
        S.op("dve", lambda e: e.tensor_tensor(out=kmax2[:, kcol:kcol + 1], in0=kmax2[:, kcol:kcol + 1], in1=kred[:], op=ALU.max),
             reads=[tkmax2, tkred], writes=[tkmax2])

    nB = 0
    try:
        ckpt(11)
    except _Stop:
        return nc
    for tb in range(NBLK):
        tsl = slice(tb * TB, (tb + 1) * TB)
        try:
            norm_block(tsl)
            ckpt(12)
            load_rope("cosk", "sink", tsl)
            if tb == 0:
                S.op("pool", lambda e: e.dma_start(out=W[:], in_=wkv_d[:, :, 0:1024]), writes=[tW], chan=cW)
                S.op("pool", lambda e: e.dma_start(out=WT[:], in_=wkv_d[:, :, 1024:1536]), writes=[tWT], chan=cW)
            ckpt(13)
        except _Stop:
            return nc
        for slot in range(6):
            br, gl = slot // 2, slot % 2
            b = proj_fm(slot * 128)
            try:
                ckpt(14)
            except _Stop:
                return nc
            if br == 0:
                rope_to(b, cmpin[:, 0, gl, tsl], [tcmpin[0][gl]])
                try:
                    ckpt(15)
                except _Stop:
                    return nc
            else:
                rope_to(b, kTs[:, br - 1, gl, tsl], [tkTs[br - 1][gl][tb]])
                colnorm_max(kTs[:, br - 1, gl, tsl], [tkTs[br - 1][gl][tb]], 2 + (br - 1) * 2 + gl)
                try:
                    ckpt(16)
                except _Stop:
                    return nc
        for gl in range(2):
            b = proj_fm((6 + gl) * 128)
            S.op("act", lambda e, b=b, gl=gl, tsl=tsl: e.copy(out=cmpin[:, 1, gl, tsl], in_=psA[b][:, 0:TB]), reads=[tpsA[b]], writes=[tcmpin[1][gl]])
        for j in range(TPB):
            Tq = tb * TPB + j
            jsl = slice(j * 128, (j + 1) * 128)
            b = nB % 2
            nB += 1
            for c in range(DC):
                S.op("pe", lambda e, c=c, b=b, jsl=jsl: e.matmul(psB[b][:], hn[:, c, jsl], WT[:, c, :], start=(c == 0), stop=(c == DC - 1)),
                     reads=[tWT, thn[c][0]], writes=[tpsB[b]])
            S.op("act", lambda e, b=b, Tq=Tq: e.copy(out=vsel[:, Tq, :, 0:128], in_=psB[b][:, 0:256].rearrange("p (g d) -> p g d", g=2)),
                 reads=[tpsB[b]], writes=[tvsel[Tq]])
            S.op("dve", lambda e, b=b, Tq=Tq: e.tensor_copy(out=vwin[:, Tq, :, 0:128], in_=psB[b][:, 256:512].rearrange("p (g d) -> p g d", g=2)),
                 reads=[tpsB[b]], writes=[tvwin[Tq]])

    if stop == 1:
        S.op("pool", lambda e: e.dma_start(out=y_d[0:128, 0:256], in_=h_sb[:, 0, :]), reads=[th[0][0]], chan=cmisc)
        S.emit(final_waits=[cmisc])
        C.st.close()
        return nc
    Wflat = W[:].rearrange("p a b -> p (a b)")
    w1 = Wflat[:, 0:8192].rearrange("p (k l o) -> p k l o", k=2, l=32)
    w2 = Wflat[:, 8192:8448].rearrange("p (k o) -> p k o", k=2)
    posb = Wflat[:, 8448:8512].rearrange("p (k l) -> p k l", k=2)
    S.op("pool", lambda e: e.dma_start(out=w1, in_=w1_d), writes=[tW], chan=cW)
    S.op("pool", lambda e: e.dma_start(out=w2, in_=w2_d), writes=[tW], chan=cW)
    S.op("pool", lambda e: e.dma_start(out=posb, in_=pos_d), writes=[tW], chan=cW)
    S.op("pool", lambda e: e.dma_start(out=vcmp[:, 0, :, 129:193], in_=tabs["ov"]), writes=[tvcmp], chan=cmisc)
    S.op("pool", lambda e: e.dma_start(out=vcmp[:, 1, :, 129:193], in_=tabs["ov"]), writes=[tvcmp], chan=cmisc)
    bias1 = C.sb([128, 2], F32, "bias1")
    xg, t1, t2 = rstd, xc, xs
    H1g = C.sb([128, CH * 128], BF16, "H1g")
    tbias1, tH1g = T_(), T_()
    txg, tt1, tt2 = trstd, txc, txs
    NCP = NCMP
    for kv in range(2):
        for l in range(32):
            S.op("pe", lambda e, kv=kv, l=l: e.matmul(psS[:, kv:kv + 1], w1[:, kv, l, :], posb[:, kv, l:l + 1], start=(l == 0), stop=(l == 31)),
                 reads=[tW], writes=[tpsS])
        S.op("dve", lambda e, kv=kv: e.tensor_copy(out=bias1[:, kv:kv + 1], in_=psS[:, kv:kv + 1]), reads=[tpsS], writes=[tbias1])
    S.op("dve", lambda e: e.memset(H1g[:], 0.0), writes=[tH1g])
    for kv in range(2):
        for gl in range(2):
            b = nA[0] % 2
            nA[0] += 1
            for l in range(32):
                S.op("pe", lambda e, kv=kv, gl=gl, l=l, b=b: e.matmul(psA[b][:, 0:NCP], w1[:, kv, l, :],
                                                                     cmpin[:, kv, gl, l:l + 16 * (NCP - 1) + 1:16],
                                                                     start=(l == 0), stop=(l == 31)),
                     reads=[tW, tcmpin[kv][gl]], writes=[tpsA[b]])
            for c0 in range(0, NCP, 256):
                n = min(256, NCP - c0)
                S.op("dve", lambda e, b=b, kv=kv, c0=c0, n=n: e.tensor_scalar(out=xg[:, 0:n], in0=psA[b][:, c0:c0 + n], scalar1=bias1[:, kv:kv + 1],
                                                                            scalar2=None, op0=ALU.add), reads=[tpsA[b], tbias1], writes=[txg])
                emit_gelu(S, H1g[:, c0:c0 + n], xg[:, 0:n], t1[:, 0:n], t2[:, 0:n], [txg], [tH1g], tt1, tt2)
            if kv == 0:
                bb = nA[0] % 2
                nA[0] += 1
                S.op("pe", lambda e, bb=bb: e.matmul(psA[bb][:, 0:NCP], w2[:, 0, :], H1g[:, 0:NCP], start=True, stop=True),
                     reads=[tW, tH1g], writes=[tpsA[bb]])
                S.op("act", lambda e, bb=bb, gl=gl: e.copy(out=kcmpT[:, gl, 0:NCP], in_=psA[bb][:, 0:NCP]), reads=[tpsA[bb]], writes=[tkcmpT])
                colnorm_max(kcmpT[:, gl, 0:NCP], [tkcmpT], gl, ncols=NCP)
            else:
                for ch in range(CH):
                    bb = nA[0] % 2
                    nA[0] += 1
                    S.op("pe", lambda e, bb=bb, ch=ch: e.matmul(psA[bb][:, 0:128], H1g[:, ch * 128:(ch + 1) * 128], w2[:, 1, :], start=True, stop=True),
                         reads=[tW, tH1g], writes=[tpsA[bb]])
                    S.op("act", lambda e, bb=bb, gl=gl, ch=ch: e.copy(out=vcmp[:, gl, ch, 0:128], in_=psA[bb][:, 0:128]),
                         reads=[tpsA[bb]], writes=[tvcmp])

    if stop == 2:
        S.op("pool", lambda e: e.dma_start(out=y_d[0:128, 0:256], in_=h_sb[:, 0, :]), reads=[th[0][0]], chan=cmisc)
        S.emit(final_waits=[cmisc])
        C.st.close()
        return nc
    S.op("pool", lambda e: e.dma_start(out=W[:], in_=wq_d), writes=[tW], chan=cW)
    qT = C.sb([128, TPB, 8, 128], BF16, "qT")
    tqT = [T_() for _ in range(8)]
    gsb = [C.sb([128, 24], F32, "gsb") for _ in range(2)]
    tgsb = [T_(), T_()]
    cmk = [C.sb([128, CH, 4, 128], BF16, "cmk") for _ in range(2)]
    tcmk = [T_(), T_()]
    ccmk = [S.chan(), S.chan()]
    fpt = [C.sb([128, 64], F32, "fpos") for _ in range(2)]
    fnt = [C.sb([128, 64], F32, "fneg") for _ in range(2)]
    tfp = [T_(), T_()]
    cfp = [S.chan(), S.chan()]
    P = [C.sb([128, 512], BF16, "P") for _ in range(3)]
    tP = [T_() for _ in range(3)]
    qsq = C.sb([128, 512], F32R, "qsq")
    tqsq = T_()
    mq2 = C.sb([128, 1], F32, "mq2")
    nbias = C.sb([128, 3], F32, "nbias")
    tmq2, tnbias = T_(), T_()
    rz = C.sb([128, 4], F32, "rz")
    coef = C.sb([128, 4], F32, "coef")
    trz, tcoef = T_(), T_()
    imp = C.sb([128, 64], F32, "imp")
    imp2 = C.sb([128, 64], F32, "imp2")
    mx8 = C.sb([128, 8], F32, "mx8")
    mx8b = C.sb([128, 8], F32, "mx8b")
    negblk = C.sb([128, 64], F32, "negblk")
    negblkT = C.sb([64, 4, 128], BF16, "negblkT")
    timp, timp2, tmx8, tmx8b, tnegblk, tnegblkT = [T_() for _ in range(6)]
    yt0 = C.sb([128, 1024], F32, "yt")
    yt = [yt0, yt0]
    tyth = [T_() for _ in range(8)]
    cy = [S.chan(), S.chan()]
    nP = [0]
    nSc = [0]

    def score(mm_list, br):
        b = nSc[0] % 3
        nSc[0] += 1
        n = len(mm_list)
        for i, (lhsT, rhs, rd) in enumerate(mm_list):
            S.op("pe", lambda e, lhsT=lhsT, rhs=rhs, i=i, b=b: e.matmul(psB[b][:], lhsT, rhs, start=(i == 0), stop=(i == n - 1)),
                 reads=rd, writes=[tpsB[b]])
        p = nP[0] % 3
        nP[0] += 1
        S.op("act", lambda e, b=b, p=p, br=br: e.activation(out=P[p][:], in_=psB[b][:], func=AF.Exp, bias=nbias[:, br:br + 1], scale=1.0),
             reads=[tpsB[b], tnbias], writes=[tP[p]])
        return p

    def pv(p, rhs_v, first, last, width):
        for r in range(4):
            S.op("pe", lambda e, r=r, p=p: e.matmul(psO[r][:, 0:width], P[p][:, r * 128:(r + 1) * 128], rhs_v[0],
                                                  start=first, stop=last),
                 reads=[tP[p]] + rhs_v[1], writes=[tpsO[r]])

    def evac(br, gl, kk, width, first_branch, with_imp=False):
        first_branch = (br == branches[0])
        for r in range(4):
            S.op("dve", lambda e, r=r: e.tensor_scalar(out=rz[:, r:r + 1], in0=psO[r][:, 128:129], scalar1=1.0e-30, scalar2=None, op0=ALU.add),
                 reads=[tpsO[r]], writes=[trz])
        S.op("dve", lambda e: e.reciprocal(out=rz[:], in_=rz[:]), reads=[trz], writes=[trz])
        gc0 = br * 8 + gl * 4
        S.op("dve", lambda e, gc0=gc0, kk=kk: e.tensor_tensor(out=coef[:], in0=rz[:], in1=gsb[kk][:, gc0:gc0 + 4], op=ALU.mult),
             reads=[trz, tgsb[kk]], writes=[tcoef])
        for r in range(4):
            o_ap = psO[r][:, 0:128]
            ysl = slice((gl * 4 + r) * 128, (gl * 4 + r + 1) * 128)
            if br not in branches:
                pass
            elif first_branch:
                S.op("dve", lambda e, r=r, kk=kk, ysl=ysl, o_ap=o_ap: e.tensor_scalar(out=yt[kk][:, ysl], in0=o_ap, scalar1=coef[:, r:r + 1], scalar2=None, op0=ALU.mult),
                     reads=[tpsO[r], tcoef], writes=[tyth[gl * 4 + r]])
            else:
                S.op("dve", lambda e, r=r, kk=kk, ysl=ysl, o_ap=o_ap: e.scalar_tensor_tensor(out=yt[kk][:, ysl], in0=o_ap, scalar=coef[:, r:r + 1], in1=yt[kk][:, ysl],
                                                                                         op0=ALU.mult, op1=ALU.add),
                     reads=[tpsO[r], tcoef, tyth[gl * 4 + r]], writes=[tyth[gl * 4 + r]])
        if with_imp:
            for r in range(4):
                i_ap = psO[r][:, 129:193]
                if r == 0:
                    S.op("dve", lambda e, r=r, i_ap=i_ap: e.tensor_scalar(out=imp[:], in0=i_ap, scalar1=rz[:, r:r + 1], scalar2=None, op0=ALU.mult),
                         reads=[tpsO[r], trz], writes=[timp])
                else:
                    S.op("dve", lambda e, r=r, i_ap=i_ap: e.scalar_tensor_tensor(out=imp[:], in0=i_ap, scalar=rz[:, r:r + 1], in1=imp[:], op0=ALU.mult, op1=ALU.add),
                         reads=[tpsO[r], trz, timp], writes=[timp])

    ntile = 0
    for tb in range(NBLK):
        tsl = slice(tb * TB, (tb + 1) * TB)
        norm_block(tsl)
        load_rope("cosq", "sinq", tsl)
        for hd in range(8):
            b = proj_fm(hd * 128)
            rope_to(b, qT[:, :, hd, :], [tqT[hd]], split=True)
        for j in range(TPB):
            Tq = tb * TPB + j
            jsl = slice(j * 128, (j + 1) * 128)
            kk = ntile % 2
            ntile += 1
            for c in range(DC):
                S.op("pe", lambda e, c=c, jsl=jsl: e.matmul(psS[:, 0:24], hn[:, c, jsl], Wgt[:, c, :], start=(c == 0), stop=(c == DC - 1)),
                     reads=[tWgt, thn[c][0]], writes=[tpsS])
            S.op("act", lambda e, kk=kk: e.activation(out=gsb[kk][:], in_=psS[:, 0:24], func=AF.Sigmoid), reads=[tpsS], writes=[tgsb[kk]])
            for r4 in range(4):
                S.op("pool", lambda e, kk=kk, Tq=Tq, r4=r4: e.dma_start(out=cmk[kk][:, :, r4, :], in_=tabs["cmpmask"][Tq]), writes=[tcmk[kk]], chan=ccmk[kk])
            S.op("sp", lambda e, kk=kk, Tq=Tq: e.dma_start(out=fpt[kk][:], in_=tabs["fpos"][Tq]), writes=[tfp[kk]], chan=cfp[kk])
            S.op("sp", lambda e, kk=kk, Tq=Tq: e.dma_start(out=fnt[kk][:], in_=tabs["fneg"][Tq]), writes=[tfp[kk]], chan=cfp[kk])
            for gl in range(2):
                qv = qT[:, j, gl * 4:(gl + 1) * 4, :]
                qrd = [tqT[gl * 4 + r] for r in range(4)]
                S.op("act", lambda e, qv=qv: e.activation(out=qsq[:].rearrange("p (r q) -> p r q", r=4), in_=qv, func=AF.Square), reads=qrd, writes=[tqsq])
                S.op("pe", lambda e: e.matmul(psS[:, 0:512], ones_r[:], qsq[:], start=True, stop=True), reads=[tqsq, tones], writes=[tpsS])
                S.op("dve", lambda e: e.tensor_reduce(out=mq2[:], in_=psS[:, 0:512], axis=AX.X, op=ALU.max), reads=[tpsS], writes=[tmq2])
                S.op("dve", lambda e, gl=gl: e.tensor_scalar(out=nbias[:], in0=kmax2[:, gl:gl + 5:2], scalar1=mq2[:, 0:1], scalar2=None, op0=ALU.mult),
                     reads=[tkmax2, tmq2], writes=[tnbias])
                S.op("act", lambda e: e.activation(out=nbias[:], in_=nbias[:], func=AF.Sqrt), reads=[tnbias], writes=[tnbias])
                S.op("dve", lambda e: e.tensor_scalar(out=nbias[:], in0=nbias[:], scalar1=-1.0, scalar2=None, op0=ALU.mult), reads=[tnbias], writes=[tnbias])
                chunks = []
                chs = [ch for ch in range(CH) if 16 * (ch * 128) + 31 <= Tq * 128 + 127]
                for ci, ch in enumerate(chs):
                    mm = [(kcmpT[:, gl, ch * 128:(ch + 1) * 128], qv, [tkcmpT] + qrd),
                          (identb[:], cmk[kk][:, ch, :, :], [tidb, tcmk[kk]])]
                    chunks.append((0, mm, (vcmp[:, gl, ch, :], [tvcmp]), ci == 0, ci == len(chs) - 1, 193))
                k0 = max(0, Tq - 4)
                for kc in range(k0, Tq + 1):
                    mm = [(kTs[:, 1, gl, kc * 128:(kc + 1) * 128], qv, [tkTs[1][gl][kc // TPB]] + qrd)]
                    if kc == Tq:
                        mm.append((identb[:], causal[:], [tidb, tcausal]))
                    if kc == Tq - 4:
                        mm.append((identb[:], winneg[:], [tidb, twinneg]))
                    chunks.append((2, mm, (vwin[:, kc, gl, :], [tvwin[kc]]), kc == k0, kc == Tq, 129))
                for kc in range(Tq + 1):
                    mm = [(kTs[:, 0, gl, kc * 128:(kc + 1) * 128], qv, [tkTs[0][gl][kc // TPB]] + qrd),
                          (Eb[:, kc, :], negblkT[:], [tE, tnegblkT])]
                    if kc == Tq:
                        mm.append((identb[:], causal[:], [tidb, tcausal]))
                    chunks.append((1, mm, (vsel[:, kc, gl, :], [tvsel[kc]]), kc == 0, kc == Tq, 129))

                def topk_dve(kk=kk):
                    S.op("dve", lambda e: e.tensor_tensor(out=imp[:], in0=imp[:], in1=fpt[kk][:], op=ALU.max), reads=[timp, tfp[kk]], writes=[timp])
                    S.op("dve", lambda e: e.tensor_tensor(out=imp[:], in0=imp[:], in1=fnt[kk][:], op=ALU.min), reads=[timp, tfp[kk]], writes=[timp])
                    S.op("dve", lambda e: e.max(out=mx8[:], in_=imp[:]), reads=[timp], writes=[tmx8])
                    S.op("dve", lambda e: e.match_replace(out=imp2[:], in_to_replace=mx8[:], in_values=imp[:], imm_value=-3.0e38),
                         reads=[timp, tmx8], writes=[timp2])
                    S.op("dve", lambda e: e.max(out=mx8b[:], in_=imp2[:]), reads=[timp2], writes=[tmx8b])
                    S.op("dve", lambda e: e.tensor_reduce(out=kred[:], in_=mx8b[:], axis=AX.X, op=ALU.min), reads=[tmx8b], writes=[tkred])
                    S.op("dve", lambda e: e.tensor_scalar(out=negblk[:], in0=imp[:], scalar1=kred[:, 0:1], scalar2=None, op0=ALU.is_ge),
                         reads=[timp, tkred], writes=[tnegblk])
                    S.op("dve", lambda e: e.tensor_scalar(out=negblk[:], in0=negblk[:], scalar1=1.0e30, scalar2=-1.0e30, op0=ALU.mult, op1=ALU.add),
                         reads=[tnegblk], writes=[tnegblk])

                def topk_T():
                    S.op("pe", lambda e: e.transpose(psS[0:64, 0:128], negblk[:], ident[:]), reads=[tnegblk, tid], writes=[tpsS])
                    S.op("act", lambda e: e.copy(out=negblkT[:], in_=psS[0:64, 0:128][:, None, :].to_broadcast([64, 4, 128])), reads=[tpsS], writes=[tnegblkT])

                nch = len(chunks)
                first_sel = next(i for i, c in enumerate(chunks) if c[0] == 1)

                def do_score(i):
                    if i == first_sel:
                        topk_T()
                    return score(chunks[i][1], chunks[i][0])

                cmp_last = max(i for i, c in enumerate(chunks) if c[0] == 0)
                pend = []
                nxt = [0]

                def fill(i):
                    while nxt[0] < nch and nxt[0] <= i + 2:
                        if nxt[0] == first_sel and i <= cmp_last:
                            break
                        pend.append(do_score(nxt[0]))
                        nxt[0] += 1

                fill(-1)
                for i in range(nch):
                    br_i, _, rhs_v, first, last, width = chunks[i]
                    fill(i)
                    pv(pend.pop(0), rhs_v, first, last, width)
                    if last:
                        evac(br_i, gl, kk, width, br_i == 0, with_imp=(br_i == 0))
                        if br_i == 0:
                            topk_dve()
            S.op("sp", lambda e, kk=kk, Tq=Tq: e.dma_start(out=y_d[Tq * 128:(Tq + 1) * 128, :], in_=yt[kk][:]), reads=tyth, chan=cy[kk])
    S.emit(final_waits=cy)
    C.st.close()
    return nc


def _lay(w):
    dc = w.shape[0] // 128
    return np.ascontiguousarray(w.reshape(dc, 128, -1).transpose(1, 0, 2))


def _gT(g):
    return np.ascontiguousarray(np.asarray(g, np.float32).reshape(-1, 128).T)


def _prep_ffn_w(Wg, Wu, Wd):
    D, FF = Wg.shape
    DC, FC = D // 128, FF // 128
    t = lambda W: np.ascontiguousarray(W.reshape(DC, 128, FC, 128).transpose(2, 1, 0, 3).reshape(FC, 128, DC * 128))
    wd = np.ascontiguousarray(Wd.reshape(FC, 128, DC, 128).transpose(2, 1, 0, 3).reshape(DC, 128, FC * 128))
    return t(Wg), t(Wu), wd


def _prep_wo(Wo):
    DC = Wo.shape[0] // 128
    DO = Wo.shape[1] // 128
    return np.ascontiguousarray(Wo.reshape(DC, 128, DO, 128).transpose(2, 1, 0, 3).reshape(DO, 128, DC * 128))


def _prep_ab(w_in, gate_b, g_norm, g_ws, g_bs, conv_w, m_norm, hp):
    o1, o2, o3, o4 = 2048, 4096, 5120, 6144
    wz = _lay(w_in[:, :o1])
    hs = slice(hp * 512, (hp + 1) * 512)
    wqk = _lay(np.concatenate([w_in[:, o1:o1 + 1024][:, hs], w_in[:, o1 + 1024:o2][:, hs]], 1))
    gi = w_in[:, o4:o4 + 4][:, hp * 2:hp * 2 + 2]
    gf = w_in[:, o4 + 4:o4 + 8][:, hp * 2:hp * 2 + 2]
    wvog = _lay(np.concatenate([w_in[:, o2:o3][:, hs], w_in[:, o3:o4][:, hs], gi, gf], 1))
    gb = np.concatenate([gate_b[0, hp * 2:hp * 2 + 2], gate_b[1, hp * 2:hp * 2 + 2]])
    gb = np.ascontiguousarray(np.broadcast_to(gb[None, :], (128, 4))).astype(np.float32)
    cc = np.concatenate([conv_w[:, :1024][:, hs], conv_w[:, 1024:][:, hs]], 1)
    conv = np.ascontiguousarray(cc.T.reshape(8, 128, 4).transpose(1, 0, 2))
    mn = np.ascontiguousarray(np.broadcast_to(m_norm[hs][None, :], (128, 512)))
    gn = np.ascontiguousarray(np.broadcast_to(g_norm[None, :], (128, 1024)))
    wsT = np.ascontiguousarray(g_ws.transpose(2, 0, 1))
    bs = np.ascontiguousarray(g_bs.T)
    ident = np.eye(128, dtype=np.float32)
    mask = np.triu(np.ones((128, 128), np.float32))
    return dict(wz=wz, wqk=wqk, wvog=wvog, gb=gb, conv=conv, mn=mn, gn=gn, wsT=wsT, bs=bs, ident=ident, mask=mask)


def _prep_nsa(w_in, cmp_pos, cmp_w1, cmp_w2, hp):
    wq = _lay(w_in[:, hp * 1024:(hp + 1) * 1024])

    def kvcol(br, kv, g):
        o = 2048 + ((br * 2 + kv) * 4 + g) * 128
        return w_in[:, o:o + 128]
    cols = []
    for br in range(3):
        for gl in range(2):
            cols.append(kvcol(br, 0, 2 * hp + gl))
    for gl in range(2):
        cols.append(kvcol(0, 1, 2 * hp + gl))
    for br in (1, 2):
        for gl in range(2):
            cols.append(kvcol(br, 1, 2 * hp + gl))
    wkv = _lay(np.concatenate(cols, 1))
    og = 2048 + 3072
    gc = []
    for br in range(3):
        for gl in range(2):
            g = 2 * hp + gl
            gc.append(w_in[:, og + br * 16 + g * 4: og + br * 16 + g * 4 + 4])
    wgt = _lay(np.concatenate(gc, 1))
    w1 = np.ascontiguousarray(cmp_w1.reshape(2, 32, 128, 128).transpose(2, 0, 1, 3))
    w2 = np.ascontiguousarray(cmp_w2.transpose(1, 0, 2))
    posT = np.ascontiguousarray(cmp_pos.transpose(2, 0, 1))
    return dict(wq=wq, wkv=wkv, wgt=wgt, w1=w1, w2=w2, posT=posT)


_PROGS = {}


def _prog(key, fn):
    if key not in _PROGS:
        _PROGS[key] = fn()
    return _PROGS[key]


def _run(nc, maps):
    res = run_bass_kernel_spmd(nc, maps, core_ids=list(range(8)))
    return res.results


def kernel(x, ffn_norm, ffn_w_gate, ffn_w_up, ffn_w_down, mix_norm, ab_w_in, mlstm_gate_bias,
           gmlp_norm, gmlp_w_s, gmlp_b_s, mlstm_conv, mlstm_norm, ab_w_out, nsa_w_in, nsa_cmp_pos,
           nsa_cmp_w1, nsa_cmp_w2, nsa_w_out, final_norm):
    f = lambda a: np.asarray(a, dtype=np.float32)
    x = f(x)
    B, Sq, D = x.shape
    FF = ffn_w_gate.shape[-1]
    NTC = B * Sq // 8
    hT = [np.ascontiguousarray(x[c // 2, (c % 2) * NTC:(c % 2 + 1) * NTC, :].T) for c in range(8)]

    def ffn(hT, layer, which, yT=None, wo=None, final=False):
        wg, wu, wd = _prep_ffn_w(f(ffn_w_gate[layer, which]), f(ffn_w_up[layer, which]), f(ffn_w_down[layer, which]))
        g = _gT(ffn_norm[layer, which])
        pre = yT is not None
        nc = _prog(("ffn", pre, final), lambda: build_ffn(D, FF, NTC, 1024, 2, pre=pre, final=final))
        maps = []
        for c in range(8):
            m = {"hT": hT[c], "g": g, "wg": wg, "wu": wu, "wd": wd}
            if pre:
                m["yT"] = yT[c]
                m["wo"] = wo
            if final:
                m["gf"] = _gT(final_norm)
            maps.append(m)
        r = _run(nc, maps)
        return [r[c]["oT"] for c in range(8)]

    def full_seq(hT, b):
        return np.ascontiguousarray(np.concatenate([hT[2 * b], hT[2 * b + 1]], axis=1))

    hT = ffn(hT, 0, 0)
    nc = _prog(("ab",), lambda: build_ab(Sq, D))
    maps = []
    for c in range(8):
        b, hp = c // 2, c % 2
        m = _prep_ab(f(ab_w_in[0]), f(mlstm_gate_bias[0]), f(gmlp_norm[0]), f(gmlp_w_s[0]), f(gmlp_b_s[0]), f(mlstm_conv[0]),
                     f(mlstm_norm[0]), hp)
        m["hT"] = full_seq(hT, b)
        m["hTh"] = hT[c]
        m["g"] = _gT(mix_norm[0])
        maps.append(m)
    r = _run(nc, maps)
    yT = []
    for c in range(8):
        b, hp = c // 2, c % 2
        sl = slice(hp * NTC, (hp + 1) * NTC)
        ya = r[c]["ya"]
        yb = np.concatenate([r[2 * b]["yb"][sl], r[2 * b + 1]["yb"][sl]], axis=1)
        yT.append(np.ascontiguousarray(np.concatenate([ya, yb], axis=1).T))
    hT = ffn(hT, 0, 1, yT=yT, wo=_prep_wo(f(ab_w_out[0])))
    hT = ffn(hT, 1, 0)
    nc = _prog(("nsa",), lambda: build_nsa(Sq, D))
    tabs = nsa_tables(Sq)
    maps = []
    for c in range(8):
        b, hp = c // 2, c % 2
        m = _prep_nsa(f(nsa_w_in[0]), f(nsa_cmp_pos[0]), f(nsa_cmp_w1[0]), f(nsa_cmp_w2[0]), hp)
        m.update(tabs)
        m["hT"] = full_seq(hT, b)
        m["g"] = _gT(mix_norm[1])
        maps.append(m)
    r = _run(nc, maps)
    yT = []
    for c in range(8):
        b, hp = c // 2, c % 2
        sl = slice(hp * NTC, (hp + 1) * NTC)
        y = np.concatenate([r[2 * b]["y"][sl], r[2 * b + 1]["y"][sl]], axis=1)
        yT.append(np.ascontiguousarray(y.T))
    hT = ffn(hT, 1, 1, yT=yT, wo=_prep_wo(f(nsa_w_out[0])), final=True)
    out = np.empty((B, Sq, D), np.float32)
    for c in range(8):
        out[c // 2, (c % 2) * NTC:(c % 2 + 1) * NTC, :] = hT[c].T
    return out
```

```python
import bisect
import contextlib
import numpy as np
import concourse.bass as bass
import concourse.mybir as mybir
from concourse.bass_utils import run_bass_kernel_spmd

F32 = mybir.dt.float32
F32R = mybir.dt.float32r
BF16 = mybir.dt.bfloat16
AF = mybir.ActivationFunctionType
ALU = mybir.AluOpType
AX = mybir.AxisListType

ENGS = ("pe", "act", "dve", "pool", "sp")


class Tile:
    __slots__ = ("name", "w", "r", "psum")

    def __init__(self, name="", psum=False):
        self.name = name
        self.w = None
        self.r = []
        self.psum = psum


class Chan:
    __slots__ = ("sem", "cnt", "ops")

    def __init__(self):
        self.sem = None
        self.cnt = 0
        self.ops = []


class Sched:
    def __init__(self, nc):
        self.nc = nc
        self.ops = []
        self.chans = []

    def chan(self):
        c = Chan()
        self.chans.append(c)
        return c

    def op(self, eng, fn, reads=(), writes=(), chan=None, nosync_same=False):
        idx = len(self.ops)
        deps = set()
        for t in reads:
            if t.w is not None:
                deps.add(t.w)
            if t.psum:
                for r in t.r:
                    if self.ops[r]["eng"] != eng:
                        deps.add(r)
        for t in writes:
            if t.w is not None:
                deps.add(t.w)
            for r in t.r:
                deps.add(r)
        deps.discard(idx)
        rec = dict(eng=eng, fn=fn, deps=deps, chan=chan, idx=idx, nosync_same=nosync_same,
                   sig=None)
        if chan is not None:
            chan.cnt += 16
            rec["sig"] = (chan, chan.cnt)
            chan.ops.append(idx)
        self.ops.append(rec)
        for t in reads:
            t.r.append(idx)
        for t in writes:
            t.w = idx
            t.r = []
        return idx

    def emit(self, final_waits=()):
        nc = self.nc
        ops = self.ops
        needed = set()
        for o in ops:
            for d in o["deps"]:
                od = ops[d]
                if od["chan"] is None:
                    if od["eng"] == o["eng"] and (o["nosync_same"] or od["eng"] == "pe" or od["eng"] == "sp"):
                        continue
                    needed.add(d)
        cnt = {e: 0 for e in ENGS}
        for o in ops:
            if o["chan"] is None and o["idx"] in needed:
                cnt[o["eng"]] += 1
                o["sig"] = (o["eng"], cnt[o["eng"]])
        with contextlib.ExitStack() as st:
            esem = {e: st.enter_context(nc.semaphore("s_" + e)) for e in ENGS}
            for i, c in enumerate(self.chans):
                c.sem = st.enter_context(nc.semaphore("c%d" % i))
            block = st.enter_context(nc.Block())

            def run_engine(ename, eobj):
                known = {}
                for o in ops:
                    if o["eng"] != ename:
                        continue
                    want = {}
                    for d in o["deps"]:
                        od = ops[d]
                        sig = od["sig"]
                        if sig is None:
                            continue
                        key, val = sig
                        if od["chan"] is None and od["eng"] == ename and (
                                o["nosync_same"] or ename in ("pe", "sp")):
                            continue
                        if isinstance(key, Chan):
                            val = 16 * bisect.bisect_left(key.ops, o["idx"])
                        if want.get(key, 0) < val:
                            want[key] = val
                    for key, val in want.items():
                        if known.get(key, 0) >= val:
                            continue
                        known[key] = val
                        sem = key.sem if isinstance(key, Chan) else esem[key]
                        eobj.wait_ge(sem, val)
                    ins = o["fn"](eobj)
                    if o["sig"] is not None:
                        key, val = o["sig"]
                        if isinstance(key, Chan):
                            ins.then_inc(key.sem, 16)
                        else:
                            ins.then_inc(esem[key], 1)
                if ename == "sp":
                    for c in final_waits:
                        eobj.wait_ge(c.sem, c.cnt)

            @block.tensor
            def _(e):
                run_engine("pe", e)

            @block.scalar
            def _(e):
                run_engine("act", e)

            @block.vector
            def _(e):
                run_engine("dve", e)

            @block.gpsimd
            def _(e):
                run_engine("pool", e)

            @block.sync
            def _(e):
                run_engine("sp", e)


class Ctx:
    def __init__(self):
        self.nc = bass.Bass("TRN2", target_bir_lowering=False)
        self.st = contextlib.ExitStack()
        self.S = Sched(self.nc)
        self.n = 0

    def sb(self, shape, dt, name=None):
        self.n += 1
        return self.st.enter_context(self.nc.sbuf_tensor("%s_%d" % (name or "sb", self.n), list(shape), dt))

    def ps(self, shape=(128, 512), dt=F32, name=None):
        self.n += 1
        return self.st.enter_context(self.nc.psum_tensor("%s_%d" % (name or "ps", self.n), list(shape), dt))

    def din(self, name, shape, dt=F32):
        return self.nc.dram_tensor(name, list(shape), dt, kind="ExternalInput").ap()

    def dout(self, name, shape, dt=F32):
        return self.nc.dram_tensor(name, list(shape), dt, kind="ExternalOutput").ap()


EPS = 1e-6


def emit_rstd(S, rstd, trstd, ps_ssq, tssq, Dn, SUB=512):
    S.op("act", lambda e: e.activation(out=rstd[:, 0:SUB], in_=ps_ssq[:, 0:SUB], func=AF.Sqrt, bias=EPSB[0][:, 0:1], scale=1.0 / Dn),
         reads=[tssq, EPSB[1]], writes=[trstd])
    S.op("dve", lambda e: e.reciprocal(out=rstd[:, 0:SUB], in_=rstd[:, 0:SUB]), reads=[trstd], writes=[trstd])


EPSB = [None, None]


def emit_consts(C):
    S = C.S
    eps = C.sb([128, 1], F32, "eps")
    teps = Tile()
    S.op("dve", lambda e: e.memset(eps[:], EPS), writes=[teps])
    EPSB[0], EPSB[1] = eps, teps
    ones32 = C.sb([128, 128], F32, "ones32")
    ones_r = C.sb([128, 128], F32R, "ones")
    t32, tones = Tile(), Tile()
    S.op("dve", lambda e: e.memset(ones32[:], 1.0), writes=[t32])
    S.op("dve", lambda e: e.tensor_copy(out=ones_r[:], in_=ones32[:]), reads=[t32], writes=[tones])
    return ones_r, tones


def emit_rmsnorm_T(C, h_sb, th, aT, taT, g_sb, tg, ones_r, tones, DC, TB, sq, tsq, ps_ssq, tssq, rstd, trstd,
                   Dn, out_dt_cast=None):
    S = C.S
    SUB = min(512, TB)
    NS = TB // SUB
    for s in range(NS):
        sl = slice(s * SUB, (s + 1) * SUB)
        for c in range(DC):
            k = (s * DC + c) % 2
            S.op("act", lambda e, c=c, k=k, sl=sl: e.activation(out=sq[k][:, 0:SUB], in_=h_sb[:, c, sl], func=AF.Square),
                 reads=[th[c][s]], writes=[tsq[k]])
            S.op("pe", lambda e, c=c, k=k: e.matmul(ps_ssq[:, 0:SUB], ones_r[:], sq[k][:, 0:SUB], start=(c == 0), stop=(c == DC - 1)),
                 reads=[tsq[k], tones], writes=[tssq])
        emit_rstd(S, rstd, trstd, ps_ssq, tssq, Dn, SUB)
        for c in range(DC):
            S.op("dve", lambda e, c=c, sl=sl: e.scalar_tensor_tensor(out=aT[:, c, sl], in0=h_sb[:, c, sl],
                                                                      scalar=g_sb[:, c:c + 1], in1=rstd[:, 0:SUB],
                                                                      op0=ALU.mult, op1=ALU.mult),
                 reads=[th[c][s], tg, trstd], writes=[taT[c][s]])


def build_ffn(D, FF, NT, TB, NH, pre=False, final=False):
    C = Ctx()
    nc, S = C.nc, C.S
    DC, FC = D // 128, FF // 128
    FH = FC // NH
    NS = TB // 512
    NB = NT // TB
    hT = C.din("hT", [D, NT])
    g_d = C.din("g", [128, DC])
    wg_d = C.din("wg", [FC, 128, DC * 128])
    wu_d = C.din("wu", [FC, 128, DC * 128])
    wd_d = C.din("wd", [DC, 128, FC * 128])
    if pre:
        yT = C.din("yT", [D, NT])
        wo_d = C.din("wo", [DC, 128, DC * 128])
    if final:
        gf_d = C.din("gf", [128, DC])
    oT = C.dout("oT", [D, NT])
    hTv = hT.rearrange("(c p) t -> p c t", p=128)
    oTv = oT.rearrange("(c p) t -> p c t", p=128)

    h_sb = C.sb([128, DC, TB], F32, "h")
    aT = C.sb([128, DC, TB], BF16, "aT")
    HT = C.sb([128, FH, TB], BF16, "HT")
    wg = [C.sb([128, DC * 128], BF16, "wg") for _ in range(2)]
    wu = [C.sb([128, DC * 128], BF16, "wu") for _ in range(2)]
    wd = [C.sb([128, FH * 128], BF16, "wd") for _ in range(2)]
    sq = [C.sb([128, 512], F32R, "sq") for _ in range(2)]
    sg = [C.sb([128, 512], F32, "sg") for _ in range(2)]
    rstd = C.sb([128, 512], F32, "rstd")
    g_sb = C.sb([128, DC], F32, "g")
    ps_ssq = C.ps(name="ssq")
    psG = [C.ps(name="G") for _ in range(2)]
    psU = [C.ps(name="U") for _ in range(2)]
    psY = [C.ps(name="Y") for _ in range(2)]
    if pre:
        wo = [C.sb([128, DC * 128], BF16, "wo") for _ in range(2)]
        two = [Tile() for _ in range(2)]
        cwo = [S.chan() for _ in range(2)]
    if final:
        gf_sb = C.sb([128, DC], F32, "gf")
        tgf = Tile()
        fo = C.sb([128, DC, TB], F32, "fo") if False else None

    th = [[Tile() for _ in range(NS)] for _ in range(DC)]
    taT = [[Tile() for _ in range(NS)] for _ in range(DC)]
    tHT = [[Tile() for _ in range(NS)] for _ in range(FH)]
    twg = [Tile() for _ in range(2)]
    twu = [Tile() for _ in range(2)]
    twd = [Tile() for _ in range(2)]
    tsq = [Tile() for _ in range(2)]
    tsg = [Tile() for _ in range(2)]
    trstd, tg, tssq = Tile(), Tile(), Tile()
    tG = [Tile(psum=True) for _ in range(2)]
    tU = [Tile(psum=True) for _ in range(2)]
    tY = [Tile(psum=True) for _ in range(2)]
    cwg = [S.chan() for _ in range(2)]
    cwu = [S.chan() for _ in range(2)]
    cwd = [S.chan() for _ in range(2)]
    ch_in = S.chan()
    ch_out = S.chan()
    cg = S.chan()

    S.op("sp", lambda e: e.dma_start(out=g_sb[:], in_=g_d), writes=[tg], chan=cg)
    if final:
        cgf = S.chan()
        S.op("sp", lambda e: e.dma_start(out=gf_sb[:], in_=gf_d), writes=[tgf], chan=cgf)
    ones_r, tones = emit_consts(C)

    all_h = [th[c][s] for c in range(DC) for s in range(NS)]
    all_aT = [taT[c][s] for c in range(DC) for s in range(NS)]
    CG = min(4, DC)
    nwd = 0
    nwgu = 0
    for tb in range(NB):
        tsl = slice(tb * TB, (tb + 1) * TB)
        for c0 in range(0, DC, CG):
            S.op("sp", lambda e, c0=c0, tsl=tsl: e.dma_start(out=h_sb[:, c0:c0 + CG, :], in_=hTv[:, c0:c0 + CG, tsl]),
                 writes=[th[c][s] for c in range(c0, c0 + CG) for s in range(NS)], chan=ch_in)
        if pre:
            for c0 in range(0, DC, CG):
                yv = yT.rearrange("(c p) t -> p c t", p=128)
                S.op("pool", lambda e, c0=c0, tsl=tsl, yv=yv: e.dma_start(out=aT[:, c0:c0 + CG, :], in_=yv[:, c0:c0 + CG, tsl]),
                     writes=[taT[c][s] for c in range(c0, c0 + CG) for s in range(NS)], chan=ch_in)
            for dc in range(DC):
                k = dc % 2
                S.op("pool", lambda e, dc=dc, k=k: e.dma_start(out=wo[k][:], in_=wo_d[dc]), writes=[two[k]], chan=cwo[k])
                for s in range(NS):
                    sl = slice(s * 512, (s + 1) * 512)
                    b = (dc * NS + s) % 2
                    for c in range(DC):
                        S.op("pe", lambda e, c=c, k=k, b=b, sl=sl: e.matmul(psY[b][:], wo[k][:, c * 128:(c + 1) * 128], aT[:, c, sl],
                                                                          start=(c == 0), stop=(c == DC - 1)),
                             reads=[two[k], taT[c][s]], writes=[tY[b]])
                    S.op("dve", lambda e, dc=dc, b=b, sl=sl: e.tensor_tensor(out=h_sb[:, dc, sl], in0=psY[b][:], in1=h_sb[:, dc, sl], op=ALU.add),
                         reads=[tY[b], th[dc][s]], writes=[th[dc][s]])
        emit_rmsnorm_T(C, h_sb, th, aT, taT, g_sb, tg, ones_r, tones, DC, TB, sq, tsq, ps_ssq, tssq, rstd, trstd, D)
        for hf in range(NH):
            for fi in range(FH):
                f = hf * FH + fi
                k = nwgu % 2
                nwgu += 1
                S.op("pool", lambda e, f=f, k=k: e.dma_start(out=wg[k][:], in_=wg_d[f]), writes=[twg[k]], chan=cwg[k])
                S.op("pool", lambda e, f=f, k=k: e.dma_start(out=wu[k][:], in_=wu_d[f]), writes=[twu[k]], chan=cwu[k])
                for s in range(NS):
                    sl = slice(s * 512, (s + 1) * 512)
                    b = (fi * NS + s) % 2
                    for c in range(DC):
                        S.op("pe", lambda e, c=c, k=k, b=b, sl=sl: e.matmul(psG[b][:], wg[k][:, c * 128:(c + 1) * 128], aT[:, c, sl],
                                                                          start=(c == 0), stop=(c == DC - 1)),
                             reads=[twg[k], taT[c][s]], writes=[tG[b]])
                    for c in range(DC):
                        S.op("pe", lambda e, c=c, k=k, b=b, sl=sl: e.matmul(psU[b][:], wu[k][:, c * 128:(c + 1) * 128], aT[:, c, sl],
                                                                          start=(c == 0), stop=(c == DC - 1)),
                             reads=[twu[k], taT[c][s]], writes=[tU[b]])
                    S.op("act", lambda e, b=b: e.activation(out=sg[b][:], in_=psG[b][:], func=AF.Silu),
                         reads=[tG[b]], writes=[tsg[b]])
                    S.op("dve", lambda e, b=b, fi=fi, sl=sl: e.tensor_tensor(out=HT[:, fi, sl], in0=psU[b][:], in1=sg[b][:], op=ALU.mult),
                         reads=[tU[b], tsg[b]], writes=[tHT[fi][s]])
            for dc in range(DC):
                k = nwd % 2
                nwd += 1
                S.op("pool", lambda e, dc=dc, hf=hf, k=k: e.dma_start(out=wd[k][:], in_=wd_d[dc, :, hf * FH * 128:(hf + 1) * FH * 128]),
                     writes=[twd[k]], chan=cwd[k])
                for s in range(NS):
                    sl = slice(s * 512, (s + 1) * 512)
                    b = (dc * NS + s) % 2
                    for fi in range(FH):
                        S.op("pe", lambda e, fi=fi, k=k, b=b, sl=sl: e.matmul(psY[b][:], wd[k][:, fi * 128:(fi + 1) * 128], HT[:, fi, sl],
                                                                            start=(fi == 0), stop=(fi == FH - 1)),
                             reads=[twd[k], tHT[fi][s]], writes=[tY[b]])
                    S.op("dve", lambda e, dc=dc, b=b, sl=sl: e.scalar_tensor_tensor(out=h_sb[:, dc, sl], in0=psY[b][:], scalar=0.5,
                                                                                 in1=h_sb[:, dc, sl], op0=ALU.mult, op1=ALU.add),
                         reads=[tY[b], th[dc][s]], writes=[th[dc][s]])
        if final:
            NSx = NS
            for s in range(NSx):
                sl = slice(s * 512, (s + 1) * 512)
                for c in range(DC):
                    k = (s * DC + c) % 2
                    S.op("act", lambda e, c=c, k=k, sl=sl: e.activation(out=sq[k][:], in_=h_sb[:, c, sl], func=AF.Square),
                         reads=[th[c][s]], writes=[tsq[k]])
                    S.op("pe", lambda e, c=c, k=k: e.matmul(ps_ssq[:], ones_r[:], sq[k][:], start=(c == 0), stop=(c == DC - 1)),
                         reads=[tsq[k], tones], writes=[tssq])
                emit_rstd(S, rstd, trstd, ps_ssq, tssq, D)
                for c in range(DC):
                    S.op("dve", lambda e, c=c, sl=sl: e.scalar_tensor_tensor(out=h_sb[:, c, sl], in0=h_sb[:, c, sl],
                                                                              scalar=gf_sb[:, c:c + 1], in1=rstd[:],
                                                                              op0=ALU.mult, op1=ALU.mult),
                         reads=[th[c][s], tgf, trstd], writes=[th[c][s]])
        for c0 in range(0, DC, CG):
            S.op("sp", lambda e, c0=c0, tsl=tsl: e.dma_start(out=oTv[:, c0:c0 + CG, tsl], in_=h_sb[:, c0:c0 + CG, :]),
                 reads=[th[c][s] for c in range(c0, c0 + CG) for s in range(NS)], chan=ch_out)
    S.emit(final_waits=[ch_out])
    C.st.close()
    return nc


GELU_C = 1.5957691216057308


def emit_gelu(S, out_ap, x_ap, t1_ap, t2_ap, reads, writes, tt1, tt2, eng2="dve"):
    S.op("act", lambda e: e.activation(out=t1_ap, in_=x_ap, func=AF.Square), reads=reads, writes=[tt1])
    S.op("dve", lambda e: e.tensor_scalar(out=t1_ap, in0=t1_ap, scalar1=0.044715, scalar2=1.0, op0=ALU.mult, op1=ALU.add),
         reads=[tt1], writes=[tt1])
    S.op("dve", lambda e: e.tensor_tensor(out=t1_ap, in0=x_ap, in1=t1_ap, op=ALU.mult), reads=reads + [tt1], writes=[tt1])
    S.op("act", lambda e: e.activation(out=t2_ap, in_=t1_ap, func=AF.Sigmoid, scale=GELU_C), reads=[tt1], writes=[tt2])
    S.op("dve", lambda e: e.tensor_tensor(out=out_ap, in0=x_ap, in1=t2_ap, op=ALU.mult), reads=reads + [tt2], writes=writes)


def build_ab(Sq, D=2048, debug=False):
    C = Ctx()
    nc, S = C.nc, C.S
    DC = D // 128
    TB = 512
    NBLK = Sq // TB
    SH = Sq // 2
    DH = 256
    hT = C.din("hT", [D, Sq])
    hTh = C.din("hTh", [D, SH])
    g_d = C.din("g", [128, DC])
    wz_d = C.din("wz", [128, DC, 2048])
    wqk_d = C.din("wqk", [128, DC, 1024])
    wvog_d = C.din("wvog", [128, DC, 1028])
    gb_d = C.din("gb", [128, 4])
    conv_d = C.din("conv", [128, 8, 4])
    mn_d = C.din("mn", [128, 512])
    gn_d = C.din("gn", [128, 1024])
    wsT_d = C.din("wsT", [128, 8, 128])
    bs_d = C.din("bs", [128, 8])
    ident_d = C.din("ident", [128, 128])
    mask_d = C.din("mask", [128, 128])
    ya = C.dout("ya", [SH, 1024])
    yb = C.dout("yb", [Sq, 512])
    hTv = hT.rearrange("(c p) t -> p c t", p=128)
    hThv = hTh.rearrange("(c p) t -> p c t", p=128)
    if debug:
        dbg = C.dout("dbg", [Sq // 128, 128, 16])
        dbg_sb = C.sb([128, 16], F32, "dbg")
        tdbg = Tile()
        cdbg = S.chan()

    W = C.sb([128, DC, 2052], BF16, "W")
    h_sb = C.sb([128, DC, TB], F32, "h")
    hn = C.sb([128, DC, TB], BF16, "hn")
    sq = [C.sb([128, 512], F32R, "sq") for _ in range(2)]
    rstd = C.sb([128, 512], F32, "rstd")
    g_sb = C.sb([128, DC], F32, "g")
    gb_sb = C.sb([128, 4], F32, "gb")
    conv_sb = C.sb([128, 8, 4], F32, "conv")
    mn_sb = C.sb([128, 512], F32, "mn")
    gn_sb = C.sb([128, 1024], F32, "gn")
    wsT32 = C.sb([128, 8, 128], F32, "wsT32")
    wsT = C.sb([128, 8, 128], BF16, "wsT")
    bs_sb = C.sb([128, 8], F32, "bs")
    ident = C.sb([128, 128], F32, "ident")
    identb = C.sb([128, 128], BF16, "identb")
    mask = C.sb([128, 128], F32, "mask")
    ones32 = C.sb([128, 128], F32, "ones32")
    qkpre = C.sb([128, 8, 3 + TB], F32, "qkpre")
    acc = C.sb([128, TB], F32, "acc")
    qkT = C.sb([128, 8, TB], BF16, "qkT")
    vaug = [C.sb([128, 2, 257], F32, "vaug") for _ in range(2)]
    so = [C.sb([128, 512], F32, "so") for _ in range(2)]
    gts = [C.sb([128, 4], F32, "gts") for _ in range(2)]
    lf = C.sb([128, 2], F32, "lf")
    iv = C.sb([128, 2], F32, "iv")
    bcol = C.sb([128, 2], F32, "bcol")
    acol = C.sb([128, 2], F32, "acol")
    a_bc = C.sb([128, 2, 128], F32, "a_bc")
    amax = C.sb([128, 2], F32, "amax")
    Mx = C.sb([128, 2], F32, "Mx")
    mprev = C.sb([128, 2], F32, "mprev")
    wprev = C.sb([128, 2], F32, "wprev")
    ws = C.sb([128, 2], F32, "ws")
    thr = C.sb([128, 2], F32, "thr")
    tmp2 = C.sb([128, 2], F32, "tmp2")
    sTm_h = [C.sb([128, 128], BF16, "sTm") for _ in range(2)]
    vw_h = [C.sb([128, 257], BF16, "vw") for _ in range(2)]
    CT = C.sb([128, 2, 2, 257], F32, "CT")
    CTb_h = [C.sb([128, 2, 257], BF16, "CTb") for _ in range(2)]
    ktok_h = [C.sb([128, 256], BF16, "ktok") for _ in range(2)]
    den_h = [C.sb([128, 1], F32, "den") for _ in range(2)]
    hh_h = [C.sb([128, 256], F32, "hh") for _ in range(2)]
    junk = C.sb([128, 256], F32, "junk")
    ssq1 = C.sb([128, 2], F32, "ssq1")
    ybt = [C.sb([128, 512], F32, "ybt") for _ in range(2)]
    u_sb = C.sb([128, 1024], F32, "u")
    v_sb = C.sb([128, 1024], F32, "v")
    vn = C.sb([128, 1024], BF16, "vn")
    t1 = C.sb([128, 512], F32, "t1")
    t2 = C.sb([128, 512], F32, "t2")
    yat = [C.sb([128, 1024], F32, "yat") for _ in range(2)]

    psA = [C.ps(name="A") for _ in range(2)]
    psB = [C.ps(name="B") for _ in range(2)]
    psS = C.ps(name="S")
    ps_ssq = psS
    psN_h = [C.ps(name="N") for _ in range(2)]
    psT = C.ps([128, 512], BF16, name="T")

    T = Tile
    tW, tg, tgb, tconv, tmn, tgn, tws32, tws, tbs, tid, tidb, tmask, tones32 = [T() for _ in range(13)]
    th = [[T()] for _ in range(DC)]
    thn = [[T()] for _ in range(DC)]
    tsq = [T(), T()]
    trstd = T()
    tqkpre = [T() for _ in range(8)]
    tacc = T()
    tqkT = [T() for _ in range(8)]
    tvaug = [T(), T()]
    tso = [T(), T()]
    tgts = [T(), T()]
    tlf, tiv, tbcol, tacol, tabc, tamax, tMx, tmprev, twprev, tws_, tthr, ttmp2 = [T() for _ in range(12)]
    tCT, tjunk, tssq1 = T(), T(), T()
    tsTm_h, tvw_h, tCTb_h, tktok_h, tden_h, thh_h, tssq1_h, tCT_h = [[T(), T()] for _ in range(8)]
    tybt_h = [[T(), T()], [T(), T()]]
    tu, tv, tvn, tt1, tt2 = [T() for _ in range(5)]
    tyat = [T(), T()]
    tpsA = [T(psum=True), T(psum=True)]
    tpsB = [T(psum=True), T(psum=True)]
    tpsS, tpsT = T(psum=True), T(psum=True)
    tpsN_h = [T(psum=True), T(psum=True)]

    cmisc = S.chan()
    cW = S.chan()
    ch_in = S.chan()
    cyb = [S.chan(), S.chan()]
    cya = [S.chan(), S.chan()]

    def ld(dst, src, tile, eng="sp", chan=None):
        S.op(eng, lambda e: e.dma_start(out=dst, in_=src), writes=[tile], chan=chan or cmisc)

    ld(g_sb[:], g_d, tg)
    ld(gb_sb[:], gb_d, tgb)
    ld(conv_sb[:], conv_d, tconv)
    ld(mn_sb[:], mn_d, tmn)
    ld(gn_sb[:], gn_d, tgn)
    ld(wsT32[:], wsT_d, tws32)
    ld(bs_sb[:], bs_d, tbs)
    ld(ident[:], ident_d, tid)
    ld(mask[:], mask_d, tmask)
    ones_r, tones = emit_consts(C)
    S.op("dve", lambda e: e.memset(ones32[:], 1.0), writes=[tones32])
    S.op("dve", lambda e: e.tensor_copy(out=identb[:], in_=ident[:]), reads=[tid], writes=[tidb])
    for g in range(8):
        S.op("dve", lambda e, g=g: e.tensor_tensor(out=wsT[:, g, :], in0=wsT32[:, g, :], in1=mask[:], op=ALU.mult),
             reads=[tws32, tmask], writes=[tws])
    S.op("pool", lambda e: e.dma_start(out=W[:, :, 0:1024], in_=wqk_d), writes=[tW], chan=cW)
    S.op("pool", lambda e: e.dma_start(out=W[:, :, 1024:2052], in_=wvog_d), writes=[tW], chan=cW)
    tssq = tpsS
    S.op("dve", lambda e: e.memset(CT[:], 0.0), writes=tCT_h)
    S.op("dve", lambda e: e.memset(mprev[:], 0.0), writes=[tmprev])
    for k in range(2):
        S.op("dve", lambda e, k=k: e.memset(vaug[k][:], 1.0), writes=[tvaug[k]])
    for m in range(8):
        S.op("dve", lambda e, m=m: e.memset(qkpre[:, m, 0:3], 0.0), writes=[tqkpre[m]])

    def norm_block(src_v, tsl):
        CG = 4
        for c0 in range(0, DC, CG):
            S.op("sp", lambda e, c0=c0: e.dma_start(out=h_sb[:, c0:c0 + CG, :], in_=src_v[:, c0:c0 + CG, tsl]),
                 writes=[th[c][0] for c in range(c0, c0 + CG)], chan=ch_in)
        emit_rmsnorm_T(C, h_sb, th, hn, thn, g_sb, tg, ones_r, tones, DC, TB, sq, tsq, ps_ssq, tssq, rstd, trstd, D)

    all_hn = [thn[c][0] for c in range(DC)]
    nA = 0
    nB = 0
    for tb in range(NBLK):
        norm_block(hTv, slice(tb * TB, (tb + 1) * TB))
        for m in range(8):
            b = nA % 2
            nA += 1
            for c in range(DC):
                S.op("pe", lambda e, c=c, m=m, b=b: e.matmul(psA[b][:], W[:, c, m * 128:(m + 1) * 128], hn[:, c, :],
                                                          start=(c == 0), stop=(c == DC - 1)),
                     reads=[tW, thn[c][0]], writes=[tpsA[b]])
            S.op("act", lambda e, m=m, b=b: e.copy(out=qkpre[:, m, 3:3 + TB], in_=psA[b][:]), reads=[tpsA[b]], writes=[tqkpre[m]])
            S.op("dve", lambda e, m=m: e.tensor_scalar(out=acc[:], in0=qkpre[:, m, 0:TB], scalar1=conv_sb[:, m, 0:1], scalar2=None,
                                                       op0=ALU.mult), reads=[tqkpre[m], tconv], writes=[tacc])
            for k in range(1, 4):
                S.op("dve", lambda e, m=m, k=k: e.scalar_tensor_tensor(out=acc[:], in0=qkpre[:, m, k:k + TB], scalar=conv_sb[:, m, k:k + 1],
                                                                        in1=acc[:], op0=ALU.mult, op1=ALU.add),
                     reads=[tqkpre[m], tconv, tacc], writes=[tacc])
            S.op("act", lambda e, m=m: e.activation(out=qkT[:, m, :], in_=acc[:], func=AF.Silu), reads=[tacc], writes=[tqkT[m]])
            S.op("dve", lambda e, m=m: e.tensor_copy(out=qkpre[:, m, 0:3], in_=qkpre[:, m, TB:TB + 3]), reads=[tqkpre[m]], writes=[tqkpre[m]])
        def proj_chunk(j):
            nonlocal nB
            jsl = slice(j * 128, (j + 1) * 128)
            kk = j % 2
            b = nB % 2
            nB += 1
            for c in range(DC):
                S.op("pe", lambda e, c=c, b=b, jsl=jsl: e.matmul(psB[b][:], hn[:, c, jsl], W[:, c, 1024:1536], start=(c == 0), stop=(c == DC - 1)),
                     reads=[tW, thn[c][0]], writes=[tpsB[b]])
            for h in range(2):
                S.op("act", lambda e, h=h, b=b, kk=kk: e.copy(out=vaug[kk][:, h, 0:256], in_=psB[b][:, h * 256:(h + 1) * 256]),
                     reads=[tpsB[b]], writes=[tvaug[kk]])
            b = nB % 2
            nB += 1
            for c in range(DC):
                S.op("pe", lambda e, c=c, b=b, jsl=jsl: e.matmul(psB[b][:], hn[:, c, jsl], W[:, c, 1536:2048], start=(c == 0), stop=(c == DC - 1)),
                     reads=[tW, thn[c][0]], writes=[tpsB[b]])
            S.op("act", lambda e, b=b, kk=kk: e.activation(out=so[kk][:], in_=psB[b][:], func=AF.Sigmoid), reads=[tpsB[b]], writes=[tso[kk]])
            for c in range(DC):
                S.op("pe", lambda e, c=c, jsl=jsl: e.matmul(psS[:, 0:4], hn[:, c, jsl], W[:, c, 2048:2052], start=(c == 0), stop=(c == DC - 1)),
                     reads=[tW, thn[c][0]], writes=[tpsS])
            S.op("dve", lambda e, kk=kk: e.tensor_tensor(out=gts[kk][:], in0=psS[:, 0:4], in1=gb_sb[:], op=ALU.add),
                 reads=[tpsS, tgb], writes=[tgts[kk]])

        proj_chunk(0)
        for j in range(TB // 128):
            jsl = slice(j * 128, (j + 1) * 128)
            kk = j % 2
            S.op("act", lambda e, kk=kk: e.activation(out=lf[:], in_=gts[kk][:, 2:4], func=AF.Exp, scale=-1.0), reads=[tgts[kk]], writes=[tlf])
            S.op("act", lambda e: e.activation(out=lf[:], in_=lf[:], func=AF.Ln, bias=ones32[:, 0:1], scale=1.0), reads=[tlf, tones32], writes=[tlf])
            S.op("dve", lambda e: e.tensor_scalar(out=lf[:], in0=lf[:], scalar1=-1.0, scalar2=None, op0=ALU.mult), reads=[tlf], writes=[tlf])
            S.op("pe", lambda e: e.matmul(psS[:, 8:10], mask[:], lf[:], start=True, stop=True), reads=[tmask, tlf], writes=[tpsS])
            S.op("pe", lambda e: e.matmul(psS[:, 16:18], ones32[:], lf[:], start=True, stop=True), reads=[tones32, tlf], writes=[tpsS])
            S.op("dve", lambda e: e.tensor_copy(out=bcol[:], in_=psS[:, 8:10]), reads=[tpsS], writes=[tbcol])
            S.op("dve", lambda e, kk=kk: e.tensor_tensor(out=acol[:], in0=gts[kk][:, 0:2], in1=bcol[:], op=ALU.subtract),
                 reads=[tgts[kk], tbcol], writes=[tacol])
            for h in range(2):
                S.op("dve", lambda e, h=h: e.tensor_copy(out=a_bc[:, h, :], in_=acol[:, h:h + 1].to_broadcast([128, 128])),
                     reads=[tacol], writes=[tabc])
            for h in range(2):
                S.op("pe", lambda e, h=h: e.matmul(psS[:, 128 + h * 128:256 + h * 128], a_bc[:, h, :], ident[:], start=True, stop=True),
                     reads=[tabc, tid], writes=[tpsS])
            S.op("dve", lambda e: e.tensor_reduce(out=amax[:], in_=psS[:, 128:384].rearrange("p (h s) -> p h s", h=2), axis=AX.X, op=ALU.max),
                 reads=[tpsS], writes=[tamax])
            S.op("dve", lambda e: e.tensor_tensor(out=Mx[:], in0=amax[:], in1=mprev[:], op=ALU.max), reads=[tamax, tmprev], writes=[tMx])
            S.op("dve", lambda e: e.tensor_tensor(out=tmp2[:], in0=mprev[:], in1=Mx[:], op=ALU.subtract), reads=[tmprev, tMx], writes=[ttmp2])
            S.op("act", lambda e: e.activation(out=wprev[:], in_=tmp2[:], func=AF.Exp), reads=[ttmp2], writes=[twprev])
            S.op("dve", lambda e: e.tensor_tensor(out=tmp2[:], in0=acol[:], in1=Mx[:], op=ALU.subtract), reads=[tacol, tMx], writes=[ttmp2])
            S.op("act", lambda e: e.activation(out=ws[:], in_=tmp2[:], func=AF.Exp), reads=[ttmp2], writes=[tws_])
            S.op("dve", lambda e: e.tensor_tensor(out=tmp2[:], in0=bcol[:], in1=Mx[:], op=ALU.add), reads=[tbcol, tMx], writes=[ttmp2])
            S.op("act", lambda e: e.activation(out=thr[:], in_=tmp2[:], func=AF.Exp, scale=-1.0), reads=[ttmp2], writes=[tthr])
            S.op("dve", lambda e: e.tensor_tensor(out=mprev[:], in0=psS[:, 16:18], in1=Mx[:], op=ALU.add), reads=[tpsS, tMx], writes=[tmprev])
            if debug:
                for i_, (src_, tl_) in enumerate(((lf, tlf), (acol, tacol), (bcol, tbcol), (amax, tamax), (Mx, tMx), (wprev, twprev),
                                                  (ws, tws_), (thr, tthr))):
                    S.op("dve", lambda e, i_=i_, src_=src_: e.tensor_copy(out=dbg_sb[:, 2 * i_:2 * i_ + 2], in_=src_[:]),
                         reads=[tl_], writes=[tdbg])
                cidx = tb * (TB // 128) + j
                S.op("sp", lambda e, cidx=cidx: e.dma_start(out=dbg[cidx], in_=dbg_sb[:]), reads=[tdbg], chan=cdbg)
            QI = [[h * 2, h * 2 + 1] for h in range(2)]
            KI = [[4 + h * 2, 4 + h * 2 + 1] for h in range(2)]
            for h in range(2):
                for dc in range(2):
                    S.op("pe", lambda e, dc=dc, h=h, jsl=jsl: e.matmul(psA[0][:, h * 128:(h + 1) * 128], qkT[:, KI[h][dc], jsl], qkT[:, QI[h][dc], jsl],
                                                                    start=(dc == 0), stop=(dc == 1)),
                         reads=[tqkT[KI[h][dc]], tqkT[QI[h][dc]]], writes=[tpsA[0]])
            for h in range(2):
                for dc in range(2):
                    S.op("pe", lambda e, dc=dc, h=h, jsl=jsl: e.transpose(psT[:, h * 256 + dc * 128:h * 256 + (dc + 1) * 128], qkT[:, KI[h][dc], jsl], identb[:]),
                         reads=[tqkT[KI[h][dc]], tidb], writes=[tpsT])
            if j + 1 < TB // 128:
                proj_chunk(j + 1)
            for h in range(2):
                S.op("dve", lambda e, h=h: e.scalar_tensor_tensor(out=sTm_h[h][:], in0=psA[0][:, h * 128:(h + 1) * 128], scalar=DH ** -0.5, in1=mask[:],
                                                                  op0=ALU.mult, op1=ALU.mult), reads=[tpsA[0], tmask], writes=[tsTm_h[h]])
            for h in range(2):
                S.op("act", lambda e, h=h: e.copy(out=ktok_h[h][:], in_=psT[:, h * 256:(h + 1) * 256]), reads=[tpsT], writes=[tktok_h[h]])
            for h in range(2):
                S.op("dve", lambda e, h=h, kk=kk: e.tensor_scalar(out=vw_h[h][:], in0=vaug[kk][:, h, :], scalar1=ws[:, h:h + 1], scalar2=None, op0=ALU.mult),
                     reads=[tvaug[kk], tws_], writes=[tvw_h[h]])
            for h in range(2):
                S.op("dve", lambda e, h=h: e.tensor_scalar(out=CT[:, h, :, :], in0=CT[:, h, :, :], scalar1=wprev[:, h:h + 1], scalar2=None, op0=ALU.mult),
                     reads=[tCT_h[h], twprev], writes=[tCT_h[h]])
            for h in range(2):
                S.op("act", lambda e, h=h: e.copy(out=CTb_h[h][:], in_=CT[:, h, :, :]), reads=[tCT_h[h]], writes=[tCTb_h[h]])
            for h in range(2):
                S.op("pe", lambda e, h=h: e.matmul(psN_h[h][:, 0:257], sTm_h[h][:], vw_h[h][:], start=True, stop=False),
                     reads=[tsTm_h[h], tvw_h[h]], writes=[tpsN_h[h]])
                for dc in range(2):
                    S.op("pe", lambda e, dc=dc, h=h, jsl=jsl: e.matmul(psN_h[h][:, 0:257], qkT[:, QI[h][dc], jsl], CTb_h[h][:, dc, :], start=False, stop=(dc == 1)),
                         reads=[tqkT[QI[h][dc]], tCTb_h[h]], writes=[tpsN_h[h]])
            for h in range(2):
                S.op("act", lambda e, h=h: e.activation(out=den_h[h][:], in_=psN_h[h][:, 256:257], func=AF.Abs), reads=[tpsN_h[h]], writes=[tden_h[h]])
            for h in range(2):
                S.op("dve", lambda e, h=h: e.tensor_tensor(out=den_h[h][:], in0=den_h[h][:], in1=thr[:, h:h + 1], op=ALU.max),
                     reads=[tden_h[h], tthr], writes=[tden_h[h]])
            for h in range(2):
                S.op("dve", lambda e, h=h: e.reciprocal(out=den_h[h][:], in_=den_h[h][:]), reads=[tden_h[h]], writes=[tden_h[h]])
            for h in range(2):
                S.op("dve", lambda e, h=h: e.tensor_scalar(out=hh_h[h][:], in0=psN_h[h][:, 0:256], scalar1=den_h[h][:, 0:1], scalar2=None, op0=ALU.mult),
                     reads=[tpsN_h[h], tden_h[h]], writes=[thh_h[h]])
            for h in range(2):
                S.op("dve", lambda e, h=h: e.memset(ssq1[:, h:h + 1], 0.0), writes=[tssq1_h[h]])
            for h in range(2):
                S.op("act", lambda e, h=h: e.activation(out=junk[:], in_=hh_h[h][:], func=AF.Square, accum_out=ssq1[:, h:h + 1]),
                     reads=[thh_h[h]], writes=[tjunk, tssq1_h[h]])
            for h in range(2):
                S.op("act", lambda e, h=h: e.activation(out=ssq1[:, h:h + 1], in_=ssq1[:, h:h + 1], func=AF.Sqrt, bias=EPSB[0][:, 0:1], scale=1.0 / DH),
                     reads=[tssq1_h[h], EPSB[1]], writes=[tssq1_h[h]])
            for h in range(2):
                S.op("dve", lambda e, h=h: e.reciprocal(out=ssq1[:, h:h + 1], in_=ssq1[:, h:h + 1]), reads=[tssq1_h[h]], writes=[tssq1_h[h]])
            for h in range(2):
                S.op("dve", lambda e, h=h, kk=kk: e.scalar_tensor_tensor(out=ybt[kk][:, h * 256:(h + 1) * 256], in0=hh_h[h][:], scalar=ssq1[:, h:h + 1],
                                                                          in1=mn_sb[:, h * 256:(h + 1) * 256], op0=ALU.mult, op1=ALU.mult),
                     reads=[thh_h[h], tssq1_h[h], tmn], writes=[tybt_h[kk][h]])
            for h in range(2):
                S.op("dve", lambda e, h=h, kk=kk: e.tensor_tensor(out=ybt[kk][:, h * 256:(h + 1) * 256], in0=ybt[kk][:, h * 256:(h + 1) * 256],
                                                                  in1=so[kk][:, h * 256:(h + 1) * 256], op=ALU.mult),
                     reads=[tybt_h[kk][h], tso[kk]], writes=[tybt_h[kk][h]])
            for h in range(2):
                for dc in range(2):
                    S.op("pe", lambda e, dc=dc, h=h: e.matmul(psA[1][:, 0:257], ktok_h[h][:, dc * 128:(dc + 1) * 128], vw_h[h][:], start=True, stop=True),
                         reads=[tktok_h[h], tvw_h[h]], writes=[tpsA[1]])
                    S.op("dve", lambda e, dc=dc, h=h: e.scalar_tensor_tensor(out=CT[:, h, dc, :], in0=psA[1][:, 0:257], scalar=DH ** -0.5,
                                                                              in1=CT[:, h, dc, :], op0=ALU.mult, op1=ALU.add),
                         reads=[tCT_h[h], tpsA[1]], writes=[tCT_h[h]])
            tok0 = tb * TB + j * 128
            S.op("sp", lambda e, kk=kk, tok0=tok0: e.dma_start(out=yb[tok0:tok0 + 128, :], in_=ybt[kk][:]), reads=tybt_h[kk], chan=cyb[kk])

    S.op("pool", lambda e: e.dma_start(out=W[:, :, 0:2048], in_=wz_d), writes=[tW], chan=cW)
    nj = 0
    for tb in range(SH // TB):
        norm_block(hThv, slice(tb * TB, (tb + 1) * TB))
        for j in range(TB // 128):
            jsl = slice(j * 128, (j + 1) * 128)
            kk = nj % 2
            nj += 1
            for cb in range(4):
                b = cb % 2
                for c in range(DC):
                    S.op("pe", lambda e, c=c, b=b, cb=cb, jsl=jsl: e.matmul(psA[b][:], hn[:, c, jsl], W[:, c, cb * 512:(cb + 1) * 512],
                                                                        start=(c == 0), stop=(c == DC - 1)),
                         reads=[tW, thn[c][0]], writes=[tpsA[b]])
                dst = u_sb if cb < 2 else v_sb
                tdst = tu if cb < 2 else tv
                csl = slice((cb % 2) * 512, (cb % 2 + 1) * 512)
                emit_gelu(S, dst[:, csl], psA[b][:], t1[:], t2[:], [tpsA[b]], [tdst], tt1, tt2)
            S.op("dve", lambda e: e.memset(ssq1[:], 0.0), writes=[tssq1] + tssq1_h)
            for q in range(2):
                S.op("act", lambda e, q=q: e.activation(out=t1[:], in_=v_sb[:, q * 512:(q + 1) * 512], func=AF.Square, accum_out=ssq1[:, q:q + 1]),
                     reads=[tv], writes=[tt1, tssq1])
            S.op("dve", lambda e: e.tensor_tensor(out=ssq1[:, 0:1], in0=ssq1[:, 0:1], in1=ssq1[:, 1:2], op=ALU.add), reads=[tssq1], writes=[tssq1])
            S.op("act", lambda e: e.activation(out=ssq1[:, 0:1], in_=ssq1[:, 0:1], func=AF.Sqrt, bias=EPSB[0][:, 0:1], scale=1.0 / 1024),
                 reads=[tssq1, EPSB[1]], writes=[tssq1])
            S.op("dve", lambda e: e.reciprocal(out=ssq1[:, 0:1], in_=ssq1[:, 0:1]), reads=[tssq1], writes=[tssq1])
            S.op("dve", lambda e: e.scalar_tensor_tensor(out=vn[:], in0=v_sb[:], scalar=ssq1[:, 0:1], in1=gn_sb[:], op0=ALU.mult, op1=ALU.mult),
                 reads=[tv, tssq1, tgn], writes=[tvn])
            for g in range(8):
                b = g // 4
                S.op("pe", lambda e, g=g, b=b: e.matmul(psB[b][:, (g % 4) * 128:(g % 4 + 1) * 128], wsT[:, g, :], vn[:, g * 128:(g + 1) * 128],
                                                    start=True, stop=True), reads=[tws, tvn], writes=[tpsB[b]])
            for g in range(8):
                b = g // 4
                S.op("dve", lambda e, g=g, b=b, kk=kk: e.scalar_tensor_tensor(out=yat[kk][:, g * 128:(g + 1) * 128],
                                                                               in0=psB[b][:, (g % 4) * 128:(g % 4 + 1) * 128],
                                                                               scalar=bs_sb[:, g:g + 1], in1=u_sb[:, g * 128:(g + 1) * 128],
                                                                               op0=ALU.add, op1=ALU.mult),
                     reads=[tpsB[b], tbs, tu], writes=[tyat[kk]])
            tok0 = tb * TB + j * 128
            S.op("sp", lambda e, kk=kk, tok0=tok0: e.dma_start(out=ya[tok0:tok0 + 128, :], in_=yat[kk][:]), reads=[tyat[kk]], chan=cya[kk])
    S.emit(final_waits=cyb + cya)
    C.st.close()
    return nc


NEG = -1.0e30


def nsa_tables(Sq):
    NT = Sq // 128
    NCMP = (Sq - 32) // 16 + 1
    CH = (NCMP + 127) // 128
    NSL = Sq // 64
    half = 16
    inv = 1.0 / (500000.0 ** (np.arange(half, dtype=np.float32) / half))
    ang = np.arange(Sq, dtype=np.float32)[None, :] * inv[:, None]
    cos = np.ones((128, Sq), np.float32)
    sin = np.zeros((128, Sq), np.float32)
    cos[0:16] = np.cos(ang); cos[16:32] = np.cos(ang)
    sin[0:16] = np.sin(ang); sin[16:32] = np.sin(ang)
    sc = np.float32(128 ** -0.5)
    psw = np.zeros((128, 128), np.float32)
    for d in range(16):
        psw[d + 16, d] = -1.0
        psw[d, d + 16] = 1.0
    k = np.arange(128)[:, None]
    q = np.arange(128)[None, :]
    causal = np.where(k > q, NEG, 0.0).astype(np.float32)
    winneg = np.where(k <= q, NEG, 0.0).astype(np.float32)
    cm = np.zeros((NT, 128, CH, 128), np.float32)
    for T in range(NT):
        for ch in range(CH):
            c = ch * 128 + np.arange(128)[:, None]
            t = T * 128 + np.arange(128)[None, :]
            cm[T, :, ch, :] = np.where((16 * c + 31 > t) | (c >= NCMP), NEG, 0.0)
    E = np.zeros((64, NT, 128), np.float32)
    for kc in range(NT):
        for kk in range(128):
            E[(kc * 128 + kk) // 64, kc, kk] = 1.0
    fpos = np.zeros((NT, 128, 64), np.float32)
    fneg = np.full((NT, 128, 64), 1.0e30, np.float32)
    j = np.arange(64)[None, :]
    for T in range(NT):
        cur = ((T * 128 + np.arange(128)) // 64)[:, None]
        fp = np.zeros((128, 64), np.float32)
        fp = np.where(j == cur - 1, 1.0e9, fp)
        fp = np.where(j == cur, 2.0e9, fp)
        fp = np.where(j == 0, 3.0e9, fp)
        fpos[T] = fp
        fneg[T] = np.where((j > cur) | (j >= NSL), -1.0e9, 1.0e30)
    ci = np.arange(CH * 128)[:, None] * 16
    sj = np.arange(64)[None, :] * 64
    ov = np.clip(np.minimum(ci + 32, sj + 64) - np.maximum(ci, sj), 0, None).astype(np.float32) / 32.0
    ov[NCMP:] = 0.0
    ov = np.ascontiguousarray(ov.reshape(CH, 128, 64).transpose(1, 0, 2))
    return dict(cosq=cos * sc, sinq=sin * sc, cosk=cos, sink=sin, psw=psw, ident=np.eye(128, dtype=np.float32),
                causal=causal, winneg=winneg, cmpmask=cm, E=E, fpos=fpos, fneg=fneg, ov=ov)


def build_nsa(Sq, D=2048, stop=0, branches=(0, 1, 2)):
    C = Ctx()
    nc, S = C.nc, C.S
    DC = D // 128
    TB = 256
    TPB = TB // 128
    NBLK = Sq // TB
    NT = Sq // 128
    NCMP = (Sq - 32) // 16 + 1
    CH = (NCMP + 127) // 128
    T_ = Tile

    hT = C.din("hT", [D, Sq])
    g_d = C.din("g", [128, DC])
    wkv_d = C.din("wkv", [128, DC, 1536])
    wq_d = C.din("wq", [128, DC, 1024])
    wgt_d = C.din("wgt", [128, DC, 24])
    w1_d = C.din("w1", [128, 2, 32, 128])
    w2_d = C.din("w2", [128, 2, 128])
    pos_d = C.din("posT", [128, 2, 32])
    tabs = {}
    for nm, shp in (("cosq", [128, Sq]), ("sinq", [128, Sq]), ("cosk", [128, Sq]), ("sink", [128, Sq]), ("psw", [128, 128]),
                    ("ident", [128, 128]), ("causal", [128, 128]), ("winneg", [128, 128]), ("cmpmask", [NT, 128, CH, 128]),
                    ("E", [64, NT, 128]), ("fpos", [NT, 128, 64]), ("fneg", [NT, 128, 64]), ("ov", [128, CH, 64])):
        tabs[nm] = C.din(nm, shp)
    y_d = C.dout("y", [Sq, 1024])
    hTv = hT.rearrange("(c p) t -> p c t", p=128)

    W = C.sb([128, DC, 1024], BF16, "W")
    Wgt = C.sb([128, DC, 24], BF16, "Wgt")
    WT = C.sb([128, DC, 512], BF16, "WT")
    tWT = Tile()
    tWgt = Tile()
    h_sb = C.sb([128, DC, TB], BF16, "h")
    hn = C.sb([128, DC, TB], BF16, "hn")
    sq = [C.sb([128, 256], F32R, "sq") for _ in range(2)]
    rstd = C.sb([128, 256], F32, "rstd")
    g_sb = C.sb([128, DC], F32, "g")
    kTs = C.sb([128, 2, 2, Sq], BF16, "kTs")
    cmpin = C.sb([128, 2, 2, Sq], BF16, "cmpin")
    vsel = C.sb([128, NT, 2, 129], BF16, "vsel")
    vwin = C.sb([128, NT, 2, 129], BF16, "vwin")
    kcmpT = C.sb([128, 2, CH * 128], BF16, "kcmpT")
    vcmp = C.sb([128, 2, CH, 193], BF16, "vcmp")
    rope_c = C.sb([128, TB], F32, "ropec")
    rope_s = C.sb([128, TB], F32, "ropes")
    psw = C.sb([128, 128], BF16, "psw")
    identb = C.sb([128, 128], BF16, "identb")
    ident = C.sb([128, 128], F32, "ident")
    causal = C.sb([128, 4, 128], BF16, "causal")
    winneg = C.sb([128, 4, 128], BF16, "winneg")
    Eb = C.sb([64, NT, 128], BF16, "E")
    xb = C.sb([128, TB], BF16, "xb")
    xc = C.sb([128, TB], F32, "xc")
    xs = C.sb([128, TB], F32, "xs")
    kmax2 = C.sb([128, 6], F32, "kmax2")
    kred = C.sb([128, 1], F32, "kred")
    ones32 = C.sb([128, 128], F32, "ones32")
    sqf = C.sb([128, 256], F32, "sqf")
    tsqf = Tile()

    psB = [C.ps(name="B") for _ in range(3)]
    psA = psB
    psO = [C.ps(name="O") for _ in range(4)]
    psS = C.ps(name="S")
    ps_ssq = psS

    tW, tg = T_(), T_()
    th = [[T_()] for _ in range(DC)]
    thn = [[T_()] for _ in range(DC)]
    tsq = [T_(), T_()]
    trstd = T_()
    tkTs = [[[T_() for _ in range(NBLK)] for _ in range(2)] for _ in range(2)]
    tcmpin = [[T_() for _ in range(2)] for _ in range(2)]
    tvsel = [T_() for _ in range(NT)]
    tvwin = [T_() for _ in range(NT)]
    tkcmpT, tvcmp = T_(), T_()
    trope, tpsw, tidb, tid, tcausal, twinneg, tE = [T_() for _ in range(7)]
    txb, txc, txs, tkmax2, tkred, tones32 = [T_() for _ in range(6)]
    tpsB = [T_(psum=True), T_(psum=True), T_(psum=True)]
    tpsA = tpsB
    tpsO = [T_(psum=True) for _ in range(4)]
    tpsS = T_(psum=True)
    tssq = tpsS

    cmisc = S.chan()
    cW = S.chan()
    ch_in = S.chan()
    crope = S.chan()

    S.op("sp", lambda e: e.dma_start(out=g_sb[:], in_=g_d), writes=[tg], chan=cmisc)
    S.op("sp", lambda e: e.dma_start(out=ident[:], in_=tabs["ident"]), writes=[tid], chan=cmisc)
    for dst, nm, tl in ((psw, "psw", tpsw), (identb, "ident", tidb), (Eb, "E", tE)):
        S.op("pool", lambda e, dst=dst, nm=nm: e.dma_start(out=dst[:], in_=tabs[nm]), writes=[tl], chan=cmisc)
    for dst, nm, tl in ((causal, "causal", tcausal), (winneg, "winneg", twinneg)):
        for r4 in range(4):
            S.op("pool", lambda e, dst=dst, nm=nm, r4=r4: e.dma_start(out=dst[:, r4, :], in_=tabs[nm]), writes=[tl], chan=cmisc)
    ones_r, tones = emit_consts(C)
    S.op("dve", lambda e: e.memset(ones32[:], 1.0), writes=[tones32])
    S.op("dve", lambda e: e.memset(kmax2[:], 0.0), writes=[tkmax2])
    S.op("dve", lambda e: e.memset(vsel[:], 1.0), writes=tvsel)
    S.op("dve", lambda e: e.memset(vwin[:], 1.0), writes=tvwin)
    S.op("dve", lambda e: e.memset(vcmp[:], 1.0), writes=[tvcmp])
    S.op("dve", lambda e: e.memset(kcmpT[:], 0.0), writes=[tkcmpT])
    S.op("pool", lambda e: e.dma_start(out=Wgt[:], in_=wgt_d), writes=[tWgt], chan=cmisc)

    class _Stop(Exception):
        pass

    def ckpt(v):
        if stop == v:
            S.op("sp", lambda e: e.dma_start(out=y_d[0:128, 0:128], in_=ident[:]), reads=[tid], chan=cmisc)
            S.emit(final_waits=[cmisc])
            S.op = lambda *a, **k: None
            S.emit = lambda *a, **k: None

    def norm_block(tsl):
        CG = 4
        for c0 in range(0, DC, CG):
            S.op("pool", lambda e, c0=c0: e.dma_start(out=h_sb[:, c0:c0 + CG, :], in_=hTv[:, c0:c0 + CG, tsl]),
                 writes=[th[c][0] for c in range(c0, c0 + CG)], chan=ch_in)
        emit_rmsnorm_T(C, h_sb, th, hn, thn, g_sb, tg, ones_r, tones, DC, TB, sq, tsq, ps_ssq, tssq, rstd, trstd, D)

    def load_rope(cn, sn, tsl):
        S.op("sp", lambda e: e.dma_start(out=rope_c[:], in_=tabs[cn][:, tsl]), writes=[trope], chan=crope)
        S.op("sp", lambda e: e.dma_start(out=rope_s[:], in_=tabs[sn][:, tsl]), writes=[trope], chan=crope)

    nA = [0]

    def proj_fm(col0):
        b = nA[0] % 2
        nA[0] += 1
        for c in range(DC):
            S.op("pe", lambda e, c=c, b=b: e.matmul(psA[b][:, 0:TB], W[:, c, col0:col0 + 128], hn[:, c, :], start=(c == 0), stop=(c == DC - 1)),
                 reads=[tW, thn[c][0]], writes=[tpsA[b]])
        return b

    def rope_to(b, dst_ap, dst_tiles, split=False):
        S.op("act", lambda e: e.copy(out=xb[:], in_=psA[b][:, 0:TB]), reads=[tpsA[b]], writes=[txb])
        ckpt(151)
        S.op("dve", lambda e: e.tensor_tensor(out=xc[:], in0=psA[b][:, 0:TB], in1=rope_c[:], op=ALU.mult), reads=[tpsA[b], trope, txb], writes=[txc])
        ckpt(152)
        bb = nA[0] % 2
        nA[0] += 1
        S.op("pe", lambda e: e.matmul(psA[bb][:, 0:TB], psw[:], xb[:], start=True, stop=True), reads=[tpsw, txb], writes=[tpsA[bb]])
        ckpt(153)
        S.op("dve", lambda e: e.tensor_tensor(out=xs[:], in0=psA[bb][:, 0:TB], in1=rope_s[:], op=ALU.mult), reads=[tpsA[bb], trope], writes=[txs])
        ckpt(154)
        if split:
            S.op("dve", lambda e: e.tensor_tensor(out=dst_ap, in0=xs[:].rearrange("p (j q) -> p j q", q=128),
                                                  in1=xc[:].rearrange("p (j q) -> p j q", q=128), op=ALU.add), reads=[txs, txc], writes=dst_tiles)
        else:
            S.op("dve", lambda e: e.tensor_tensor(out=dst_ap, in0=xs[:], in1=xc[:], op=ALU.add), reads=[txs, txc], writes=dst_tiles)

    def colnorm_max(src_ap, src_tiles, kcol, ncols=TB):
        S.op("act", lambda e: e.activation(out=sqf[:, 0:ncols], in_=src_ap, func=AF.Square), reads=src_tiles, writes=[tsqf])
        S.op("pe", lambda e: e.matmul(psS[:, 0:ncols], ones32[:], sqf[:, 0:ncols], start=True, stop=True), reads=[tsqf, tones32], writes=[tpsS])
        S.op("dve", lambda e: e.tensor_reduce(out=kred[:], in_=psS[:, 0:ncols], axis=AX.X, op=ALU.max), reads=[tpsS], writes=[tkred])
        S.op("dve", lambda e: e.tensor_tensor(out=kmax2[:, kcol:kcol + 1], in0=kmax2[:, kcol:kcol + 1], in1=kred[:], op=ALU.max),
             reads=[tkmax2, tkred], writes=[tkmax2])

    nB = 0
    try:
        ckpt(11)
    except _Stop:
        return nc
    for tb in range(NBLK):
        tsl = slice(tb * TB, (tb + 1) * TB)
        try:
            norm_block(tsl)
            ckpt(12)
            load_rope("cosk", "sink", tsl)
            if tb == 0:
                S.op("pool", lambda e: e.dma_start(out=W[:], in_=wkv_d[:, :, 0:1024]), writes=[tW], chan=cW)
                S.op("pool", lambda e: e.dma_start(out=WT[:], in_=wkv_d[:, :, 1024:1536]), writes=[tWT], chan=cW)
            ckpt(13)
        except _Stop:
            return nc
        for slot in range(6):
            br, gl = slot // 2, slot % 2
            b = proj_fm(slot * 128)
            try:
                ckpt(14)
            except _Stop:
                return nc
            if br == 0:
                rope_to(b, cmpin[:, 0, gl, tsl], [tcmpin[0][gl]])
                try:
                    ckpt(15)
                except _Stop:
                    return nc
            else:
                rope_to(b, kTs[:, br - 1, gl, tsl], [tkTs[br - 1][gl][tb]])
                colnorm_max(kTs[:, br - 1, gl, tsl], [tkTs[br - 1][gl][tb]], 2 + (br - 1) * 2 + gl)
                try:
                    ckpt(16)
                except _Stop:
                    return nc
        for gl in range(2):
            b = proj_fm((6 + gl) * 128)
            S.op("act", lambda e, b=b, gl=gl, tsl=tsl: e.copy(out=cmpin[:, 1, gl, tsl], in_=psA[b][:, 0:TB]), reads=[tpsA[b]], writes=[tcmpin[1][gl]])
        for j in range(TPB):
            Tq = tb * TPB + j
            jsl = slice(j * 128, (j + 1) * 128)
            b = nB % 2
            nB += 1
            for c in range(DC):
                S.op("pe", lambda e, c=c, b=b, jsl=jsl: e.matmul(psB[b][:], hn[:, c, jsl], WT[:, c, :], start=(c == 0), stop=(c == DC - 1)),
                     reads=[tWT, thn[c][0]], writes=[tpsB[b]])
            S.op("act", lambda e, b=b, Tq=Tq: e.copy(out=vsel[:, Tq, :, 0:128], in_=psB[b][:, 0:256].rearrange("p (g d) -> p g d", g=2)),
                 reads=[tpsB[b]], writes=[tvsel[Tq]])
            S.op("dve", lambda e, b=b, Tq=Tq: e.tensor_copy(out=vwin[:, Tq, :, 0:128], in_=psB[b][:, 256:512].rearrange("p (g d) -> p g d", g=2)),
                 reads=[tpsB[b]], writes=[tvwin[Tq]])

    if stop == 1:
        S.op("pool", lambda e: e.dma_start(out=y_d[0:128, 0:256], in_=h_sb[:, 0, :]), reads=[th[0][0]], chan=cmisc)
        S.emit(final_waits=[cmisc])
        C.st.close()
        return nc
    Wflat = W[:].rearrange("p a b -> p (a b)")
    w1 = Wflat[:, 0:8192].rearrange("p (k l o) -> p k l o", k=2, l=32)
    w2 = Wflat[:, 8192:8448].rearrange("p (k o) -> p k o", k=2)
    posb = Wflat[:, 8448:8512].rearrange("p (k l) -> p k l", k=2)
    S.op("pool", lambda e: e.dma_start(out=w1, in_=w1_d), writes=[tW], chan=cW)
    S.op("pool", lambda e: e.dma_start(out=w2, in_=w2_d), writes=[tW], chan=cW)
    S.op("pool", lambda e: e.dma_start(out=posb, in_=pos_d), writes=[tW], chan=cW)
    S.op("pool", lambda e: e.dma_start(out=vcmp[:, 0, :, 129:193], in_=tabs["ov"]), writes=[tvcmp], chan=cmisc)
    S.op("pool", lambda e: e.dma_start(out=vcmp[:, 1, :, 129:193], in_=tabs["ov"]), writes=[tvcmp], chan=cmisc)
    bias1 = C.sb([128, 2], F32, "bias1")
    xg, t1, t2 = rstd, xc, xs
    H1g = C.sb([128, CH * 128], BF16, "H1g")
    tbias1, tH1g = T_(), T_()
    txg, tt1, tt2 = trstd, txc, txs
    NCP = NCMP
    for kv in range(2):
        for l in range(32):
            S.op("pe", lambda e, kv=kv, l=l: e.matmul(psS[:, kv:kv + 1], w1[:, kv, l, :], posb[:, kv, l:l + 1], start=(l == 0), stop=(l == 31)),
                 reads=[tW], writes=[tpsS])
        S.op("dve", lambda e, kv=kv: e.tensor_copy(out=bias1[:, kv:kv + 1], in_=psS[:, kv:kv + 1]), reads=[tpsS], writes=[tbias1])
    S.op("dve", lambda e: e.memset(H1g[:], 0.0), writes=[tH1g])
    for kv in range(2):
        for gl in range(2):
            b = nA[0] % 2
            nA[0] += 1
            for l in range(32):
                S.op("pe", lambda e, kv=kv, gl=gl, l=l, b=b: e.matmul(psA[b][:, 0:NCP], w1[:, kv, l, :],
                                                                     cmpin[:, kv, gl, l:l + 16 * (NCP - 1) + 1:16],
                                                                     start=(l == 0), stop=(l == 31)),
                     reads=[tW, tcmpin[kv][gl]], writes=[tpsA[b]])
            for c0 in range(0, NCP, 256):
                n = min(256, NCP - c0)
                S.op("dve", lambda e, b=b, kv=kv, c0=c0, n=n: e.tensor_scalar(out=xg[:, 0:n], in0=psA[b][:, c0:c0 + n], scalar1=bias1[:, kv:kv + 1],
                                                                            scalar2=None, op0=ALU.add), reads=[tpsA[b], tbias1], writes=[txg])
                emit_gelu(S, H1g[:, c0:c0 + n], xg[:, 0:n], t1[:, 0:n], t2[:, 0:n], [txg], [tH1g], tt1, tt2)
            if kv == 0:
                bb = nA[0] % 2
                nA[0] += 1
                S.op("pe", lambda e, bb=bb: e.matmul(psA[bb][:, 0:NCP], w2[:, 0, :], H1g[:, 0:NCP], start=True, stop=True),
                     reads=[tW, tH1g], writes=[tpsA[bb]])
                S.op("act", lambda e, bb=bb, gl=gl: e.copy(out=kcmpT[:, gl, 0:NCP], in_=psA[bb][:, 0:NCP]), reads=[tpsA[bb]], writes=[tkcmpT])
                colnorm_max(kcmpT[:, gl, 0:NCP], [tkcmpT], gl, ncols=NCP)
            else:
                for ch in range(CH):
                    bb = nA[0] % 2
                    nA[0] += 1
                    S.op("pe", lambda e, bb=bb, ch=ch: e.matmul(psA[bb][:, 0:128], H1g[:, ch * 128:(ch + 1) * 128], w2[:, 1, :], start=True, stop=True),
                         reads=[tW, tH1g], writes=[tpsA[bb]])
                    S.op("act", lambda e, bb=bb, gl=gl, ch=ch: e.copy(out=vcmp[:, gl, ch, 0:128], in_=psA[bb][:, 0:128]),
                         reads=[tpsA[bb]], writes=[tvcmp])

    if stop == 2:
        S.op("pool", lambda e: e.dma_start(out=y_d[0:128, 0:256], in_=h_sb[:, 0, :]), reads=[th[0][0]], chan=cmisc)
        S.emit(final_waits=[cmisc])
        C.st.close()
        return nc
    S.op("pool", lambda e: e.dma_start(out=W[:], in_=wq_d), writes=[tW], chan=cW)
    qT = C.sb([128, TPB, 8, 128], BF16, "qT")
    tqT = [T_() for _ in range(8)]
    gsb = [C.sb([128, 24], F32, "gsb") for _ in range(2)]
    tgsb = [T_(), T_()]
    cmk = [C.sb([128, CH, 4, 128], BF16, "cmk") for _ in range(2)]
    tcmk = [T_(), T_()]
    ccmk = [S.chan(), S.chan()]
    fpt = [C.sb([128, 64], F32, "fpos") for _ in range(2)]
    fnt = [C.sb([128, 64], F32, "fneg") for _ in range(2)]
    tfp = [T_(), T_()]
    cfp = [S.chan(), S.chan()]
    P = [C.sb([128, 512], BF16, "P") for _ in range(3)]
    tP = [T_() for _ in range(3)]
    qsq = C.sb([128, 512], F32R, "qsq")
    tqsq = T_()
    mq2 = C.sb([128, 1], F32, "mq2")
    nbias = C.sb([128, 3], F32, "nbias")
    tmq2, tnbias = T_(), T_()
    rz = C.sb([128, 4], F32, "rz")
    coef = C.sb([128, 4], F32, "coef")
    trz, tcoef = T_(), T_()
    imp = C.sb([128, 64], F32, "imp")
    imp2 = C.sb([128, 64], F32, "imp2")
    mx8 = C.sb([128, 8], F32, "mx8")
    mx8b = C.sb([128, 8], F32, "mx8b")
    negblk = C.sb([128, 64], F32, "negblk")
    negblkT = C.sb([64, 4, 128], BF16, "negblkT")
    timp, timp2, tmx8, tmx8b, tnegblk, tnegblkT = [T_() for _ in range(6)]
    yt0 = C.sb([128, 1024], F32, "yt")
    yt = [yt0, yt0]
    tyth = [T_() for _ in range(8)]
    cy = [S.chan(), S.chan()]
    nP = [0]
    nSc = [0]

    def score(mm_list, br):
        b = nSc[0] % 3
        nSc[0] += 1
        n = len(mm_list)
        for i, (lhsT, rhs, rd) in enumerate(mm_list):
            S.op("pe", lambda e, lhsT=lhsT, rhs=rhs, i=i, b=b: e.matmul(psB[b][:], lhsT, rhs, start=(i == 0), stop=(i == n - 1)),
                 reads=rd, writes=[tpsB[b]])
        p = nP[0] % 3
        nP[0] += 1
        S.op("act", lambda e, b=b, p=p, br=br: e.activation(out=P[p][:], in_=psB[b][:], func=AF.Exp, bias=nbias[:, br:br + 1], scale=1.0),
             reads=[tpsB[b], tnbias], writes=[tP[p]])
        return p

    oset = [0]

    def pv(p, rhs_v, first, last, width):
        for r in range(4):
            bk = oset[0] * 2 + r // 2
            off = (r % 2) * 256
            S.op("pe", lambda e, r=r, p=p, bk=bk, off=off: e.matmul(psO[bk][:, off:off + width], P[p][:, r * 128:(r + 1) * 128], rhs_v[0],
                                                                  start=(first and r % 2 == 0), stop=last),
                 reads=[tP[p]] + rhs_v[1], writes=[tpsO[bk]])

    def evac(br, gl, kk, width, first_branch, with_imp=False):
        first_branch = (br == branches[0])
        bks = [oset[0] * 2 + r // 2 for r in range(4)]
        offs = [(r % 2) * 256 for r in range(4)]
        oset[0] ^= 1
        for r in range(4):
            S.op("dve", lambda e, r=r: e.tensor_scalar(out=rz[:, r:r + 1], in0=psO[bks[r]][:, offs[r] + 128:offs[r] + 129], scalar1=1.0e-30, scalar2=None, op0=ALU.add),
                 reads=[tpsO[bks[r]]], writes=[trz])
        S.op("dve", lambda e: e.reciprocal(out=rz[:], in_=rz[:]), reads=[trz], writes=[trz])
        gc0 = br * 8 + gl * 4
        S.op("dve", lambda e, gc0=gc0, kk=kk: e.tensor_tensor(out=coef[:], in0=rz[:], in1=gsb[kk][:, gc0:gc0 + 4], op=ALU.mult),
             reads=[trz, tgsb[kk]], writes=[tcoef])
        for r in range(4):
            o_ap = psO[bks[r]][:, offs[r]:offs[r] + 128]
            ysl = slice((gl * 4 + r) * 128, (gl * 4 + r + 1) * 128)
            if br not in branches:
                pass
            elif first_branch:
                S.op("dve", lambda e, r=r, kk=kk, ysl=ysl, o_ap=o_ap: e.tensor_scalar(out=yt[kk][:, ysl], in0=o_ap, scalar1=coef[:, r:r + 1], scalar2=None, op0=ALU.mult),
                     reads=[tpsO[bks[r]], tcoef], writes=[tyth[gl * 4 + r]])
            else:
                S.op("dve", lambda e, r=r, kk=kk, ysl=ysl, o_ap=o_ap: e.scalar_tensor_tensor(out=yt[kk][:, ysl], in0=o_ap, scalar=coef[:, r:r + 1], in1=yt[kk][:, ysl],
                                                                                         op0=ALU.mult, op1=ALU.add),
                     reads=[tpsO[bks[r]], tcoef, tyth[gl * 4 + r]], writes=[tyth[gl * 4 + r]])
        if with_imp:
            for r in range(4):
                i_ap = psO[bks[r]][:, offs[r] + 129:offs[r] + 193]
                if r == 0:
                    S.op("dve", lambda e, r=r, i_ap=i_ap: e.tensor_scalar(out=imp[:], in0=i_ap, scalar1=rz[:, r:r + 1], scalar2=None, op0=ALU.mult),
                         reads=[tpsO[bks[r]], trz], writes=[timp])
                else:
                    S.op("dve", lambda e, r=r, i_ap=i_ap: e.scalar_tensor_tensor(out=imp[:], in0=i_ap, scalar=rz[:, r:r + 1], in1=imp[:], op0=ALU.mult, op1=ALU.add),
                         reads=[tpsO[bks[r]], trz, timp], writes=[timp])

    ntile = 0
    for tb in range(NBLK):
        tsl = slice(tb * TB, (tb + 1) * TB)
        norm_block(tsl)
        load_rope("cosq", "sinq", tsl)
        for hd in range(8):
            b = proj_fm(hd * 128)
            rope_to(b, qT[:, :, hd, :], [tqT[hd]], split=True)
        for j in range(TPB):
            Tq = tb * TPB + j
            jsl = slice(j * 128, (j + 1) * 128)
            kk = ntile % 2
            ntile += 1
            for c in range(DC):
                S.op("pe", lambda e, c=c, jsl=jsl: e.matmul(psS[:, 0:24], hn[:, c, jsl], Wgt[:, c, :], start=(c == 0), stop=(c == DC - 1)),
                     reads=[tWgt, thn[c][0]], writes=[tpsS])
            S.op("act", lambda e, kk=kk: e.activation(out=gsb[kk][:], in_=psS[:, 0:24], func=AF.Sigmoid), reads=[tpsS], writes=[tgsb[kk]])
            for r4 in range(4):
                S.op("pool", lambda e, kk=kk, Tq=Tq, r4=r4: e.dma_start(out=cmk[kk][:, :, r4, :], in_=tabs["cmpmask"][Tq]), writes=[tcmk[kk]], chan=ccmk[kk])
            S.op("sp", lambda e, kk=kk, Tq=Tq: e.dma_start(out=fpt[kk][:], in_=tabs["fpos"][Tq]), writes=[tfp[kk]], chan=cfp[kk])
            S.op("sp", lambda e, kk=kk, Tq=Tq: e.dma_start(out=fnt[kk][:], in_=tabs["fneg"][Tq]), writes=[tfp[kk]], chan=cfp[kk])
            for gl in range(2):
                qv = qT[:, j, gl * 4:(gl + 1) * 4, :]
                qrd = [tqT[gl * 4 + r] for r in range(4)]
                S.op("act", lambda e, qv=qv: e.activation(out=qsq[:].rearrange("p (r q) -> p r q", r=4), in_=qv, func=AF.Square), reads=qrd, writes=[tqsq])
                S.op("pe", lambda e: e.matmul(psS[:, 0:512], ones_r[:], qsq[:], start=True, stop=True), reads=[tqsq, tones], writes=[tpsS])
                S.op("dve", lambda e: e.tensor_reduce(out=mq2[:], in_=psS[:, 0:512], axis=AX.X, op=ALU.max), reads=[tpsS], writes=[tmq2])
                S.op("dve", lambda e, gl=gl: e.tensor_scalar(out=nbias[:], in0=kmax2[:, gl:gl + 5:2], scalar1=mq2[:, 0:1], scalar2=None, op0=ALU.mult),
                     reads=[tkmax2, tmq2], writes=[tnbias])
                S.op("act", lambda e: e.activation(out=nbias[:], in_=nbias[:], func=AF.Sqrt), reads=[tnbias], writes=[tnbias])
                S.op("dve", lambda e: e.tensor_scalar(out=nbias[:], in0=nbias[:], scalar1=-1.0, scalar2=None, op0=ALU.mult), reads=[tnbias], writes=[tnbias])
                chunks = []
                chs = [ch for ch in range(CH) if 16 * (ch * 128) + 31 <= Tq * 128 + 127]
                for ci, ch in enumerate(chs):
                    mm = [(kcmpT[:, gl, ch * 128:(ch + 1) * 128], qv, [tkcmpT] + qrd),
                          (identb[:], cmk[kk][:, ch, :, :], [tidb, tcmk[kk]])]
                    chunks.append((0, mm, (vcmp[:, gl, ch, :], [tvcmp]), ci == 0, ci == len(chs) - 1, 193))
                k0 = max(0, Tq - 4)
                for kc in range(k0, Tq + 1):
                    mm = [(kTs[:, 1, gl, kc * 128:(kc + 1) * 128], qv, [tkTs[1][gl][kc // TPB]] + qrd)]
                    if kc == Tq:
                        mm.append((identb[:], causal[:], [tidb, tcausal]))
                    if kc == Tq - 4:
                        mm.append((identb[:], winneg[:], [tidb, twinneg]))
                    chunks.append((2, mm, (vwin[:, kc, gl, :], [tvwin[kc]]), kc == k0, kc == Tq, 129))
                for kc in range(Tq + 1):
                    mm = [(kTs[:, 0, gl, kc * 128:(kc + 1) * 128], qv, [tkTs[0][gl][kc // TPB]] + qrd),
                          (Eb[:, kc, :], negblkT[:], [tE, tnegblkT])]
                    if kc == Tq:
                        mm.append((identb[:], causal[:], [tidb, tcausal]))
                    chunks.append((1, mm, (vsel[:, kc, gl, :], [tvsel[kc]]), kc == 0, kc == Tq, 129))

                def topk_dve(kk=kk):
                    S.op("dve", lambda e: e.tensor_tensor(out=imp[:], in0=imp[:], in1=fpt[kk][:], op=ALU.max), reads=[timp, tfp[kk]], writes=[timp])
                    S.op("dve", lambda e: e.tensor_tensor(out=imp[:], in0=imp[:], in1=fnt[kk][:], op=ALU.min), reads=[timp, tfp[kk]], writes=[timp])
                    S.op("dve", lambda e: e.max(out=mx8[:], in_=imp[:]), reads=[timp], writes=[tmx8])
                    S.op("dve", lambda e: e.match_replace(out=imp2[:], in_to_replace=mx8[:], in_values=imp[:], imm_value=-3.0e38),
                         reads=[timp, tmx8], writes=[timp2])
                    S.op("dve", lambda e: e.max(out=mx8b[:], in_=imp2[:]), reads=[timp2], writes=[tmx8b])
                    S.op("dve", lambda e: e.tensor_reduce(out=kred[:], in_=mx8b[:], axis=AX.X, op=ALU.min), reads=[tmx8b], writes=[tkred])
                    S.op("dve", lambda e: e.tensor_scalar(out=negblk[:], in0=imp[:], scalar1=kred[:, 0:1], scalar2=None, op0=ALU.is_ge),
                         reads=[timp, tkred], writes=[tnegblk])
                    S.op("dve", lambda e: e.tensor_scalar(out=negblk[:], in0=negblk[:], scalar1=1.0e30, scalar2=-1.0e30, op0=ALU.mult, op1=ALU.add),
                         reads=[tnegblk], writes=[tnegblk])

                def topk_T():
                    S.op("pe", lambda e: e.transpose(psS[0:64, 0:128], negblk[:], ident[:]), reads=[tnegblk, tid], writes=[tpsS])
                    S.op("act", lambda e: e.copy(out=negblkT[:], in_=psS[0:64, 0:128][:, None, :].to_broadcast([64, 4, 128])), reads=[tpsS], writes=[tnegblkT])

                nch = len(chunks)
                first_sel = next(i for i, c in enumerate(chunks) if c[0] == 1)

                def do_score(i):
                    if i == first_sel:
                        topk_T()
                    return score(chunks[i][1], chunks[i][0])

                cmp_last = max(i for i, c in enumerate(chunks) if c[0] == 0)
                pend = []
                nxt = [0]

                def fill(i):
                    while nxt[0] < nch and nxt[0] <= i + 2:
                        if nxt[0] == first_sel and i <= cmp_last:
                            break
                        pend.append(do_score(nxt[0]))
                        nxt[0] += 1

                fill(-1)
                for i in range(nch):
                    br_i, _, rhs_v, first, last, width = chunks[i]
                    fill(i)
                    pv(pend.pop(0), rhs_v, first, last, width)
                    if last:
                        evac(br_i, gl, kk, width, br_i == 0, with_imp=(br_i == 0))
                        if br_i == 0:
                            topk_dve()
            S.op("sp", lambda e, kk=kk, Tq=Tq: e.dma_start(out=y_d[Tq * 128:(Tq + 1) * 128, :], in_=yt[kk][:]), reads=tyth, chan=cy[kk])
    S.emit(final_waits=cy)
    C.st.close()
    return nc


def _lay(w):
    dc = w.shape[0] // 128
    return np.ascontiguousarray(w.reshape(dc, 128, -1).transpose(1, 0, 2))


def _gT(g):
    return np.ascontiguousarray(np.asarray(g, np.float32).reshape(-1, 128).T)


def _prep_ffn_w(Wg, Wu, Wd):
    D, FF = Wg.shape
    DC, FC = D // 128, FF // 128
    t = lambda W: np.ascontiguousarray(W.reshape(DC, 128, FC, 128).transpose(2, 1, 0, 3).reshape(FC, 128, DC * 128))
    wd = np.ascontiguousarray(Wd.reshape(FC, 128, DC, 128).transpose(2, 1, 0, 3).reshape(DC, 128, FC * 128))
    return t(Wg), t(Wu), wd


def _prep_wo(Wo):
    DC = Wo.shape[0] // 128
    DO = Wo.shape[1] // 128
    return np.ascontiguousarray(Wo.reshape(DC, 128, DO, 128).transpose(2, 1, 0, 3).reshape(DO, 128, DC * 128))


def _prep_ab(w_in, gate_b, g_norm, g_ws, g_bs, conv_w, m_norm, hp):
    o1, o2, o3, o4 = 2048, 4096, 5120, 6144
    wz = _lay(w_in[:, :o1])
    hs = slice(hp * 512, (hp + 1) * 512)
    wqk = _lay(np.concatenate([w_in[:, o1:o1 + 1024][:, hs], w_in[:, o1 + 1024:o2][:, hs]], 1))
    gi = w_in[:, o4:o4 + 4][:, hp * 2:hp * 2 + 2]
    gf = w_in[:, o4 + 4:o4 + 8][:, hp * 2:hp * 2 + 2]
    wvog = _lay(np.concatenate([w_in[:, o2:o3][:, hs], w_in[:, o3:o4][:, hs], gi, gf], 1))
    gb = np.concatenate([gate_b[0, hp * 2:hp * 2 + 2], gate_b[1, hp * 2:hp * 2 + 2]])
    gb = np.ascontiguousarray(np.broadcast_to(gb[None, :], (128, 4))).astype(np.float32)
    cc = np.concatenate([conv_w[:, :1024][:, hs], conv_w[:, 1024:][:, hs]], 1)
    conv = np.ascontiguousarray(cc.T.reshape(8, 128, 4).transpose(1, 0, 2))
    mn = np.ascontiguousarray(np.broadcast_to(m_norm[hs][None, :], (128, 512)))
    gn = np.ascontiguousarray(np.broadcast_to(g_norm[None, :], (128, 1024)))
    wsT = np.ascontiguousarray(g_ws.transpose(2, 0, 1))
    bs = np.ascontiguousarray(g_bs.T)
    ident = np.eye(128, dtype=np.float32)
    mask = np.triu(np.ones((128, 128), np.float32))
    return dict(wz=wz, wqk=wqk, wvog=wvog, gb=gb, conv=conv, mn=mn, gn=gn, wsT=wsT, bs=bs, ident=ident, mask=mask)


def _prep_nsa(w_in, cmp_pos, cmp_w1, cmp_w2, hp):
    wq = _lay(w_in[:, hp * 1024:(hp + 1) * 1024])

    def kvcol(br, kv, g):
        o = 2048 + ((br * 2 + kv) * 4 + g) * 128
        return w_in[:, o:o + 128]
    cols = []
    for br in range(3):
        for gl in range(2):
            cols.append(kvcol(br, 0, 2 * hp + gl))
    for gl in range(2):
        cols.append(kvcol(0, 1, 2 * hp + gl))
    for br in (1, 2):
        for gl in range(2):
            cols.append(kvcol(br, 1, 2 * hp + gl))
    wkv = _lay(np.concatenate(cols, 1))
    og = 2048 + 3072
    gc = []
    for br in range(3):
        for gl in range(2):
            g = 2 * hp + gl
            gc.append(w_in[:, og + br * 16 + g * 4: og + br * 16 + g * 4 + 4])
    wgt = _lay(np.concatenate(gc, 1))
    w1 = np.ascontiguousarray(cmp_w1.reshape(2, 32, 128, 128).transpose(2, 0, 1, 3))
    w2 = np.ascontiguousarray(cmp_w2.transpose(1, 0, 2))
    posT = np.ascontiguousarray(cmp_pos.transpose(2, 0, 1))
    return dict(wq=wq, wkv=wkv, wgt=wgt, w1=w1, w2=w2, posT=posT)


_PROGS = {}


def _prog(key, fn):
    if key not in _PROGS:
        _PROGS[key] = fn()
    return _PROGS[key]


def _run(nc, maps):
    res = run_bass_kernel_spmd(nc, maps, core_ids=list(range(8)))
    return res.results


def kernel(x, ffn_norm, ffn_w_gate, ffn_w_up, ffn_w_down, mix_norm, ab_w_in, mlstm_gate_bias,
           gmlp_norm, gmlp_w_s, gmlp_b_s, mlstm_conv, mlstm_norm, ab_w_out, nsa_w_in, nsa_cmp_pos,
           nsa_cmp_w1, nsa_cmp_w2, nsa_w_out, final_norm):
    f = lambda a: np.asarray(a, dtype=np.float32)
    x = f(x)
    B, Sq, D = x.shape
    FF = ffn_w_gate.shape[-1]
    NTC = B * Sq // 8
    hT = [np.ascontiguousarray(x[c // 2, (c % 2) * NTC:(c % 2 + 1) * NTC, :].T) for c in range(8)]

    def ffn(hT, layer, which, yT=None, wo=None, final=False):
        wg, wu, wd = _prep_ffn_w(f(ffn_w_gate[layer, which]), f(ffn_w_up[layer, which]), f(ffn_w_down[layer, which]))
        g = _gT(ffn_norm[layer, which])
        pre = yT is not None
        nc = _prog(("ffn", pre, final), lambda: build_ffn(D, FF, NTC, 1024, 2, pre=pre, final=final))
        maps = []
        for c in range(8):
            m = {"hT": hT[c], "g": g, "wg": wg, "wu": wu, "wd": wd}
            if pre:
                m["yT"] = yT[c]
                m["wo"] = wo
            if final:
                m["gf"] = _gT(final_norm)
            maps.append(m)
        r = _run(nc, maps)
        return [r[c]["oT"] for c in range(8)]

    def full_seq(hT, b):
        return np.ascontiguousarray(np.concatenate([hT[2 * b], hT[2 * b + 1]], axis=1))

    hT = ffn(hT, 0, 0)
    nc = _prog(("ab",), lambda: build_ab(Sq, D))
    maps = []
    for c in range(8):
        b, hp = c // 2, c % 2
        m = _prep_ab(f(ab_w_in[0]), f(mlstm_gate_bias[0]), f(gmlp_norm[0]), f(gmlp_w_s[0]), f(gmlp_b_s[0]), f(mlstm_conv[0]),
                     f(mlstm_norm[0]), hp)
        m["hT"] = full_seq(hT, b)
        m["hTh"] = hT[c]
        m["g"] = _gT(mix_norm[0])
        maps.append(m)
    r = _run(nc, maps)
    yT = []
    for c in range(8):
        b, hp = c // 2, c % 2
        sl = slice(hp * NTC, (hp + 1) * NTC)
        ya = r[c]["ya"]
        yb = np.concatenate([r[2 * b]["yb"][sl], r[2 * b + 1]["yb"][sl]], axis=1)
        yT.append(np.ascontiguousarray(np.concatenate([ya, yb], axis=1).T))
    hT = ffn(hT, 0, 1, yT=yT, wo=_prep_wo(f(ab_w_out[0])))
    hT = ffn(hT, 1, 0)
    nc = _prog(("nsa",), lambda: build_nsa(Sq, D))
    tabs = nsa_tables(Sq)
    maps = []
    for c in range(8):
        b, hp = c // 2, c % 2
        m = _prep_nsa(f(nsa_w_in[0]), f(nsa_cmp_pos[0]), f(nsa_cmp_w1[0]), f(nsa_cmp_w2[0]), hp)
        m.update(tabs)
        m["hT"] = full_seq(hT, b)
        m["g"] = _gT(mix_norm[1])
        maps.append(m)
    r = _run(nc, maps)
    yT = []
    for c in range(8):
        b, hp = c // 2, c % 2
        sl = slice(hp * NTC, (hp + 1) * NTC)
        y = np.concatenate([r[2 * b]["y"][sl], r[2 * b + 1]["y"][sl]], axis=1)
        yT.append(np.ascontiguousarray(y.T))
    hT = ffn(hT, 1, 1, yT=yT, wo=_prep_wo(f(nsa_w_out[0])), final=True)
    out = np.empty((B, Sq, D), np.float32)
    for c in range(8):
        out[c // 2, (c % 2) * NTC:(c % 2 + 1) * NTC, :] = hT[c].T
    return out
```

```python
import bisect
import contextlib
import numpy as np
import concourse.bass as bass
import concourse.mybir as mybir
from concourse.bass_utils import run_bass_kernel_spmd

F32 = mybir.dt.float32
F32R = mybir.dt.float32r
BF16 = mybir.dt.bfloat16
AF = mybir.ActivationFunctionType
ALU = mybir.AluOpType
AX = mybir.AxisListType

ENGS = ("pe", "act", "dve", "pool", "sp")


class Tile:
    __slots__ = ("name", "w", "r", "psum")

    def __init__(self, name="", psum=False):
        self.name = name
        self.w = None
        self.r = []
        self.psum = psum


class Chan:
    __slots__ = ("sem", "cnt", "ops")

    def __init__(self):
        self.sem = None
        self.cnt = 0
        self.ops = []


class Sched:
    def __init__(self, nc):
        self.nc = nc
        self.ops = []
        self.chans = []

    def chan(self):
        c = Chan()
        self.chans.append(c)
        return c

    def op(self, eng, fn, reads=(), writes=(), chan=None, nosync_same=False):
        idx = len(self.ops)
        deps = set()
        for t in reads:
            if t.w is not None:
                deps.add(t.w)
            if t.psum:
                for r in t.r:
                    if self.ops[r]["eng"] != eng:
                        deps.add(r)
        for t in writes:
            if t.w is not None:
                deps.add(t.w)
            for r in t.r:
                deps.add(r)
        deps.discard(idx)
        rec = dict(eng=eng, fn=fn, deps=deps, chan=chan, idx=idx, nosync_same=nosync_same,
                   sig=None)
        if chan is not None:
            chan.cnt += 16
            rec["sig"] = (chan, chan.cnt)
            chan.ops.append(idx)
        self.ops.append(rec)
        for t in reads:
            t.r.append(idx)
        for t in writes:
            t.w = idx
            t.r = []
        return idx

    def emit(self, final_waits=()):
        nc = self.nc
        ops = self.ops
        needed = set()
        for o in ops:
            for d in o["deps"]:
                od = ops[d]
                if od["chan"] is None:
                    if od["eng"] == o["eng"] and (o["nosync_same"] or od["eng"] == "pe" or od["eng"] == "sp"):
                        continue
                    needed.add(d)
        cnt = {e: 0 for e in ENGS}
        for o in ops:
            if o["chan"] is None and o["idx"] in needed:
                cnt[o["eng"]] += 1
                o["sig"] = (o["eng"], cnt[o["eng"]])
        with contextlib.ExitStack() as st:
            esem = {e: st.enter_context(nc.semaphore("s_" + e)) for e in ENGS}
            for i, c in enumerate(self.chans):
                c.sem = st.enter_context(nc.semaphore("c%d" % i))
            block = st.enter_context(nc.Block())

            def run_engine(ename, eobj):
                known = {}
                for o in ops:
                    if o["eng"] != ename:
                        continue
                    want = {}
                    for d in o["deps"]:
                        od = ops[d]
                        sig = od["sig"]
                        if sig is None:
                            continue
                        key, val = sig
                        if od["chan"] is None and od["eng"] == ename and (
                                o["nosync_same"] or ename in ("pe", "sp")):
                            continue
                        if isinstance(key, Chan):
                            val = 16 * bisect.bisect_left(key.ops, o["idx"])
                        if want.get(key, 0) < val:
                            want[key] = val
                    for key, val in want.items():
                        if known.get(key, 0) >= val:
                            continue
                        known[key] = val
                        sem = key.sem if isinstance(key, Chan) else esem[key]
                        eobj.wait_ge(sem, val)
                    ins = o["fn"](eobj)
                    if o["sig"] is not None:
                        key, val = o["sig"]
                        if isinstance(key, Chan):
                            ins.then_inc(key.sem, 16)
                        else:
                            ins.then_inc(esem[key], 1)
                if ename == "sp":
                    for c in final_waits:
                        eobj.wait_ge(c.sem, c.cnt)

            @block.tensor
            def _(e):
                run_engine("pe", e)

            @block.scalar
            def _(e):
                run_engine("act", e)

            @block.vector
            def _(e):
                run_engine("dve", e)

            @block.gpsimd
            def _(e):
                run_engine("pool", e)

            @block.sync
            def _(e):
                run_engine("sp", e)


class Ctx:
    def __init__(self):
        self.nc = bass.Bass("TRN2", target_bir_lowering=False)
        self.st = contextlib.ExitStack()
        self.S = Sched(self.nc)
        self.n = 0

    def sb(self, shape, dt, name=None):
        self.n += 1
        return self.st.enter_context(self.nc.sbuf_tensor("%s_%d" % (name or "sb", self.n), list(shape), dt))

    def ps(self, shape=(128, 512), dt=F32, name=None):
        self.n += 1
        return self.st.enter_context(self.nc.psum_tensor("%s_%d" % (name or "ps", self.n), list(shape), dt))

    def din(self, name, shape, dt=F32):
        return self.nc.dram_tensor(name, list(shape), dt, kind="ExternalInput").ap()

    def dout(self, name, shape, dt=F32):
        return self.nc.dram_tensor(name, list(shape), dt, kind="ExternalOutput").ap()


EPS = 1e-6


def emit_rstd(S, rstd, trstd, ps_ssq, tssq, Dn, SUB=512):
    S.op("act", lambda e: e.activation(out=rstd[:, 0:SUB], in_=ps_ssq[:, 0:SUB], func=AF.Sqrt, bias=EPSB[0][:, 0:1], scale=1.0 / Dn),
         reads=[tssq, EPSB[1]], writes=[trstd])
    S.op("dve", lambda e: e.reciprocal(out=rstd[:, 0:SUB], in_=rstd[:, 0:SUB]), reads=[trstd], writes=[trstd])


EPSB = [None, None]


def emit_consts(C):
    S = C.S
    eps = C.sb([128, 1], F32, "eps")
    teps = Tile()
    S.op("dve", lambda e: e.memset(eps[:], EPS), writes=[teps])
    EPSB[0], EPSB[1] = eps, teps
    ones32 = C.sb([128, 128], F32, "ones32")
    ones_r = C.sb([128, 128], F32R, "ones")
    t32, tones = Tile(), Tile()
    S.op("dve", lambda e: e.memset(ones32[:], 1.0), writes=[t32])
    S.op("dve", lambda e: e.tensor_copy(out=ones_r[:], in_=ones32[:]), reads=[t32], writes=[tones])
    return ones_r, tones


def emit_rmsnorm_T(C, h_sb, th, aT, taT, g_sb, tg, ones_r, tones, DC, TB, sq, tsq, ps_ssq, tssq, rstd, trstd,
                   Dn, out_dt_cast=None):
    S = C.S
    SUB = min(512, TB)
    NS = TB // SUB
    for s in range(NS):
        sl = slice(s * SUB, (s + 1) * SUB)
        for c in range(DC):
            k = (s * DC + c) % 2
            S.op("act", lambda e, c=c, k=k, sl=sl: e.activation(out=sq[k][:, 0:SUB], in_=h_sb[:, c, sl], func=AF.Square),
                 reads=[th[c][s]], writes=[tsq[k]])
            S.op("pe", lambda e, c=c, k=k: e.matmul(ps_ssq[:, 0:SUB], ones_r[:], sq[k][:, 0:SUB], start=(c == 0), stop=(c == DC - 1)),
                 reads=[tsq[k], tones], writes=[tssq])
        emit_rstd(S, rstd, trstd, ps_ssq, tssq, Dn, SUB)
        for c in range(DC):
            S.op("dve", lambda e, c=c, sl=sl: e.scalar_tensor_tensor(out=aT[:, c, sl], in0=h_sb[:, c, sl],
                                                                      scalar=g_sb[:, c:c + 1], in1=rstd[:, 0:SUB],
                                                                      op0=ALU.mult, op1=ALU.mult),
                 reads=[th[c][s], tg, trstd], writes=[taT[c][s]])


def build_ffn(D, FF, NT, TB, NH, pre=False, final=False):
    C = Ctx()
    nc, S = C.nc, C.S
    DC, FC = D // 128, FF // 128
    FH = FC // NH
    NS = TB // 512
    NB = NT // TB
    hT = C.din("hT", [D, NT])
    g_d = C.din("g", [128, DC])
    wg_d = C.din("wg", [FC, 128, DC * 128])
    wu_d = C.din("wu", [FC, 128, DC * 128])
    wd_d = C.din("wd", [DC, 128, FC * 128])
    if pre:
        yT = C.din("yT", [D, NT])
        wo_d = C.din("wo", [DC, 128, DC * 128])
    if final:
        gf_d = C.din("gf", [128, DC])
    oT = C.dout("oT", [D, NT])
    hTv = hT.rearrange("(c p) t -> p c t", p=128)
    oTv = oT.rearrange("(c p) t -> p c t", p=128)

    h_sb = C.sb([128, DC, TB], F32, "h")
    aT = C.sb([128, DC, TB], BF16, "aT")
    HT = C.sb([128, FH, TB], BF16, "HT")
    wg = [C.sb([128, DC * 128], BF16, "wg") for _ in range(2)]
    wu = [C.sb([128, DC * 128], BF16, "wu") for _ in range(2)]
    wd = [C.sb([128, FH * 128], BF16, "wd") for _ in range(2)]
    sq = [C.sb([128, 512], F32R, "sq") for _ in range(2)]
    sg = [C.sb([128, 512], F32, "sg") for _ in range(2)]
    rstd = C.sb([128, 512], F32, "rstd")
    g_sb = C.sb([128, DC], F32, "g")
    ps_ssq = C.ps(name="ssq")
    psG = [C.ps(name="G") for _ in range(2)]
    psU = [C.ps(name="U") for _ in range(2)]
    psY = [C.ps(name="Y") for _ in range(2)]
    if pre:
        wo = [C.sb([128, DC * 128], BF16, "wo") for _ in range(2)]
        two = [Tile() for _ in range(2)]
        cwo = [S.chan() for _ in range(2)]
    if final:
        gf_sb = C.sb([128, DC], F32, "gf")
        tgf = Tile()
        fo = C.sb([128, DC, TB], F32, "fo") if False else None

    th = [[Tile() for _ in range(NS)] for _ in range(DC)]
    taT = [[Tile() for _ in range(NS)] for _ in range(DC)]
    tHT = [[Tile() for _ in range(NS)] for _ in range(FH)]
    twg = [Tile() for _ in range(2)]
    twu = [Tile() for _ in range(2)]
    twd = [Tile() for _ in range(2)]
    tsq = [Tile() for _ in range(2)]
    tsg = [Tile() for _ in range(2)]
    trstd, tg, tssq = Tile(), Tile(), Tile()
    tG = [Tile(psum=True) for _ in range(2)]
    tU = [Tile(psum=True) for _ in range(2)]
    tY = [Tile(psum=True) for _ in range(2)]
    cwg = [S.chan() for _ in range(2)]
    cwu = [S.chan() for _ in range(2)]
    cwd = [S.chan() for _ in range(2)]
    ch_in = S.chan()
    ch_out = S.chan()
    cg = S.chan()

    S.op("sp", lambda e: e.dma_start(out=g_sb[:], in_=g_d), writes=[tg], chan=cg)
    if final:
        cgf = S.chan()
        S.op("sp", lambda e: e.dma_start(out=gf_sb[:], in_=gf_d), writes=[tgf], chan=cgf)
    ones_r, tones = emit_consts(C)

    all_h = [th[c][s] for c in range(DC) for s in range(NS)]
    all_aT = [taT[c][s] for c in range(DC) for s in range(NS)]
    CG = min(4, DC)
    nwd = 0
    nwgu = 0
    for tb in range(NB):
        tsl = slice(tb * TB, (tb + 1) * TB)
        early = not final
        if tb == 0 or not early:
            for c0 in range(0, DC, CG):
                S.op("sp", lambda e, c0=c0, tsl=tsl: e.dma_start(out=h_sb[:, c0:c0 + CG, :], in_=hTv[:, c0:c0 + CG, tsl]),
                     writes=[th[c][s] for c in range(c0, c0 + CG) for s in range(NS)], chan=ch_in)
        if pre:
            for c0 in range(0, DC, CG):
                yv = yT.rearrange("(c p) t -> p c t", p=128)
                S.op("pool", lambda e, c0=c0, tsl=tsl, yv=yv: e.dma_start(out=aT[:, c0:c0 + CG, :], in_=yv[:, c0:c0 + CG, tsl]),
                     writes=[taT[c][s] for c in range(c0, c0 + CG) for s in range(NS)], chan=ch_in)
            for dc in range(DC):
                k = dc % 2
                S.op("pool", lambda e, dc=dc, k=k: e.dma_start(out=wo[k][:], in_=wo_d[dc]), writes=[two[k]], chan=cwo[k])
                for s in range(NS):
                    sl = slice(s * 512, (s + 1) * 512)
                    b = (dc * NS + s) % 2
                    for c in range(DC):
                        S.op("pe", lambda e, c=c, k=k, b=b, sl=sl: e.matmul(psY[b][:], wo[k][:, c * 128:(c + 1) * 128], aT[:, c, sl],
                                                                          start=(c == 0), stop=(c == DC - 1)),
                             reads=[two[k], taT[c][s]], writes=[tY[b]])
                    S.op("dve", lambda e, dc=dc, b=b, sl=sl: e.tensor_tensor(out=h_sb[:, dc, sl], in0=psY[b][:], in1=h_sb[:, dc, sl], op=ALU.add),
                         reads=[tY[b], th[dc][s]], writes=[th[dc][s]])
        emit_rmsnorm_T(C, h_sb, th, aT, taT, g_sb, tg, ones_r, tones, DC, TB, sq, tsq, ps_ssq, tssq, rstd, trstd, D)
        for hf in range(NH):
            for fi in range(FH):
                f = hf * FH + fi
                k = nwgu % 2
                nwgu += 1
                S.op("pool", lambda e, f=f, k=k: e.dma_start(out=wg[k][:], in_=wg_d[f]), writes=[twg[k]], chan=cwg[k])
                S.op("pool", lambda e, f=f, k=k: e.dma_start(out=wu[k][:], in_=wu_d[f]), writes=[twu[k]], chan=cwu[k])
                for s in range(NS):
                    sl = slice(s * 512, (s + 1) * 512)
                    b = (fi * NS + s) % 2
                    for c in range(DC):
                        S.op("pe", lambda e, c=c, k=k, b=b, sl=sl: e.matmul(psG[b][:], wg[k][:, c * 128:(c + 1) * 128], aT[:, c, sl],
                                                                          start=(c == 0), stop=(c == DC - 1)),
                             reads=[twg[k], taT[c][s]], writes=[tG[b]])
                    for c in range(DC):
                        S.op("pe", lambda e, c=c, k=k, b=b, sl=sl: e.matmul(psU[b][:], wu[k][:, c * 128:(c + 1) * 128], aT[:, c, sl],
                                                                          start=(c == 0), stop=(c == DC - 1)),
                             reads=[twu[k], taT[c][s]], writes=[tU[b]])
                    S.op("act", lambda e, b=b: e.activation(out=sg[b][:], in_=psG[b][:], func=AF.Silu),
                         reads=[tG[b]], writes=[tsg[b]])
                    S.op("dve", lambda e, b=b, fi=fi, sl=sl: e.tensor_tensor(out=HT[:, fi, sl], in0=psU[b][:], in1=sg[b][:], op=ALU.mult),
                         reads=[tU[b], tsg[b]], writes=[tHT[fi][s]])
            for dc in range(DC):
                k = nwd % 2
                nwd += 1
                S.op("pool", lambda e, dc=dc, hf=hf, k=k: e.dma_start(out=wd[k][:], in_=wd_d[dc, :, hf * FH * 128:(hf + 1) * FH * 128]),
                     writes=[twd[k]], chan=cwd[k])
                for s in range(NS):
                    sl = slice(s * 512, (s + 1) * 512)
                    b = (dc * NS + s) % 2
                    for fi in range(FH):
                        S.op("pe", lambda e, fi=fi, k=k, b=b, sl=sl: e.matmul(psY[b][:], wd[k][:, fi * 128:(fi + 1) * 128], HT[:, fi, sl],
                                                                            start=(fi == 0), stop=(fi == FH - 1)),
                             reads=[twd[k], tHT[fi][s]], writes=[tY[b]])
                    S.op("dve", lambda e, dc=dc, b=b, sl=sl: e.scalar_tensor_tensor(out=h_sb[:, dc, sl], in0=psY[b][:], scalar=0.5,
                                                                                 in1=h_sb[:, dc, sl], op0=ALU.mult, op1=ALU.add),
                         reads=[tY[b], th[dc][s]], writes=[th[dc][s]])
                if early and hf == NH - 1 and (dc + 1) % CG == 0:
                    c0 = dc + 1 - CG
                    S.op("sp", lambda e, c0=c0, tsl=tsl: e.dma_start(out=oTv[:, c0:c0 + CG, tsl], in_=h_sb[:, c0:c0 + CG, :]),
                         reads=[th[c][s] for c in range(c0, c0 + CG) for s in range(NS)], chan=ch_out)
                    if tb + 1 < NB:
                        nsl = slice((tb + 1) * TB, (tb + 2) * TB)
                        S.op("sp", lambda e, c0=c0, nsl=nsl: e.dma_start(out=h_sb[:, c0:c0 + CG, :], in_=hTv[:, c0:c0 + CG, nsl]),
                             writes=[th[c][s] for c in range(c0, c0 + CG) for s in range(NS)], chan=ch_in)
        if final:
            NSx = NS
            for s in range(NSx):
                sl = slice(s * 512, (s + 1) * 512)
                for c in range(DC):
                    k = (s * DC + c) % 2
                    S.op("act", lambda e, c=c, k=k, sl=sl: e.activation(out=sq[k][:], in_=h_sb[:, c, sl], func=AF.Square),
                         reads=[th[c][s]], writes=[tsq[k]])
                    S.op("pe", lambda e, c=c, k=k: e.matmul(ps_ssq[:], ones_r[:], sq[k][:], start=(c == 0), stop=(c == DC - 1)),
                         reads=[tsq[k], tones], writes=[tssq])
                emit_rstd(S, rstd, trstd, ps_ssq, tssq, D)
                for c in range(DC):
                    S.op("dve", lambda e, c=c, sl=sl: e.scalar_tensor_tensor(out=h_sb[:, c, sl], in0=h_sb[:, c, sl],
                                                                              scalar=gf_sb[:, c:c + 1], in1=rstd[:],
                                                                              op0=ALU.mult, op1=ALU.mult),
                         reads=[th[c][s], tgf, trstd], writes=[th[c][s]])
        if not early:
            for c0 in range(0, DC, CG):
                S.op("sp", lambda e, c0=c0, tsl=tsl: e.dma_start(out=oTv[:, c0:c0 + CG, tsl], in_=h_sb[:, c0:c0 + CG, :]),
                     reads=[th[c][s] for c in range(c0, c0 + CG) for s in range(NS)], chan=ch_out)
    S.emit(final_waits=[ch_out])
    C.st.close()
    return nc


GELU_C = 1.5957691216057308


def emit_gelu(S, out_ap, x_ap, t1_ap, t2_ap, reads, writes, tt1, tt2, eng2="dve"):
    S.op("act", lambda e: e.activation(out=t1_ap, in_=x_ap, func=AF.Square), reads=reads, writes=[tt1])
    S.op("dve", lambda e: e.tensor_scalar(out=t1_ap, in0=t1_ap, scalar1=0.044715, scalar2=1.0, op0=ALU.mult, op1=ALU.add),
         reads=[tt1], writes=[tt1])
    S.op("dve", lambda e: e.tensor_tensor(out=t1_ap, in0=x_ap, in1=t1_ap, op=ALU.mult), reads=reads + [tt1], writes=[tt1])
    S.op("act", lambda e: e.activation(out=t2_ap, in_=t1_ap, func=AF.Sigmoid, scale=GELU_C), reads=[tt1], writes=[tt2])
    S.op("dve", lambda e: e.tensor_tensor(out=out_ap, in0=x_ap, in1=t2_ap, op=ALU.mult), reads=reads + [tt2], writes=writes)


def build_ab(Sq, D=2048, debug=False):
    C = Ctx()
    nc, S = C.nc, C.S
    DC = D // 128
    TB = 512
    NBLK = Sq // TB
    SH = Sq // 2
    DH = 256
    hT = C.din("hT", [D, Sq])
    hTh = C.din("hTh", [D, SH])
    g_d = C.din("g", [128, DC])
    wz_d = C.din("wz", [128, DC, 2048])
    wqk_d = C.din("wqk", [128, DC, 1024])
    wvog_d = C.din("wvog", [128, DC, 1028])
    gb_d = C.din("gb", [128, 4])
    conv_d = C.din("conv", [128, 8, 4])
    mn_d = C.din("mn", [128, 512])
    gn_d = C.din("gn", [128, 1024])
    wsT_d = C.din("wsT", [128, 8, 128])
    bs_d = C.din("bs", [128, 8])
    ident_d = C.din("ident", [128, 128])
    mask_d = C.din("mask", [128, 128])
    ya = C.dout("ya", [SH, 1024])
    yb = C.dout("yb", [Sq, 512])
    hTv = hT.rearrange("(c p) t -> p c t", p=128)
    hThv = hTh.rearrange("(c p) t -> p c t", p=128)
    if debug:
        dbg = C.dout("dbg", [Sq // 128, 128, 16])
        dbg_sb = C.sb([128, 16], F32, "dbg")
        tdbg = Tile()
        cdbg = S.chan()

    W = C.sb([128, DC, 2052], BF16, "W")
    h_sb = C.sb([128, DC, TB], F32, "h")
    hn = C.sb([128, DC, TB], BF16, "hn")
    sq = [C.sb([128, 512], F32R, "sq") for _ in range(2)]
    rstd = C.sb([128, 512], F32, "rstd")
    g_sb = C.sb([128, DC], F32, "g")
    gb_sb = C.sb([128, 4], F32, "gb")
    conv_sb = C.sb([128, 8, 4], F32, "conv")
    mn_sb = C.sb([128, 512], F32, "mn")
    gn_sb = C.sb([128, 1024], F32, "gn")
    wsT32 = C.sb([128, 8, 128], F32, "wsT32")
    wsT = C.sb([128, 8, 128], BF16, "wsT")
    bs_sb = C.sb([128, 8], F32, "bs")
    ident = C.sb([128, 128], F32, "ident")
    identb = C.sb([128, 128], BF16, "identb")
    mask = C.sb([128, 128], F32, "mask")
    ones32 = C.sb([128, 128], F32, "ones32")
    qkpre = C.sb([128, 8, 3 + TB], F32, "qkpre")
    acc = C.sb([128, TB], F32, "acc")
    qkT = C.sb([128, 8, TB], BF16, "qkT")
    vaug = [C.sb([128, 2, 257], F32, "vaug") for _ in range(2)]
    so = [C.sb([128, 512], F32, "so") for _ in range(2)]
    gts = [C.sb([128, 4], F32, "gts") for _ in range(2)]
    lf = C.sb([128, 2], F32, "lf")
    iv = C.sb([128, 2], F32, "iv")
    bcol = C.sb([128, 2], F32, "bcol")
    acol = C.sb([128, 2], F32, "acol")
    a_bc = C.sb([128, 2, 128], F32, "a_bc")
    amax = C.sb([128, 2], F32, "amax")
    Mx = C.sb([128, 2], F32, "Mx")
    mprev = C.sb([128, 2], F32, "mprev")
    wprev = C.sb([128, 2], F32, "wprev")
    ws = C.sb([128, 2], F32, "ws")
    thr = C.sb([128, 2], F32, "thr")
    tmp2 = C.sb([128, 2], F32, "tmp2")
    sTm_h = [C.sb([128, 128], BF16, "sTm") for _ in range(2)]
    vw_h = [C.sb([128, 257], BF16, "vw") for _ in range(2)]
    CT = C.sb([128, 2, 2, 257], F32, "CT")
    CTb_h = [C.sb([128, 2, 257], BF16, "CTb") for _ in range(2)]
    ktok_h = [C.sb([128, 256], BF16, "ktok") for _ in range(2)]
    den_h = [C.sb([128, 1], F32, "den") for _ in range(2)]
    hh_h = [C.sb([128, 256], F32, "hh") for _ in range(2)]
    junk = C.sb([128, 256], F32, "junk")
    ssq1 = C.sb([128, 2], F32, "ssq1")
    ybt = [C.sb([128, 512], F32, "ybt") for _ in range(2)]
    u_sb = C.sb([128, 1024], F32, "u")
    v_sb = C.sb([128, 1024], F32, "v")
    vn = C.sb([128, 1024], BF16, "vn")
    t1 = C.sb([128, 512], F32, "t1")
    t2 = C.sb([128, 512], F32, "t2")
    yat = [C.sb([128, 1024], F32, "yat") for _ in range(2)]

    psA = [C.ps(name="A") for _ in range(2)]
    psB = [C.ps(name="B") for _ in range(2)]
    psS = C.ps(name="S")
    ps_ssq = psS
    psN_h = [C.ps(name="N") for _ in range(2)]
    psT = C.ps([128, 512], BF16, name="T")

    T = Tile
    tW, tg, tgb, tconv, tmn, tgn, tws32, tws, tbs, tid, tidb, tmask, tones32 = [T() for _ in range(13)]
    th = [[T()] for _ in range(DC)]
    thn = [[T()] for _ in range(DC)]
    tsq = [T(), T()]
    trstd = T()
    tqkpre = [T() for _ in range(8)]
    tacc = T()
    tqkT = [T() for _ in range(8)]
    tvaug = [T(), T()]
    tso = [T(), T()]
    tgts = [T(), T()]
    tlf, tiv, tbcol, tacol, tabc, tamax, tMx, tmprev, twprev, tws_, tthr, ttmp2 = [T() for _ in range(12)]
    tCT, tjunk, tssq1 = T(), T(), T()
    tsTm_h, tvw_h, tCTb_h, tktok_h, tden_h, thh_h, tssq1_h, tCT_h = [[T(), T()] for _ in range(8)]
    tybt_h = [[T(), T()], [T(), T()]]
    tu, tv, tvn, tt1, tt2 = [T() for _ in range(5)]
    tyat = [T(), T()]
    tpsA = [T(psum=True), T(psum=True)]
    tpsB = [T(psum=True), T(psum=True)]
    tpsS, tpsT = T(psum=True), T(psum=True)
    tpsN_h = [T(psum=True), T(psum=True)]

    cmisc = S.chan()
    cW = S.chan()
    ch_in = S.chan()
    cyb = [S.chan(), S.chan()]
    cya = [S.chan(), S.chan()]

    def ld(dst, src, tile, eng="sp", chan=None):
        S.op(eng, lambda e: e.dma_start(out=dst, in_=src), writes=[tile], chan=chan or cmisc)

    ld(g_sb[:], g_d, tg)
    ld(gb_sb[:], gb_d, tgb)
    ld(conv_sb[:], conv_d, tconv)
    ld(mn_sb[:], mn_d, tmn)
    ld(gn_sb[:], gn_d, tgn)
    ld(wsT32[:], wsT_d, tws32)
    ld(bs_sb[:], bs_d, tbs)
    ld(ident[:], ident_d, tid)
    ld(mask[:], mask_d, tmask)
    ones_r, tones = emit_consts(C)
    S.op("dve", lambda e: e.memset(ones32[:], 1.0), writes=[tones32])
    S.op("dve", lambda e: e.tensor_copy(out=identb[:], in_=ident[:]), reads=[tid], writes=[tidb])
    for g in range(8):
        S.op("dve", lambda e, g=g: e.tensor_tensor(out=wsT[:, g, :], in0=wsT32[:, g, :], in1=mask[:], op=ALU.mult),
             reads=[tws32, tmask], writes=[tws])
    S.op("pool", lambda e: e.dma_start(out=W[:, :, 0:1024], in_=wqk_d), writes=[tW], chan=cW)
    S.op("pool", lambda e: e.dma_start(out=W[:, :, 1024:2052], in_=wvog_d), writes=[tW], chan=cW)
    tssq = tpsS
    S.op("dve", lambda e: e.memset(CT[:], 0.0), writes=tCT_h)
    S.op("dve", lambda e: e.memset(mprev[:], 0.0), writes=[tmprev])
    for k in range(2):
        S.op("dve", lambda e, k=k: e.memset(vaug[k][:], 1.0), writes=[tvaug[k]])
    for m in range(8):
        S.op("dve", lambda e, m=m: e.memset(qkpre[:, m, 0:3], 0.0), writes=[tqkpre[m]])

    def norm_block(src_v, tsl):
        CG = 4
        for c0 in range(0, DC, CG):
            S.op("sp", lambda e, c0=c0: e.dma_start(out=h_sb[:, c0:c0 + CG, :], in_=src_v[:, c0:c0 + CG, tsl]),
                 writes=[th[c][0] for c in range(c0, c0 + CG)], chan=ch_in)
        emit_rmsnorm_T(C, h_sb, th, hn, thn, g_sb, tg, ones_r, tones, DC, TB, sq, tsq, ps_ssq, tssq, rstd, trstd, D)

    all_hn = [thn[c][0] for c in range(DC)]
    nA = 0
    nB = 0
    for tb in range(NBLK):
        norm_block(hTv, slice(tb * TB, (tb + 1) * TB))
        for m in range(8):
            b = nA % 2
            nA += 1
            for c in range(DC):
                S.op("pe", lambda e, c=c, m=m, b=b: e.matmul(psA[b][:], W[:, c, m * 128:(m + 1) * 128], hn[:, c, :],
                                                          start=(c == 0), stop=(c == DC - 1)),
                     reads=[tW, thn[c][0]], writes=[tpsA[b]])
            S.op("act", lambda e, m=m, b=b: e.copy(out=qkpre[:, m, 3:3 + TB], in_=psA[b][:]), reads=[tpsA[b]], writes=[tqkpre[m]])
            S.op("dve", lambda e, m=m: e.tensor_scalar(out=acc[:], in0=qkpre[:, m, 0:TB], scalar1=conv_sb[:, m, 0:1], scalar2=None,
                                                       op0=ALU.mult), reads=[tqkpre[m], tconv], writes=[tacc])
            for k in range(1, 4):
                S.op("dve", lambda e, m=m, k=k: e.scalar_tensor_tensor(out=acc[:], in0=qkpre[:, m, k:k + TB], scalar=conv_sb[:, m, k:k + 1],
                                                                        in1=acc[:], op0=ALU.mult, op1=ALU.add),
                     reads=[tqkpre[m], tconv, tacc], writes=[tacc])
            S.op("act", lambda e, m=m: e.activation(out=qkT[:, m, :], in_=acc[:], func=AF.Silu), reads=[tacc], writes=[tqkT[m]])
            S.op("dve", lambda e, m=m: e.tensor_copy(out=qkpre[:, m, 0:3], in_=qkpre[:, m, TB:TB + 3]), reads=[tqkpre[m]], writes=[tqkpre[m]])
        def proj_chunk(j):
            nonlocal nB
            jsl = slice(j * 128, (j + 1) * 128)
            kk = j % 2
            b = nB % 2
            nB += 1
            for c in range(DC):
                S.op("pe", lambda e, c=c, b=b, jsl=jsl: e.matmul(psB[b][:], hn[:, c, jsl], W[:, c, 1024:1536], start=(c == 0), stop=(c == DC - 1)),
                     reads=[tW, thn[c][0]], writes=[tpsB[b]])
            for h in range(2):
                S.op("act", lambda e, h=h, b=b, kk=kk: e.copy(out=vaug[kk][:, h, 0:256], in_=psB[b][:, h * 256:(h + 1) * 256]),
                     reads=[tpsB[b]], writes=[tvaug[kk]])
            b = nB % 2
            nB += 1
            for c in range(DC):
                S.op("pe", lambda e, c=c, b=b, jsl=jsl: e.matmul(psB[b][:], hn[:, c, jsl], W[:, c, 1536:2048], start=(c == 0), stop=(c == DC - 1)),
                     reads=[tW, thn[c][0]], writes=[tpsB[b]])
            S.op("act", lambda e, b=b, kk=kk: e.activation(out=so[kk][:], in_=psB[b][:], func=AF.Sigmoid), reads=[tpsB[b]], writes=[tso[kk]])
            for c in range(DC):
                S.op("pe", lambda e, c=c, jsl=jsl: e.matmul(psS[:, 0:4], hn[:, c, jsl], W[:, c, 2048:2052], start=(c == 0), stop=(c == DC - 1)),
                     reads=[tW, thn[c][0]], writes=[tpsS])
            S.op("dve", lambda e, kk=kk: e.tensor_tensor(out=gts[kk][:], in0=psS[:, 0:4], in1=gb_sb[:], op=ALU.add),
                 reads=[tpsS, tgb], writes=[tgts[kk]])

        proj_chunk(0)
        for j in range(TB // 128):
            jsl = slice(j * 128, (j + 1) * 128)
            kk = j % 2
            S.op("act", lambda e, kk=kk: e.activation(out=lf[:], in_=gts[kk][:, 2:4], func=AF.Exp, scale=-1.0), reads=[tgts[kk]], writes=[tlf])
            S.op("act", lambda e: e.activation(out=lf[:], in_=lf[:], func=AF.Ln, bias=ones32[:, 0:1], scale=1.0), reads=[tlf, tones32], writes=[tlf])
            S.op("dve", lambda e: e.tensor_scalar(out=lf[:], in0=lf[:], scalar1=-1.0, scalar2=None, op0=ALU.mult), reads=[tlf], writes=[tlf])
            S.op("pe", lambda e: e.matmul(psS[:, 8:10], mask[:], lf[:], start=True, stop=True), reads=[tmask, tlf], writes=[tpsS])
            S.op("pe", lambda e: e.matmul(psS[:, 16:18], ones32[:], lf[:], start=True, stop=True), reads=[tones32, tlf], writes=[tpsS])
            S.op("dve", lambda e: e.tensor_copy(out=bcol[:], in_=psS[:, 8:10]), reads=[tpsS], writes=[tbcol])
            S.op("dve", lambda e, kk=kk: e.tensor_tensor(out=acol[:], in0=gts[kk][:, 0:2], in1=bcol[:], op=ALU.subtract),
                 reads=[tgts[kk], tbcol], writes=[tacol])
            for h in range(2):
                S.op("dve", lambda e, h=h: e.tensor_copy(out=a_bc[:, h, :], in_=acol[:, h:h + 1].to_broadcast([128, 128])),
                     reads=[tacol], writes=[tabc])
            for h in range(2):
                S.op("pe", lambda e, h=h: e.matmul(psS[:, 128 + h * 128:256 + h * 128], a_bc[:, h, :], ident[:], start=True, stop=True),
                     reads=[tabc, tid], writes=[tpsS])
            S.op("dve", lambda e: e.tensor_reduce(out=amax[:], in_=psS[:, 128:384].rearrange("p (h s) -> p h s", h=2), axis=AX.X, op=ALU.max),
                 reads=[tpsS], writes=[tamax])
            S.op("dve", lambda e: e.tensor_tensor(out=Mx[:], in0=amax[:], in1=mprev[:], op=ALU.max), reads=[tamax, tmprev], writes=[tMx])
            S.op("dve", lambda e: e.tensor_tensor(out=tmp2[:], in0=mprev[:], in1=Mx[:], op=ALU.subtract), reads=[tmprev, tMx], writes=[ttmp2])
            S.op("act", lambda e: e.activation(out=wprev[:], in_=tmp2[:], func=AF.Exp), reads=[ttmp2], writes=[twprev])
            S.op("dve", lambda e: e.tensor_tensor(out=tmp2[:], in0=acol[:], in1=Mx[:], op=ALU.subtract), reads=[tacol, tMx], writes=[ttmp2])
            S.op("act", lambda e: e.activation(out=ws[:], in_=tmp2[:], func=AF.Exp), reads=[ttmp2], writes=[tws_])
            S.op("dve", lambda e: e.tensor_tensor(out=tmp2[:], in0=bcol[:], in1=Mx[:], op=ALU.add), reads=[tbcol, tMx], writes=[ttmp2])
            S.op("act", lambda e: e.activation(out=thr[:], in_=tmp2[:], func=AF.Exp, scale=-1.0), reads=[ttmp2], writes=[tthr])
            S.op("dve", lambda e: e.tensor_tensor(out=mprev[:], in0=psS[:, 16:18], in1=Mx[:], op=ALU.add), reads=[tpsS, tMx], writes=[tmprev])
            if debug:
                for i_, (src_, tl_) in enumerate(((lf, tlf), (acol, tacol), (bcol, tbcol), (amax, tamax), (Mx, tMx), (wprev, twprev),
                                                  (ws, tws_), (thr, tthr))):
                    S.op("dve", lambda e, i_=i_, src_=src_: e.tensor_copy(out=dbg_sb[:, 2 * i_:2 * i_ + 2], in_=src_[:]),
                         reads=[tl_], writes=[tdbg])
                cidx = tb * (TB // 128) + j
                S.op("sp", lambda e, cidx=cidx: e.dma_start(out=dbg[cidx], in_=dbg_sb[:]), reads=[tdbg], chan=cdbg)
            QI = [[h * 2, h * 2 + 1] for h in range(2)]
            KI = [[4 + h * 2, 4 + h * 2 + 1] for h in range(2)]
            for h in range(2):
                for dc in range(2):
                    S.op("pe", lambda e, dc=dc, h=h, jsl=jsl: e.matmul(psA[0][:, h * 128:(h + 1) * 128], qkT[:, KI[h][dc], jsl], qkT[:, QI[h][dc], jsl],
                                                                    start=(dc == 0), stop=(dc == 1)),
                         reads=[tqkT[KI[h][dc]], tqkT[QI[h][dc]]], writes=[tpsA[0]])
            for h in range(2):
                for dc in range(2):
                    S.op("pe", lambda e, dc=dc, h=h, jsl=jsl: e.transpose(psT[:, h * 256 + dc * 128:h * 256 + (dc + 1) * 128], qkT[:, KI[h][dc], jsl], identb[:]),
                         reads=[tqkT[KI[h][dc]], tidb], writes=[tpsT])
            if j + 1 < TB // 128:
                proj_chunk(j + 1)
            for h in range(2):
                S.op("dve", lambda e, h=h: e.scalar_tensor_tensor(out=sTm_h[h][:], in0=psA[0][:, h * 128:(h + 1) * 128], scalar=DH ** -0.5, in1=mask[:],
                                                                  op0=ALU.mult, op1=ALU.mult), reads=[tpsA[0], tmask], writes=[tsTm_h[h]])
            for h in range(2):
                S.op("act", lambda e, h=h: e.copy(out=ktok_h[h][:], in_=psT[:, h * 256:(h + 1) * 256]), reads=[tpsT], writes=[tktok_h[h]])
            for h in range(2):
                S.op("dve", lambda e, h=h, kk=kk: e.tensor_scalar(out=vw_h[h][:], in0=vaug[kk][:, h, :], scalar1=ws[:, h:h + 1], scalar2=None, op0=ALU.mult),
                     reads=[tvaug[kk], tws_], writes=[tvw_h[h]])
            for h in range(2):
                S.op("dve", lambda e, h=h: e.tensor_scalar(out=CT[:, h, :, :], in0=CT[:, h, :, :], scalar1=wprev[:, h:h + 1], scalar2=None, op0=ALU.mult),
                     reads=[tCT_h[h], twprev], writes=[tCT_h[h]])
            for h in range(2):
                S.op("act", lambda e, h=h: e.copy(out=CTb_h[h][:], in_=CT[:, h, :, :]), reads=[tCT_h[h]], writes=[tCTb_h[h]])
            for h in range(2):
                S.op("pe", lambda e, h=h: e.matmul(psN_h[h][:, 0:257], sTm_h[h][:], vw_h[h][:], start=True, stop=False),
                     reads=[tsTm_h[h], tvw_h[h]], writes=[tpsN_h[h]])
                for dc in range(2):
                    S.op("pe", lambda e, dc=dc, h=h, jsl=jsl: e.matmul(psN_h[h][:, 0:257], qkT[:, QI[h][dc], jsl], CTb_h[h][:, dc, :], start=False, stop=(dc == 1)),
                         reads=[tqkT[QI[h][dc]], tCTb_h[h]], writes=[tpsN_h[h]])
            for h in range(2):
                S.op("act", lambda e, h=h: e.activation(out=den_h[h][:], in_=psN_h[h][:, 256:257], func=AF.Abs), reads=[tpsN_h[h]], writes=[tden_h[h]])
            for h in range(2):
                S.op("dve", lambda e, h=h: e.tensor_tensor(out=den_h[h][:], in0=den_h[h][:], in1=thr[:, h:h + 1], op=ALU.max),
                     reads=[tden_h[h], tthr], writes=[tden_h[h]])
            for h in range(2):
                S.op("dve", lambda e, h=h: e.reciprocal(out=den_h[h][:], in_=den_h[h][:]), reads=[tden_h[h]], writes=[tden_h[h]])
            for h in range(2):
                S.op("dve", lambda e, h=h: e.tensor_scalar(out=hh_h[h][:], in0=psN_h[h][:, 0:256], scalar1=den_h[h][:, 0:1], scalar2=None, op0=ALU.mult),
                     reads=[tpsN_h[h], tden_h[h]], writes=[thh_h[h]])
            for h in range(2):
                S.op("dve", lambda e, h=h: e.memset(ssq1[:, h:h + 1], 0.0), writes=[tssq1_h[h]])
            for h in range(2):
                S.op("act", lambda e, h=h: e.activation(out=junk[:], in_=hh_h[h][:], func=AF.Square, accum_out=ssq1[:, h:h + 1]),
                     reads=[thh_h[h]], writes=[tjunk, tssq1_h[h]])
            for h in range(2):
                S.op("act", lambda e, h=h: e.activation(out=ssq1[:, h:h + 1], in_=ssq1[:, h:h + 1], func=AF.Sqrt, bias=EPSB[0][:, 0:1], scale=1.0 / DH),
                     reads=[tssq1_h[h], EPSB[1]], writes=[tssq1_h[h]])
            for h in range(2):
                S.op("dve", lambda e, h=h: e.reciprocal(out=ssq1[:, h:h + 1], in_=ssq1[:, h:h + 1]), reads=[tssq1_h[h]], writes=[tssq1_h[h]])
            for h in range(2):
                S.op("dve", lambda e, h=h, kk=kk: e.scalar_tensor_tensor(out=ybt[kk][:, h * 256:(h + 1) * 256], in0=hh_h[h][:], scalar=ssq1[:, h:h + 1],
                                                                          in1=mn_sb[:, h * 256:(h + 1) * 256], op0=ALU.mult, op1=ALU.mult),
                     reads=[thh_h[h], tssq1_h[h], tmn], writes=[tybt_h[kk][h]])
            for h in range(2):
                S.op("dve", lambda e, h=h, kk=kk: e.tensor_tensor(out=ybt[kk][:, h * 256:(h + 1) * 256], in0=ybt[kk][:, h * 256:(h + 1) * 256],
                                                                  in1=so[kk][:, h * 256:(h + 1) * 256], op=ALU.mult),
                     reads=[tybt_h[kk][h], tso[kk]], writes=[tybt_h[kk][h]])
            for h in range(2):
                for dc in range(2):
                    S.op("pe", lambda e, dc=dc, h=h: e.matmul(psA[1][:, 0:257], ktok_h[h][:, dc * 128:(dc + 1) * 128], vw_h[h][:], start=True, stop=True),
                         reads=[tktok_h[h], tvw_h[h]], writes=[tpsA[1]])
                    S.op("dve", lambda e, dc=dc, h=h: e.scalar_tensor_tensor(out=CT[:, h, dc, :], in0=psA[1][:, 0:257], scalar=DH ** -0.5,
                                                                              in1=CT[:, h, dc, :], op0=ALU.mult, op1=ALU.add),
                         reads=[tCT_h[h], tpsA[1]], writes=[tCT_h[h]])
            tok0 = tb * TB + j * 128
            S.op("sp", lambda e, kk=kk, tok0=tok0: e.dma_start(out=yb[tok0:tok0 + 128, :], in_=ybt[kk][:]), reads=tybt_h[kk], chan=cyb[kk])

    S.op("pool", lambda e: e.dma_start(out=W[:, :, 0:2048], in_=wz_d), writes=[tW], chan=cW)
    nj = 0
    for tb in range(SH // TB):
        norm_block(hThv, slice(tb * TB, (tb + 1) * TB))
        for j in range(TB // 128):
            jsl = slice(j * 128, (j + 1) * 128)
            kk = nj % 2
            nj += 1
            for cb in range(4):
                b = cb % 2
                for c in range(DC):
                    S.op("pe", lambda e, c=c, b=b, cb=cb, jsl=jsl: e.matmul(psA[b][:], hn[:, c, jsl], W[:, c, cb * 512:(cb + 1) * 512],
                                                                        start=(c == 0), stop=(c == DC - 1)),
                         reads=[tW, thn[c][0]], writes=[tpsA[b]])
                dst = u_sb if cb < 2 else v_sb
                tdst = tu if cb < 2 else tv
                csl = slice((cb % 2) * 512, (cb % 2 + 1) * 512)
                emit_gelu(S, dst[:, csl], psA[b][:], t1[:], t2[:], [tpsA[b]], [tdst], tt1, tt2)
            S.op("dve", lambda e: e.memset(ssq1[:], 0.0), writes=[tssq1] + tssq1_h)
            for q in range(2):
                S.op("act", lambda e, q=q: e.activation(out=t1[:], in_=v_sb[:, q * 512:(q + 1) * 512], func=AF.Square, accum_out=ssq1[:, q:q + 1]),
                     reads=[tv], writes=[tt1, tssq1])
            S.op("dve", lambda e: e.tensor_tensor(out=ssq1[:, 0:1], in0=ssq1[:, 0:1], in1=ssq1[:, 1:2], op=ALU.add), reads=[tssq1], writes=[tssq1])
            S.op("act", lambda e: e.activation(out=ssq1[:, 0:1], in_=ssq1[:, 0:1], func=AF.Sqrt, bias=EPSB[0][:, 0:1], scale=1.0 / 1024),
                 reads=[tssq1, EPSB[1]], writes=[tssq1])
            S.op("dve", lambda e: e.reciprocal(out=ssq1[:, 0:1], in_=ssq1[:, 0:1]), reads=[tssq1], writes=[tssq1])
            S.op("dve", lambda e: e.scalar_tensor_tensor(out=vn[:], in0=v_sb[:], scalar=ssq1[:, 0:1], in1=gn_sb[:], op0=ALU.mult, op1=ALU.mult),
                 reads=[tv, tssq1, tgn], writes=[tvn])
            for g in range(8):
                b = g // 4
                S.op("pe", lambda e, g=g, b=b: e.matmul(psB[b][:, (g % 4) * 128:(g % 4 + 1) * 128], wsT[:, g, :], vn[:, g * 128:(g + 1) * 128],
                                                    start=True, stop=True), reads=[tws, tvn], writes=[tpsB[b]])
            for g in range(8):
                b = g // 4
                S.op("dve", lambda e, g=g, b=b, kk=kk: e.scalar_tensor_tensor(out=yat[kk][:, g * 128:(g + 1) * 128],
                                                                               in0=psB[b][:, (g % 4) * 128:(g % 4 + 1) * 128],
                                                                               scalar=bs_sb[:, g:g + 1], in1=u_sb[:, g * 128:(g + 1) * 128],
                                                                               op0=ALU.add, op1=ALU.mult),
                     reads=[tpsB[b], tbs, tu], writes=[tyat[kk]])
            tok0 = tb * TB + j * 128
            S.op("sp", lambda e, kk=kk, tok0=tok0: e.dma_start(out=ya[tok0:tok0 + 128, :], in_=yat[kk][:]), reads=[tyat[kk]], chan=cya[kk])
    S.emit(final_waits=cyb + cya)
    C.st.close()
    return nc


NEG = -1.0e30


def nsa_tables(Sq):
    NT = Sq // 128
    NCMP = (Sq - 32) // 16 + 1
    CH = (NCMP + 127) // 128
    NSL = Sq // 64
    half = 16
    inv = 1.0 / (500000.0 ** (np.arange(half, dtype=np.float32) / half))
    ang = np.arange(Sq, dtype=np.float32)[None, :] * inv[:, None]
    cos = np.ones((128, Sq), np.float32)
    sin = np.zeros((128, Sq), np.float32)
    cos[0:16] = np.cos(ang); cos[16:32] = np.cos(ang)
    sin[0:16] = np.sin(ang); sin[16:32] = np.sin(ang)
    sc = np.float32(128 ** -0.5)
    psw = np.zeros((128, 128), np.float32)
    for d in range(16):
        psw[d + 16, d] = -1.0
        psw[d, d + 16] = 1.0
    k = np.arange(128)[:, None]
    q = np.arange(128)[None, :]
    causal = np.where(k > q, NEG, 0.0).astype(np.float32)
    winneg = np.where(k <= q, NEG, 0.0).astype(np.float32)
    cm = np.zeros((NT, 128, CH, 128), np.float32)
    for T in range(NT):
        for ch in range(CH):
            c = ch * 128 + np.arange(128)[:, None]
            t = T * 128 + np.arange(128)[None, :]
            cm[T, :, ch, :] = np.where((16 * c + 31 > t) | (c >= NCMP), NEG, 0.0)
    E = np.zeros((64, NT, 128), np.float32)
    for kc in range(NT):
        for kk in range(128):
            E[(kc * 128 + kk) // 64, kc, kk] = 1.0
    fpos = np.zeros((NT, 128, 64), np.float32)
    fneg = np.full((NT, 128, 64), 1.0e30, np.float32)
    j = np.arange(64)[None, :]
    for T in range(NT):
        cur = ((T * 128 + np.arange(128)) // 64)[:, None]
        fp = np.zeros((128, 64), np.float32)
        fp = np.where(j == cur - 1, 1.0e9, fp)
        fp = np.where(j == cur, 2.0e9, fp)
        fp = np.where(j == 0, 3.0e9, fp)
        fpos[T] = fp
        fneg[T] = np.where((j > cur) | (j >= NSL), -1.0e9, 1.0e30)
    ci = np.arange(CH * 128)[:, None] * 16
    sj = np.arange(64)[None, :] * 64
    ov = np.clip(np.minimum(ci + 32, sj + 64) - np.maximum(ci, sj), 0, None).astype(np.float32) / 32.0
    ov[NCMP:] = 0.0
    ov = np.ascontiguousarray(ov.reshape(CH, 128, 64).transpose(1, 0, 2))
    return dict(cosq=cos * sc, sinq=sin * sc, cosk=cos, sink=sin, psw=psw, ident=np.eye(128, dtype=np.float32),
                causal=causal, winneg=winneg, cmpmask=cm, E=E, fpos=fpos, fneg=fneg, ov=ov)


def build_nsa(Sq, D=2048, stop=0, branches=(0, 1, 2)):
    C = Ctx()
    nc, S = C.nc, C.S
    DC = D // 128
    TB = 256
    TPB = TB // 128
    NBLK = Sq // TB
    NT = Sq // 128
    NCMP = (Sq - 32) // 16 + 1
    CH = (NCMP + 127) // 128
    T_ = Tile

    hT = C.din("hT", [D, Sq])
    g_d = C.din("g", [128, DC])
    wkv_d = C.din("wkv", [128, DC, 1536])
    wq_d = C.din("wq", [128, DC, 1024])
    wgt_d = C.din("wgt", [128, DC, 24])
    w1_d = C.din("w1", [128, 2, 32, 128])
    w2_d = C.din("w2", [128, 2, 128])
    pos_d = C.din("posT", [128, 2, 32])
    tabs = {}
    for nm, shp in (("cosq", [128, Sq]), ("sinq", [128, Sq]), ("cosk", [128, Sq]), ("sink", [128, Sq]), ("psw", [128, 128]),
                    ("ident", [128, 128]), ("causal", [128, 128]), ("winneg", [128, 128]), ("cmpmask", [NT, 128, CH, 128]),
                    ("E", [64, NT, 128]), ("fpos", [NT, 128, 64]), ("fneg", [NT, 128, 64]), ("ov", [128, CH, 64])):
        tabs[nm] = C.din(nm, shp)
    y_d = C.dout("y", [Sq, 1024])
    hTv = hT.rearrange("(c p) t -> p c t", p=128)

    W = C.sb([128, DC, 1024], BF16, "W")
    Wgt = C.sb([128, DC, 24], BF16, "Wgt")
    WT = C.sb([128, DC, 512], BF16, "WT")
    tWT = Tile()
    tWgt = Tile()
    h_sb = C.sb([128, DC, TB], BF16, "h")
    hn = C.sb([128, DC, TB], BF16, "hn")
    sq = [C.sb([128, 256], F32R, "sq") for _ in range(2)]
    rstd = C.sb([128, 256], F32, "rstd")
    g_sb = C.sb([128, DC], F32, "g")
    kTs = C.sb([128, 2, 2, Sq], BF16, "kTs")
    cmpin = C.sb([128, 2, 2, Sq], BF16, "cmpin")
    vsel = C.sb([128, NT, 2, 129], BF16, "vsel")
    vwin = C.sb([128, NT, 2, 129], BF16, "vwin")
    kcmpT = C.sb([128, 2, CH * 128], BF16, "kcmpT")
    vcmp = C.sb([128, 2, CH, 193], BF16, "vcmp")
    rope_c = C.sb([128, TB], F32, "ropec")
    rope_s = C.sb([128, TB], F32, "ropes")
    psw = C.sb([128, 128], BF16, "psw")
    identb = C.sb([128, 128], BF16, "identb")
    ident = C.sb([128, 128], F32, "ident")
    causal = C.sb([128, 4, 128], BF16, "causal")
    winneg = C.sb([128, 4, 128], BF16, "winneg")
    Eb = C.sb([64, NT, 128], BF16, "E")
    xb = C.sb([128, TB], BF16, "xb")
    xc = C.sb([128, TB], F32, "xc")
    xs = C.sb([128, TB], F32, "xs")
    kmax2 = C.sb([128, 6], F32, "kmax2")
    kred = C.sb([128, 1], F32, "kred")
    ones32 = C.sb([128, 128], F32, "ones32")
    sqf = C.sb([128, 256], F32, "sqf")
    tsqf = Tile()

    psB = [C.ps(name="B") for _ in range(3)]
    psA = psB
    psO = [C.ps(name="O") for _ in range(4)]
    psS = C.ps(name="S")
    ps_ssq = psS

    tW, tg = T_(), T_()
    th = [[T_()] for _ in range(DC)]
    thn = [[T_()] for _ in range(DC)]
    tsq = [T_(), T_()]
    trstd = T_()
    tkTs = [[[T_() for _ in range(NBLK)] for _ in range(2)] for _ in range(2)]
    tcmpin = [[T_() for _ in range(2)] for _ in range(2)]
    tvsel = [T_() for _ in range(NT)]
    tvwin = [T_() for _ in range(NT)]
    tkcmpT, tvcmp = T_(), T_()
    trope, tpsw, tidb, tid, tcausal, twinneg, tE = [T_() for _ in range(7)]
    txb, txc, txs, tkmax2, tkred, tones32 = [T_() for _ in range(6)]
    tpsB = [T_(psum=True), T_(psum=True), T_(psum=True)]
    tpsA = tpsB
    tpsO = [T_(psum=True) for _ in range(4)]
    tpsS = T_(psum=True)
    tssq = tpsS

    cmisc = S.chan()
    cW = S.chan()
    ch_in = S.chan()
    crope = S.chan()

    S.op("sp", lambda e: e.dma_start(out=g_sb[:], in_=g_d), writes=[tg], chan=cmisc)
    S.op("sp", lambda e: e.dma_start(out=ident[:], in_=tabs["ident"]), writes=[tid], chan=cmisc)
    for dst, nm, tl in ((psw, "psw", tpsw), (identb, "ident", tidb), (Eb, "E", tE)):
        S.op("pool", lambda e, dst=dst, nm=nm: e.dma_start(out=dst[:], in_=tabs[nm]), writes=[tl], chan=cmisc)
    for dst, nm, tl in ((causal, "causal", tcausal), (winneg, "winneg", twinneg)):
        for r4 in range(4):
            S.op("pool", lambda e, dst=dst, nm=nm, r4=r4: e.dma_start(out=dst[:, r4, :], in_=tabs[nm]), writes=[tl], chan=cmisc)
    ones_r, tones = emit_consts(C)
    S.op("dve", lambda e: e.memset(ones32[:], 1.0), writes=[tones32])
    S.op("dve", lambda e: e.memset(kmax2[:], 0.0), writes=[tkmax2])
    S.op("dve", lambda e: e.memset(vsel[:], 1.0), writes=tvsel)
    S.op("dve", lambda e: e.memset(vwin[:], 1.0), writes=tvwin)
    S.op("dve", lambda e: e.memset(vcmp[:], 1.0), writes=[tvcmp])
    S.op("dve", lambda e: e.memset(kcmpT[:], 0.0), writes=[tkcmpT])
    S.op("pool", lambda e: e.dma_start(out=Wgt[:], in_=wgt_d), writes=[tWgt], chan=cmisc)

    class _Stop(Exception):
        pass

    def ckpt(v):
        if stop == v:
            S.op("sp", lambda e: e.dma_start(out=y_d[0:128, 0:128], in_=ident[:]), reads=[tid], chan=cmisc)
            S.emit(final_waits=[cmisc])
            S.op = lambda *a, **k: None
            S.emit = lambda *a, **k: None

    def norm_block(tsl):
        CG = 4
        for c0 in range(0, DC, CG):
            S.op("pool", lambda e, c0=c0: e.dma_start(out=h_sb[:, c0:c0 + CG, :], in_=hTv[:, c0:c0 + CG, tsl]),
                 writes=[th[c][0] for c in range(c0, c0 + CG)], chan=ch_in)
        emit_rmsnorm_T(C, h_sb, th, hn, thn, g_sb, tg, ones_r, tones, DC, TB, sq, tsq, ps_ssq, tssq, rstd, trstd, D)

    def load_rope(cn, sn, tsl):
        S.op("sp", lambda e: e.dma_start(out=rope_c[:], in_=tabs[cn][:, tsl]), writes=[trope], chan=crope)
        S.op("sp", lambda e: e.dma_start(out=rope_s[:], in_=tabs[sn][:, tsl]), writes=[trope], chan=crope)

    nA = [0]

    def proj_fm(col0):
        b = nA[0] % 2
        nA[0] += 1
        for c in range(DC):
            S.op("pe", lambda e, c=c, b=b: e.matmul(psA[b][:, 0:TB], W[:, c, col0:col0 + 128], hn[:, c, :], start=(c == 0), stop=(c == DC - 1)),
                 reads=[tW, thn[c][0]], writes=[tpsA[b]])
        return b

    def rope_to(b, dst_ap, dst_tiles, split=False):
        S.op("act", lambda e: e.copy(out=xb[:], in_=psA[b][:, 0:TB]), reads=[tpsA[b]], writes=[txb])
        ckpt(151)
        S.op("dve", lambda e: e.tensor_tensor(out=xc[:], in0=psA[b][:, 0:TB], in1=rope_c[:], op=ALU.mult), reads=[tpsA[b], trope, txb], writes=[txc])
        ckpt(152)
        bb = nA[0] % 2
        nA[0] += 1
        S.op("pe", lambda e: e.matmul(psA[bb][:, 0:TB], psw[:], xb[:], start=True, stop=True), reads=[tpsw, txb], writes=[tpsA[bb]])
        ckpt(153)
        S.op("dve", lambda e: e.tensor_tensor(out=xs[:], in0=psA[bb][:, 0:TB], in1=rope_s[:], op=ALU.mult), reads=[tpsA[bb], trope], writes=[txs])
        ckpt(154)
        if split:
            S.op("dve", lambda e: e.tensor_tensor(out=dst_ap, in0=xs[:].rearrange("p (j q) -> p j q", q=128),
                                                  in1=xc[:].rearrange("p (j q) -> p j q", q=128), op=ALU.add), reads=[txs, txc], writes=dst_tiles)
        else:
            S.op("dve", lambda e: e.tensor_tensor(out=dst_ap, in0=xs[:], in1=xc[:], op=ALU.add), reads=[txs, txc], writes=dst_tiles)

    def colnorm_max(src_ap, src_tiles, kcol, ncols=TB):
        S.op("act", lambda e: e.activation(out=sqf[:, 0:ncols], in_=src_ap, func=AF.Square), reads=src_tiles, writes=[tsqf])
        S.op("pe", lambda e: e.matmul(psS[:, 0:ncols], ones32[:], sqf[:, 0:ncols], start=True, stop=True), reads=[tsqf, tones32], writes=[tpsS])
        S.op("dve", lambda e: e.tensor_reduce(out=kred[:], in_=psS[:, 0:ncols], axis=AX.X, op=ALU.max), reads=[tpsS], writes=[tkred])
        S.op("dve", lambda e: e.tensor_tensor(out=kmax2[:, kcol:kcol + 1], in0=kmax2[:, kcol:kcol + 1], in1=kred[:], op=ALU.max),
             reads=[tkmax2, tkred], writes=[tkmax2])

    nB = 0
    try:
        ckpt(11)
    except _Stop:
        return nc
    for tb in range(NBLK):
        tsl = slice(tb * TB, (tb + 1) * TB)
        try:
            norm_block(tsl)
            ckpt(12)
            load_rope("cosk", "sink", tsl)
            if tb == 0:
                S.op("pool", lambda e: e.dma_start(out=W[:], in_=wkv_d[:, :, 0:1024]), writes=[tW], chan=cW)
                S.op("pool", lambda e: e.dma_start(out=WT[:], in_=wkv_d[:, :, 1024:1536]), writes=[tWT], chan=cW)
            ckpt(13)
        except _Stop:
            return nc
        for slot in range(6):
            br, gl = slot // 2, slot % 2
            b = proj_fm(slot * 128)
            try:
                ckpt(14)
            except _Stop:
                return nc
            if br == 0:
                rope_to(b, cmpin[:, 0, gl, tsl], [tcmpin[0][gl]])
                try:
                    ckpt(15)
                except _Stop:
                    return nc
            else:
                rope_to(b, kTs[:, br - 1, gl, tsl], [tkTs[br - 1][gl][tb]])
                colnorm_max(kTs[:, br - 1, gl, tsl], [tkTs[br - 1][gl][tb]], 2 + (br - 1) * 2 + gl)
                try:
                    ckpt(16)
                except _Stop:
                    return nc
        for gl in range(2):
            b = proj_fm((6 + gl) * 128)
            S.op("act", lambda e, b=b, gl=gl, tsl=tsl: e.copy(out=cmpin[:, 1, gl, tsl], in_=psA[b][:, 0:TB]), reads=[tpsA[b]], writes=[tcmpin[1][gl]])
        for j in range(TPB):
            Tq = tb * TPB + j
            jsl = slice(j * 128, (j + 1) * 128)
            b = nB % 2
            nB += 1
            for c in range(DC):
                S.op("pe", lambda e, c=c, b=b, jsl=jsl: e.matmul(psB[b][:], hn[:, c, jsl], WT[:, c, :], start=(c == 0), stop=(c == DC - 1)),
                     reads=[tWT, thn[c][0]], writes=[tpsB[b]])
            S.op("act", lambda e, b=b, Tq=Tq: e.copy(out=vsel[:, Tq, :, 0:128], in_=psB[b][:, 0:256].rearrange("p (g d) -> p g d", g=2)),
                 reads=[tpsB[b]], writes=[tvsel[Tq]])
            S.op("dve", lambda e, b=b, Tq=Tq: e.tensor_copy(out=vwin[:, Tq, :, 0:128], in_=psB[b][:, 256:512].rearrange("p (g d) -> p g d", g=2)),
                 reads=[tpsB[b]], writes=[tvwin[Tq]])

    if stop == 1:
        S.op("pool", lambda e: e.dma_start(out=y_d[0:128, 0:256], in_=h_sb[:, 0, :]), reads=[th[0][0]], chan=cmisc)
        S.emit(final_waits=[cmisc])
        C.st.close()
        return nc
    Wflat = W[:].rearrange("p a b -> p (a b)")
    w1 = Wflat[:, 0:8192].rearrange("p (k l o) -> p k l o", k=2, l=32)
    w2 = Wflat[:, 8192:8448].rearrange("p (k o) -> p k o", k=2)
    posb = Wflat[:, 8448:8512].rearrange("p (k l) -> p k l", k=2)
    S.op("pool", lambda e: e.dma_start(out=w1, in_=w1_d), writes=[tW], chan=cW)
    S.op("pool", lambda e: e.dma_start(out=w2, in_=w2_d), writes=[tW], chan=cW)
    S.op("pool", lambda e: e.dma_start(out=posb, in_=pos_d), writes=[tW], chan=cW)
    S.op("pool", lambda e: e.dma_start(out=vcmp[:, 0, :, 129:193], in_=tabs["ov"]), writes=[tvcmp], chan=cmisc)
    S.op("pool", lambda e: e.dma_start(out=vcmp[:, 1, :, 129:193], in_=tabs["ov"]), writes=[tvcmp], chan=cmisc)
    bias1 = C.sb([128, 2], F32, "bias1")
    xg, t1, t2 = rstd, xc, xs
    H1g = C.sb([128, CH * 128], BF16, "H1g")
    tbias1, tH1g = T_(), T_()
    txg, tt1, tt2 = trstd, txc, txs
    NCP = NCMP
    for kv in range(2):
        for l in range(32):
            S.op("pe", lambda e, kv=kv, l=l: e.matmul(psS[:, kv:kv + 1], w1[:, kv, l, :], posb[:, kv, l:l + 1], start=(l == 0), stop=(l == 31)),
                 reads=[tW], writes=[tpsS])
        S.op("dve", lambda e, kv=kv: e.tensor_copy(out=bias1[:, kv:kv + 1], in_=psS[:, kv:kv + 1]), reads=[tpsS], writes=[tbias1])
    S.op("dve", lambda e: e.memset(H1g[:], 0.0), writes=[tH1g])
    for kv in range(2):
        for gl in range(2):
            b = nA[0] % 2
            nA[0] += 1
            for l in range(32):
                S.op("pe", lambda e, kv=kv, gl=gl, l=l, b=b: e.matmul(psA[b][:, 0:NCP], w1[:, kv, l, :],
                                                                     cmpin[:, kv, gl, l:l + 16 * (NCP - 1) + 1:16],
                                                                     start=(l == 0), stop=(l == 31)),
                     reads=[tW, tcmpin[kv][gl]], writes=[tpsA[b]])
            for c0 in range(0, NCP, 256):
                n = min(256, NCP - c0)
                S.op("dve", lambda e, b=b, kv=kv, c0=c0, n=n: e.tensor_scalar(out=xg[:, 0:n], in0=psA[b][:, c0:c0 + n], scalar1=bias1[:, kv:kv + 1],
                                                                            scalar2=None, op0=ALU.add), reads=[tpsA[b], tbias1], writes=[txg])
                emit_gelu(S, H1g[:, c0:c0 + n], xg[:, 0:n], t1[:, 0:n], t2[:, 0:n], [txg], [tH1g], tt1, tt2)
            if kv == 0:
                bb = nA[0] % 2
                nA[0] += 1
                S.op("pe", lambda e, bb=bb: e.matmul(psA[bb][:, 0:NCP], w2[:, 0, :], H1g[:, 0:NCP], start=True, stop=True),
                     reads=[tW, tH1g], writes=[tpsA[bb]])
                S.op("act", lambda e, bb=bb, gl=gl: e.copy(out=kcmpT[:, gl, 0:NCP], in_=psA[bb][:, 0:NCP]), reads=[tpsA[bb]], writes=[tkcmpT])
                colnorm_max(kcmpT[:, gl, 0:NCP], [tkcmpT], gl, ncols=NCP)
            else:
                for ch in range(CH):
                    bb = nA[0] % 2
                    nA[0] += 1
                    S.op("pe", lambda e, bb=bb, ch=ch: e.matmul(psA[bb][:, 0:128], H1g[:, ch * 128:(ch + 1) * 128], w2[:, 1, :], start=True, stop=True),
                         reads=[tW, tH1g], writes=[tpsA[bb]])
                    S.op("act", lambda e, bb=bb, gl=gl, ch=ch: e.copy(out=vcmp[:, gl, ch, 0:128], in_=psA[bb][:, 0:128]),
                         reads=[tpsA[bb]], writes=[tvcmp])

    if stop == 2:
        S.op("pool", lambda e: e.dma_start(out=y_d[0:128, 0:256], in_=h_sb[:, 0, :]), reads=[th[0][0]], chan=cmisc)
        S.emit(final_waits=[cmisc])
        C.st.close()
        return nc
    S.op("pool", lambda e: e.dma_start(out=W[:], in_=wq_d), writes=[tW], chan=cW)
    qT = C.sb([128, TPB, 8, 128], BF16, "qT")
    tqT = [T_() for _ in range(8)]
    gsb = [C.sb([128, 24], F32, "gsb") for _ in range(2)]
    tgsb = [T_(), T_()]
    cmk = [C.sb([128, CH, 4, 128], BF16, "cmk") for _ in range(2)]
    tcmk = [T_(), T_()]
    ccmk = [S.chan(), S.chan()]
    fpt = [C.sb([128, 64], F32, "fpos") for _ in range(2)]
    fnt = [C.sb([128, 64], F32, "fneg") for _ in range(2)]
    tfp = [T_(), T_()]
    cfp = [S.chan(), S.chan()]
    P = [C.sb([128, 512], BF16, "P") for _ in range(3)]
    tP = [T_() for _ in range(3)]
    qsq = C.sb([128, 512], F32R, "qsq")
    tqsq = T_()
    mq2 = C.sb([128, 1], F32, "mq2")
    nbias = C.sb([128, 3], F32, "nbias")
    tmq2, tnbias = T_(), T_()
    rz = C.sb([128, 4], F32, "rz")
    coef = C.sb([128, 4], F32, "coef")
    trz, tcoef = T_(), T_()
    imp = C.sb([128, 64], F32, "imp")
    imp2 = C.sb([128, 64], F32, "imp2")
    mx8 = C.sb([128, 8], F32, "mx8")
    mx8b = C.sb([128, 8], F32, "mx8b")
    negblk = C.sb([128, 64], F32, "negblk")
    negblkT = C.sb([64, 4, 128], BF16, "negblkT")
    timp, timp2, tmx8, tmx8b, tnegblk, tnegblkT = [T_() for _ in range(6)]
    yt0 = C.sb([128, 1024], F32, "yt")
    yt = [yt0, yt0]
    tyth = [T_() for _ in range(8)]
    cy = [S.chan(), S.chan()]
    nP = [0]
    nSc = [0]

    def score(mm_list, br):
        b = nSc[0] % 3
        nSc[0] += 1
        n = len(mm_list)
        for i, (lhsT, rhs, rd) in enumerate(mm_list):
            S.op("pe", lambda e, lhsT=lhsT, rhs=rhs, i=i, b=b: e.matmul(psB[b][:], lhsT, rhs, start=(i == 0), stop=(i == n - 1)),
                 reads=rd, writes=[tpsB[b]])
        p = nP[0] % 3
        nP[0] += 1
        S.op("act", lambda e, b=b, p=p, br=br: e.activation(out=P[p][:], in_=psB[b][:], func=AF.Exp, bias=nbias[:, br:br + 1], scale=1.0),
             reads=[tpsB[b], tnbias], writes=[tP[p]])
        return p

    oset = [0]

    def pv(p, rhs_v, first, last, width):
        for r in range(4):
            bk = oset[0] * 2 + r // 2
            off = (r % 2) * 256
            S.op("pe", lambda e, r=r, p=p, bk=bk, off=off: e.matmul(psO[bk][:, off:off + width], P[p][:, r * 128:(r + 1) * 128], rhs_v[0],
                                                                  start=(first and r % 2 == 0), stop=last),
                 reads=[tP[p]] + rhs_v[1], writes=[tpsO[bk]])

    def evac(br, gl, kk, width, first_branch, with_imp=False):
        first_branch = (br == branches[0])
        bks = [oset[0] * 2 + r // 2 for r in range(4)]
        offs = [(r % 2) * 256 for r in range(4)]
        oset[0] ^= 1
        for r in range(4):
            S.op("dve", lambda e, r=r: e.tensor_scalar(out=rz[:, r:r + 1], in0=psO[bks[r]][:, offs[r] + 128:offs[r] + 129], scalar1=1.0e-30, scalar2=None, op0=ALU.add),
                 reads=[tpsO[bks[r]]], writes=[trz])
        S.op("dve", lambda e: e.reciprocal(out=rz[:], in_=rz[:]), reads=[trz], writes=[trz])
        gc0 = br * 8 + gl * 4
        S.op("dve", lambda e, gc0=gc0, kk=kk: e.tensor_tensor(out=coef[:], in0=rz[:], in1=gsb[kk][:, gc0:gc0 + 4], op=ALU.mult),
             reads=[trz, tgsb[kk]], writes=[tcoef])
        for r in range(4):
            o_ap = psO[bks[r]][:, offs[r]:offs[r] + 128]
            ysl = slice((gl * 4 + r) * 128, (gl * 4 + r + 1) * 128)
            if br not in branches:
                pass
            elif first_branch:
                S.op("dve", lambda e, r=r, kk=kk, ysl=ysl, o_ap=o_ap: e.tensor_scalar(out=yt[kk][:, ysl], in0=o_ap, scalar1=coef[:, r:r + 1], scalar2=None, op0=ALU.mult),
                     reads=[tpsO[bks[r]], tcoef], writes=[tyth[gl * 4 + r]])
            else:
                S.op("dve", lambda e, r=r, kk=kk, ysl=ysl, o_ap=o_ap: e.scalar_tensor_tensor(out=yt[kk][:, ysl], in0=o_ap, scalar=coef[:, r:r + 1], in1=yt[kk][:, ysl],
                                                                                         op0=ALU.mult, op1=ALU.add),
                     reads=[tpsO[bks[r]], tcoef, tyth[gl * 4 + r]], writes=[tyth[gl * 4 + r]])
        if with_imp:
            for r in range(4):
                i_ap = psO[bks[r]][:, offs[r] + 129:offs[r] + 193]
                if r == 0:
                    S.op("dve", lambda e, r=r, i_ap=i_ap: e.tensor_scalar(out=imp[:], in0=i_ap, scalar1=rz[:, r:r + 1], scalar2=None, op0=ALU.mult),
                         reads=[tpsO[bks[r]], trz], writes=[timp])
                else:
                    S.op("dve", lambda e, r=r, i_ap=i_ap: e.scalar_tensor_tensor(out=imp[:], in0=i_ap, scalar=rz[:, r:r + 1], in1=imp[:], op0=ALU.mult, op1=ALU.add),
                         reads=[tpsO[bks[r]], trz, timp], writes=[timp])

    ntile = 0
    for tb in range(NBLK):
        tsl = slice(tb * TB, (tb + 1) * TB)
        norm_block(tsl)
        load_rope("cosq", "sinq", tsl)
        for hd in range(8):
            b = proj_fm(hd * 128)
            rope_to(b, qT[:, :, hd, :], [tqT[hd]], split=True)
        for j in range(TPB):
            Tq = tb * TPB + j
            jsl = slice(j * 128, (j + 1) * 128)
            kk = ntile % 2
            ntile += 1
            for c in range(DC):
                S.op("pe", lambda e, c=c, jsl=jsl: e.matmul(psS[:, 0:24], hn[:, c, jsl], Wgt[:, c, :], start=(c == 0), stop=(c == DC - 1)),
                     reads=[tWgt, thn[c][0]], writes=[tpsS])
            S.op("act", lambda e, kk=kk: e.activation(out=gsb[kk][:], in_=psS[:, 0:24], func=AF.Sigmoid), reads=[tpsS], writes=[tgsb[kk]])
            for r4 in range(4):
                S.op("pool", lambda e, kk=kk, Tq=Tq, r4=r4: e.dma_start(out=cmk[kk][:, :, r4, :], in_=tabs["cmpmask"][Tq]), writes=[tcmk[kk]], chan=ccmk[kk])
            S.op("sp", lambda e, kk=kk, Tq=Tq: e.dma_start(out=fpt[kk][:], in_=tabs["fpos"][Tq]), writes=[tfp[kk]], chan=cfp[kk])
            S.op("sp", lambda e, kk=kk, Tq=Tq: e.dma_start(out=fnt[kk][:], in_=tabs["fneg"][Tq]), writes=[tfp[kk]], chan=cfp[kk])
            for gl in range(2):
                qv = qT[:, j, gl * 4:(gl + 1) * 4, :]
                qrd = [tqT[gl * 4 + r] for r in range(4)]
                S.op("act", lambda e, qv=qv: e.activation(out=qsq[:].rearrange("p (r q) -> p r q", r=4), in_=qv, func=AF.Square), reads=qrd, writes=[tqsq])
                S.op("pe", lambda e: e.matmul(psS[:, 0:512], ones_r[:], qsq[:], start=True, stop=True), reads=[tqsq, tones], writes=[tpsS])
                S.op("dve", lambda e: e.tensor_reduce(out=mq2[:], in_=psS[:, 0:512], axis=AX.X, op=ALU.max), reads=[tpsS], writes=[tmq2])
                S.op("dve", lambda e, gl=gl: e.tensor_scalar(out=nbias[:], in0=kmax2[:, gl:gl + 5:2], scalar1=mq2[:, 0:1], scalar2=None, op0=ALU.mult),
                     reads=[tkmax2, tmq2], writes=[tnbias])
                S.op("act", lambda e: e.activation(out=nbias[:], in_=nbias[:], func=AF.Sqrt), reads=[tnbias], writes=[tnbias])
                S.op("dve", lambda e: e.tensor_scalar(out=nbias[:], in0=nbias[:], scalar1=-1.0, scalar2=None, op0=ALU.mult), reads=[tnbias], writes=[tnbias])
                chunks = []
                chs = [ch for ch in range(CH) if 16 * (ch * 128) + 31 <= Tq * 128 + 127]
                for ci, ch in enumerate(chs):
                    mm = [(kcmpT[:, gl, ch * 128:(ch + 1) * 128], qv, [tkcmpT] + qrd),
                          (identb[:], cmk[kk][:, ch, :, :], [tidb, tcmk[kk]])]
                    chunks.append((0, mm, (vcmp[:, gl, ch, :], [tvcmp]), ci == 0, ci == len(chs) - 1, 193))
                k0 = max(0, Tq - 4)
                for kc in range(k0, Tq + 1):
                    mm = [(kTs[:, 1, gl, kc * 128:(kc + 1) * 128], qv, [tkTs[1][gl][kc // TPB]] + qrd)]
                    if kc == Tq:
                        mm.append((identb[:], causal[:], [tidb, tcausal]))
                    if kc == Tq - 4:
                        mm.append((identb[:], winneg[:], [tidb, twinneg]))
                    chunks.append((2, mm, (vwin[:, kc, gl, :], [tvwin[kc]]), kc == k0, kc == Tq, 129))
                for kc in range(Tq + 1):
                    mm = [(kTs[:, 0, gl, kc * 128:(kc + 1) * 128], qv, [tkTs[0][gl][kc // TPB]] + qrd),
                          (Eb[:, kc, :], negblkT[:], [tE, tnegblkT])]
                    if kc == Tq:
                        mm.append((identb[:], causal[:], [tidb, tcausal]))
                    chunks.append((1, mm, (vsel[:, kc, gl, :], [tvsel[kc]]), kc == 0, kc == Tq, 129))

                def topk_dve(kk=kk):
                    S.op("dve", lambda e: e.tensor_tensor(out=imp[:], in0=imp[:], in1=fpt[kk][:], op=ALU.max), reads=[timp, tfp[kk]], writes=[timp])
                    S.op("dve", lambda e: e.tensor_tensor(out=imp[:], in0=imp[:], in1=fnt[kk][:], op=ALU.min), reads=[timp, tfp[kk]], writes=[timp])
                    S.op("dve", lambda e: e.max(out=mx8[:], in_=imp[:]), reads=[timp], writes=[tmx8])
                    S.op("dve", lambda e: e.match_replace(out=imp2[:], in_to_replace=mx8[:], in_values=imp[:], imm_value=-3.0e38),
                         reads=[timp, tmx8], writes=[timp2])
                    S.op("dve", lambda e: e.max(out=mx8b[:], in_=imp2[:]), reads=[timp2], writes=[tmx8b])
                    S.op("dve", lambda e: e.tensor_reduce(out=kred[:], in_=mx8b[:], axis=AX.X, op=ALU.min), reads=[tmx8b], writes=[tkred])
                    S.op("dve", lambda e: e.tensor_scalar(out=negblk[:], in0=imp[:], scalar1=kred[:, 0:1], scalar2=None, op0=ALU.is_ge),
                         reads=[timp, tkred], writes=[tnegblk])
                    S.op("dve", lambda e: e.tensor_scalar(out=negblk[:], in0=negblk[:], scalar1=1.0e30, scalar2=-1.0e30, op0=ALU.mult, op1=ALU.add),
                         reads=[tnegblk], writes=[tnegblk])

                def topk_T():
                    S.op("pe", lambda e: e.transpose(psS[0:64, 0:128], negblk[:], ident[:]), reads=[tnegblk, tid], writes=[tpsS])
                    S.op("act", lambda e: e.copy(out=negblkT[:], in_=psS[0:64, 0:128][:, None, :].to_broadcast([64, 4, 128])), reads=[tpsS], writes=[tnegblkT])

                nch = len(chunks)
                first_sel = next(i for i, c in enumerate(chunks) if c[0] == 1)

                def do_score(i):
                    if i == first_sel:
                        topk_T()
                    return score(chunks[i][1], chunks[i][0])

                cmp_last = max(i for i, c in enumerate(chunks) if c[0] == 0)
                pend = []
                nxt = [0]

                def fill(i):
                    while nxt[0] < nch and nxt[0] <= i + 2:
                        if nxt[0] == first_sel and i <= cmp_last:
                            break
                        pend.append(do_score(nxt[0]))
                        nxt[0] += 1

                fill(-1)
                for i in range(nch):
                    br_i, _, rhs_v, first, last, width = chunks[i]
                    fill(i)
                    pv(pend.pop(0), rhs_v, first, last, width)
                    if last:
                        evac(br_i, gl, kk, width, br_i == 0, with_imp=(br_i == 0))
                        if br_i == 0:
                            topk_dve()
            S.op("sp", lambda e, kk=kk, Tq=Tq: e.dma_start(out=y_d[Tq * 128:(Tq + 1) * 128, :], in_=yt[kk][:]), reads=tyth, chan=cy[kk])
    S.emit(final_waits=cy)
    C.st.close()
    return nc


def _lay(w):
    dc = w.shape[0] // 128
    return np.ascontiguousarray(w.reshape(dc, 128, -1).transpose(1, 0, 2))


def _gT(g):
    return np.ascontiguousarray(np.asarray(g, np.float32).reshape(-1, 128).T)


def _prep_ffn_w(Wg, Wu, Wd):
    D, FF = Wg.shape
    DC, FC = D // 128, FF // 128
    t = lambda W: np.ascontiguousarray(W.reshape(DC, 128, FC, 128).transpose(2, 1, 0, 3).reshape(FC, 128, DC * 128))
    wd = np.ascontiguousarray(Wd.reshape(FC, 128, DC, 128).transpose(2, 1, 0, 3).reshape(DC, 128, FC * 128))
    return t(Wg), t(Wu), wd


def _prep_wo(Wo):
    DC = Wo.shape[0] // 128
    DO = Wo.shape[1] // 128
    return np.ascontiguousarray(Wo.reshape(DC, 128, DO, 128).transpose(2, 1, 0, 3).reshape(DO, 128, DC * 128))


def _prep_ab(w_in, gate_b, g_norm, g_ws, g_bs, conv_w, m_norm, hp):
    o1, o2, o3, o4 = 2048, 4096, 5120, 6144
    wz = _lay(w_in[:, :o1])
    hs = slice(hp * 512, (hp + 1) * 512)
    wqk = _lay(np.concatenate([w_in[:, o1:o1 + 1024][:, hs], w_in[:, o1 + 1024:o2][:, hs]], 1))
    gi = w_in[:, o4:o4 + 4][:, hp * 2:hp * 2 + 2]
    gf = w_in[:, o4 + 4:o4 + 8][:, hp * 2:hp * 2 + 2]
    wvog = _lay(np.concatenate([w_in[:, o2:o3][:, hs], w_in[:, o3:o4][:, hs], gi, gf], 1))
    gb = np.concatenate([gate_b[0, hp * 2:hp * 2 + 2], gate_b[1, hp * 2:hp * 2 + 2]])
    gb = np.ascontiguousarray(np.broadcast_to(gb[None, :], (128, 4))).astype(np.float32)
    cc = np.concatenate([conv_w[:, :1024][:, hs], conv_w[:, 1024:][:, hs]], 1)
    conv = np.ascontiguousarray(cc.T.reshape(8, 128, 4).transpose(1, 0, 2))
    mn = np.ascontiguousarray(np.broadcast_to(m_norm[hs][None, :], (128, 512)))
    gn = np.ascontiguousarray(np.broadcast_to(g_norm[None, :], (128, 1024)))
    wsT = np.ascontiguousarray(g_ws.transpose(2, 0, 1))
    bs = np.ascontiguousarray(g_bs.T)
    ident = np.eye(128, dtype=np.float32)
    mask = np.triu(np.ones((128, 128), np.float32))
    return dict(wz=wz, wqk=wqk, wvog=wvog, gb=gb, conv=conv, mn=mn, gn=gn, wsT=wsT, bs=bs, ident=ident, mask=mask)


def _prep_nsa(w_in, cmp_pos, cmp_w1, cmp_w2, hp):
    wq = _lay(w_in[:, hp * 1024:(hp + 1) * 1024])

    def kvcol(br, kv, g):
        o = 2048 + ((br * 2 + kv) * 4 + g) * 128
        return w_in[:, o:o + 128]
    cols = []
    for br in range(3):
        for gl in range(2):
            cols.append(kvcol(br, 0, 2 * hp + gl))
    for gl in range(2):
        cols.append(kvcol(0, 1, 2 * hp + gl))
    for br in (1, 2):
        for gl in range(2):
            cols.append(kvcol(br, 1, 2 * hp + gl))
    wkv = _lay(np.concatenate(cols, 1))
    og = 2048 + 3072
    gc = []
    for br in range(3):
        for gl in range(2):
            g = 2 * hp + gl
            gc.append(w_in[:, og + br * 16 + g * 4: og + br * 16 + g * 4 + 4])
    wgt = _lay(np.concatenate(gc, 1))
    w1 = np.ascontiguousarray(cmp_w1.reshape(2, 32, 128, 128).transpose(2, 0, 1, 3))
    w2 = np.ascontiguousarray(cmp_w2.transpose(1, 0, 2))
    posT = np.ascontiguousarray(cmp_pos.transpose(2, 0, 1))
    return dict(wq=wq, wkv=wkv, wgt=wgt, w1=w1, w2=w2, posT=posT)


_PROGS = {}


def _prog(key, fn):
    if key not in _PROGS:
        _PROGS[key] = fn()
    return _PROGS[key]


def _run(nc, maps):
    res = run_bass_kernel_spmd(nc, maps, core_ids=list(range(8)))
    return res.results


def kernel(x, ffn_norm, ffn_w_gate, ffn_w_up, ffn_w_down, mix_norm, ab_w_in, mlstm_gate_bias,
           gmlp_norm, gmlp_w_s, gmlp_b_s, mlstm_conv, mlstm_norm, ab_w_out, nsa_w_in, nsa_cmp_pos,
           nsa_cmp_w1, nsa_cmp_w2, nsa_w_out, final_norm):
    f = lambda a: np.asarray(a, dtype=np.float32)
    x = f(x)
    B, Sq, D = x.shape
    FF = ffn_w_gate.shape[-1]
    NTC = B * Sq // 8
    hT = [np.ascontiguousarray(x[c // 2, (c % 2) * NTC:(c % 2 + 1) * NTC, :].T) for c in range(8)]

    def ffn(hT, layer, which, yT=None, wo=None, final=False):
        wg, wu, wd = _prep_ffn_w(f(ffn_w_gate[layer, which]), f(ffn_w_up[layer, which]), f(ffn_w_down[layer, which]))
        g = _gT(ffn_norm[layer, which])
        pre = yT is not None
        nc = _prog(("ffn", pre, final), lambda: build_ffn(D, FF, NTC, 1024, 2, pre=pre, final=final))
        maps = []
        for c in range(8):
            m = {"hT": hT[c], "g": g, "wg": wg, "wu": wu, "wd": wd}
            if pre:
                m["yT"] = yT[c]
                m["wo"] = wo
            if final:
                m["gf"] = _gT(final_norm)
            maps.append(m)
        r = _run(nc, maps)
        return [r[c]["oT"] for c in range(8)]

    def full_seq(hT, b):
        return np.ascontiguousarray(np.concatenate([hT[2 * b], hT[2 * b + 1]], axis=1))

    hT = ffn(hT, 0, 0)
    nc = _prog(("ab",), lambda: build_ab(Sq, D))
    maps = []
    for c in range(8):
        b, hp = c // 2, c % 2
        m = _prep_ab(f(ab_w_in[0]), f(mlstm_gate_bias[0]), f(gmlp_norm[0]), f(gmlp_w_s[0]), f(gmlp_b_s[0]), f(mlstm_conv[0]),
                     f(mlstm_norm[0]), hp)
        m["hT"] = full_seq(hT, b)
        m["hTh"] = hT[c]
        m["g"] = _gT(mix_norm[0])
        maps.append(m)
    r = _run(nc, maps)
    yT = []
    for c in range(8):
        b, hp = c // 2, c % 2
        sl = slice(hp * NTC, (hp + 1) * NTC)
        ya = r[c]["ya"]
        yb = np.concatenate([r[2 * b]["yb"][sl], r[2 * b + 1]["yb"][sl]], axis=1)
        yT.append(np.ascontiguousarray(np.concatenate([ya, yb], axis=1).T))
    hT = ffn(hT, 0, 1, yT=yT, wo=_prep_wo(f(ab_w_out[0])))
    hT = ffn(hT, 1, 0)
    nc = _prog(("nsa",), lambda: build_nsa(Sq, D))
    tabs = nsa_tables(Sq)
    maps = []
    for c in range(8):
        b, hp = c // 2, c % 2
        m = _prep_nsa(f(nsa_w_in[0]), f(nsa_cmp_pos[0]), f(nsa_cmp_w1[0]), f(nsa_cmp_w2[0]), hp)
        m.update(tabs)
        m["hT"] = full_seq(hT, b)
        m["g"] = _gT(mix_norm[1])
        maps.append(m)
    r = _run(nc, maps)
    yT = []
    for c in range(8):
        b, hp = c // 2, c % 2
        sl = slice(hp * NTC, (hp + 1) * NTC)
        y = np.concatenate([r[2 * b]["y"][sl], r[2 * b + 1]["y"][sl]], axis=1)
        yT.append(np.ascontiguousarray(y.T))
    hT = ffn(hT, 1, 1, yT=yT, wo=_prep_wo(f(nsa_w_out[0])), final=True)
    out = np.empty((B, Sq, D), np.float32)
    for c in range(8):
        out[c // 2, (c % 2) * NTC:(c % 2 + 1) * NTC, :] = hT[c].T
    return out
```

```python
import bisect
import contextlib
import numpy as np
import concourse.bass as bass
import concourse.mybir as mybir
from concourse.bass_utils import run_bass_kernel_spmd

F32 = mybir.dt.float32
F32R = mybir.dt.float32r
BF16 = mybir.dt.bfloat16
AF = mybir.ActivationFunctionType
ALU = mybir.AluOpType
AX = mybir.AxisListType

ENGS = ("pe", "act", "dve", "pool", "sp")


class Tile:
    __slots__ = ("name", "w", "r", "psum")

    def __init__(self, name="", psum=False):
        self.name = name
        self.w = None
        self.r = []
        self.psum = psum


class Chan:
    __slots__ = ("sem", "cnt", "ops")

    def __init__(self):
        self.sem = None
        self.cnt = 0
        self.ops = []


class Sched:
    def __init__(self, nc):
        self.nc = nc
        self.ops = []
        self.chans = []

    def chan(self):
        c = Chan()
        self.chans.append(c)
        return c

    def op(self, eng, fn, reads=(), writes=(), chan=None, nosync_same=False):
        idx = len(self.ops)
        deps = set()
        for t in reads:
            if t.w is not None:
                deps.add(t.w)
            if t.psum:
                for r in t.r:
                    if self.ops[r]["eng"] != eng:
                        deps.add(r)
        for t in writes:
            if t.w is not None:
                deps.add(t.w)
            for r in t.r:
                deps.add(r)
        deps.discard(idx)
        rec = dict(eng=eng, fn=fn, deps=deps, chan=chan, idx=idx, nosync_same=nosync_same,
                   sig=None)
        if chan is not None:
            chan.cnt += 16
            rec["sig"] = (chan, chan.cnt)
            chan.ops.append(idx)
        self.ops.append(rec)
        for t in reads:
            t.r.append(idx)
        for t in writes:
            t.w = idx
            t.r = []
        return idx

    def emit(self, final_waits=()):
        nc = self.nc
        ops = self.ops
        needed = set()
        for o in ops:
            for d in o["deps"]:
                od = ops[d]
                if od["chan"] is None:
                    if od["eng"] == o["eng"] and (o["nosync_same"] or od["eng"] == "pe" or od["eng"] == "sp"):
                        continue
                    needed.add(d)
        cnt = {e: 0 for e in ENGS}
        for o in ops:
            if o["chan"] is None and o["idx"] in needed:
                cnt[o["eng"]] += 1
                o["sig"] = (o["eng"], cnt[o["eng"]])
        with contextlib.ExitStack() as st:
            esem = {e: st.enter_context(nc.semaphore("s_" + e)) for e in ENGS}
            for i, c in enumerate(self.chans):
                c.sem = st.enter_context(nc.semaphore("c%d" % i))
            block = st.enter_context(nc.Block())

            def run_engine(ename, eobj):
                known = {}
                for o in ops:
                    if o["eng"] != ename:
                        continue
                    want = {}
                    for d in o["deps"]:
                        od = ops[d]
                        sig = od["sig"]
                        if sig is None:
                            continue
                        key, val = sig
                        if od["chan"] is None and od["eng"] == ename and (
                                o["nosync_same"] or ename in ("pe", "sp")):
                            continue
                        if isinstance(key, Chan):
                            val = 16 * bisect.bisect_left(key.ops, o["idx"])
                        if want.get(key, 0) < val:
                            want[key] = val
                    for key, val in want.items():
                        if known.get(key, 0) >= val:
                            continue
                        known[key] = val
                        sem = key.sem if isinstance(key, Chan) else esem[key]
                        eobj.wait_ge(sem, val)
                    ins = o["fn"](eobj)
                    if o["sig"] is not None:
                        key, val = o["sig"]
                        if isinstance(key, Chan):
                            ins.then_inc(key.sem, 16)
                        else:
                            ins.then_inc(esem[key], 1)
                if ename == "sp":
                    for c in final_waits:
                        eobj.wait_ge(c.sem, c.cnt)

            @block.tensor
            def _(e):
                run_engine("pe", e)

            @block.scalar
            def _(e):
                run_engine("act", e)

            @block.vector
            def _(e):
                run_engine("dve", e)

            @block.gpsimd
            def _(e):
                run_engine("pool", e)

            @block.sync
            def _(e):
                run_engine("sp", e)


class Ctx:
    def __init__(self):
        self.nc = bass.Bass("TRN2", target_bir_lowering=False)
        self.st = contextlib.ExitStack()
        self.S = Sched(self.nc)
        self.n = 0

    def sb(self, shape, dt, name=None):
        self.n += 1
        return self.st.enter_context(self.nc.sbuf_tensor("%s_%d" % (name or "sb", self.n), list(shape), dt))

    def ps(self, shape=(128, 512), dt=F32, name=None):
        self.n += 1
        return self.st.enter_context(self.nc.psum_tensor("%s_%d" % (name or "ps", self.n), list(shape), dt))

    def din(self, name, shape, dt=F32):
        return self.nc.dram_tensor(name, list(shape), dt, kind="ExternalInput").ap()

    def dout(self, name, shape, dt=F32):
        return self.nc.dram_tensor(name, list(shape), dt, kind="ExternalOutput").ap()


EPS = 1e-6


def emit_rstd(S, rstd, trstd, ps_ssq, tssq, Dn, SUB=512):
    S.op("act", lambda e: e.activation(out=rstd[:, 0:SUB], in_=ps_ssq[:, 0:SUB], func=AF.Sqrt, bias=EPSB[0][:, 0:1], scale=1.0 / Dn),
         reads=[tssq, EPSB[1]], writes=[trstd])
    S.op("dve", lambda e: e.reciprocal(out=rstd[:, 0:SUB], in_=rstd[:, 0:SUB]), reads=[trstd], writes=[trstd])


EPSB = [None, None]


def emit_consts(C):
    S = C.S
    eps = C.sb([128, 1], F32, "eps")
    teps = Tile()
    S.op("dve", lambda e: e.memset(eps[:], EPS), writes=[teps])
    EPSB[0], EPSB[1] = eps, teps
    ones32 = C.sb([128, 128], F32, "ones32")
    ones_r = C.sb([128, 128], F32R, "ones")
    t32, tones = Tile(), Tile()
    S.op("dve", lambda e: e.memset(ones32[:], 1.0), writes=[t32])
    S.op("dve", lambda e: e.tensor_copy(out=ones_r[:], in_=ones32[:]), reads=[t32], writes=[tones])
    return ones_r, tones


def emit_rmsnorm_T(C, h_sb, th, aT, taT, g_sb, tg, ones_r, tones, DC, TB, sq, tsq, ps_ssq, tssq, rstd, trstd,
                   Dn, out_dt_cast=None):
    S = C.S
    SUB = min(512, TB)
    NS = TB // SUB
    for s in range(NS):
        sl = slice(s * SUB, (s + 1) * SUB)
        for c in range(DC):
            k = (s * DC + c) % 2
            S.op("act", lambda e, c=c, k=k, sl=sl: e.activation(out=sq[k][:, 0:SUB], in_=h_sb[:, c, sl], func=AF.Square),
                 reads=[th[c][s]], writes=[tsq[k]])
            S.op("pe", lambda e, c=c, k=k: e.matmul(ps_ssq[:, 0:SUB], ones_r[:], sq[k][:, 0:SUB], start=(c == 0), stop=(c == DC - 1)),
                 reads=[tsq[k], tones], writes=[tssq])
        emit_rstd(S, rstd, trstd, ps_ssq, tssq, Dn, SUB)
        for c in range(DC):
            S.op("dve", lambda e, c=c, sl=sl: e.scalar_tensor_tensor(out=aT[:, c, sl], in0=h_sb[:, c, sl],
                                                                      scalar=g_sb[:, c:c + 1], in1=rstd[:, 0:SUB],
                                                                      op0=ALU.mult, op1=ALU.mult),
                 reads=[th[c][s], tg, trstd], writes=[taT[c][s]])


def build_ffn(D, FF, NT, TB, NH, pre=False, final=False, nstage=1):
    C = Ctx()
    nc, S = C.nc, C.S
    DC, FC = D // 128, FF // 128
    FH = FC // NH
    NS = TB // 512
    NB = NT // TB
    hT = C.din("hT", [D, NT])
    sfx = ["", "2", "3", "4"]
    g_ds = [C.din("g" + sfx[i], [128, DC]) for i in range(nstage)]
    wg_ds = [C.din("wg" + sfx[i], [FC, 128, DC * 128]) for i in range(nstage)]
    wu_ds = [C.din("wu" + sfx[i], [FC, 128, DC * 128]) for i in range(nstage)]
    wd_ds = [C.din("wd" + sfx[i], [DC, 128, FC * 128]) for i in range(nstage)]
    if pre:
        yT = C.din("yT", [D, NT])
        wo_d = C.din("wo", [DC, 128, DC * 128])
    if final:
        gf_d = C.din("gf", [128, DC])
    oT = C.dout("oT", [D, NT])
    hTv = hT.rearrange("(c p) t -> p c t", p=128)
    oTv = oT.rearrange("(c p) t -> p c t", p=128)

    h_sb = C.sb([128, DC, TB], F32, "h")
    aT = C.sb([128, DC, TB], BF16, "aT")
    HT = C.sb([128, FH, TB], BF16, "HT")
    wg = [C.sb([128, DC * 128], BF16, "wg") for _ in range(2)]
    wu = [C.sb([128, DC * 128], BF16, "wu") for _ in range(2)]
    wd = [C.sb([128, FH * 128], BF16, "wd") for _ in range(2)]
    sq = [C.sb([128, 512], F32R, "sq") for _ in range(2)]
    sg = [C.sb([128, 512], F32, "sg") for _ in range(2)]
    rstd = C.sb([128, 512], F32, "rstd")
    g_sbs = [C.sb([128, DC], F32, "g") for _ in range(nstage)]
    ps_ssq = C.ps(name="ssq")
    psG = [C.ps(name="G") for _ in range(2)]
    psU = [C.ps(name="U") for _ in range(2)]
    psY = [C.ps(name="Y") for _ in range(2)]
    if pre:
        wo = [C.sb([128, DC * 128], BF16, "wo") for _ in range(2)]
        two = [Tile() for _ in range(2)]
        cwo = [S.chan() for _ in range(2)]
    if final:
        gf_sb = C.sb([128, DC], F32, "gf")
        tgf = Tile()
        fo = C.sb([128, DC, TB], F32, "fo") if False else None

    th = [[Tile() for _ in range(NS)] for _ in range(DC)]
    taT = [[Tile() for _ in range(NS)] for _ in range(DC)]
    tHT = [[Tile() for _ in range(NS)] for _ in range(FH)]
    twg = [Tile() for _ in range(2)]
    twu = [Tile() for _ in range(2)]
    twd = [Tile() for _ in range(2)]
    tsq = [Tile() for _ in range(2)]
    tsg = [Tile() for _ in range(2)]
    trstd, tssq = Tile(), Tile()
    tgs = [Tile() for _ in range(nstage)]
    tG = [Tile(psum=True) for _ in range(2)]
    tU = [Tile(psum=True) for _ in range(2)]
    tY = [Tile(psum=True) for _ in range(2)]
    cwg = [S.chan() for _ in range(2)]
    cwu = [S.chan() for _ in range(2)]
    cwd = [S.chan() for _ in range(2)]
    ch_in = S.chan()
    ch_out = S.chan()
    cg = S.chan()

    for i in range(nstage):
        S.op("sp", lambda e, i=i: e.dma_start(out=g_sbs[i][:], in_=g_ds[i]), writes=[tgs[i]], chan=cg)
    if final:
        cgf = S.chan()
        S.op("sp", lambda e: e.dma_start(out=gf_sb[:], in_=gf_d), writes=[tgf], chan=cgf)
    ones_r, tones = emit_consts(C)

    all_h = [th[c][s] for c in range(DC) for s in range(NS)]
    all_aT = [taT[c][s] for c in range(DC) for s in range(NS)]
    CG = min(4, DC)
    nwd = 0
    nwgu = 0
    for tb in range(NB):
        tsl = slice(tb * TB, (tb + 1) * TB)
        early = not final
        if tb == 0 or not early:
            for c0 in range(0, DC, CG):
                S.op("sp", lambda e, c0=c0, tsl=tsl: e.dma_start(out=h_sb[:, c0:c0 + CG, :], in_=hTv[:, c0:c0 + CG, tsl]),
                     writes=[th[c][s] for c in range(c0, c0 + CG) for s in range(NS)], chan=ch_in)
        if pre:
            for c0 in range(0, DC, CG):
                yv = yT.rearrange("(c p) t -> p c t", p=128)
                S.op("pool", lambda e, c0=c0, tsl=tsl, yv=yv: e.dma_start(out=aT[:, c0:c0 + CG, :], in_=yv[:, c0:c0 + CG, tsl]),
                     writes=[taT[c][s] for c in range(c0, c0 + CG) for s in range(NS)], chan=ch_in)
            for dc in range(DC):
                k = dc % 2
                S.op("pool", lambda e, dc=dc, k=k: e.dma_start(out=wo[k][:], in_=wo_d[dc]), writes=[two[k]], chan=cwo[k])
                for s in range(NS):
                    sl = slice(s * 512, (s + 1) * 512)
                    b = (dc * NS + s) % 2
                    for c in range(DC):
                        S.op("pe", lambda e, c=c, k=k, b=b, sl=sl: e.matmul(psY[b][:], wo[k][:, c * 128:(c + 1) * 128], aT[:, c, sl],
                                                                          start=(c == 0), stop=(c == DC - 1)),
                             reads=[two[k], taT[c][s]], writes=[tY[b]])
                    S.op("dve", lambda e, dc=dc, b=b, sl=sl: e.tensor_tensor(out=h_sb[:, dc, sl], in0=psY[b][:], in1=h_sb[:, dc, sl], op=ALU.add),
                         reads=[tY[b], th[dc][s]], writes=[th[dc][s]])
        for st in range(nstage):
            emit_rmsnorm_T(C, h_sb, th, aT, taT, g_sbs[st], tgs[st], ones_r, tones, DC, TB, sq, tsq, ps_ssq, tssq, rstd, trstd, D)
            for hf in range(NH):
                for fi in range(FH):
                    f = hf * FH + fi
                    k = nwgu % 2
                    nwgu += 1
                    S.op("pool", lambda e, f=f, k=k, src=wg_ds[st]: e.dma_start(out=wg[k][:], in_=src[f]), writes=[twg[k]], chan=cwg[k])
                    S.op("pool", lambda e, f=f, k=k, src=wu_ds[st]: e.dma_start(out=wu[k][:], in_=src[f]), writes=[twu[k]], chan=cwu[k])
                    for s in range(NS):
                        sl = slice(s * 512, (s + 1) * 512)
                        b = (fi * NS + s) % 2
                        for c in range(DC):
                            S.op("pe", lambda e, c=c, k=k, b=b, sl=sl: e.matmul(psG[b][:], wg[k][:, c * 128:(c + 1) * 128], aT[:, c, sl],
                                                                              start=(c == 0), stop=(c == DC - 1)),
                                 reads=[twg[k], taT[c][s]], writes=[tG[b]])
                        for c in range(DC):
                            S.op("pe", lambda e, c=c, k=k, b=b, sl=sl: e.matmul(psU[b][:], wu[k][:, c * 128:(c + 1) * 128], aT[:, c, sl],
                                                                              start=(c == 0), stop=(c == DC - 1)),
                                 reads=[twu[k], taT[c][s]], writes=[tU[b]])
                        S.op("act", lambda e, b=b: e.activation(out=sg[b][:], in_=psG[b][:], func=AF.Silu),
                             reads=[tG[b]], writes=[tsg[b]])
                        S.op("dve", lambda e, b=b, fi=fi, sl=sl: e.tensor_tensor(out=HT[:, fi, sl], in0=psU[b][:], in1=sg[b][:], op=ALU.mult),
                             reads=[tU[b], tsg[b]], writes=[tHT[fi][s]])
                for dc in range(DC):
                    k = nwd % 2
                    nwd += 1
                    S.op("pool", lambda e, dc=dc, hf=hf, k=k, src=wd_ds[st]: e.dma_start(out=wd[k][:], in_=src[dc, :, hf * FH * 128:(hf + 1) * FH * 128]),
                         writes=[twd[k]], chan=cwd[k])
                    for s in range(NS):
                        sl = slice(s * 512, (s + 1) * 512)
                        b = (dc * NS + s) % 2
                        for fi in range(FH):
                            S.op("pe", lambda e, fi=fi, k=k, b=b, sl=sl: e.matmul(psY[b][:], wd[k][:, fi * 128:(fi + 1) * 128], HT[:, fi, sl],
                                                                                start=(fi == 0), stop=(fi == FH - 1)),
                                 reads=[twd[k], tHT[fi][s]], writes=[tY[b]])
                        S.op("dve", lambda e, dc=dc, b=b, sl=sl: e.scalar_tensor_tensor(out=h_sb[:, dc, sl], in0=psY[b][:], scalar=0.5,
                                                                                     in1=h_sb[:, dc, sl], op0=ALU.mult, op1=ALU.add),
                             reads=[tY[b], th[dc][s]], writes=[th[dc][s]])
                    if early and st == nstage - 1 and hf == NH - 1 and (dc + 1) % CG == 0:
                        c0 = dc + 1 - CG
                        S.op("sp", lambda e, c0=c0, tsl=tsl: e.dma_start(out=oTv[:, c0:c0 + CG, tsl], in_=h_sb[:, c0:c0 + CG, :]),
                             reads=[th[c][s] for c in range(c0, c0 + CG) for s in range(NS)], chan=ch_out)
                        if tb + 1 < NB:
                            nsl = slice((tb + 1) * TB, (tb + 2) * TB)
                            S.op("sp", lambda e, c0=c0, nsl=nsl: e.dma_start(out=h_sb[:, c0:c0 + CG, :], in_=hTv[:, c0:c0 + CG, nsl]),
                                 writes=[th[c][s] for c in range(c0, c0 + CG) for s in range(NS)], chan=ch_in)
        if final:
            NSx = NS
            for s in range(NSx):
                sl = slice(s * 512, (s + 1) * 512)
                for c in range(DC):
                    k = (s * DC + c) % 2
                    S.op("act", lambda e, c=c, k=k, sl=sl: e.activation(out=sq[k][:], in_=h_sb[:, c, sl], func=AF.Square),
                         reads=[th[c][s]], writes=[tsq[k]])
                    S.op("pe", lambda e, c=c, k=k: e.matmul(ps_ssq[:], ones_r[:], sq[k][:], start=(c == 0), stop=(c == DC - 1)),
                         reads=[tsq[k], tones], writes=[tssq])
                emit_rstd(S, rstd, trstd, ps_ssq, tssq, D)
                for c in range(DC):
                    S.op("dve", lambda e, c=c, sl=sl: e.scalar_tensor_tensor(out=h_sb[:, c, sl], in0=h_sb[:, c, sl],
                                                                              scalar=gf_sb[:, c:c + 1], in1=rstd[:],
                                                                              op0=ALU.mult, op1=ALU.mult),
                         reads=[th[c][s], tgf, trstd], writes=[th[c][s]])
        if not early:
            for c0 in range(0, DC, CG):
                S.op("sp", lambda e, c0=c0, tsl=tsl: e.dma_start(out=oTv[:, c0:c0 + CG, tsl], in_=h_sb[:, c0:c0 + CG, :]),
                     reads=[th[c][s] for c in range(c0, c0 + CG) for s in range(NS)], chan=ch_out)
    S.emit(final_waits=[ch_out])
    C.st.close()
    return nc


GELU_C = 1.5957691216057308


def emit_gelu(S, out_ap, x_ap, t1_ap, t2_ap, reads, writes, tt1, tt2, eng2="dve"):
    S.op("act", lambda e: e.activation(out=t1_ap, in_=x_ap, func=AF.Square), reads=reads, writes=[tt1])
    S.op("dve", lambda e: e.tensor_scalar(out=t1_ap, in0=t1_ap, scalar1=0.044715, scalar2=1.0, op0=ALU.mult, op1=ALU.add),
         reads=[tt1], writes=[tt1])
    S.op("dve", lambda e: e.tensor_tensor(out=t1_ap, in0=x_ap, in1=t1_ap, op=ALU.mult), reads=reads + [tt1], writes=[tt1])
    S.op("act", lambda e: e.activation(out=t2_ap, in_=t1_ap, func=AF.Sigmoid, scale=GELU_C), reads=[tt1], writes=[tt2])
    S.op("dve", lambda e: e.tensor_tensor(out=out_ap, in0=x_ap, in1=t2_ap, op=ALU.mult), reads=reads + [tt2], writes=writes)


def build_ab(Sq, D=2048, debug=False):
    C = Ctx()
    nc, S = C.nc, C.S
    DC = D // 128
    TB = 512
    NBLK = Sq // TB
    SH = Sq // 2
    DH = 256
    hT = C.din("hT", [D, Sq])
    hTh = C.din("hTh", [D, SH])
    g_d = C.din("g", [128, DC])
    wz_d = C.din("wz", [128, DC, 2048])
    wqk_d = C.din("wqk", [128, DC, 1024])
    wvog_d = C.din("wvog", [128, DC, 1028])
    gb_d = C.din("gb", [128, 4])
    conv_d = C.din("conv", [128, 8, 4])
    mn_d = C.din("mn", [128, 512])
    gn_d = C.din("gn", [128, 1024])
    wsT_d = C.din("wsT", [128, 8, 128])
    bs_d = C.din("bs", [128, 8])
    ident_d = C.din("ident", [128, 128])
    mask_d = C.din("mask", [128, 128])
    ya = C.dout("ya", [SH, 1024])
    yb = C.dout("yb", [Sq, 512])
    hTv = hT.rearrange("(c p) t -> p c t", p=128)
    hThv = hTh.rearrange("(c p) t -> p c t", p=128)
    if debug:
        dbg = C.dout("dbg", [Sq // 128, 128, 16])
        dbg_sb = C.sb([128, 16], F32, "dbg")
        tdbg = Tile()
        cdbg = S.chan()

    W = C.sb([128, DC, 2052], BF16, "W")
    h_sb = C.sb([128, DC, TB], F32, "h")
    hn = C.sb([128, DC, TB], BF16, "hn")
    sq = [C.sb([128, 512], F32R, "sq") for _ in range(2)]
    rstd = C.sb([128, 512], F32, "rstd")
    g_sb = C.sb([128, DC], F32, "g")
    gb_sb = C.sb([128, 4], F32, "gb")
    conv_sb = C.sb([128, 8, 4], F32, "conv")
    mn_sb = C.sb([128, 512], F32, "mn")
    gn_sb = C.sb([128, 1024], F32, "gn")
    wsT32 = C.sb([128, 8, 128], F32, "wsT32")
    wsT = C.sb([128, 8, 128], BF16, "wsT")
    bs_sb = C.sb([128, 8], F32, "bs")
    ident = C.sb([128, 128], F32, "ident")
    identb = C.sb([128, 128], BF16, "identb")
    mask = C.sb([128, 128], F32, "mask")
    ones32 = C.sb([128, 128], F32, "ones32")
    qkpre = C.sb([128, 8, 3 + TB], F32, "qkpre")
    acc = C.sb([128, TB], F32, "acc")
    qkT = C.sb([128, 8, TB], BF16, "qkT")
    vaug = [C.sb([128, 2, 257], F32, "vaug") for _ in range(2)]
    so = [C.sb([128, 512], F32, "so") for _ in range(2)]
    gts = [C.sb([128, 4], F32, "gts") for _ in range(2)]
    lf = C.sb([128, 2], F32, "lf")
    iv = C.sb([128, 2], F32, "iv")
    bcol = C.sb([128, 2], F32, "bcol")
    acol = C.sb([128, 2], F32, "acol")
    a_bc = C.sb([128, 2, 128], F32, "a_bc")
    amax = C.sb([128, 2], F32, "amax")
    Mx = C.sb([128, 2], F32, "Mx")
    mprev = C.sb([128, 2], F32, "mprev")
    wprev = C.sb([128, 2], F32, "wprev")
    ws = C.sb([128, 2], F32, "ws")
    thr = C.sb([128, 2], F32, "thr")
    tmp2 = C.sb([128, 2], F32, "tmp2")
    sTm_h = [C.sb([128, 128], BF16, "sTm") for _ in range(2)]
    vw_h = [C.sb([128, 257], BF16, "vw") for _ in range(2)]
    CT = C.sb([128, 2, 2, 257], F32, "CT")
    CTb_h = [C.sb([128, 2, 257], BF16, "CTb") for _ in range(2)]
    ktok_h = [C.sb([128, 256], BF16, "ktok") for _ in range(2)]
    den_h = [C.sb([128, 1], F32, "den") for _ in range(2)]
    hh_h = [C.sb([128, 256], F32, "hh") for _ in range(2)]
    junk = C.sb([128, 256], F32, "junk")
    ssq1 = C.sb([128, 2], F32, "ssq1")
    ybt = [C.sb([128, 512], F32, "ybt") for _ in range(2)]
    u_sb = C.sb([128, 1024], F32, "u")
    v_sb = C.sb([128, 1024], F32, "v")
    vn = C.sb([128, 1024], BF16, "vn")
    t1 = C.sb([128, 512], F32, "t1")
    t2 = C.sb([128, 512], F32, "t2")
    yat = [C.sb([128, 1024], F32, "yat") for _ in range(2)]

    psA = [C.ps(name="A") for _ in range(2)]
    psB = [C.ps(name="B") for _ in range(2)]
    psS = C.ps(name="S")
    ps_ssq = psS
    psN_h = [C.ps(name="N") for _ in range(2)]
    psT = C.ps([128, 512], BF16, name="T")

    T = Tile
    tW, tg, tgb, tconv, tmn, tgn, tws32, tws, tbs, tid, tidb, tmask, tones32 = [T() for _ in range(13)]
    th = [[T()] for _ in range(DC)]
    thn = [[T()] for _ in range(DC)]
    tsq = [T(), T()]
    trstd = T()
    tqkpre = [T() for _ in range(8)]
    tacc = T()
    tqkT = [T() for _ in range(8)]
    tvaug = [T(), T()]
    tso = [T(), T()]
    tgts = [T(), T()]
    tlf, tiv, tbcol, tacol, tabc, tamax, tMx, tmprev, twprev, tws_, tthr, ttmp2 = [T() for _ in range(12)]
    tCT, tjunk, tssq1 = T(), T(), T()
    tsTm_h, tvw_h, tCTb_h, tktok_h, tden_h, thh_h, tssq1_h, tCT_h = [[T(), T()] for _ in range(8)]
    tybt_h = [[T(), T()], [T(), T()]]
    tu, tv, tvn, tt1, tt2 = [T() for _ in range(5)]
    tyat = [T(), T()]
    tpsA = [T(psum=True), T(psum=True)]
    tpsB = [T(psum=True), T(psum=True)]
    tpsS, tpsT = T(psum=True), T(psum=True)
    tpsN_h = [T(psum=True), T(psum=True)]

    cmisc = S.chan()
    cW = S.chan()
    ch_in = S.chan()
    cyb = [S.chan(), S.chan()]
    cya = [S.chan(), S.chan()]

    def ld(dst, src, tile, eng="sp", chan=None):
        S.op(eng, lambda e: e.dma_start(out=dst, in_=src), writes=[tile], chan=chan or cmisc)

    ld(g_sb[:], g_d, tg)
    ld(gb_sb[:], gb_d, tgb)
    ld(conv_sb[:], conv_d, tconv)
    ld(mn_sb[:], mn_d, tmn)
    ld(gn_sb[:], gn_d, tgn)
    ld(wsT32[:], wsT_d, tws32)
    ld(bs_sb[:], bs_d, tbs)
    ld(ident[:], ident_d, tid)
    ld(mask[:], mask_d, tmask)
    ones_r, tones = emit_consts(C)
    S.op("dve", lambda e: e.memset(ones32[:], 1.0), writes=[tones32])
    S.op("dve", lambda e: e.tensor_copy(out=identb[:], in_=ident[:]), reads=[tid], writes=[tidb])
    for g in range(8):
        S.op("dve", lambda e, g=g: e.tensor_tensor(out=wsT[:, g, :], in0=wsT32[:, g, :], in1=mask[:], op=ALU.mult),
             reads=[tws32, tmask], writes=[tws])
    S.op("pool", lambda e: e.dma_start(out=W[:, :, 0:1024], in_=wqk_d), writes=[tW], chan=cW)
    S.op("pool", lambda e: e.dma_start(out=W[:, :, 1024:2052], in_=wvog_d), writes=[tW], chan=cW)
    tssq = tpsS
    S.op("dve", lambda e: e.memset(CT[:], 0.0), writes=tCT_h)
    S.op("dve", lambda e: e.memset(mprev[:], 0.0), writes=[tmprev])
    for k in range(2):
        S.op("dve", lambda e, k=k: e.memset(vaug[k][:], 1.0), writes=[tvaug[k]])
    for m in range(8):
        S.op("dve", lambda e, m=m: e.memset(qkpre[:, m, 0:3], 0.0), writes=[tqkpre[m]])

    def norm_block(src_v, tsl):
        CG = 4
        for c0 in range(0, DC, CG):
            S.op("sp", lambda e, c0=c0: e.dma_start(out=h_sb[:, c0:c0 + CG, :], in_=src_v[:, c0:c0 + CG, tsl]),
                 writes=[th[c][0] for c in range(c0, c0 + CG)], chan=ch_in)
        emit_rmsnorm_T(C, h_sb, th, hn, thn, g_sb, tg, ones_r, tones, DC, TB, sq, tsq, ps_ssq, tssq, rstd, trstd, D)

    all_hn = [thn[c][0] for c in range(DC)]
    nA = 0
    nB = 0
    for tb in range(NBLK):
        norm_block(hTv, slice(tb * TB, (tb + 1) * TB))
        for m in range(8):
            b = nA % 2
            nA += 1
            for c in range(DC):
                S.op("pe", lambda e, c=c, m=m, b=b: e.matmul(psA[b][:], W[:, c, m * 128:(m + 1) * 128], hn[:, c, :],
                                                          start=(c == 0), stop=(c == DC - 1)),
                     reads=[tW, thn[c][0]], writes=[tpsA[b]])
            S.op("act", lambda e, m=m, b=b: e.copy(out=qkpre[:, m, 3:3 + TB], in_=psA[b][:]), reads=[tpsA[b]], writes=[tqkpre[m]])
            S.op("dve", lambda e, m=m: e.tensor_scalar(out=acc[:], in0=qkpre[:, m, 0:TB], scalar1=conv_sb[:, m, 0:1], scalar2=None,
                                                       op0=ALU.mult), reads=[tqkpre[m], tconv], writes=[tacc])
            for k in range(1, 4):
                S.op("dve", lambda e, m=m, k=k: e.scalar_tensor_tensor(out=acc[:], in0=qkpre[:, m, k:k + TB], scalar=conv_sb[:, m, k:k + 1],
                                                                        in1=acc[:], op0=ALU.mult, op1=ALU.add),
                     reads=[tqkpre[m], tconv, tacc], writes=[tacc])
            S.op("act", lambda e, m=m: e.activation(out=qkT[:, m, :], in_=acc[:], func=AF.Silu), reads=[tacc], writes=[tqkT[m]])
            S.op("dve", lambda e, m=m: e.tensor_copy(out=qkpre[:, m, 0:3], in_=qkpre[:, m, TB:TB + 3]), reads=[tqkpre[m]], writes=[tqkpre[m]])
        def proj_chunk(j):
            nonlocal nB
            jsl = slice(j * 128, (j + 1) * 128)
            kk = j % 2
            b = nB % 2
            nB += 1
            for c in range(DC):
                S.op("pe", lambda e, c=c, b=b, jsl=jsl: e.matmul(psB[b][:], hn[:, c, jsl], W[:, c, 1024:1536], start=(c == 0), stop=(c == DC - 1)),
                     reads=[tW, thn[c][0]], writes=[tpsB[b]])
            for h in range(2):
                S.op("act", lambda e, h=h, b=b, kk=kk: e.copy(out=vaug[kk][:, h, 0:256], in_=psB[b][:, h * 256:(h + 1) * 256]),
                     reads=[tpsB[b]], writes=[tvaug[kk]])
            b = nB % 2
            nB += 1
            for c in range(DC):
                S.op("pe", lambda e, c=c, b=b, jsl=jsl: e.matmul(psB[b][:], hn[:, c, jsl], W[:, c, 1536:2048], start=(c == 0), stop=(c == DC - 1)),
                     reads=[tW, thn[c][0]], writes=[tpsB[b]])
            S.op("act", lambda e, b=b, kk=kk: e.activation(out=so[kk][:], in_=psB[b][:], func=AF.Sigmoid), reads=[tpsB[b]], writes=[tso[kk]])
            for c in range(DC):
                S.op("pe", lambda e, c=c, jsl=jsl: e.matmul(psS[:, 0:4], hn[:, c, jsl], W[:, c, 2048:2052], start=(c == 0), stop=(c == DC - 1)),
                     reads=[tW, thn[c][0]], writes=[tpsS])
            S.op("dve", lambda e, kk=kk: e.tensor_tensor(out=gts[kk][:], in0=psS[:, 0:4], in1=gb_sb[:], op=ALU.add),
                 reads=[tpsS, tgb], writes=[tgts[kk]])

        proj_chunk(0)
        for j in range(TB // 128):
            jsl = slice(j * 128, (j + 1) * 128)
            kk = j % 2
            S.op("act", lambda e, kk=kk: e.activation(out=lf[:], in_=gts[kk][:, 2:4], func=AF.Exp, scale=-1.0), reads=[tgts[kk]], writes=[tlf])
            S.op("act", lambda e: e.activation(out=lf[:], in_=lf[:], func=AF.Ln, bias=ones32[:, 0:1], scale=1.0), reads=[tlf, tones32], writes=[tlf])
            S.op("dve", lambda e: e.tensor_scalar(out=lf[:], in0=lf[:], scalar1=-1.0, scalar2=None, op0=ALU.mult), reads=[tlf], writes=[tlf])
            S.op("pe", lambda e: e.matmul(psS[:, 8:10], mask[:], lf[:], start=True, stop=True), reads=[tmask, tlf], writes=[tpsS])
            S.op("pe", lambda e: e.matmul(psS[:, 16:18], ones32[:], lf[:], start=True, stop=True), reads=[tones32, tlf], writes=[tpsS])
            S.op("dve", lambda e: e.tensor_copy(out=bcol[:], in_=psS[:, 8:10]), reads=[tpsS], writes=[tbcol])
            S.op("dve", lambda e, kk=kk: e.tensor_tensor(out=acol[:], in0=gts[kk][:, 0:2], in1=bcol[:], op=ALU.subtract),
                 reads=[tgts[kk], tbcol], writes=[tacol])
            for h in range(2):
                S.op("dve", lambda e, h=h: e.tensor_copy(out=a_bc[:, h, :], in_=acol[:, h:h + 1].to_broadcast([128, 128])),
                     reads=[tacol], writes=[tabc])
            for h in range(2):
                S.op("pe", lambda e, h=h: e.matmul(psS[:, 128 + h * 128:256 + h * 128], a_bc[:, h, :], ident[:], start=True, stop=True),
                     reads=[tabc, tid], writes=[tpsS])
            S.op("dve", lambda e: e.tensor_reduce(out=amax[:], in_=psS[:, 128:384].rearrange("p (h s) -> p h s", h=2), axis=AX.X, op=ALU.max),
                 reads=[tpsS], writes=[tamax])
            S.op("dve", lambda e: e.tensor_tensor(out=Mx[:], in0=amax[:], in1=mprev[:], op=ALU.max), reads=[tamax, tmprev], writes=[tMx])
            S.op("dve", lambda e: e.tensor_tensor(out=tmp2[:], in0=mprev[:], in1=Mx[:], op=ALU.subtract), reads=[tmprev, tMx], writes=[ttmp2])
            S.op("act", lambda e: e.activation(out=wprev[:], in_=tmp2[:], func=AF.Exp), reads=[ttmp2], writes=[twprev])
            S.op("dve", lambda e: e.tensor_tensor(out=tmp2[:], in0=acol[:], in1=Mx[:], op=ALU.subtract), reads=[tacol, tMx], writes=[ttmp2])
            S.op("act", lambda e: e.activation(out=ws[:], in_=tmp2[:], func=AF.Exp), reads=[ttmp2], writes=[tws_])
            S.op("dve", lambda e: e.tensor_tensor(out=tmp2[:], in0=bcol[:], in1=Mx[:], op=ALU.add), reads=[tbcol, tMx], writes=[ttmp2])
            S.op("act", lambda e: e.activation(out=thr[:], in_=tmp2[:], func=AF.Exp, scale=-1.0), reads=[ttmp2], writes=[tthr])
            S.op("dve", lambda e: e.tensor_tensor(out=mprev[:], in0=psS[:, 16:18], in1=Mx[:], op=ALU.add), reads=[tpsS, tMx], writes=[tmprev])
            if debug:
                for i_, (src_, tl_) in enumerate(((lf, tlf), (acol, tacol), (bcol, tbcol), (amax, tamax), (Mx, tMx), (wprev, twprev),
                                                  (ws, tws_), (thr, tthr))):
                    S.op("dve", lambda e, i_=i_, src_=src_: e.tensor_copy(out=dbg_sb[:, 2 * i_:2 * i_ + 2], in_=src_[:]),
                         reads=[tl_], writes=[tdbg])
                cidx = tb * (TB // 128) + j
                S.op("sp", lambda e, cidx=cidx: e.dma_start(out=dbg[cidx], in_=dbg_sb[:]), reads=[tdbg], chan=cdbg)
            QI = [[h * 2, h * 2 + 1] for h in range(2)]
            KI = [[4 + h * 2, 4 + h * 2 + 1] for h in range(2)]
            for h in range(2):
                for dc in range(2):
                    S.op("pe", lambda e, dc=dc, h=h, jsl=jsl: e.matmul(psA[0][:, h * 128:(h + 1) * 128], qkT[:, KI[h][dc], jsl], qkT[:, QI[h][dc], jsl],
                                                                    start=(dc == 0), stop=(dc == 1)),
                         reads=[tqkT[KI[h][dc]], tqkT[QI[h][dc]]], writes=[tpsA[0]])
            for h in range(2):
                for dc in range(2):
                    S.op("pe", lambda e, dc=dc, h=h, jsl=jsl: e.transpose(psT[:, h * 256 + dc * 128:h * 256 + (dc + 1) * 128], qkT[:, KI[h][dc], jsl], identb[:]),
                         reads=[tqkT[KI[h][dc]], tidb], writes=[tpsT])
            if j + 1 < TB // 128:
                proj_chunk(j + 1)
            for h in range(2):
                S.op("dve", lambda e, h=h: e.scalar_tensor_tensor(out=sTm_h[h][:], in0=psA[0][:, h * 128:(h + 1) * 128], scalar=DH ** -0.5, in1=mask[:],
                                                                  op0=ALU.mult, op1=ALU.mult), reads=[tpsA[0], tmask], writes=[tsTm_h[h]])
            for h in range(2):
                S.op("act", lambda e, h=h: e.copy(out=ktok_h[h][:], in_=psT[:, h * 256:(h + 1) * 256]), reads=[tpsT], writes=[tktok_h[h]])
            for h in range(2):
                S.op("dve", lambda e, h=h, kk=kk: e.tensor_scalar(out=vw_h[h][:], in0=vaug[kk][:, h, :], scalar1=ws[:, h:h + 1], scalar2=None, op0=ALU.mult),
                     reads=[tvaug[kk], tws_], writes=[tvw_h[h]])
            for h in range(2):
                S.op("dve", lambda e, h=h: e.tensor_scalar(out=CT[:, h, :, :], in0=CT[:, h, :, :], scalar1=wprev[:, h:h + 1], scalar2=None, op0=ALU.mult),
                     reads=[tCT_h[h], twprev], writes=[tCT_h[h]])
            for h in range(2):
                S.op("act", lambda e, h=h: e.copy(out=CTb_h[h][:], in_=CT[:, h, :, :]), reads=[tCT_h[h]], writes=[tCTb_h[h]])
            for h in range(2):
                S.op("pe", lambda e, h=h: e.matmul(psN_h[h][:, 0:257], sTm_h[h][:], vw_h[h][:], start=True, stop=False),
                     reads=[tsTm_h[h], tvw_h[h]], writes=[tpsN_h[h]])
                for dc in range(2):
                    S.op("pe", lambda e, dc=dc, h=h, jsl=jsl: e.matmul(psN_h[h][:, 0:257], qkT[:, QI[h][dc], jsl], CTb_h[h][:, dc, :], start=False, stop=(dc == 1)),
                         reads=[tqkT[QI[h][dc]], tCTb_h[h]], writes=[tpsN_h[h]])
            for h in range(2):
                S.op("act", lambda e, h=h: e.activation(out=den_h[h][:], in_=psN_h[h][:, 256:257], func=AF.Abs), reads=[tpsN_h[h]], writes=[tden_h[h]])
            for h in range(2):
                S.op("dve", lambda e, h=h: e.tensor_tensor(out=den_h[h][:], in0=den_h[h][:], in1=thr[:, h:h + 1], op=ALU.max),
                     reads=[tden_h[h], tthr], writes=[tden_h[h]])
            for h in range(2):
                S.op("dve", lambda e, h=h: e.reciprocal(out=den_h[h][:], in_=den_h[h][:]), reads=[tden_h[h]], writes=[tden_h[h]])
            for h in range(2):
                S.op("dve", lambda e, h=h: e.tensor_scalar(out=hh_h[h][:], in0=psN_h[h][:, 0:256], scalar1=den_h[h][:, 0:1], scalar2=None, op0=ALU.mult),
                     reads=[tpsN_h[h], tden_h[h]], writes=[thh_h[h]])
            for h in range(2):
                S.op("dve", lambda e, h=h: e.memset(ssq1[:, h:h + 1], 0.0), writes=[tssq1_h[h]])
            for h in range(2):
                S.op("act", lambda e, h=h: e.activation(out=junk[:], in_=hh_h[h][:], func=AF.Square, accum_out=ssq1[:, h:h + 1]),
                     reads=[thh_h[h]], writes=[tjunk, tssq1_h[h]])
            for h in range(2):
                S.op("act", lambda e, h=h: e.activation(out=ssq1[:, h:h + 1], in_=ssq1[:, h:h + 1], func=AF.Sqrt, bias=EPSB[0][:, 0:1], scale=1.0 / DH),
                     reads=[tssq1_h[h], EPSB[1]], writes=[tssq1_h[h]])
            for h in range(2):
                S.op("dve", lambda e, h=h: e.reciprocal(out=ssq1[:, h:h + 1], in_=ssq1[:, h:h + 1]), reads=[tssq1_h[h]], writes=[tssq1_h[h]])
            for h in range(2):
                S.op("dve", lambda e, h=h, kk=kk: e.scalar_tensor_tensor(out=ybt[kk][:, h * 256:(h + 1) * 256], in0=hh_h[h][:], scalar=ssq1[:, h:h + 1],
                                                                          in1=mn_sb[:, h * 256:(h + 1) * 256], op0=ALU.mult, op1=ALU.mult),
                     reads=[thh_h[h], tssq1_h[h], tmn], writes=[tybt_h[kk][h]])
            for h in range(2):
                S.op("dve", lambda e, h=h, kk=kk: e.tensor_tensor(out=ybt[kk][:, h * 256:(h + 1) * 256], in0=ybt[kk][:, h * 256:(h + 1) * 256],
                                                                  in1=so[kk][:, h * 256:(h + 1) * 256], op=ALU.mult),
                     reads=[tybt_h[kk][h], tso[kk]], writes=[tybt_h[kk][h]])
            for h in range(2):
                for dc in range(2):
                    S.op("pe", lambda e, dc=dc, h=h: e.matmul(psA[1][:, 0:257], ktok_h[h][:, dc * 128:(dc + 1) * 128], vw_h[h][:], start=True, stop=True),
                         reads=[tktok_h[h], tvw_h[h]], writes=[tpsA[1]])
                    S.op("dve", lambda e, dc=dc, h=h: e.scalar_tensor_tensor(out=CT[:, h, dc, :], in0=psA[1][:, 0:257], scalar=DH ** -0.5,
                                                                              in1=CT[:, h, dc, :], op0=ALU.mult, op1=ALU.add),
                         reads=[tCT_h[h], tpsA[1]], writes=[tCT_h[h]])
            tok0 = tb * TB + j * 128
            S.op("sp", lambda e, kk=kk, tok0=tok0: e.dma_start(out=yb[tok0:tok0 + 128, :], in_=ybt[kk][:]), reads=tybt_h[kk], chan=cyb[kk])

    S.op("pool", lambda e: e.dma_start(out=W[:, :, 0:2048], in_=wz_d), writes=[tW], chan=cW)
    nj = 0
    for tb in range(SH // TB):
        norm_block(hThv, slice(tb * TB, (tb + 1) * TB))
        for j in range(TB // 128):
            jsl = slice(j * 128, (j + 1) * 128)
            kk = nj % 2
            nj += 1
            for cb in range(4):
                b = cb % 2
                for c in range(DC):
                    S.op("pe", lambda e, c=c, b=b, cb=cb, jsl=jsl: e.matmul(psA[b][:], hn[:, c, jsl], W[:, c, cb * 512:(cb + 1) * 512],
                                                                        start=(c == 0), stop=(c == DC - 1)),
                         reads=[tW, thn[c][0]], writes=[tpsA[b]])
                dst = u_sb if cb < 2 else v_sb
                tdst = tu if cb < 2 else tv
                csl = slice((cb % 2) * 512, (cb % 2 + 1) * 512)
                emit_gelu(S, dst[:, csl], psA[b][:], t1[:], t2[:], [tpsA[b]], [tdst], tt1, tt2)
            S.op("dve", lambda e: e.memset(ssq1[:], 0.0), writes=[tssq1] + tssq1_h)
            for q in range(2):
                S.op("act", lambda e, q=q: e.activation(out=t1[:], in_=v_sb[:, q * 512:(q + 1) * 512], func=AF.Square, accum_out=ssq1[:, q:q + 1]),
                     reads=[tv], writes=[tt1, tssq1])
            S.op("dve", lambda e: e.tensor_tensor(out=ssq1[:, 0:1], in0=ssq1[:, 0:1], in1=ssq1[:, 1:2], op=ALU.add), reads=[tssq1], writes=[tssq1])
            S.op("act", lambda e: e.activation(out=ssq1[:, 0:1], in_=ssq1[:, 0:1], func=AF.Sqrt, bias=EPSB[0][:, 0:1], scale=1.0 / 1024),
                 reads=[tssq1, EPSB[1]], writes=[tssq1])
            S.op("dve", lambda e: e.reciprocal(out=ssq1[:, 0:1], in_=ssq1[:, 0:1]), reads=[tssq1], writes=[tssq1])
            S.op("dve", lambda e: e.scalar_tensor_tensor(out=vn[:], in0=v_sb[:], scalar=ssq1[:, 0:1], in1=gn_sb[:], op0=ALU.mult, op1=ALU.mult),
                 reads=[tv, tssq1, tgn], writes=[tvn])
            for g in range(8):
                b = g // 4
                S.op("pe", lambda e, g=g, b=b: e.matmul(psB[b][:, (g % 4) * 128:(g % 4 + 1) * 128], wsT[:, g, :], vn[:, g * 128:(g + 1) * 128],
                                                    start=True, stop=True), reads=[tws, tvn], writes=[tpsB[b]])
            for g in range(8):
                b = g // 4
                S.op("dve", lambda e, g=g, b=b, kk=kk: e.scalar_tensor_tensor(out=yat[kk][:, g * 128:(g + 1) * 128],
                                                                               in0=psB[b][:, (g % 4) * 128:(g % 4 + 1) * 128],
                                                                               scalar=bs_sb[:, g:g + 1], in1=u_sb[:, g * 128:(g + 1) * 128],
                                                                               op0=ALU.add, op1=ALU.mult),
                     reads=[tpsB[b], tbs, tu], writes=[tyat[kk]])
            tok0 = tb * TB + j * 128
            S.op("sp", lambda e, kk=kk, tok0=tok0: e.dma_start(out=ya[tok0:tok0 + 128, :], in_=yat[kk][:]), reads=[tyat[kk]], chan=cya[kk])
    S.emit(final_waits=cyb + cya)
    C.st.close()
    return nc


NEG = -1.0e30


def nsa_tables(Sq):
    NT = Sq // 128
    NCMP = (Sq - 32) // 16 + 1
    CH = (NCMP + 127) // 128
    NSL = Sq // 64
    half = 16
    inv = 1.0 / (500000.0 ** (np.arange(half, dtype=np.float32) / half))
    ang = np.arange(Sq, dtype=np.float32)[None, :] * inv[:, None]
    cos = np.ones((128, Sq), np.float32)
    sin = np.zeros((128, Sq), np.float32)
    cos[0:16] = np.cos(ang); cos[16:32] = np.cos(ang)
    sin[0:16] = np.sin(ang); sin[16:32] = np.sin(ang)
    sc = np.float32(128 ** -0.5)
    psw = np.zeros((128, 128), np.float32)
    for d in range(16):
        psw[d + 16, d] = -1.0
        psw[d, d + 16] = 1.0
    k = np.arange(128)[:, None]
    q = np.arange(128)[None, :]
    causal = np.where(k > q, NEG, 0.0).astype(np.float32)
    winneg = np.where(k <= q, NEG, 0.0).astype(np.float32)
    cm = np.zeros((NT, 128, CH, 128), np.float32)
    for T in range(NT):
        for ch in range(CH):
            c = ch * 128 + np.arange(128)[:, None]
            t = T * 128 + np.arange(128)[None, :]
            cm[T, :, ch, :] = np.where((16 * c + 31 > t) | (c >= NCMP), NEG, 0.0)
    E = np.zeros((64, NT, 128), np.float32)
    for kc in range(NT):
        for kk in range(128):
            E[(kc * 128 + kk) // 64, kc, kk] = 1.0
    fpos = np.zeros((NT, 128, 64), np.float32)
    fneg = np.full((NT, 128, 64), 1.0e30, np.float32)
    j = np.arange(64)[None, :]
    for T in range(NT):
        cur = ((T * 128 + np.arange(128)) // 64)[:, None]
        fp = np.zeros((128, 64), np.float32)
        fp = np.where(j == cur - 1, 1.0e9, fp)
        fp = np.where(j == cur, 2.0e9, fp)
        fp = np.where(j == 0, 3.0e9, fp)
        fpos[T] = fp
        fneg[T] = np.where((j > cur) | (j >= NSL), -1.0e9, 1.0e30)
    ci = np.arange(CH * 128)[:, None] * 16
    sj = np.arange(64)[None, :] * 64
    ov = np.clip(np.minimum(ci + 32, sj + 64) - np.maximum(ci, sj), 0, None).astype(np.float32) / 32.0
    ov[NCMP:] = 0.0
    ov = np.ascontiguousarray(ov.reshape(CH, 128, 64).transpose(1, 0, 2))
    return dict(cosq=cos * sc, sinq=sin * sc, cosk=cos, sink=sin, psw=psw, ident=np.eye(128, dtype=np.float32),
                causal=causal, winneg=winneg, cmpmask=cm, E=E, fpos=fpos, fneg=fneg, ov=ov)


def build_nsa(Sq, D=2048, stop=0, branches=(0, 1, 2)):
    C = Ctx()
    nc, S = C.nc, C.S
    DC = D // 128
    TB = 256
    TPB = TB // 128
    NBLK = Sq // TB
    NT = Sq // 128
    NCMP = (Sq - 32) // 16 + 1
    CH = (NCMP + 127) // 128
    T_ = Tile

    hT = C.din("hT", [D, Sq])
    g_d = C.din("g", [128, DC])
    wkv_d = C.din("wkv", [128, DC, 1536])
    wq_d = C.din("wq", [128, DC, 1024])
    wgt_d = C.din("wgt", [128, DC, 24])
    w1_d = C.din("w1", [128, 2, 32, 128])
    w2_d = C.din("w2", [128, 2, 128])
    pos_d = C.din("posT", [128, 2, 32])
    tabs = {}
    for nm, shp in (("cosq", [128, Sq]), ("sinq", [128, Sq]), ("cosk", [128, Sq]), ("sink", [128, Sq]), ("psw", [128, 128]),
                    ("ident", [128, 128]), ("causal", [128, 128]), ("winneg", [128, 128]), ("cmpmask", [NT, 128, CH, 128]),
                    ("E", [64, NT, 128]), ("fpos", [NT, 128, 64]), ("fneg", [NT, 128, 64]), ("ov", [128, CH, 64])):
        tabs[nm] = C.din(nm, shp)
    y_d = C.dout("y", [Sq, 1024])
    hTv = hT.rearrange("(c p) t -> p c t", p=128)

    W = C.sb([128, DC, 1024], BF16, "W")
    Wgt = C.sb([128, DC, 24], BF16, "Wgt")
    WT = C.sb([128, DC, 512], BF16, "WT")
    tWT = Tile()
    tWgt = Tile()
    h_sb = C.sb([128, DC, TB], BF16, "h")
    hn = C.sb([128, DC, TB], BF16, "hn")
    sq = [C.sb([128, 256], F32R, "sq") for _ in range(2)]
    rstd = C.sb([128, 256], F32, "rstd")
    g_sb = C.sb([128, DC], F32, "g")
    kTs = C.sb([128, 2, 2, Sq], BF16, "kTs")
    cmpin = C.sb([128, 2, 2, Sq], BF16, "cmpin")
    vsel = C.sb([128, NT, 2, 129], BF16, "vsel")
    vwin = C.sb([128, NT, 2, 129], BF16, "vwin")
    kcmpT = C.sb([128, 2, CH * 128], BF16, "kcmpT")
    vcmp = C.sb([128, 2, CH, 193], BF16, "vcmp")
    rope_c = C.sb([128, TB], F32, "ropec")
    rope_s = C.sb([128, TB], F32, "ropes")
    psw = C.sb([128, 128], BF16, "psw")
    identb = C.sb([128, 128], BF16, "identb")
    ident = C.sb([128, 128], F32, "ident")
    causal = C.sb([128, 4, 128], BF16, "causal")
    winneg = C.sb([128, 4, 128], BF16, "winneg")
    Eb = C.sb([64, NT, 128], BF16, "E")
    xb = C.sb([128, TB], BF16, "xb")
    xc = C.sb([128, TB], F32, "xc")
    xs = C.sb([128, TB], F32, "xs")
    kmax2 = C.sb([128, 6], F32, "kmax2")
    kred = C.sb([128, 1], F32, "kred")
    ones32 = C.sb([128, 128], F32, "ones32")
    sqf = C.sb([128, 256], F32, "sqf")
    tsqf = Tile()

    psB = [C.ps(name="B") for _ in range(3)]
    psA = psB
    psO = [C.ps(name="O") for _ in range(4)]
    psS = C.ps(name="S")
    ps_ssq = psS

    tW, tg = T_(), T_()
    th = [[T_()] for _ in range(DC)]
    thn = [[T_()] for _ in range(DC)]
    tsq = [T_(), T_()]
    trstd = T_()
    tkTs = [[[T_() for _ in range(NBLK)] for _ in range(2)] for _ in range(2)]
    tcmpin = [[T_() for _ in range(2)] for _ in range(2)]
    tvsel = [T_() for _ in range(NT)]
    tvwin = [T_() for _ in range(NT)]
    tkcmpT, tvcmp = T_(), T_()
    trope, tpsw, tidb, tid, tcausal, twinneg, tE = [T_() for _ in range(7)]
    txb, txc, txs, tkmax2, tkred, tones32 = [T_() for _ in range(6)]
    tpsB = [T_(psum=True), T_(psum=True), T_(psum=True)]
    tpsA = tpsB
    tpsO = [T_(psum=True) for _ in range(4)]
    tpsS = T_(psum=True)
    tssq = tpsS

    cmisc = S.chan()
    cW = S.chan()
    ch_in = S.chan()
    crope = S.chan()

    S.op("sp", lambda e: e.dma_start(out=g_sb[:], in_=g_d), writes=[tg], chan=cmisc)
    S.op("sp", lambda e: e.dma_start(out=ident[:], in_=tabs["ident"]), writes=[tid], chan=cmisc)
    for dst, nm, tl in ((psw, "psw", tpsw), (identb, "ident", tidb), (Eb, "E", tE)):
        S.op("pool", lambda e, dst=dst, nm=nm: e.dma_start(out=dst[:], in_=tabs[nm]), writes=[tl], chan=cmisc)
    for dst, nm, tl in ((causal, "causal", tcausal), (winneg, "winneg", twinneg)):
        for r4 in range(4):
            S.op("pool", lambda e, dst=dst, nm=nm, r4=r4: e.dma_start(out=dst[:, r4, :], in_=tabs[nm]), writes=[tl], chan=cmisc)
    ones_r, tones = emit_consts(C)
    S.op("dve", lambda e: e.memset(ones32[:], 1.0), writes=[tones32])
    S.op("dve", lambda e: e.memset(kmax2[:], 0.0), writes=[tkmax2])
    S.op("dve", lambda e: e.memset(vsel[:], 1.0), writes=tvsel)
    S.op("dve", lambda e: e.memset(vwin[:], 1.0), writes=tvwin)
    S.op("dve", lambda e: e.memset(vcmp[:], 1.0), writes=[tvcmp])
    S.op("dve", lambda e: e.memset(kcmpT[:], 0.0), writes=[tkcmpT])
    S.op("pool", lambda e: e.dma_start(out=Wgt[:], in_=wgt_d), writes=[tWgt], chan=cmisc)

    class _Stop(Exception):
        pass

    def ckpt(v):
        if stop == v:
            S.op("sp", lambda e: e.dma_start(out=y_d[0:128, 0:128], in_=ident[:]), reads=[tid], chan=cmisc)
            S.emit(final_waits=[cmisc])
            S.op = lambda *a, **k: None
            S.emit = lambda *a, **k: None

    def norm_block(tsl):
        CG = 4
        for c0 in range(0, DC, CG):
            S.op("pool", lambda e, c0=c0: e.dma_start(out=h_sb[:, c0:c0 + CG, :], in_=hTv[:, c0:c0 + CG, tsl]),
                 writes=[th[c][0] for c in range(c0, c0 + CG)], chan=ch_in)
        emit_rmsnorm_T(C, h_sb, th, hn, thn, g_sb, tg, ones_r, tones, DC, TB, sq, tsq, ps_ssq, tssq, rstd, trstd, D)

    def load_rope(cn, sn, tsl):
        S.op("sp", lambda e: e.dma_start(out=rope_c[:], in_=tabs[cn][:, tsl]), writes=[trope], chan=crope)
        S.op("sp", lambda e: e.dma_start(out=rope_s[:], in_=tabs[sn][:, tsl]), writes=[trope], chan=crope)

    nA = [0]

    def proj_fm(col0):
        b = nA[0] % 2
        nA[0] += 1
        for c in range(DC):
            S.op("pe", lambda e, c=c, b=b: e.matmul(psA[b][:, 0:TB], W[:, c, col0:col0 + 128], hn[:, c, :], start=(c == 0), stop=(c == DC - 1)),
                 reads=[tW, thn[c][0]], writes=[tpsA[b]])
        return b

    def rope_to(b, dst_ap, dst_tiles, split=False):
        S.op("act", lambda e: e.copy(out=xb[:], in_=psA[b][:, 0:TB]), reads=[tpsA[b]], writes=[txb])
        ckpt(151)
        S.op("dve", lambda e: e.tensor_tensor(out=xc[:], in0=psA[b][:, 0:TB], in1=rope_c[:], op=ALU.mult), reads=[tpsA[b], trope, txb], writes=[txc])
        ckpt(152)
        bb = nA[0] % 2
        nA[0] += 1
        S.op("pe", lambda e: e.matmul(psA[bb][:, 0:TB], psw[:], xb[:], start=True, stop=True), reads=[tpsw, txb], writes=[tpsA[bb]])
        ckpt(153)
        S.op("dve", lambda e: e.tensor_tensor(out=xs[:], in0=psA[bb][:, 0:TB], in1=rope_s[:], op=ALU.mult), reads=[tpsA[bb], trope], writes=[txs])
        ckpt(154)
        if split:
            S.op("dve", lambda e: e.tensor_tensor(out=dst_ap, in0=xs[:].rearrange("p (j q) -> p j q", q=128),
                                                  in1=xc[:].rearrange("p (j q) -> p j q", q=128), op=ALU.add), reads=[txs, txc], writes=dst_tiles)
        else:
            S.op("dve", lambda e: e.tensor_tensor(out=dst_ap, in0=xs[:], in1=xc[:], op=ALU.add), reads=[txs, txc], writes=dst_tiles)

    def colnorm_max(src_ap, src_tiles, kcol, ncols=TB):
        S.op("act", lambda e: e.activation(out=sqf[:, 0:ncols], in_=src_ap, func=AF.Square), reads=src_tiles, writes=[tsqf])
        S.op("pe", lambda e: e.matmul(psS[:, 0:ncols], ones32[:], sqf[:, 0:ncols], start=True, stop=True), reads=[tsqf, tones32], writes=[tpsS])
        S.op("dve", lambda e: e.tensor_reduce(out=kred[:], in_=psS[:, 0:ncols], axis=AX.X, op=ALU.max), reads=[tpsS], writes=[tkred])
        S.op("dve", lambda e: e.tensor_tensor(out=kmax2[:, kcol:kcol + 1], in0=kmax2[:, kcol:kcol + 1], in1=kred[:], op=ALU.max),
             reads=[tkmax2, tkred], writes=[tkmax2])

    nB = 0
    try:
        ckpt(11)
    except _Stop:
        return nc
    for tb in range(NBLK):
        tsl = slice(tb * TB, (tb + 1) * TB)
        try:
            norm_block(tsl)
            ckpt(12)
            load_rope("cosk", "sink", tsl)
            if tb == 0:
                S.op("pool", lambda e: e.dma_start(out=W[:], in_=wkv_d[:, :, 0:1024]), writes=[tW], chan=cW)
                S.op("pool", lambda e: e.dma_start(out=WT[:], in_=wkv_d[:, :, 1024:1536]), writes=[tWT], chan=cW)
            ckpt(13)
        except _Stop:
            return nc
        for slot in range(6):
            br, gl = slot // 2, slot % 2
            b = proj_fm(slot * 128)
            try:
                ckpt(14)
            except _Stop:
                return nc
            if br == 0:
                rope_to(b, cmpin[:, 0, gl, tsl], [tcmpin[0][gl]])
                try:
                    ckpt(15)
                except _Stop:
                    return nc
            else:
                rope_to(b, kTs[:, br - 1, gl, tsl], [tkTs[br - 1][gl][tb]])
                colnorm_max(kTs[:, br - 1, gl, tsl], [tkTs[br - 1][gl][tb]], 2 + (br - 1) * 2 + gl)
                try:
                    ckpt(16)
                except _Stop:
                    return nc
        for gl in range(2):
            b = proj_fm((6 + gl) * 128)
            S.op("act", lambda e, b=b, gl=gl, tsl=tsl: e.copy(out=cmpin[:, 1, gl, tsl], in_=psA[b][:, 0:TB]), reads=[tpsA[b]], writes=[tcmpin[1][gl]])
        for j in range(TPB):
            Tq = tb * TPB + j
            jsl = slice(j * 128, (j + 1) * 128)
            b = nB % 2
            nB += 1
            for c in range(DC):
                S.op("pe", lambda e, c=c, b=b, jsl=jsl: e.matmul(psB[b][:], hn[:, c, jsl], WT[:, c, :], start=(c == 0), stop=(c == DC - 1)),
                     reads=[tWT, thn[c][0]], writes=[tpsB[b]])
            S.op("act", lambda e, b=b, Tq=Tq: e.copy(out=vsel[:, Tq, :, 0:128], in_=psB[b][:, 0:256].rearrange("p (g d) -> p g d", g=2)),
                 reads=[tpsB[b]], writes=[tvsel[Tq]])
            S.op("dve", lambda e, b=b, Tq=Tq: e.tensor_copy(out=vwin[:, Tq, :, 0:128], in_=psB[b][:, 256:512].rearrange("p (g d) -> p g d", g=2)),
                 reads=[tpsB[b]], writes=[tvwin[Tq]])

    if stop == 1:
        S.op("pool", lambda e: e.dma_start(out=y_d[0:128, 0:256], in_=h_sb[:, 0, :]), reads=[th[0][0]], chan=cmisc)
        S.emit(final_waits=[cmisc])
        C.st.close()
        return nc
    Wflat = W[:].rearrange("p a b -> p (a b)")
    w1 = Wflat[:, 0:8192].rearrange("p (k l o) -> p k l o", k=2, l=32)
    w2 = Wflat[:, 8192:8448].rearrange("p (k o) -> p k o", k=2)
    posb = Wflat[:, 8448:8512].rearrange("p (k l) -> p k l", k=2)
    S.op("pool", lambda e: e.dma_start(out=w1, in_=w1_d), writes=[tW], chan=cW)
    S.op("pool", lambda e: e.dma_start(out=w2, in_=w2_d), writes=[tW], chan=cW)
    S.op("pool", lambda e: e.dma_start(out=posb, in_=pos_d), writes=[tW], chan=cW)
    S.op("pool", lambda e: e.dma_start(out=vcmp[:, 0, :, 129:193], in_=tabs["ov"]), writes=[tvcmp], chan=cmisc)
    S.op("pool", lambda e: e.dma_start(out=vcmp[:, 1, :, 129:193], in_=tabs["ov"]), writes=[tvcmp], chan=cmisc)
    bias1 = C.sb([128, 2], F32, "bias1")
    xg, t1, t2 = rstd, xc, xs
    H1g = C.sb([128, CH * 128], BF16, "H1g")
    tbias1, tH1g = T_(), T_()
    txg, tt1, tt2 = trstd, txc, txs
    NCP = NCMP
    for kv in range(2):
        for l in range(32):
            S.op("pe", lambda e, kv=kv, l=l: e.matmul(psS[:, kv:kv + 1], w1[:, kv, l, :], posb[:, kv, l:l + 1], start=(l == 0), stop=(l == 31)),
                 reads=[tW], writes=[tpsS])
        S.op("dve", lambda e, kv=kv: e.tensor_copy(out=bias1[:, kv:kv + 1], in_=psS[:, kv:kv + 1]), reads=[tpsS], writes=[tbias1])
    S.op("dve", lambda e: e.memset(H1g[:], 0.0), writes=[tH1g])
    for kv in range(2):
        for gl in range(2):
            b = nA[0] % 2
            nA[0] += 1
            for l in range(32):
                S.op("pe", lambda e, kv=kv, gl=gl, l=l, b=b: e.matmul(psA[b][:, 0:NCP], w1[:, kv, l, :],
                                                                     cmpin[:, kv, gl, l:l + 16 * (NCP - 1) + 1:16],
                                                                     start=(l == 0), stop=(l == 31)),
                     reads=[tW, tcmpin[kv][gl]], writes=[tpsA[b]])
            for c0 in range(0, NCP, 256):
                n = min(256, NCP - c0)
                S.op("dve", lambda e, b=b, kv=kv, c0=c0, n=n: e.tensor_scalar(out=xg[:, 0:n], in0=psA[b][:, c0:c0 + n], scalar1=bias1[:, kv:kv + 1],
                                                                            scalar2=None, op0=ALU.add), reads=[tpsA[b], tbias1], writes=[txg])
                emit_gelu(S, H1g[:, c0:c0 + n], xg[:, 0:n], t1[:, 0:n], t2[:, 0:n], [txg], [tH1g], tt1, tt2)
            if kv == 0:
                bb = nA[0] % 2
                nA[0] += 1
                S.op("pe", lambda e, bb=bb: e.matmul(psA[bb][:, 0:NCP], w2[:, 0, :], H1g[:, 0:NCP], start=True, stop=True),
                     reads=[tW, tH1g], writes=[tpsA[bb]])
                S.op("act", lambda e, bb=bb, gl=gl: e.copy(out=kcmpT[:, gl, 0:NCP], in_=psA[bb][:, 0:NCP]), reads=[tpsA[bb]], writes=[tkcmpT])
                colnorm_max(kcmpT[:, gl, 0:NCP], [tkcmpT], gl, ncols=NCP)
            else:
                for ch in range(CH):
                    bb = nA[0] % 2
                    nA[0] += 1
                    S.op("pe", lambda e, bb=bb, ch=ch: e.matmul(psA[bb][:, 0:128], H1g[:, ch * 128:(ch + 1) * 128], w2[:, 1, :], start=True, stop=True),
                         reads=[tW, tH1g], writes=[tpsA[bb]])
                    S.op("act", lambda e, bb=bb, gl=gl, ch=ch: e.copy(out=vcmp[:, gl, ch, 0:128], in_=psA[bb][:, 0:128]),
                         reads=[tpsA[bb]], writes=[tvcmp])

    if stop == 2:
        S.op("pool", lambda e: e.dma_start(out=y_d[0:128, 0:256], in_=h_sb[:, 0, :]), reads=[th[0][0]], chan=cmisc)
        S.emit(final_waits=[cmisc])
        C.st.close()
        return nc
    S.op("pool", lambda e: e.dma_start(out=W[:], in_=wq_d), writes=[tW], chan=cW)
    qT = C.sb([128, TPB, 8, 128], BF16, "qT")
    tqT = [T_() for _ in range(8)]
    gsb = [C.sb([128, 24], F32, "gsb") for _ in range(2)]
    tgsb = [T_(), T_()]
    cmk = [C.sb([128, CH, 4, 128], BF16, "cmk") for _ in range(2)]
    tcmk = [T_(), T_()]
    ccmk = [S.chan(), S.chan()]
    fpt = [C.sb([128, 64], F32, "fpos") for _ in range(2)]
    fnt = [C.sb([128, 64], F32, "fneg") for _ in range(2)]
    tfp = [T_(), T_()]
    cfp = [S.chan(), S.chan()]
    P = [C.sb([128, 512], BF16, "P") for _ in range(3)]
    tP = [T_() for _ in range(3)]
    qsq = C.sb([128, 512], F32R, "qsq")
    tqsq = T_()
    mq2 = C.sb([128, 1], F32, "mq2")
    nbias = C.sb([128, 3], F32, "nbias")
    tmq2, tnbias = T_(), T_()
    rz = C.sb([128, 4], F32, "rz")
    coef = C.sb([128, 4], F32, "coef")
    trz, tcoef = T_(), T_()
    imp = C.sb([128, 64], F32, "imp")
    imp2 = C.sb([128, 64], F32, "imp2")
    mx8 = C.sb([128, 8], F32, "mx8")
    mx8b = C.sb([128, 8], F32, "mx8b")
    negblk = C.sb([128, 64], F32, "negblk")
    negblkT = C.sb([64, 4, 128], BF16, "negblkT")
    timp, timp2, tmx8, tmx8b, tnegblk, tnegblkT = [T_() for _ in range(6)]
    yt0 = C.sb([128, 1024], F32, "yt")
    yt = [yt0, yt0]
    tyth = [T_() for _ in range(8)]
    cy = [S.chan(), S.chan()]
    nP = [0]
    nSc = [0]

    def score(mm_list, br):
        b = nSc[0] % 3
        nSc[0] += 1
        n = len(mm_list)
        for i, (lhsT, rhs, rd) in enumerate(mm_list):
            S.op("pe", lambda e, lhsT=lhsT, rhs=rhs, i=i, b=b: e.matmul(psB[b][:], lhsT, rhs, start=(i == 0), stop=(i == n - 1)),
                 reads=rd, writes=[tpsB[b]])
        p = nP[0] % 3
        nP[0] += 1
        S.op("act", lambda e, b=b, p=p, br=br: e.activation(out=P[p][:], in_=psB[b][:], func=AF.Exp, bias=nbias[:, br:br + 1], scale=1.0),
             reads=[tpsB[b], tnbias], writes=[tP[p]])
        return p

    oset = [0]

    def pv(p, rhs_v, first, last, width):
        for r in range(4):
            bk = oset[0] * 2 + r // 2
            off = (r % 2) * 256
            S.op("pe", lambda e, r=r, p=p, bk=bk, off=off: e.matmul(psO[bk][:, off:off + width], P[p][:, r * 128:(r + 1) * 128], rhs_v[0],
                                                                  start=(first and r % 2 == 0), stop=last),
                 reads=[tP[p]] + rhs_v[1], writes=[tpsO[bk]])

    def evac(br, gl, kk, width, first_branch, with_imp=False):
        first_branch = (br == branches[0])
        bks = [oset[0] * 2 + r // 2 for r in range(4)]
        offs = [(r % 2) * 256 for r in range(4)]
        oset[0] ^= 1
        for r in range(4):
            S.op("dve", lambda e, r=r: e.tensor_scalar(out=rz[:, r:r + 1], in0=psO[bks[r]][:, offs[r] + 128:offs[r] + 129], scalar1=1.0e-30, scalar2=None, op0=ALU.add),
                 reads=[tpsO[bks[r]]], writes=[trz])
        S.op("dve", lambda e: e.reciprocal(out=rz[:], in_=rz[:]), reads=[trz], writes=[trz])
        gc0 = br * 8 + gl * 4
        S.op("dve", lambda e, gc0=gc0, kk=kk: e.tensor_tensor(out=coef[:], in0=rz[:], in1=gsb[kk][:, gc0:gc0 + 4], op=ALU.mult),
             reads=[trz, tgsb[kk]], writes=[tcoef])
        for r in range(4):
            o_ap = psO[bks[r]][:, offs[r]:offs[r] + 128]
            ysl = slice((gl * 4 + r) * 128, (gl * 4 + r + 1) * 128)
            if br not in branches:
                pass
            elif first_branch:
                S.op("dve", lambda e, r=r, kk=kk, ysl=ysl, o_ap=o_ap: e.tensor_scalar(out=yt[kk][:, ysl], in0=o_ap, scalar1=coef[:, r:r + 1], scalar2=None, op0=ALU.mult),
                     reads=[tpsO[bks[r]], tcoef], writes=[tyth[gl * 4 + r]])
            else:
                S.op("dve", lambda e, r=r, kk=kk, ysl=ysl, o_ap=o_ap: e.scalar_tensor_tensor(out=yt[kk][:, ysl], in0=o_ap, scalar=coef[:, r:r + 1], in1=yt[kk][:, ysl],
                                                                                         op0=ALU.mult, op1=ALU.add),
                     reads=[tpsO[bks[r]], tcoef, tyth[gl * 4 + r]], writes=[tyth[gl * 4 + r]])
        if with_imp:
            for r in range(4):
                i_ap = psO[bks[r]][:, offs[r] + 129:offs[r] + 193]
                if r == 0:
                    S.op("dve", lambda e, r=r, i_ap=i_ap: e.tensor_scalar(out=imp[:], in0=i_ap, scalar1=rz[:, r:r + 1], scalar2=None, op0=ALU.mult),
                         reads=[tpsO[bks[r]], trz], writes=[timp])
                else:
                    S.op("dve", lambda e, r=r, i_ap=i_ap: e.scalar_tensor_tensor(out=imp[:], in0=i_ap, scalar=rz[:, r:r + 1], in1=imp[:], op0=ALU.mult, op1=ALU.add),
                         reads=[tpsO[bks[r]], trz, timp], writes=[timp])

    ntile = 0
    for tb in range(NBLK):
        tsl = slice(tb * TB, (tb + 1) * TB)
        norm_block(tsl)
        load_rope("cosq", "sinq", tsl)
        for hd in range(8):
            b = proj_fm(hd * 128)
            rope_to(b, qT[:, :, hd, :], [tqT[hd]], split=True)
        for j in range(TPB):
            Tq = tb * TPB + j
            jsl = slice(j * 128, (j + 1) * 128)
            kk = ntile % 2
            ntile += 1
            for c in range(DC):
                S.op("pe", lambda e, c=c, jsl=jsl: e.matmul(psS[:, 0:24], hn[:, c, jsl], Wgt[:, c, :], start=(c == 0), stop=(c == DC - 1)),
                     reads=[tWgt, thn[c][0]], writes=[tpsS])
            S.op("act", lambda e, kk=kk: e.activation(out=gsb[kk][:], in_=psS[:, 0:24], func=AF.Sigmoid), reads=[tpsS], writes=[tgsb[kk]])
            for r4 in range(4):
                S.op("pool", lambda e, kk=kk, Tq=Tq, r4=r4: e.dma_start(out=cmk[kk][:, :, r4, :], in_=tabs["cmpmask"][Tq]), writes=[tcmk[kk]], chan=ccmk[kk])
            S.op("sp", lambda e, kk=kk, Tq=Tq: e.dma_start(out=fpt[kk][:], in_=tabs["fpos"][Tq]), writes=[tfp[kk]], chan=cfp[kk])
            S.op("sp", lambda e, kk=kk, Tq=Tq: e.dma_start(out=fnt[kk][:], in_=tabs["fneg"][Tq]), writes=[tfp[kk]], chan=cfp[kk])
            for gl in range(2):
                qv = qT[:, j, gl * 4:(gl + 1) * 4, :]
                qrd = [tqT[gl * 4 + r] for r in range(4)]
                S.op("act", lambda e, qv=qv: e.activation(out=qsq[:].rearrange("p (r q) -> p r q", r=4), in_=qv, func=AF.Square), reads=qrd, writes=[tqsq])
                S.op("pe", lambda e: e.matmul(psS[:, 0:512], ones_r[:], qsq[:], start=True, stop=True), reads=[tqsq, tones], writes=[tpsS])
                S.op("dve", lambda e: e.tensor_reduce(out=mq2[:], in_=psS[:, 0:512], axis=AX.X, op=ALU.max), reads=[tpsS], writes=[tmq2])
                S.op("dve", lambda e, gl=gl: e.tensor_scalar(out=nbias[:], in0=kmax2[:, gl:gl + 5:2], scalar1=mq2[:, 0:1], scalar2=None, op0=ALU.mult),
                     reads=[tkmax2, tmq2], writes=[tnbias])
                S.op("act", lambda e: e.activation(out=nbias[:], in_=nbias[:], func=AF.Sqrt), reads=[tnbias], writes=[tnbias])
                S.op("dve", lambda e: e.tensor_scalar(out=nbias[:], in0=nbias[:], scalar1=-1.0, scalar2=None, op0=ALU.mult), reads=[tnbias], writes=[tnbias])
                chunks = []
                chs = [ch for ch in range(CH) if 16 * (ch * 128) + 31 <= Tq * 128 + 127]
                for ci, ch in enumerate(chs):
                    mm = [(kcmpT[:, gl, ch * 128:(ch + 1) * 128], qv, [tkcmpT] + qrd),
                          (identb[:], cmk[kk][:, ch, :, :], [tidb, tcmk[kk]])]
                    chunks.append((0, mm, (vcmp[:, gl, ch, :], [tvcmp]), ci == 0, ci == len(chs) - 1, 193))
                k0 = max(0, Tq - 4)
                for kc in range(k0, Tq + 1):
                    mm = [(kTs[:, 1, gl, kc * 128:(kc + 1) * 128], qv, [tkTs[1][gl][kc // TPB]] + qrd)]
                    if kc == Tq:
                        mm.append((identb[:], causal[:], [tidb, tcausal]))
                    if kc == Tq - 4:
                        mm.append((identb[:], winneg[:], [tidb, twinneg]))
                    chunks.append((2, mm, (vwin[:, kc, gl, :], [tvwin[kc]]), kc == k0, kc == Tq, 129))
                for kc in range(Tq + 1):
                    mm = [(kTs[:, 0, gl, kc * 128:(kc + 1) * 128], qv, [tkTs[0][gl][kc // TPB]] + qrd),
                          (Eb[:, kc, :], negblkT[:], [tE, tnegblkT])]
                    if kc == Tq:
                        mm.append((identb[:], causal[:], [tidb, tcausal]))
                    chunks.append((1, mm, (vsel[:, kc, gl, :], [tvsel[kc]]), kc == 0, kc == Tq, 129))

                def topk_dve(kk=kk):
                    S.op("dve", lambda e: e.tensor_tensor(out=imp[:], in0=imp[:], in1=fpt[kk][:], op=ALU.max), reads=[timp, tfp[kk]], writes=[timp])
                    S.op("dve", lambda e: e.tensor_tensor(out=imp[:], in0=imp[:], in1=fnt[kk][:], op=ALU.min), reads=[timp, tfp[kk]], writes=[timp])
                    S.op("dve", lambda e: e.max(out=mx8[:], in_=imp[:]), reads=[timp], writes=[tmx8])
                    S.op("dve", lambda e: e.match_replace(out=imp2[:], in_to_replace=mx8[:], in_values=imp[:], imm_value=-3.0e38),
                         reads=[timp, tmx8], writes=[timp2])
                    S.op("dve", lambda e: e.max(out=mx8b[:], in_=imp2[:]), reads=[timp2], writes=[tmx8b])
                    S.op("dve", lambda e: e.tensor_reduce(out=kred[:], in_=mx8b[:], axis=AX.X, op=ALU.min), reads=[tmx8b], writes=[tkred])
                    S.op("dve", lambda e: e.tensor_scalar(out=negblk[:], in0=imp[:], scalar1=kred[:, 0:1], scalar2=None, op0=ALU.is_ge),
                         reads=[timp, tkred], writes=[tnegblk])
                    S.op("dve", lambda e: e.tensor_scalar(out=negblk[:], in0=negblk[:], scalar1=1.0e30, scalar2=-1.0e30, op0=ALU.mult, op1=ALU.add),
                         reads=[tnegblk], writes=[tnegblk])

                def topk_T():
                    S.op("pe", lambda e: e.transpose(psS[0:64, 0:128], negblk[:], ident[:]), reads=[tnegblk, tid], writes=[tpsS])
                    S.op("act", lambda e: e.copy(out=negblkT[:], in_=psS[0:64, 0:128][:, None, :].to_broadcast([64, 4, 128])), reads=[tpsS], writes=[tnegblkT])

                nch = len(chunks)
                first_sel = next(i for i, c in enumerate(chunks) if c[0] == 1)

                def do_score(i):
                    if i == first_sel:
                        topk_T()
                    return score(chunks[i][1], chunks[i][0])

                cmp_last = max(i for i, c in enumerate(chunks) if c[0] == 0)
                pend = []
                nxt = [0]

                def fill(i):
                    while nxt[0] < nch and nxt[0] <= i + 2:
                        if nxt[0] == first_sel and i <= cmp_last:
                            break
                        pend.append(do_score(nxt[0]))
                        nxt[0] += 1

                fill(-1)
                for i in range(nch):
                    br_i, _, rhs_v, first, last, width = chunks[i]
                    fill(i)
                    pv(pend.pop(0), rhs_v, first, last, width)
                    if last:
                        evac(br_i, gl, kk, width, br_i == 0, with_imp=(br_i == 0))
                        if br_i == 0:
                            topk_dve()
            S.op("sp", lambda e, kk=kk, Tq=Tq: e.dma_start(out=y_d[Tq * 128:(Tq + 1) * 128, :], in_=yt[kk][:]), reads=tyth, chan=cy[kk])
    S.emit(final_waits=cy)
    C.st.close()
    return nc


def _lay(w):
    dc = w.shape[0] // 128
    return np.ascontiguousarray(w.reshape(dc, 128, -1).transpose(1, 0, 2))


def _gT(g):
    return np.ascontiguousarray(np.asarray(g, np.float32).reshape(-1, 128).T)


def _prep_ffn_w(Wg, Wu, Wd):
    D, FF = Wg.shape
    DC, FC = D // 128, FF // 128
    t = lambda W: np.ascontiguousarray(W.reshape(DC, 128, FC, 128).transpose(2, 1, 0, 3).reshape(FC, 128, DC * 128))
    wd = np.ascontiguousarray(Wd.reshape(FC, 128, DC, 128).transpose(2, 1, 0, 3).reshape(DC, 128, FC * 128))
    return t(Wg), t(Wu), wd


def _prep_wo(Wo):
    DC = Wo.shape[0] // 128
    DO = Wo.shape[1] // 128
    return np.ascontiguousarray(Wo.reshape(DC, 128, DO, 128).transpose(2, 1, 0, 3).reshape(DO, 128, DC * 128))


def _prep_ab(w_in, gate_b, g_norm, g_ws, g_bs, conv_w, m_norm, hp):
    o1, o2, o3, o4 = 2048, 4096, 5120, 6144
    wz = _lay(w_in[:, :o1])
    hs = slice(hp * 512, (hp + 1) * 512)
    wqk = _lay(np.concatenate([w_in[:, o1:o1 + 1024][:, hs], w_in[:, o1 + 1024:o2][:, hs]], 1))
    gi = w_in[:, o4:o4 + 4][:, hp * 2:hp * 2 + 2]
    gf = w_in[:, o4 + 4:o4 + 8][:, hp * 2:hp * 2 + 2]
    wvog = _lay(np.concatenate([w_in[:, o2:o3][:, hs], w_in[:, o3:o4][:, hs], gi, gf], 1))
    gb = np.concatenate([gate_b[0, hp * 2:hp * 2 + 2], gate_b[1, hp * 2:hp * 2 + 2]])
    gb = np.ascontiguousarray(np.broadcast_to(gb[None, :], (128, 4))).astype(np.float32)
    cc = np.concatenate([conv_w[:, :1024][:, hs], conv_w[:, 1024:][:, hs]], 1)
    conv = np.ascontiguousarray(cc.T.reshape(8, 128, 4).transpose(1, 0, 2))
    mn = np.ascontiguousarray(np.broadcast_to(m_norm[hs][None, :], (128, 512)))
    gn = np.ascontiguousarray(np.broadcast_to(g_norm[None, :], (128, 1024)))
    wsT = np.ascontiguousarray(g_ws.transpose(2, 0, 1))
    bs = np.ascontiguousarray(g_bs.T)
    ident = np.eye(128, dtype=np.float32)
    mask = np.triu(np.ones((128, 128), np.float32))
    return dict(wz=wz, wqk=wqk, wvog=wvog, gb=gb, conv=conv, mn=mn, gn=gn, wsT=wsT, bs=bs, ident=ident, mask=mask)


def _prep_nsa(w_in, cmp_pos, cmp_w1, cmp_w2, hp):
    wq = _lay(w_in[:, hp * 1024:(hp + 1) * 1024])

    def kvcol(br, kv, g):
        o = 2048 + ((br * 2 + kv) * 4 + g) * 128
        return w_in[:, o:o + 128]
    cols = []
    for br in range(3):
        for gl in range(2):
            cols.append(kvcol(br, 0, 2 * hp + gl))
    for gl in range(2):
        cols.append(kvcol(0, 1, 2 * hp + gl))
    for br in (1, 2):
        for gl in range(2):
            cols.append(kvcol(br, 1, 2 * hp + gl))
    wkv = _lay(np.concatenate(cols, 1))
    og = 2048 + 3072
    gc = []
    for br in range(3):
        for gl in range(2):
            g = 2 * hp + gl
            gc.append(w_in[:, og + br * 16 + g * 4: og + br * 16 + g * 4 + 4])
    wgt = _lay(np.concatenate(gc, 1))
    w1 = np.ascontiguousarray(cmp_w1.reshape(2, 32, 128, 128).transpose(2, 0, 1, 3))
    w2 = np.ascontiguousarray(cmp_w2.transpose(1, 0, 2))
    posT = np.ascontiguousarray(cmp_pos.transpose(2, 0, 1))
    return dict(wq=wq, wkv=wkv, wgt=wgt, w1=w1, w2=w2, posT=posT)


_PROGS = {}


def _prog(key, fn):
    if key not in _PROGS:
        _PROGS[key] = fn()
    return _PROGS[key]


def _run(nc, maps):
    res = run_bass_kernel_spmd(nc, maps, core_ids=list(range(8)))
    return res.results


def kernel(x, ffn_norm, ffn_w_gate, ffn_w_up, ffn_w_down, mix_norm, ab_w_in, mlstm_gate_bias,
           gmlp_norm, gmlp_w_s, gmlp_b_s, mlstm_conv, mlstm_norm, ab_w_out, nsa_w_in, nsa_cmp_pos,
           nsa_cmp_w1, nsa_cmp_w2, nsa_w_out, final_norm):
    f = lambda a: np.asarray(a, dtype=np.float32)
    x = f(x)
    B, Sq, D = x.shape
    FF = ffn_w_gate.shape[-1]
    NTC = B * Sq // 8
    hT = [np.ascontiguousarray(x[c // 2, (c % 2) * NTC:(c % 2 + 1) * NTC, :].T) for c in range(8)]

    def ffn(hT, layer, which, yT=None, wo=None, final=False, second=None):
        wg, wu, wd = _prep_ffn_w(f(ffn_w_gate[layer, which]), f(ffn_w_up[layer, which]), f(ffn_w_down[layer, which]))
        g = _gT(ffn_norm[layer, which])
        pre = yT is not None
        nst = 1 if second is None else 2
        nc = _prog(("ffn", pre, final, nst), lambda: build_ffn(D, FF, NTC, 1024, 2, pre=pre, final=final, nstage=nst))
        extra = {}
        if second is not None:
            l2, w2 = second
            wg2, wu2, wd2 = _prep_ffn_w(f(ffn_w_gate[l2, w2]), f(ffn_w_up[l2, w2]), f(ffn_w_down[l2, w2]))
            extra = {"g2": _gT(ffn_norm[l2, w2]), "wg2": wg2, "wu2": wu2, "wd2": wd2}
        maps = []
        for c in range(8):
            m = {"hT": hT[c], "g": g, "wg": wg, "wu": wu, "wd": wd}
            m.update(extra)
            if pre:
                m["yT"] = yT[c]
                m["wo"] = wo
            if final:
                m["gf"] = _gT(final_norm)
            maps.append(m)
        r = _run(nc, maps)
        return [r[c]["oT"] for c in range(8)]

    def full_seq(hT, b):
        return np.ascontiguousarray(np.concatenate([hT[2 * b], hT[2 * b + 1]], axis=1))

    hT = ffn(hT, 0, 0)
    nc = _prog(("ab",), lambda: build_ab(Sq, D))
    maps = []
    for c in range(8):
        b, hp = c // 2, c % 2
        m = _prep_ab(f(ab_w_in[0]), f(mlstm_gate_bias[0]), f(gmlp_norm[0]), f(gmlp_w_s[0]), f(gmlp_b_s[0]), f(mlstm_conv[0]),
                     f(mlstm_norm[0]), hp)
        m["hT"] = full_seq(hT, b)
        m["hTh"] = hT[c]
        m["g"] = _gT(mix_norm[0])
        maps.append(m)
    r = _run(nc, maps)
    yT = []
    for c in range(8):
        b, hp = c // 2, c % 2
        sl = slice(hp * NTC, (hp + 1) * NTC)
        ya = r[c]["ya"]
        yb = np.concatenate([r[2 * b]["yb"][sl], r[2 * b + 1]["yb"][sl]], axis=1)
        yT.append(np.ascontiguousarray(np.concatenate([ya, yb], axis=1).T))
    hT = ffn(hT, 0, 1, yT=yT, wo=_prep_wo(f(ab_w_out[0])), second=(1, 0))
    nc = _prog(("nsa",), lambda: build_nsa(Sq, D))
    tabs = nsa_tables(Sq)
    maps = []
    for c in range(8):
        b, hp = c // 2, c % 2
        m = _prep_nsa(f(nsa_w_in[0]), f(nsa_cmp_pos[0]), f(nsa_cmp_w1[0]), f(nsa_cmp_w2[0]), hp)
        m.update(tabs)
        m["hT"] = full_seq(hT, b)
        m["g"] = _gT(mix_norm[1])
        maps.append(m)
    r = _run(nc, maps)
    yT = []
    for c in range(8):
        b, hp = c // 2, c % 2
        sl = slice(hp * NTC, (hp + 1) * NTC)
        y = np.concatenate([r[2 * b]["y"][sl], r[2 * b + 1]["y"][sl]], axis=1)
        yT.append(np.ascontiguousarray(y.T))
    hT = ffn(hT, 1, 1, yT=yT, wo=_prep_wo(f(nsa_w_out[0])), final=True)
    out = np.empty((B, Sq, D), np.float32)
    for c in range(8):
        out[c // 2, (c % 2) * NTC:(c % 2 + 1) * NTC, :] = hT[c].T
    return out
```
